# Optimizing a Trainium2 kernel written in Bass

```python
import math
import jax, jax.numpy as jnp
from jax import lax
import numpy as np

D_MODEL = 1024
BATCH = 8
SEQ = 8192
DEPTH = 4
DEC_BATCH = 32
DEC_SEQ = 64
PAST_LEN = 2048

CHUNK = 64
HEAD_DIM = 64
A_HEADS = 4
A_BAND_CHUNKS = 8
A_REL_MAX = 128
A_REL_SIZE = (CHUNK - 1) + A_REL_MAX + 1
MLA_HEADS = 4
MLA_Q_RANK = 256
MLA_KV_RANK = 256
MLA_NOPE = 128
MLA_ROPE = 64
MLA_V = 128
SB_HEADS = 4
Q_BLOCK = 128
K_BLOCK = 128
Q_GROUPS = 8
BIG_POS = 2 ** 30
D_FF = 4 * D_MODEL
ROPE_THETA = 10000.0
EPS = 1e-6
NEG = -1e30
A_W = A_HEADS * HEAD_DIM
MLA_W = MLA_HEADS * MLA_V
SB_W = SB_HEADS * HEAD_DIM
MIX_W = A_W + MLA_W + SB_W
IN_COLS = 3 * A_W + MLA_Q_RANK + MLA_KV_RANK + MLA_ROPE + 3 * SB_W

kernel_name = "hybrid_chunkband_mla_stickbreak_stream_step"


def rmsnorm(x, g):
    xf = x.astype(jnp.float32)
    y = xf * lax.rsqrt(jnp.mean(xf * xf, axis=-1, keepdims=True) + EPS)
    return (y * g.astype(jnp.float32)).astype(x.dtype)


def rope(x, pos):
    half = x.shape[-1] // 2
    inv = ROPE_THETA ** (-jnp.arange(half, dtype=jnp.float32) / half)
    ang = pos.astype(jnp.float32)[:, None] * inv[None, :]
    shape = (1, pos.shape[0]) + (1,) * (x.ndim - 3) + (half,)
    cos, sin = jnp.cos(ang).reshape(shape), jnp.sin(ang).reshape(shape)
    x1, x2 = x[..., :half], x[..., half:]
    return jnp.concatenate([x1 * cos - x2 * sin, x1 * sin + x2 * cos], axis=-1).astype(x.dtype)


def _split_proj(h, pos, w_in, g_cq, g_ckv):
    B, S, _ = h.shape
    sizes = [A_W, A_W, A_W, MLA_Q_RANK, MLA_KV_RANK, MLA_ROPE, SB_W, SB_W, SB_W]
    cuts = [int(c) for c in np.cumsum(sizes)[:-1]]
    p = jnp.einsum('bsd,de->bse', h, w_in)
    qa, ka, va, cq, ckv, kr, qc, kc, vc = jnp.split(p, cuts, axis=-1)
    hd = lambda t: t.reshape(B, S, -1, HEAD_DIM)
    return (hd(qa), hd(ka), hd(va), rmsnorm(cq, g_cq), rmsnorm(ckv, g_ckv), rope(kr, pos),
            hd(qc), hd(kc), hd(vc))


def _rel_bias(rel, dist):
    idx = jnp.clip(dist, -(CHUNK - 1), A_REL_MAX) + (CHUNK - 1)
    return rel[:, idx].astype(jnp.float32)


def _band_prompt(q, k, v, rel):
    B, S, H, d = q.shape
    nc = S // CHUNK
    nb = A_BAND_CHUNKS + 1
    blk = lambda t: t.reshape(B, nc, CHUNK, H, d)

    def band(t):
        tp = jnp.pad(blk(t), ((0, 0), (A_BAND_CHUNKS, 0), (0, 0), (0, 0), (0, 0)))
        return jnp.concatenate([tp[:, i:i + nc] for i in range(nb)], axis=2)

    kb, vb = band(k), band(v)
    kpos = jnp.arange(nb * CHUNK)
    dist = A_BAND_CHUNKS * CHUNK + jnp.arange(CHUNK)[:, None] - kpos[None, :]
    bias = _rel_bias(rel, dist)
    valid = (jnp.arange(nc)[:, None] + kpos[None, :] // CHUNK) >= A_BAND_CHUNKS
    s = jnp.einsum('bcqhd,bckhd->bchqk', blk(q), kb).astype(jnp.float32) * HEAD_DIM ** -0.5 + bias[None, None]
    s = jnp.where(valid[None, :, None, None, :], s, NEG)
    p = jax.nn.softmax(s, axis=-1).astype(v.dtype)
    return jnp.einsum('bchqk,bckhd->bcqhd', p, vb).reshape(B, S, H, d)


def _band_sample(q, k, v, rel, n_cached):
    T, K = q.shape[1], k.shape[1]
    dist = (n_cached + jnp.arange(T))[:, None] - jnp.arange(K)[None, :]
    s = jnp.einsum('bqhd,bkhd->bhqk', q, k).astype(jnp.float32) * HEAD_DIM ** -0.5 + _rel_bias(rel, dist)[None]
    p = jax.nn.softmax(s, axis=-1).astype(v.dtype)
    return jnp.einsum('bhqk,bkhd->bqhd', p, v)


def _mla_q(cq, pos, w_uq):
    q = jnp.einsum('bsr,rhe->bshe', cq, w_uq)
    return q[..., :MLA_NOPE], rope(q[..., MLA_NOPE:], pos)


def _mla_kv(ckv, w_ukv):
    kv = jnp.einsum('bsr,rhe->bshe', ckv, w_ukv)
    return kv[..., :MLA_NOPE], kv[..., MLA_NOPE:]


def _mla_core(qn, qp, qpos, kn, kp, v, kpos):
    s = (jnp.einsum('bqhe,bkhe->bhqk', qn, kn).astype(jnp.float32)
         + jnp.einsum('bqhr,bkr->bhqk', qp, kp).astype(jnp.float32)) * (MLA_NOPE + MLA_ROPE) ** -0.5
    vis = (kpos[None, :] // CHUNK) <= (qpos[:, None] // CHUNK)
    s = jnp.where(vis, s, NEG)
    p = jax.nn.softmax(s, axis=-1).astype(v.dtype)
    return jnp.einsum('bhqk,bkhe->bqhe', p, v)


def _sb_core(q, qpos, k, v, kpos):
    B, K, H, d = k.shape
    pad = (-K) % K_BLOCK
    k = jnp.pad(k, ((0, 0), (0, pad), (0, 0), (0, 0)))
    v = jnp.pad(v, ((0, 0), (0, pad), (0, 0), (0, 0)))
    kpos = jnp.pad(kpos, (0, pad), constant_values=BIG_POS)
    nk = (K + pad) // K_BLOCK
    kb = k.reshape(B, nk, K_BLOCK, H, d)
    vb = v.reshape(B, nk, K_BLOCK, H, d)
    z = jnp.einsum('bqhd,bnjhd->bhqnj', q, kb).astype(jnp.float32) * HEAD_DIM ** -0.5
    causal = kpos.reshape(nk, K_BLOCK)[None] < qpos[:, None, None]
    l = jnp.where(causal, jax.nn.log_sigmoid(-z), 0.0)
    idx = jnp.arange(K_BLOCK)
    tri = (idx[:, None] >= idx[None, :]).astype(jnp.float32)
    inner = jnp.einsum('bhqnj,jk->bhqnk', l, tri)
    bidx = jnp.arange(nk)
    later = jnp.einsum('bhqm,mn->bhqn', inner[..., 0],
                       (bidx[:, None] > bidx[None, :]).astype(jnp.float32))
    a = jnp.where(causal, jnp.exp(jnp.minimum(z + inner + later[..., None], 0.0)), 0.0)
    return jnp.einsum('bhqnj,bnjhd->bqhd', a.astype(v.dtype), vb)


def _causal_sweep(core, qs, qpos, kvs, kpos):
    S = qpos.shape[0]
    nb = S // Q_BLOCK
    ng = min(Q_GROUPS, nb)
    bounds = [(i * nb) // ng for i in range(ng + 1)]
    outs = []
    for g0, g1 in zip(bounds[:-1], bounds[1:]):
        q0, q1, n = g0 * Q_BLOCK, g1 * Q_BLOCK, g1 - g0
        ks = tuple(t[:, :q1] for t in kvs)
        kp = kpos[:q1]
        blk = lambda t: jnp.moveaxis(t[:, q0:q1].reshape((t.shape[0], n, Q_BLOCK) + t.shape[2:]), 1, 0)
        xs = tuple(blk(t) for t in qs) + (qpos[q0:q1].reshape(n, Q_BLOCK),)
        out = lax.map(lambda a: core(*a[:-1], a[-1], *ks, kp), xs)
        out = jnp.moveaxis(out, 0, 1)
        outs.append(out.reshape((out.shape[0], n * Q_BLOCK) + out.shape[3:]))
    return jnp.concatenate(outs, axis=1)


def _merge(oa, om, osb, g_oa, g_om, g_os, w_out):
    B, S = oa.shape[:2]
    cat = jnp.concatenate([rmsnorm(oa.reshape(B, S, A_W), g_oa),
                           rmsnorm(om.reshape(B, S, MLA_W), g_om),
                           rmsnorm(osb.reshape(B, S, SB_W), g_os)], axis=-1)
    return jnp.einsum('bse,ed->bsd', cat, w_out)


def _ffn(h, w_up, w_down):
    u = jax.nn.relu(jnp.einsum('bsd,df->bsf', h, w_up))
    return jnp.einsum('bsf,fd->bsd', u * u, w_down)


def _mix_prompt(h, pos, w_in, g_cq, g_ckv, w_uq, w_ukv, a_rel, g_oa, g_om, g_os, w_out):
    S = h.shape[1]
    qa, ka, va, cq, ckv, kr, qc, kc, vc = _split_proj(h, pos, w_in, g_cq, g_ckv)
    oa = _band_prompt(qa, ka, va, a_rel)
    qn, qp = _mla_q(cq, pos, w_uq)
    kn, vm = _mla_kv(ckv, w_ukv)
    om = _causal_sweep(_mla_core, (qn, qp), pos, (kn, kr, vm), pos)
    osb = _causal_sweep(_sb_core, (qc,), pos, (kc, vc), pos)
    y = _merge(oa, om, osb, g_oa, g_om, g_os, w_out)
    lc = min(A_BAND_CHUNKS * CHUNK, S)
    return y, (ka[:, -lc:], va[:, -lc:], ckv, kr, kc, vc)


def _mix_sample(h, c_ak, c_av, c_ckv, c_kr, c_sk, c_sv,
                w_in, g_cq, g_ckv, w_uq, w_ukv, a_rel, g_oa, g_om, g_os, w_out):
    T = h.shape[1]
    n_past = c_ckv.shape[1]
    n_band = c_ak.shape[1]
    pos = n_past + jnp.arange(T)
    kpos = jnp.arange(n_past + T)
    qa, ka, va, cq, ckv, kr, qc, kc, vc = _split_proj(h, pos, w_in, g_cq, g_ckv)
    k_band = jnp.concatenate([c_ak, ka], axis=1)
    v_band = jnp.concatenate([c_av, va], axis=1)
    oa = _band_sample(qa, k_band, v_band, a_rel, n_band)
    qn, qp = _mla_q(cq, pos, w_uq)
    kn, vm = _mla_kv(jnp.concatenate([c_ckv, ckv], axis=1), w_ukv)
    om = _mla_core(qn, qp, pos, kn, jnp.concatenate([c_kr, kr], axis=1), vm, kpos)
    osb = _sb_core(qc, pos, jnp.concatenate([c_sk, kc], axis=1), jnp.concatenate([c_sv, vc], axis=1), kpos)
    y = _merge(oa, om, osb, g_oa, g_om, g_os, w_out)
    return y, (k_band[:, -n_band:], v_band[:, -n_band:], ckv, kr, kc, vc)


def setup_inputs(seed: int = 0) -> dict:
    key = jax.random.key(seed)
    ks = jax.random.split(key, 26)
    f32 = jnp.float32
    nrm = lambda k, shape, scale: jax.random.normal(k, shape, f32) * scale
    gain = lambda k, shape: 1.0 + 0.02 * jax.random.normal(k, shape, f32)
    la = min(A_BAND_CHUNKS * CHUNK, PAST_LEN)
    return {
        "x_prompt": nrm(ks[0], (BATCH, SEQ, D_MODEL), 1.0),
        "x_sample": nrm(ks[1], (DEC_BATCH, DEC_SEQ, D_MODEL), 1.0),
        "cache_a_k": nrm(ks[2], (DEPTH, DEC_BATCH, la, A_HEADS, HEAD_DIM), 1.0),
        "cache_a_v": nrm(ks[3], (DEPTH, DEC_BATCH, la, A_HEADS, HEAD_DIM), 1.0),
        "cache_mla_ckv": nrm(ks[4], (DEPTH, DEC_BATCH, PAST_LEN, MLA_KV_RANK), 1.0),
        "cache_mla_krope": nrm(ks[5], (DEPTH, DEC_BATCH, PAST_LEN, MLA_ROPE), 1.0),
        "cache_sb_k": nrm(ks[6], (DEPTH, DEC_BATCH, PAST_LEN, SB_HEADS, HEAD_DIM), 1.0),
        "cache_sb_v": nrm(ks[7], (DEPTH, DEC_BATCH, PAST_LEN, SB_HEADS, HEAD_DIM), 1.0),
        "g_mix": gain(ks[8], (DEPTH, D_MODEL)),
        "w_in": nrm(ks[9], (DEPTH, D_MODEL, IN_COLS), D_MODEL ** -0.5),
        "g_cq": gain(ks[10], (DEPTH, MLA_Q_RANK)),
        "g_ckv": gain(ks[11], (DEPTH, MLA_KV_RANK)),
        "w_uq": nrm(ks[12], (DEPTH, MLA_Q_RANK, MLA_HEADS, MLA_NOPE + MLA_ROPE), MLA_Q_RANK ** -0.5),
        "w_ukv": nrm(ks[13], (DEPTH, MLA_KV_RANK, MLA_HEADS, MLA_NOPE + MLA_V), MLA_KV_RANK ** -0.5),
        "a_rel_bias": nrm(ks[14], (DEPTH, A_HEADS, A_REL_SIZE), 0.5),
        "g_out_a": gain(ks[15], (DEPTH, A_W)),
        "g_out_mla": gain(ks[16], (DEPTH, MLA_W)),
        "g_out_sb": gain(ks[17], (DEPTH, SB_W)),
        "w_out": nrm(ks[18], (DEPTH, MIX_W, D_MODEL), (2.0 * MIX_W) ** -0.5),
        "g_ffn": gain(ks[19], (DEPTH, D_MODEL)),
        "w_up": nrm(ks[20], (DEPTH, D_MODEL, D_FF), D_MODEL ** -0.5),
        "w_down": nrm(ks[21], (DEPTH, D_FF, D_MODEL), (2.0 * D_FF) ** -0.5),
        "g_final": gain(ks[22], (D_MODEL,)),
    }


def reference(x_prompt, x_sample, cache_a_k, cache_a_v, cache_mla_ckv, cache_mla_krope, cache_sb_k, cache_sb_v,
              g_mix, w_in, g_cq, g_ckv, w_uq, w_ukv, a_rel_bias, g_out_a, g_out_mla, g_out_sb, w_out,
              g_ffn, w_up, w_down, g_final):
    xp, xs = x_prompt, x_sample
    pos_p = jnp.arange(xp.shape[1])
    prompt_states, sample_states = [], []
    for l in range(DEPTH):
        lw = (w_in[l], g_cq[l], g_ckv[l], w_uq[l], w_ukv[l], a_rel_bias[l],
              g_out_a[l], g_out_mla[l], g_out_sb[l], w_out[l])
        yp, st_p = _mix_prompt(rmsnorm(xp, g_mix[l]), pos_p, *lw)
        ys, st_s = _mix_sample(rmsnorm(xs, g_mix[l]), cache_a_k[l], cache_a_v[l], cache_mla_ckv[l],
                               cache_mla_krope[l], cache_sb_k[l], cache_sb_v[l], *lw)
        xp = xp + yp
        xs = xs + ys
        xp = xp + _ffn(rmsnorm(xp, g_ffn[l]), w_up[l], w_down[l])
        xs = xs + _ffn(rmsnorm(xs, g_ffn[l]), w_up[l], w_down[l])
        prompt_states.append(st_p)
        sample_states.append(st_s)
    p_a_k, p_a_v, p_ckv, p_krope, p_sb_k, p_sb_v = [jnp.stack(t, axis=0) for t in zip(*prompt_states)]
    s_a_k, s_a_v, s_ckv, s_krope, s_sb_k, s_sb_v = [jnp.stack(t, axis=0) for t in zip(*sample_states)]
    y_prompt = rmsnorm(xp, g_final)
    y_sample = rmsnorm(xs, g_final)
    return (y_prompt, y_sample, p_a_k, p_a_v, p_ckv, p_krope, p_sb_k, p_sb_v,
            s_a_k, s_a_v, s_ckv, s_krope, s_sb_k, s_sb_v)
```

```python
import os
import numpy as np
DBG = int(os.environ.get('MK_DBG', '0'))
STOPN = int(os.environ.get('MK_STOPN', '-1'))
MAXOUT = int(os.environ.get('MK_MAXOUT', '4'))


class _Stop(Exception):
    pass
from contextlib import ExitStack
import concourse.bass as bass
import concourse.mybir as mybir
from concourse.bass_utils import run_bass_kernel_spmd

F32 = mybir.dt.float32
BF16 = mybir.dt.bfloat16
AF = mybir.ActivationFunctionType
ALU = mybir.AluOpType
AX = mybir.AxisListType

D = 1024
DFF = 4096
NSQ = 4
TS = 64
EPS = 1e-6
INC = 2112


class Buf:
    __slots__ = ("t", "w", "r", "dk", "x")

    def __init__(self, t, x=False):
        self.t = t
        self.w = None
        self.r = {}
        self.dk = None
        self.x = x

    def __getitem__(self, k):
        return self.t[k]


class Sy:
    def __init__(self, nc, st, ndma=int(os.environ.get('MK_NDMA', '72'))):
        self.nc = nc
        self.eng = {"pe": nc.tensor, "act": nc.scalar, "dve": nc.vector, "pool": nc.gpsimd, "sp": nc.sync}
        self.sem, self.cnt = {}, {}
        self.seen = {e: {} for e in self.eng}
        for e in self.eng:
            self.sem[e] = st.enter_context(nc.semaphore("s_" + e))
            self.cnt[e] = 0
        self.dpool = []
        for i in range(ndma):
            k = "d%d" % i
            self.sem[k] = st.enter_context(nc.semaphore(k))
            self.cnt[k] = 0
            self.dpool.append(k)
        self.dnext = 0
        self.ninstr = 0
        self.fifo = []

    def _wait(self, e, waits):
        for k, v in waits:
            if k == "pe" and e == "pe":
                continue
            if self.seen[e].get(k, 0) >= v:
                continue
            self.eng[e].wait_ge(self.sem[k], v)
            self.seen[e][k] = v

    def _deps(self, R, W):
        waits = []
        for b in R:
            if b.w:
                waits.append(b.w)
            if b.x:
                waits.extend(b.r.items())
        for b in W:
            if b.w:
                waits.append(b.w)
            waits.extend(b.r.items())
        return waits

    def do(self, e, fn, R=(), W=()):
        if self.ninstr == STOPN:
            self.barrier()
            raise _Stop()
        self._wait(e, self._deps(R, W))
        ins = fn(self.eng[e])
        self.cnt[e] += 1
        ins.then_inc(self.sem[e], 1)
        v = self.cnt[e]
        for b in R:
            if b.x:
                b.w = (e, v)
                b.r = {}
            else:
                b.r[e] = v
        for b in W:
            b.w = (e, v)
            b.r = {}
        self.ninstr += 1
        return (e, v)

    def dma(self, q, out, in_, R=(), W=(), **kw):
        if self.ninstr == STOPN:
            self.barrier()
            raise _Stop()
        if len(self.fifo) >= MAXOUT:
            self._wait(q, [self.fifo.pop(0)])
        self._wait(q, self._deps(R, W))
        b0 = (list(W) + list(R))[0]
        if b0.dk is None:
            b0.dk = self.dpool[self.dnext % len(self.dpool)]
            self.dnext += 1
        k = b0.dk
        ins = self.eng[q].dma_start(out=out, in_=in_, **kw)
        self.cnt[k] += 16
        ins.then_inc(self.sem[k], 16)
        v = self.cnt[k]
        for b in R:
            b.r[k] = v
        for b in W:
            b.w = (k, v)
            b.r = {}
        self.ninstr += 1
        self.fifo.append((k, v))
        return (k, v)

    def barrier(self):
        allw = [(k, v) for k, v in self.cnt.items() if v > 0]
        for e in self.eng:
            self._wait(e, allw)
        self.dnext = 0


def build(S, PAST, L, stop=None):
    NT = S + NSQ * TS
    NG = S // 512
    NKB = S // 128
    NPB = PAST // 128
    nc = bass.Bass("TRN2", target_bir_lowering=False)

    def din(name, shape, dt=F32):
        return nc.dram_tensor(name, list(shape), dt, kind="ExternalInput").ap()

    def dout(name, shape):
        return nc.dram_tensor(name, list(shape), F32, kind="ExternalOutput").ap()

    def dscr(name, shape, dt=BF16):
        return nc.dram_tensor(name, list(shape), dt).ap()

    xin = din("x", [NT, D])
    cak = din("cak", [L, NSQ, 512, 256]); cav = din("cav", [L, NSQ, 512, 256])
    cckv = din("cckv", [L, NSQ, PAST, 256]); ckr = din("ckr", [L, NSQ, PAST, 64])
    csk = din("csk", [L, NSQ, PAST, 256]); csv = din("csv", [L, NSQ, PAST, 256])
    g_mix = din("g_mix", [L, D]); w_in = din("w_in", [L, D, INC])
    g_cq = din("g_cq", [L, 256]); g_ckv = din("g_ckv", [L, 256])
    w_uq = din("w_uq", [L, 256, 768]); w_ukv = din("w_ukv", [L, 256, 1024])
    a_rel = din("a_rel", [L, 768]); g_cat = din("g_cat", [L, D])
    w_out = din("w_out", [L, D, D]); g_ffn = din("g_ffn", [L, D])
    w_up = din("w_up", [L, D, DFF]); w_down = din("w_down", [L, DFF, D]); g_fin = din("g_fin", [D])
    c_ident = din("c_ident", [128, 128]); c_tri = din("c_tri", [128, 4, 128])
    c_msb = din("c_msb", [128, 4, 512]); c_mml = din("c_mml", [128, 4, 512]); c_val = din("c_val", [128, 8, 512])
    c_ropeT = din("c_ropeT", [NT, 64]); c_ropeF = din("c_ropeF", [64, 2, NT])

    y = dout("y", [NT, D])
    pak = dout("pak", [L, 512, 256]); pav = dout("pav", [L, 512, 256])
    pckv = dout("pckv", [L, S, 256]); pkr = dout("pkr", [L, S, 64])
    psk = dout("psk", [L, S, 256]); psv = dout("psv", [L, S, 256])
    sak = dout("sak", [L, NSQ, 512, 256]); sav = dout("sav", [L, NSQ, 512, 256])
    sckv = dout("sckv", [L, NSQ, TS, 256]); skr = dout("skr", [L, NSQ, TS, 64])
    ssk = dout("ssk", [L, NSQ, TS, 256]); ssv = dout("ssv", [L, NSQ, TS, 256])

    XR = dscr("XR", [NT, D], F32)
    QA = dscr("QA", [256, NT]); KA = dscr("KA", [256, NT]); QC = dscr("QC", [256, NT]); KC = dscr("KC", [256, NT])
    VA = dscr("VA", [NT, 256]); VC = dscr("VC", [NT, 256])
    QN = dscr("QN", [512, NT]); QP = dscr("QP", [256, NT]); KN = dscr("KN", [512, NT]); KR = dscr("KR", [64, NT])
    VM = dscr("VM", [NT, 512])
    KAs = dscr("KAs", [NSQ, 256, 512]); VAs = dscr("VAs", [NSQ, 512, 256])
    KNs = dscr("KNs", [NSQ, 512, PAST]); KRs = dscr("KRs", [NSQ, 64, PAST]); VMs = dscr("VMs", [NSQ, PAST, 512])
    KCs = dscr("KCs", [NSQ, 256, PAST]); VCs = dscr("VCs", [NSQ, PAST, 256])
    OCAT = dscr("OCAT", [D, NT], F32)
    EXT = dscr("EXT", [4, 1536], F32)

    with ExitStack() as st:
        st.push(lambda et, ev, tb_: et is _Stop)
        sy = Sy(nc, st)

        uniq = [0]

        def sb(stk, name, shape, dt):
            uniq[0] += 1
            return Buf(stk.enter_context(nc.sbuf_tensor("%s_%d" % (name, uniq[0]), list(shape), dt)))

        fb = [Buf(st.enter_context(nc.psum_tensor("pf%d" % i, [128, 512], F32)), x=True) for i in range(7)]
        tb = Buf(st.enter_context(nc.psum_tensor("ptb", [128, 1024], BF16)), x=True)

        identb = sb(st, "identb", [128, 128], BF16)
        trib = sb(st, "trib", [128, 4, 128], BF16)
        onesb = sb(st, "onesb", [128, 128], BF16)
        ones1f = sb(st, "ones1f", [1, 128], F32)
        gfinb = sb(st, "gfinb", [128, D], F32)
        EBs = sb(st, "EBs", [128, 4, 5, 64], BF16)
        with ExitStack() as ph:
            t1 = sb(ph, "c_t1", [128, 128], F32)
            t2 = sb(ph, "c_t2", [128, 4, 128], F32)
            sy.dma("sp", t1[:], c_ident, W=[t1])
            sy.dma("sp", t2[:], c_tri, W=[t2])
            sy.dma("sp", gfinb[:], g_fin.partition_broadcast(128), W=[gfinb])
            sy.do("dve", lambda e: e.tensor_copy(out=identb[:], in_=t1[:]), R=[t1], W=[identb])
            sy.do("dve", lambda e: e.tensor_copy(out=trib[:], in_=t2[:]), R=[t2], W=[trib])
            sy.do("dve", lambda e: e.memset(onesb[:], 1.0), W=[onesb])
            sy.do("dve", lambda e: e.memset(ones1f[:], 1.0), W=[ones1f])
            sy.barrier()

        ddb = Buf(None)
        def mm(out_b, out_ap, lhsT_b, lhsT, rhs_b, rhs, start, stop):
            return sy.do("pe", lambda e: e.matmul(out_ap, lhsT=lhsT, rhs=rhs, start=start, stop=stop),
                         R=[lhsT_b, rhs_b], W=[out_b])

        cast_rr = [0]

        def load_w(ph, dst_b, views, srcs, width):
            stg = [sb(ph, "wst%d_%d" % (sy.ninstr, i), [128, width], F32) for i in range(2)]
            for i, (v, s_) in enumerate(zip(views, srcs)):
                sg = stg[i % 2]
                sy.dma("sp", sg[:], s_, W=[sg])
                eng = ("pool", "dve")[cast_rr[0] % 2]
                cast_rr[0] += 1
                sy.do(eng, lambda e, v=v, sg=sg: e.tensor_copy(out=v, in_=sg[:]), R=[sg], W=[dst_b])

        def rstd_rows(src_b, src_ap, width, scr_b, ss_b, sq_eng="act"):
            if sq_eng == "act":
                sy.do("act", lambda e: e.activation(out=scr_b[:, 0:width], in_=src_ap, func=AF.Square), R=[src_b], W=[scr_b])
            else:
                sy.do("dve", lambda e: e.tensor_tensor(out=scr_b[:, 0:width], in0=src_ap, in1=src_ap, op=ALU.mult), R=[src_b], W=[scr_b])
            sy.do("dve", lambda e: e.reduce_sum(out=ss_b[:, 0:1], in_=scr_b[:, 0:width], axis=AX.X), R=[scr_b], W=[ss_b])
            sy.do("dve", lambda e: e.tensor_scalar(out=ss_b[:, 0:1], in0=ss_b[:, 0:1], scalar1=1.0 / width, scalar2=EPS,
                                                   op0=ALU.mult, op1=ALU.add), R=[ss_b], W=[ss_b])
            if not (DBG & 4096):
                sy.do("act", lambda e: e.activation(out=ss_b[:, 0:1], in_=ss_b[:, 0:1], func=AF.Sqrt), R=[ss_b], W=[ss_b])
            sy.do("dve", lambda e: e.reciprocal(out=ss_b[:, 0:1], in_=ss_b[:, 0:1]), R=[ss_b], W=[ss_b])

        def ckpt(k):
            if stop == k:
                sy.barrier()
                raise _Stop()

        for l in range(L):
            xsrc = xin if l == 0 else XR
            last = (l == L - 1)
            ckpt(0)

            with ExitStack() as ph:
                win = sb(ph, "win", [128, 8, INC], BF16)
                wuq = sb(ph, "wuq", [128, 2, 768], BF16)
                wuqs = sb(ph, "wuqs", [128, 2, 256], BF16)
                wkn = sb(ph, "wkn", [128, 2, 512], BF16)
                wv = sb(ph, "wv", [128, 2, 512], BF16)
                gmixb = sb(ph, "gmixb", [128, D], F32)
                gcqb = sb(ph, "gcqb", [128, 256], F32)
                gckvb = sb(ph, "gckvb", [128, 256], F32)
                sy.dma("sp", gmixb[:], g_mix[l].partition_broadcast(128), W=[gmixb])
                sy.dma("sp", gcqb[:], g_cq[l].partition_broadcast(128), W=[gcqb])
                sy.dma("sp", gckvb[:], g_ckv[l].partition_broadcast(128), W=[gckvb])
                with ExitStack() as ph2:
                    load_w(ph2, win, [win[:, c, :] for c in range(8)], [w_in[l, c * 128:(c + 1) * 128, :] for c in range(8)], INC)
                    wq32 = sb(ph2, "wq32", [128, 2, 768], F32)
                    wkv32 = sb(ph2, "wkv32", [128, 2, 1024], F32)
                    sy.dma("sp", wq32[:], w_uq[l].rearrange("(k p) e -> p k e", p=128), W=[wq32])
                    sy.dma("sp", wkv32[:], w_ukv[l].rearrange("(k p) e -> p k e", p=128), W=[wkv32])
                    sy.do("dve", lambda e: e.tensor_copy(out=wuq[:], in_=wq32[:]), R=[wq32], W=[wuq])
                    for k in range(2):
                        q4 = wq32[:, k, :].rearrange("p (h e) -> p h e", h=4)
                        d4 = wuqs[:, k, :].rearrange("p (h e) -> p h e", h=4)
                        sy.do("dve", lambda e, q4=q4, d4=d4: e.tensor_copy(out=d4[:, :, 0:32], in_=q4[:, :, 160:192]), R=[wq32], W=[wuqs])
                        sy.do("dve", lambda e, q4=q4, d4=d4: e.tensor_copy(out=d4[:, :, 32:64], in_=q4[:, :, 128:160]), R=[wq32], W=[wuqs])
                        k4 = wkv32[:, k, :].rearrange("p (h e) -> p h e", h=4)
                        sy.do("dve", lambda e, k4=k4, k=k: e.tensor_copy(out=wkn[:, k, :].rearrange("p (h e) -> p h e", h=4), in_=k4[:, :, 0:128]), R=[wkv32], W=[wkn])
                        sy.do("dve", lambda e, k4=k4, k=k: e.tensor_copy(out=wv[:, k, :].rearrange("p (h e) -> p h e", h=4), in_=k4[:, :, 128:256]), R=[wkv32], W=[wv])
                    sy.barrier()

                ckpt(1)
                xt = [sb(ph, "xt%d" % i, [128, D], F32) for i in range(3)]
                rt = [sb(ph, "rt%d" % i, [128, 4, 64], F32) for i in range(2)]
                rf = [sb(ph, "rf0", [64, 2, 512], F32)] * 2
                scr = sb(ph, "scr", [128, D], F32)
                ssx = sb(ph, "ssx", [128, 1], F32)
                ss2 = sb(ph, "ss2", [128, 1], F32)
                ss3 = sb(ph, "ss3", [128, 1], F32)
                hb = [sb(ph, "hb%d" % i, [128, D], BF16) for i in range(2)]
                hT = [sb(ph, "hT0", [128, 8, 512], BF16)] * 2
                fo = [sb(ph, "fo0", [128, 8, 512], BF16)] * 2
                va_b = [sb(ph, "va_b%d" % i, [128, 4, 256], BF16) for i in range(2)]
                vaf = sb(ph, "vaf", [128, 4, 256], F32)
                kaf = sb(ph, "kaf", [128, 4, 256], F32)
                cqn_b = sb(ph, "cqn_b", [128, 256], BF16)
                ckvn_b = sb(ph, "ckvn_b", [128, 256], BF16)
                cqT = [sb(ph, "cqT%d" % i, [128, 2, 512], BF16) for i in range(2)]
                ckvT = [sb(ph, "ckvT%d" % i, [128, 2, 512], BF16) for i in range(2)]
                ckv_f = [sb(ph, "ckv_f%d" % i, [128, 4, 256], F32) for i in range(2)]
                krr = sb(ph, "krr", [128, 64], F32)
                krt = [sb(ph, "krt%d" % i, [128, 32], F32) for i in range(4)]
                kr_f = [sb(ph, "kr_f%d" % i, [128, 4, 64], F32) for i in range(2)]
                kr_b = sb(ph, "kr_b", [128, 64], BF16)
                krT = [sb(ph, "krT%d" % i, [64, 512], BF16) for i in range(2)]
                kcvc = [sb(ph, "kcvc%d" % i, [128, 4, 512], F32) for i in range(2)]
                vc_b = [sb(ph, "vc_b%d" % i, [128, 4, 256], BF16) for i in range(2)]
                qn_b = [sb(ph, "qn_b0", [128, 4, 512], BF16)] * 2
                qp_b = [sb(ph, "qp_b0", [64, 4, 512], BF16)] * 2
                kn_b = [sb(ph, "kn_b0", [128, 4, 512], BF16)] * 2
                vm_b = [sb(ph, "vm_b0", [128, 4, 512], BF16)] * 2
                qt1 = sb(ph, "qt1", [64, 512], F32)
                qt2 = sb(ph, "qt2", [64, 512], F32)
                c32 = [sb(ph, "c32_%d" % i, [128, 4, 256], F32) for i in range(2)]
                cb16 = [sb(ph, "cb16_%d" % i, [128, 4, 256], BF16) for i in range(2)]
                ckT = [sb(ph, "ckT%d" % i, [128, 2, 512], BF16) for i in range(2)]
                kr32 = sb(ph, "kr32", [128, 4, 64], F32)
                krb16 = sb(ph, "krb16", [128, 4, 64], BF16)
                krTc = [sb(ph, "krTc%d" % i, [64, 512], BF16) for i in range(2)]
                print('P1 sbuf remaining', nc.sbuf_bytes_remaining, 'vaf', vaf.t, 'kaf', kaf.t, 'krTc', krTc[1].t)
                ev_rr = [0]
                xti = [0]

                def evac(dst_b, dst_ap, src_b, src_ap):
                    eng = ("act", "dve")[ev_rr[0] % 2]
                    ev_rr[0] += 1
                    if eng == "act":
                        sy.do("act", lambda e: e.activation(out=dst_ap, in_=src_ap, func=AF.Copy), R=[src_b], W=[dst_b])
                    else:
                        sy.do("dve", lambda e: e.tensor_copy(out=dst_ap, in_=src_ap), R=[src_b], W=[dst_b])

                def kv_up(cT, N, ntl, knb, vmb):
                    for h in range(4):
                        bk = fb[4 + h % 2]
                        for k in range(2):
                            mm(bk, bk[:, 0:N], wkn, wkn[:, k, h * 128:(h + 1) * 128], cT, cT[:, k, 0:N], k == 0, k == 1)
                        evac(knb, knb[:, h, 0:N], bk, bk[:, 0:N])
                    for t in range(ntl):
                        bk = fb[4 + t % 2]
                        for k in range(2):
                            mm(bk, bk[:, 0:512], cT, cT[:, k, t * 128:(t + 1) * 128], wv, wv[:, k, :], k == 0, k == 1)
                        evac(vmb, vmb[:, t, :], bk, bk[:, 0:512])

                ci = 0
                for b in range(NSQ):
                    for (src, isk) in ((cak, True), (cav, False)):
                        cb = c32[ci % 2]; bb = cb16[ci % 2]; tt = ckT[ci % 2]; ci += 1
                        sy.dma("sp", cb[:], src[l, b].rearrange("(t p) c -> p t c", p=128), W=[cb])
                        sy.do("pool", lambda e, cb=cb, bb=bb: e.tensor_copy(out=bb[:], in_=cb[:]), R=[cb], W=[bb])
                        if isk:
                            for m in range(2):
                                for t in range(4):
                                    sy.do("pe", lambda e, m=m, t=t, bb=bb: e.transpose(out=tb[:, t * 128:(t + 1) * 128], in_=bb[:, t, m * 128:(m + 1) * 128], identity=identb[:]),
                                          R=[bb, identb], W=[tb])
                                evac(tt, tt[:, m, :], tb, tb[:, 0:512])
                            sy.dma("sp", KAs[b].rearrange("(m p) n -> p m n", p=128), tt[:], R=[tt])
                        else:
                            sy.dma("sp", VAs[b].rearrange("(t p) c -> p t c", p=128), bb[:], R=[bb])
                    for blk in range(PAST // 512):
                        r0 = blk * 512
                        cb = c32[ci % 2]; bb = cb16[ci % 2]; tt = ckT[ci % 2]; pb = ci % 2; ci += 1
                        sy.dma("sp", cb[:], cckv[l, b, r0:r0 + 512, :].rearrange("(t p) c -> p t c", p=128), W=[cb])
                        sy.do("pool", lambda e, cb=cb, bb=bb: e.tensor_copy(out=bb[:], in_=cb[:]), R=[cb], W=[bb])
                        for m in range(2):
                            for t in range(4):
                                sy.do("pe", lambda e, m=m, t=t, bb=bb: e.transpose(out=tb[:, t * 128:(t + 1) * 128], in_=bb[:, t, m * 128:(m + 1) * 128], identity=identb[:]),
                                      R=[bb, identb], W=[tb])
                            evac(tt, tt[:, m, :], tb, tb[:, 0:512])
                        kv_up(tt, 512, 4, kn_b[pb], vm_b[pb])
                        sy.dma("sp", KNs[b][:, r0:r0 + 512].rearrange("(h p) n -> p h n", p=128), kn_b[pb][:], R=[kn_b[pb]])
                        sy.dma("sp", VMs[b][r0:r0 + 512, :].rearrange("(t p) c -> p t c", p=128), vm_b[pb][:], R=[vm_b[pb]])
                        for (src, isk) in ((csk, True), (csv, False)):
                            cb = c32[ci % 2]; bb = cb16[ci % 2]; tt = ckT[ci % 2]; ci += 1
                            sy.dma("sp", cb[:], src[l, b, r0:r0 + 512, :].rearrange("(t p) c -> p t c", p=128), W=[cb])
                            sy.do("pool", lambda e, cb=cb, bb=bb: e.tensor_copy(out=bb[:], in_=cb[:]), R=[cb], W=[bb])
                            if isk:
                                for m in range(2):
                                    for t in range(4):
                                        sy.do("pe", lambda e, m=m, t=t, bb=bb: e.transpose(out=tb[:, t * 128:(t + 1) * 128], in_=bb[:, t, m * 128:(m + 1) * 128], identity=identb[:]),
                                              R=[bb, identb], W=[tb])
                                    evac(tt, tt[:, m, :], tb, tb[:, 0:512])
                                sy.dma("sp", KCs[b][:, r0:r0 + 512].rearrange("(m p) n -> p m n", p=128), tt[:], R=[tt])
                            else:
                                sy.dma("sp", VCs[b][r0:r0 + 512, :].rearrange("(t p) c -> p t c", p=128), bb[:], R=[bb])
                        kt = krTc[blk % 2]
                        sy.dma("sp", kr32[:], ckr[l, b, r0:r0 + 512, :].rearrange("(t p) c -> p t c", p=128), W=[kr32])
                        sy.do("pool", lambda e: e.tensor_copy(out=krb16[:], in_=kr32[:]), R=[kr32], W=[krb16])
                        for t in range(4):
                            sy.do("pe", lambda e, t=t: e.transpose(out=tb[0:64, t * 128:(t + 1) * 128], in_=krb16[:, t, :], identity=identb[:]),
                                  R=[krb16, identb], W=[tb])
                        evac(kt, kt[:, :], tb, tb[0:64, 0:512])
                        sy.dma("sp", KRs[b][:, r0:r0 + 512], kt[:], R=[kt])
                    sy.dma("sp", sak[l, b, 0:448, :], cak[l, b, 64:512, :], W=[ddb])
                    sy.dma("sp", sav[l, b, 0:448, :], cav[l, b, 64:512, :], W=[ddb])

                print('ninstr at P1 start', sy.ninstr)
                ckpt(2)
                groups = [(g * 512, 4, g) for g in range(NG)] + [(S, 2, NG)]
                for (r0, ntl, gi) in groups:
                    N = ntl * 128
                    pb = gi % 2
                    is_s = (gi == NG)
                    need_a = is_s or (gi == NG - 1)
                    RT, RF, HT = rt[pb], rf[pb], hT[pb]
                    sy.dma("sp", RT[:, 0:ntl, :], c_ropeT[r0:r0 + N, :].rearrange("(t p) d -> p t d", p=128), W=[RT])
                    sy.dma("sp", RF[:, :, 0:N], c_ropeF[:, :, r0:r0 + N], W=[RF])
                    for t in range(ntl):
                        H = hb[t % 2]
                        X = xt[xti[0] % 3]; xti[0] += 1
                        sy.dma("sp", X[:], xsrc[r0 + t * 128:r0 + (t + 1) * 128, :], W=[X])
                        rstd_rows(X, X[:], D, scr, ssx)
                        sy.do("dve", lambda e, X=X, H=H: e.scalar_tensor_tensor(out=H[:], in0=X[:], scalar=ssx[:, 0:1], in1=gmixb[:],
                                                                               op0=ALU.mult, op1=ALU.mult), R=[X, ssx, gmixb], W=[H])
                        for c in range(8):
                            sy.do("pe", lambda e, c=c, H=H: e.transpose(out=tb[:, c * 128:(c + 1) * 128], in_=H[:, c * 128:(c + 1) * 128], identity=identb[:]),
                                  R=[H, identb], W=[tb])
                        evac(HT, HT[:, :, t * 128:(t + 1) * 128], tb, tb[:, :].rearrange("p (c n) -> p c n", c=8))
                        tm = [(fb[0], 512, 1024), (fb[1], 1024, 1344), (fb[2], 1600, 2112)]
                        if need_a and not (DBG & 1):
                            tm.append((fb[3], 256, 512))
                        for (bk, c0, c1) in tm:
                            for c in range(8):
                                mm(bk, bk[:, 0:c1 - c0], HT, HT[:, c, t * 128:(t + 1) * 128], win, win[:, c, c0:c1], c == 0, c == 7)
                        sy.do("act", lambda e, t=t: e.activation(out=va_b[pb][:, t, :], in_=fb[0][:, 0:256], func=AF.Copy), R=[fb[0]], W=[va_b[pb]])
                        if need_a and not (DBG & 2):
                            if DBG & 16:
                                sy.do("dve", lambda e, t=t: e.tensor_copy(out=vaf[:, t, :], in_=gcqb[:]), R=[gcqb], W=[vaf])
                            elif DBG & 8:
                                sy.do("dve", lambda e, t=t: e.tensor_copy(out=scr[:, 0:256], in_=fb[0][:, 0:256]), R=[fb[0]], W=[scr])
                            else:
                                sy.do("dve", lambda e, t=t: e.tensor_copy(out=vaf[:, t, :], in_=fb[0][:, 0:256]), R=[fb[0]], W=[vaf])
                        if need_a and not (DBG & 4):
                            sy.do("act", lambda e, t=t: e.activation(out=kaf[:, t, :], in_=fb[3][:, 0:256], func=AF.Copy), R=[fb[3]], W=[kaf])
                        if not (DBG & 1024):
                            rstd_rows(fb[0], fb[0][:, 256:512], 256, scr, ss2)
                            sy.do("dve", lambda e: e.scalar_tensor_tensor(out=cqn_b[:], in0=fb[0][:, 256:512], scalar=ss2[:, 0:1], in1=gcqb[:],
                                                                          op0=ALU.mult, op1=ALU.mult), R=[fb[0], ss2, gcqb], W=[cqn_b])
                            rstd_rows(fb[1], fb[1][:, 0:256], 256, scr, ss3)
                            sy.do("dve", lambda e, t=t: e.scalar_tensor_tensor(out=ckv_f[pb][:, t, :], in0=fb[1][:, 0:256], scalar=ss3[:, 0:1], in1=gckvb[:],
                                                                               op0=ALU.mult, op1=ALU.mult), R=[fb[1], ss3, gckvb], W=[ckv_f[pb]])
                            sy.do("pool", lambda e, t=t: e.tensor_copy(out=ckvn_b[:], in_=ckv_f[pb][:, t, :]), R=[ckv_f[pb]], W=[ckvn_b])
                            for k in range(2):
                                sy.do("pe", lambda e, k=k: e.transpose(out=tb[:, k * 128:(k + 1) * 128], in_=cqn_b[:, k * 128:(k + 1) * 128], identity=identb[:]),
                                      R=[cqn_b, identb], W=[tb])
                            evac(cqT[pb], cqT[pb][:, :, t * 128:(t + 1) * 128], tb, tb[:, 0:256].rearrange("p (c n) -> p c n", c=2))
                            for k in range(2):
                                sy.do("pe", lambda e, k=k: e.transpose(out=tb[:, k * 128:(k + 1) * 128], in_=ckvn_b[:, k * 128:(k + 1) * 128], identity=identb[:]),
                                      R=[ckvn_b, identb], W=[tb])
                            evac(ckvT[pb], ckvT[pb][:, :, t * 128:(t + 1) * 128], tb, tb[:, 0:256].rearrange("p (c n) -> p c n", c=2))
                        if not (DBG & 64):
                            sy.do("act", lambda e: e.activation(out=krr[:], in_=fb[1][:, 256:320], func=AF.Copy), R=[fb[1]], W=[krr])
                            cs, sn = RT[:, t, 0:32], RT[:, t, 32:64]
                            x1, x2 = krr[:, 0:32], krr[:, 32:64]
                            sy.do("pool", lambda e, cs=cs, x1=x1: e.tensor_tensor(out=krt[0][:], in0=x1, in1=cs, op=ALU.mult), R=[krr, RT], W=[krt[0]])
                            sy.do("pool", lambda e, sn=sn, x2=x2: e.tensor_tensor(out=krt[1][:], in0=x2, in1=sn, op=ALU.mult), R=[krr, RT], W=[krt[1]])
                            sy.do("pool", lambda e, sn=sn, x1=x1: e.tensor_tensor(out=krt[2][:], in0=x1, in1=sn, op=ALU.mult), R=[krr, RT], W=[krt[2]])
                            sy.do("pool", lambda e, cs=cs, x2=x2: e.tensor_tensor(out=krt[3][:], in0=x2, in1=cs, op=ALU.mult), R=[krr, RT], W=[krt[3]])
                            sy.do("pool", lambda e, t=t: e.tensor_tensor(out=kr_f[pb][:, t, 0:32], in0=krt[0][:], in1=krt[1][:], op=ALU.subtract), R=[krt[0], krt[1]], W=[kr_f[pb]])
                            sy.do("pool", lambda e, t=t: e.tensor_tensor(out=kr_f[pb][:, t, 32:64], in0=krt[2][:], in1=krt[3][:], op=ALU.add), R=[krt[2], krt[3]], W=[kr_f[pb]])
                            sy.do("pool", lambda e, t=t: e.tensor_copy(out=kr_b[:], in_=kr_f[pb][:, t, :]), R=[kr_f[pb]], W=[kr_b])
                            sy.do("pe", lambda e: e.transpose(out=tb[0:64, 0:128], in_=kr_b[:], identity=identb[:]), R=[kr_b, identb], W=[tb])
                            evac(krT[pb], krT[pb][:, t * 128:(t + 1) * 128], tb, tb[0:64, 0:128])
                        sy.do("act", lambda e, t=t: e.activation(out=kcvc[pb][:, t, :], in_=fb[2][:, 0:512], func=AF.Copy), R=[fb[2]], W=[kcvc[pb]])
                        sy.do("pool", lambda e, t=t: e.tensor_copy(out=vc_b[pb][:, t, :], in_=kcvc[pb][:, t, 256:512]), R=[kcvc[pb]], W=[vc_b[pb]])
                    if gi == 0: print('ninstr after tiles g0', sy.ninstr)
                    if gi == 0: ckpt(20)
                    if is_s: ckpt(25)
                    for mi, c0 in enumerate([0, 128, 256, 384, 1344, 1472, 1600, 1728]):
                        bk = fb[4 + mi % 2]
                        for c in range(8):
                            mm(bk, bk[:, 0:N], win, win[:, c, c0:c0 + 128], HT, HT[:, c, 0:N], c == 0, c == 7)
                        evac(fo[pb], fo[pb][:, mi, 0:N], bk, bk[:, 0:N])
                    if gi == 0: ckpt(21)
                    if not (DBG & 128):
                        CQ = cqT[pb]
                        for h in range(4):
                            bk = fb[4 + h % 2]
                            for k in range(2):
                                mm(bk, bk[:, 0:N], wuq, wuq[:, k, h * 192:h * 192 + 128], CQ, CQ[:, k, 0:N], k == 0, k == 1)
                            evac(qn_b[pb], qn_b[pb][:, h, 0:N], bk, bk[:, 0:N])
                            for k in range(2):
                                mm(fb[6], fb[6][0:64, 0:N], wuq, wuq[:, k, h * 192 + 128:h * 192 + 192], CQ, CQ[:, k, 0:N], k == 0, k == 1)
                            for k in range(2):
                                mm(fb[3], fb[3][0:64, 0:N], wuqs, wuqs[:, k, h * 64:(h + 1) * 64], CQ, CQ[:, k, 0:N], k == 0, k == 1)
                            sy.do("dve", lambda e: e.tensor_tensor(out=qt1[:, 0:N], in0=fb[6][0:64, 0:N], in1=RF[:, 0, 0:N], op=ALU.mult), R=[fb[6], RF], W=[qt1])
                            sy.do("dve", lambda e: e.tensor_tensor(out=qt2[:, 0:N], in0=fb[3][0:64, 0:N], in1=RF[:, 1, 0:N], op=ALU.mult), R=[fb[3], RF], W=[qt2])
                            sy.do("pool", lambda e, h=h: e.tensor_tensor(out=qp_b[pb][:, h, 0:N], in0=qt1[:, 0:N], in1=qt2[:, 0:N], op=ALU.add), R=[qt1, qt2], W=[qp_b[pb]])
                    if gi == 0: ckpt(22)
                    if not (DBG & 256):
                        kv_up(ckvT[pb], N, ntl, kn_b[pb], vm_b[pb])
                    if gi == 0: ckpt(23)
                    cs_ = slice(r0, r0 + N)
                    for j, dst in enumerate((QA, KA, QC, KC)):
                        sy.dma("sp", dst[:, cs_].rearrange("(m p) n -> p m n", p=128), fo[pb][:, 2 * j:2 * j + 2, 0:N], R=[fo[pb]])
                    sy.dma("sp", QN[:, cs_].rearrange("(h p) n -> p h n", p=128), qn_b[pb][:, :, 0:N], R=[qn_b[pb]])
                    sy.dma("sp", QP[:, cs_].rearrange("(h p) n -> p h n", p=64), qp_b[pb][:, :, 0:N], R=[qp_b[pb]])
                    sy.dma("sp", KN[:, cs_].rearrange("(h p) n -> p h n", p=128), kn_b[pb][:, :, 0:N], R=[kn_b[pb]])
                    sy.dma("sp", KR[:, cs_], krT[pb][:, 0:N], R=[krT[pb]])
                    sy.dma("sp", VM[cs_, :].rearrange("(t p) c -> p t c", p=128), vm_b[pb][:, 0:ntl, :], R=[vm_b[pb]])
                    sy.dma("sp", VA[cs_, :].rearrange("(t p) c -> p t c", p=128), va_b[pb][:, 0:ntl, :], R=[va_b[pb]])
                    sy.dma("sp", VC[cs_, :].rearrange("(t p) c -> p t c", p=128), vc_b[pb][:, 0:ntl, :], R=[vc_b[pb]])
                    if gi == 0: print('ninstr before outs g0', sy.ninstr)
                    if gi == 0: ckpt(24)
                    if not is_s:
                        sy.dma("sp", pckv[l, cs_, :].rearrange("(t p) c -> p t c", p=128), ckv_f[pb][:, 0:ntl, :], R=[ckv_f[pb]])
                        sy.dma("sp", pkr[l, cs_, :].rearrange("(t p) c -> p t c", p=128), kr_f[pb][:, 0:ntl, :], R=[kr_f[pb]])
                        sy.dma("sp", psk[l, cs_, :].rearrange("(t p) c -> p t c", p=128), kcvc[pb][:, 0:ntl, 0:256], R=[kcvc[pb]])
                        sy.dma("sp", psv[l, cs_, :].rearrange("(t p) c -> p t c", p=128), kcvc[pb][:, 0:ntl, 256:512], R=[kcvc[pb]])
                        if gi == 0: ckpt(26)
                        if need_a: ckpt(29)
                        if need_a:
                            sy.dma("sp", pak[l].rearrange("(t p) c -> p t c", p=128), kaf[:, 0:ntl, :], R=[kaf])
                            sy.dma("sp", pav[l].rearrange("(t p) c -> p t c", p=128), vaf[:, 0:ntl, :], R=[vaf])
                        if gi == NG - 1: ckpt(27)
                    else:
                        for b in range(NSQ):
                            t_, p0 = b // 2, (b % 2) * 64
                            sy.dma("sp", sckv[l, b], ckv_f[pb][p0:p0 + 64, t_, :], R=[ckv_f[pb]])
                            sy.dma("sp", skr[l, b], kr_f[pb][p0:p0 + 64, t_, :], R=[kr_f[pb]])
                            sy.dma("sp", ssk[l, b], kcvc[pb][p0:p0 + 64, t_, 0:256], R=[kcvc[pb]])
                            sy.dma("sp", ssv[l, b], kcvc[pb][p0:p0 + 64, t_, 256:512], R=[kcvc[pb]])
                            sy.dma("sp", sak[l, b, 448:512, :], kaf[p0:p0 + 64, t_, :], R=[kaf])
                            sy.dma("sp", sav[l, b, 448:512, :], vaf[p0:p0 + 64, t_, :], R=[vaf])
                    if gi == 1: ckpt(28)
                    if DBG & 32: sy.barrier()
                sy.barrier()
                ckpt(3)

            def soft_res(ph, tag):
                return ([sb(ph, "P%s%d" % (tag, i), [128, 512], BF16) for i in range(3)], sb(ph, "rD" + tag, [128, 512], F32))

            def sb_res(ph, tag):
                return ([sb(ph, "E%s%d" % (tag, i), [128, 512], F32) for i in range(3)],
                        [sb(ph, "L%s%d" % (tag, i), [128, 512], BF16) for i in range(4)],
                        [sb(ph, "X%s%d" % (tag, i), [128, 512], F32) for i in range(2)],
                        [sb(ph, "A%s%d" % (tag, i), [128, 512], BF16) for i in range(3)],
                        [sb(ph, "cr%s%d" % (tag, i), [1, 512], F32) for i in range(2)])

            def attn_soft(res, jobs, scale):
                Sb = [fb[0], fb[1]]
                Ob = [fb[2], fb[3]]
                Db = [fb[4], fb[5]]
                Pt, rD = res
                items = []
                for ji, jb in enumerate(jobs):
                    nb = len(jb["blocks"])
                    for bi, blk in enumerate(jb["blocks"]):
                        items.append((ji, jb, bi, blk, bi == 0, bi == nb - 1))

                def stage_a(t):
                    ji, jb, bi, blk, first, lastb = items[t]
                    N, m = jb["N"], blk["m"]
                    bk = Sb[t % 2]
                    n = len(blk["mms"])
                    for i, (lb, lap, rb, rap) in enumerate(blk["mms"]):
                        mm(bk, bk[0:m, 0:N], lb, lap, rb, rap, i == 0, i == n - 1)

                def stage_b(t):
                    ji, jb, bi, blk, first, lastb = items[t]
                    N, m = jb["N"], blk["m"]
                    bk, P = Sb[t % 2], Pt[t % 3]
                    sy.do("act", lambda e: e.activation(out=P[0:m, 0:N], in_=bk[0:m, 0:N], func=AF.Exp, scale=scale), R=[bk], W=[P])
                    if blk["mulb"] is not None:
                        sy.do("pool", lambda e: e.tensor_tensor(out=P[0:m, 0:N], in0=P[0:m, 0:N], in1=blk["mulap"], op=ALU.mult),
                              R=[blk["mulb"]], W=[P])

                def stage_f(t):
                    ji, jb, bi, blk, first, lastb = items[t]
                    N, m, ed = jb["N"], blk["m"], jb["e"]
                    P = Pt[t % 3]
                    O, Dn = Ob[ji % 2], Db[ji % 2]
                    mm(O, O[0:ed, 0:N], blk["vb"], blk["vap"], P, P[0:m, 0:N], first, lastb)
                    mm(Dn, Dn[0:ed, 0:N], onesb, onesb[0:m, 0:ed], P, P[0:m, 0:N], first, lastb)
                    if lastb:
                        sy.do("dve", lambda e: e.reciprocal(out=rD[0:ed, 0:N], in_=Dn[0:ed, 0:N]), R=[Dn], W=[rD])
                        sy.do("dve", lambda e: e.tensor_tensor(out=jb["dstap"], in0=O[0:ed, 0:N], in1=rD[0:ed, 0:N], op=ALU.mult),
                              R=[O, rD], W=[jb["dstb"]])
                        if jb.get("after"):
                            jb["after"]()

                T = len(items)
                for t in range(T + 1):
                    if t < T:
                        if items[t][2] == 0 and items[t][1].get("before"):
                            items[t][1]["before"]()
                        stage_a(t)
                        stage_b(t)
                    if t >= 1:
                        stage_f(t - 1)

            def attn_sb(res, jobs, scale):
                Zb = [fb[0], fb[1]]
                Bb = [fb[2], fb[3]]
                Ob = [fb[4], fb[5]]
                Eb, Lb, Xb, Ab, crow = res
                items = []
                for ji, jb in enumerate(jobs):
                    nb = len(jb["blocks"])
                    for bi, blk in enumerate(jb["blocks"]):
                        items.append((ji, jb, bi, blk, bi == 0, bi == nb - 1))

                def stage_a(t):
                    ji, jb, bi, blk, first, lastb = items[t]
                    N, m = jb["N"], blk["m"]
                    bk = Zb[t % 2]
                    lb, lap, rb, rap = blk["mm"]
                    mm(bk, bk[0:m, 0:N], lb, lap, rb, rap, True, True)

                def stage_b(t):
                    ji, jb, bi, blk, first, lastb = items[t]
                    N, m = jb["N"], blk["m"]
                    bk, E, Lt = Zb[t % 2], Eb[t % 3], Lb[t % 4]
                    sy.do("act", lambda e: e.activation(out=E[0:m, 0:N], in_=bk[0:m, 0:N], func=AF.Exp, scale=scale), R=[bk], W=[E])
                    if blk["mulb"] is not None:
                        sy.do("pool", lambda e: e.tensor_tensor(out=E[0:m, 0:N], in0=E[0:m, 0:N], in1=blk["mulap"], op=ALU.mult),
                              R=[blk["mulb"]], W=[E])
                    sy.do("act", lambda e: e.activation(out=Lt[0:m, 0:N], in_=E[0:m, 0:N], func=AF.Ln, bias=1.0), R=[E], W=[Lt])

                def stage_c(t):
                    ji, jb, bi, blk, first, lastb = items[t]
                    N, m = jb["N"], blk["m"]
                    B, Lt = Bb[t % 2], Lb[t % 4]
                    if not first:
                        cr = crow[t % 2]
                        mm(B, B[0:m, 0:N], ones1f, ones1f[0:1, 0:m], cr, cr[0:1, 0:N], True, False)
                    mm(B, B[0:m, 0:N], trib, trib[0:m, 0, 0:m], Lt, Lt[0:m, 0:N], first, True)

                def stage_d(t):
                    ji, jb, bi, blk, first, lastb = items[t]
                    N, m = jb["N"], blk["m"]
                    B, X, E, A = Bb[t % 2], Xb[t % 2], Eb[t % 3], Ab[t % 3]
                    if not lastb:
                        cr = crow[(t + 1) % 2]
                        sy.do("dve", lambda e: e.tensor_copy(out=cr[0:1, 0:N], in_=B[0:1, 0:N]), R=[B], W=[cr])
                    sy.do("act", lambda e: e.activation(out=X[0:m, 0:N], in_=B[0:m, 0:N], func=AF.Exp), R=[B], W=[X])
                    eng = ("dve", "pool")[t % 2]
                    sy.do(eng, lambda e: e.tensor_tensor(out=A[0:m, 0:N], in0=E[0:m, 0:N], in1=X[0:m, 0:N], op=ALU.mult), R=[E, X], W=[A])

                def stage_f(t):
                    ji, jb, bi, blk, first, lastb = items[t]
                    N, m, ed = jb["N"], blk["m"], jb["e"]
                    A, O = Ab[t % 3], Ob[ji % 2]
                    mm(O, O[0:ed, 0:N], blk["vb"], blk["vap"], A, A[0:m, 0:N], first, lastb)
                    if lastb:
                        sy.do("dve", lambda e: e.tensor_copy(out=jb["dstap"], in_=O[0:ed, 0:N]), R=[O], W=[jb["dstb"]])
                        if jb.get("after"):
                            jb["after"]()

                T = len(items)
                for t in range(T + 2):
                    if t < T:
                        if items[t][2] == 0 and items[t][1].get("before"):
                            items[t][1]["before"]()
                        stage_a(t)
                        stage_b(t)
                    if 1 <= t <= T:
                        stage_c(t - 1)
                        stage_d(t - 1)
                    if t >= 2:
                        stage_f(t - 2)

            with ExitStack() as ph:
                EB = sb(ph, "EB", [128, 4, 8, 512], BF16)
                with ExitStack() as ph2:
                    rel = sb(ph2, "rel", [1, 768], F32)
                    ext = sb(ph2, "ext", [1, 4, 1536], F32)
                    valb = sb(ph2, "valb", [128, 8, 512], F32)
                    ebr = [sb(ph2, "ebr%d" % i, [128, 512], F32) for i in range(2)]
                    ebrb = [sb(ph2, "ebrb%d" % i, [128, 512], BF16) for i in range(2)]
                    sy.dma("sp", rel[:], a_rel[l:l + 1, :], W=[rel])
                    sy.dma("sp", valb[:], c_val, W=[valb])
                    for h in range(4):
                        sy.do("dve", lambda e, h=h: e.tensor_copy(out=ext[0:1, h, 0:448], in_=rel[0:1, h * 192:h * 192 + 1].to_broadcast([1, 448])), R=[rel], W=[ext])
                        sy.do("dve", lambda e, h=h: e.tensor_copy(out=ext[0:1, h, 448:640], in_=rel[0:1, h * 192:(h + 1) * 192]), R=[rel], W=[ext])
                        sy.do("dve", lambda e, h=h: e.tensor_copy(out=ext[0:1, h, 640:1536], in_=rel[0:1, h * 192 + 191:h * 192 + 192].to_broadcast([1, 896])), R=[rel], W=[ext])
                    sy.do("act", lambda e: e.activation(out=ext[:], in_=ext[:], func=AF.Exp), R=[ext], W=[ext])
                    sy.dma("sp", EXT.rearrange("(o h) n -> o h n", o=1), ext[:], R=[ext])
                    sy.barrier()
                    k = 0
                    for h in range(4):
                        for i in range(8):
                            eb = ebr[k % 2]; ebb = ebrb[k % 2]; bk = fb[k % 2]; k += 1
                            src = bass.AP(EXT.tensor, h * 1536 + 896 - 128 * i, [[1, 128], [1, 512]])
                            sy.dma("sp", eb[:], src, W=[eb])
                            sy.do("pool", lambda e, eb=eb, ebb=ebb: e.tensor_copy(out=ebb[:], in_=eb[:]), R=[eb], W=[ebb])
                            mm(bk, bk[:, :], trib, trib[:, 3, :], ebb, ebb[:], True, True)
                            sy.do("dve", lambda e, h=h, i=i, bk=bk: e.tensor_tensor(out=EB[:, h, i, :], in0=bk[:, :], in1=valb[:, i, :], op=ALU.mult),
                                  R=[bk, valb], W=[EB])
                    sy.do("dve", lambda e: e.tensor_copy(out=EBs[:], in_=EB[:, :, 0:5, 0:64]), R=[EB], W=[EBs])
                    sy.barrier()
                kaT = sb(ph, "kaT", [128, 2, S], BF16)
                vaS = sb(ph, "vaS", [128, NKB, 256], BF16)
                sy.dma("sp", kaT[:], KA[:, 0:S].rearrange("(m p) n -> p m n", p=128), W=[kaT])
                sy.dma("sp", vaS[:], VA[0:S, :].rearrange("(t p) c -> p t c", p=128), W=[vaS])
                qa = [sb(ph, "qa%d" % i, [128, 2, 512], BF16) for i in range(2)]
                ost = [sb(ph, "osta%d" % i, [64, 4, 512], F32) for i in range(2)]
                jobs = []
                for g in range(NG):
                    pb = g % 2
                    for h in range(4):
                        hp, bs = h // 2, (h % 2) * 64
                        blocks = []
                        for i in range(8):
                            kb = 4 * g - 4 + i
                            if kb < 0:
                                continue
                            blocks.append(dict(mms=[(kaT, kaT[bs:bs + 64, hp, kb * 128:(kb + 1) * 128], qa[pb], qa[pb][bs:bs + 64, hp, :])],
                                               m=128, vb=vaS, vap=vaS[:, kb, h * 64:(h + 1) * 64], mulb=EB, mulap=EB[:, h, i, :]))
                        jb = dict(N=512, e=64, blocks=blocks, dstb=ost[pb], dstap=ost[pb][:, h, :])
                        if h == 0:
                            jb["before"] = (lambda g=g, pb=pb: sy.dma("sp", qa[pb][:], QA[:, g * 512:(g + 1) * 512].rearrange("(m p) n -> p m n", p=128), W=[qa[pb]]))
                        if h == 3:
                            jb["after"] = (lambda g=g, pb=pb: sy.dma("sp", OCAT[0:256, g * 512:(g + 1) * 512].rearrange("(h p) n -> p h n", p=64), ost[pb][:], R=[ost[pb]]))
                        jobs.append(jb)
                attn_soft(soft_res(ph, "a"), jobs, 0.125)
                sy.barrier()
                ckpt(4)

            with ExitStack() as ph:
                knT = sb(ph, "knT", [128, 4, S], BF16)
                krS = sb(ph, "krS", [64, S], BF16)
                vmS = sb(ph, "vmS", [128, NKB, 512], BF16)
                mml = sb(ph, "mml", [128, 4, 512], BF16)
                with ExitStack() as ph2:
                    m32 = sb(ph2, "m32", [128, 4, 512], F32)
                    sy.dma("sp", m32[:], c_mml, W=[m32])
                    sy.do("dve", lambda e: e.tensor_copy(out=mml[:], in_=m32[:]), R=[m32], W=[mml])
                    sy.barrier()
                for h in range(4):
                    sy.dma("sp", knT[:, h, :], KN[h * 128:(h + 1) * 128, 0:S], W=[knT])
                sy.dma("sp", krS[:], KR[:, 0:S], W=[krS])
                for q4 in range(0, NKB, 8):
                    sy.dma("sp", vmS[:, q4:q4 + 8, :], VM[q4 * 128:(q4 + 8) * 128, :].rearrange("(t p) c -> p t c", p=128), W=[vmS])
                qn = [sb(ph, "qn%d" % i, [128, 4, 512], BF16) for i in range(2)]
                qp = [sb(ph, "qp%d" % i, [64, 4, 512], BF16) for i in range(2)]
                ost = [sb(ph, "ostm0", [128, 4, 512], F32)] * 2
                jobs = []
                for g in range(NG):
                    pb = g % 2
                    for h in range(4):
                        blocks = []
                        for kb in range(4 * g + 4):
                            r = kb - 4 * g
                            blocks.append(dict(mms=[(knT, knT[:, h, kb * 128:(kb + 1) * 128], qn[pb], qn[pb][:, h, :]),
                                                    (krS, krS[:, kb * 128:(kb + 1) * 128], qp[pb], qp[pb][:, h, :])],
                                               m=128, vb=vmS, vap=vmS[:, kb, h * 128:(h + 1) * 128],
                                               mulb=(mml if r >= 0 else None), mulap=(mml[:, r, :] if r >= 0 else None)))
                        jb = dict(N=512, e=128, blocks=blocks, dstb=ost[pb], dstap=ost[pb][:, h, :])
                        if h == 0:
                            def bf(g=g, pb=pb):
                                sy.dma("sp", qn[pb][:], QN[:, g * 512:(g + 1) * 512].rearrange("(h p) n -> p h n", p=128), W=[qn[pb]])
                                sy.dma("sp", qp[pb][:], QP[:, g * 512:(g + 1) * 512].rearrange("(h p) n -> p h n", p=64), W=[qp[pb]])
                            jb["before"] = bf
                        if h == 3:
                            jb["after"] = (lambda g=g, pb=pb: sy.dma("sp", OCAT[256:768, g * 512:(g + 1) * 512].rearrange("(h p) n -> p h n", p=128), ost[pb][:], R=[ost[pb]]))
                        jobs.append(jb)
                attn_soft(soft_res(ph, "m"), jobs, 192.0 ** -0.5)
                sy.barrier()
                ckpt(5)

            with ExitStack() as ph:
                kcT = sb(ph, "kcT", [128, 2, S], BF16)
                vcS = sb(ph, "vcS", [128, NKB, 256], BF16)
                msb = sb(ph, "msb", [128, 4, 512], F32)
                sy.dma("sp", msb[:], c_msb, W=[msb])
                sy.dma("sp", kcT[:], KC[:, 0:S].rearrange("(m p) n -> p m n", p=128), W=[kcT])
                sy.dma("sp", vcS[:], VC[0:S, :].rearrange("(t p) c -> p t c", p=128), W=[vcS])
                qc = [sb(ph, "qc%d" % i, [128, 2, 512], BF16) for i in range(2)]
                ost = [sb(ph, "osts%d" % i, [64, 4, 512], F32) for i in range(2)]
                jobs = []
                for g in range(NG):
                    pb = g % 2
                    for h in range(4):
                        hp, bs = h // 2, (h % 2) * 64
                        blocks = []
                        for kb in range(4 * g + 3, -1, -1):
                            r = kb - 4 * g
                            blocks.append(dict(mm=(kcT, kcT[bs:bs + 64, hp, kb * 128:(kb + 1) * 128], qc[pb], qc[pb][bs:bs + 64, hp, :]),
                                               m=128, vb=vcS, vap=vcS[:, kb, h * 64:(h + 1) * 64],
                                               mulb=(msb if r >= 0 else None), mulap=(msb[:, r, :] if r >= 0 else None)))
                        jb = dict(N=512, e=64, blocks=blocks, dstb=ost[pb], dstap=ost[pb][:, h, :])
                        if h == 0:
                            jb["before"] = (lambda g=g, pb=pb: sy.dma("sp", qc[pb][:], QC[:, g * 512:(g + 1) * 512].rearrange("(m p) n -> p m n", p=128), W=[qc[pb]]))
                        if h == 3:
                            jb["after"] = (lambda g=g, pb=pb: sy.dma("sp", OCAT[768:1024, g * 512:(g + 1) * 512].rearrange("(h p) n -> p h n", p=64), ost[pb][:], R=[ost[pb]]))
                        jobs.append(jb)
                attn_sb(sb_res(ph, "s"), jobs, 0.125)
                sy.barrier()
                ckpt(6)

            with ExitStack() as ph:
                NQ = NSQ * TS
                KL = PAST + TS
                qaS = sb(ph, "qaS", [128, 2, NQ], BF16); qcS = sb(ph, "qcS", [128, 2, NQ], BF16)
                qnS = sb(ph, "qnS", [128, 4, NQ], BF16); qpS = sb(ph, "qpS", [64, 4, NQ], BF16)
                sy.dma("sp", qaS[:], QA[:, S:NT].rearrange("(m p) n -> p m n", p=128), W=[qaS])
                sy.dma("sp", qcS[:], QC[:, S:NT].rearrange("(m p) n -> p m n", p=128), W=[qcS])
                sy.dma("sp", qnS[:], QN[:, S:NT].rearrange("(h p) n -> p h n", p=128), W=[qnS])
                sy.dma("sp", qpS[:], QP[:, S:NT].rearrange("(h p) n -> p h n", p=64), W=[qpS])
                msb = sb(ph, "msbS", [64, 64], F32)
                sy.dma("sp", msb[:], c_msb[0:64, 0, 0:64], W=[msb])
                osA = sb(ph, "osA", [64, 4, NQ], F32); osM = sb(ph, "osM", [128, 4, NQ], F32); osS = sb(ph, "osS", [64, 4, NQ], F32)
                kaS = [sb(ph, "kaS%d" % i, [128, 2, 576], BF16) for i in range(2)]
                vaS = [sb(ph, "vaSs%d" % i, [128, 5, 256], BF16) for i in range(2)]
                knS = [sb(ph, "knS%d" % i, [128, 4, KL], BF16) for i in range(2)]
                krS = [sb(ph, "krSs%d" % i, [64, KL], BF16) for i in range(2)]
                vmS = [sb(ph, "vmSs%d" % i, [128, NPB + 1, 512], BF16) for i in range(2)]
                kcS = [sb(ph, "kcS%d" % i, [128, 2, KL], BF16) for i in range(2)]
                vcS = [sb(ph, "vcSs%d" % i, [128, NPB + 1, 256], BF16) for i in range(2)]
                ja, jm, js = [], [], []
                for b in range(NSQ):
                    pb = b % 2
                    c0 = S + b * TS
                    qs = slice(b * TS, (b + 1) * TS)

                    def ld(b=b, pb=pb, c0=c0):
                        sy.dma("sp", kaS[pb][:, :, 0:512], KAs[b].rearrange("(m p) n -> p m n", p=128), W=[kaS[pb]])
                        sy.dma("sp", kaS[pb][:, :, 512:576], KA[:, c0:c0 + TS].rearrange("(m p) n -> p m n", p=128), W=[kaS[pb]])
                        sy.dma("sp", vaS[pb][:, 0:4, :], VAs[b].rearrange("(t p) c -> p t c", p=128), W=[vaS[pb]])
                        sy.dma("sp", vaS[pb][0:64, 4, :], VA[c0:c0 + TS, :], W=[vaS[pb]])
                        sy.dma("sp", knS[pb][:, :, 0:PAST], KNs[b].rearrange("(h p) n -> p h n", p=128), W=[knS[pb]])
                        sy.dma("sp", knS[pb][:, :, PAST:KL], KN[:, c0:c0 + TS].rearrange("(h p) n -> p h n", p=128), W=[knS[pb]])
                        sy.dma("sp", krS[pb][:, 0:PAST], KRs[b], W=[krS[pb]])
                        sy.dma("sp", krS[pb][:, PAST:KL], KR[:, c0:c0 + TS], W=[krS[pb]])
                        sy.dma("sp", vmS[pb][:, 0:NPB, :], VMs[b].rearrange("(t p) c -> p t c", p=128), W=[vmS[pb]])
                        sy.dma("sp", vmS[pb][0:64, NPB, :], VM[c0:c0 + TS, :], W=[vmS[pb]])
                        sy.dma("sp", kcS[pb][:, :, 0:PAST], KCs[b].rearrange("(m p) n -> p m n", p=128), W=[kcS[pb]])
                        sy.dma("sp", kcS[pb][:, :, PAST:KL], KC[:, c0:c0 + TS].rearrange("(m p) n -> p m n", p=128), W=[kcS[pb]])
                        sy.dma("sp", vcS[pb][:, 0:NPB, :], VCs[b].rearrange("(t p) c -> p t c", p=128), W=[vcS[pb]])
                        sy.dma("sp", vcS[pb][0:64, NPB, :], VC[c0:c0 + TS, :], W=[vcS[pb]])
                    for h in range(4):
                        hp, bs = h // 2, (h % 2) * 64
                        blocks = []
                        for i in range(5):
                            m = 128 if i < 4 else 64
                            blocks.append(dict(mms=[(kaS[pb], kaS[pb][bs:bs + 64, hp, i * 128:i * 128 + m], qaS, qaS[bs:bs + 64, hp, qs])],
                                               m=m, vb=vaS[pb], vap=vaS[pb][0:m, i, h * 64:(h + 1) * 64], mulb=EBs, mulap=EBs[0:m, h, i, :]))
                        jb = dict(N=TS, e=64, blocks=blocks, dstb=osA, dstap=osA[:, h, qs])
                        if h == 0:
                            jb["before"] = ld
                        ja.append(jb)
                        blocks = []
                        for kb in range(NPB + 1):
                            m = 128 if kb < NPB else 64
                            blocks.append(dict(mms=[(knS[pb], knS[pb][:, h, kb * 128:kb * 128 + m], qnS, qnS[:, h, qs]),
                                                    (krS[pb], krS[pb][:, kb * 128:kb * 128 + m], qpS, qpS[:, h, qs])],
                                               m=m, vb=vmS[pb], vap=vmS[pb][0:m, kb, h * 128:(h + 1) * 128], mulb=None, mulap=None))
                        jm.append(dict(N=TS, e=128, blocks=blocks, dstb=osM, dstap=osM[:, h, qs]))
                        blocks = []
                        for kb in range(NPB, -1, -1):
                            m = 128 if kb < NPB else 64
                            blocks.append(dict(mm=(kcS[pb], kcS[pb][bs:bs + 64, hp, kb * 128:kb * 128 + m], qcS, qcS[bs:bs + 64, hp, qs]),
                                               m=m, vb=vcS[pb], vap=vcS[pb][0:m, kb, h * 64:(h + 1) * 64],
                                               mulb=(msb if kb == NPB else None), mulap=(msb[:, :] if kb == NPB else None)))
                        js.append(dict(N=TS, e=64, blocks=blocks, dstb=osS, dstap=osS[:, h, qs]))
                rsoft, rsb = soft_res(ph, "x"), sb_res(ph, "x")
                for b in range(NSQ):
                    attn_soft(rsoft, ja[4 * b:4 * b + 4], 0.125)
                    attn_soft(rsoft, jm[4 * b:4 * b + 4], 192.0 ** -0.5)
                    attn_sb(rsb, js[4 * b:4 * b + 4], 0.125)
                sy.dma("sp", OCAT[0:256, S:NT].rearrange("(h p) n -> p h n", p=64), osA[:], R=[osA])
                sy.dma("sp", OCAT[256:768, S:NT].rearrange("(h p) n -> p h n", p=128), osM[:], R=[osM])
                sy.dma("sp", OCAT[768:1024, S:NT].rearrange("(h p) n -> p h n", p=64), osS[:], R=[osS])
                sy.barrier()
                ckpt(7)

            with ExitStack() as ph:
                wout = sb(ph, "wout", [128, 8, D], BF16)
                gcat = sb(ph, "gcat", [128, 8], F32)
                sy.dma("sp", gcat[:], g_cat[l].rearrange("(c p) -> p c", p=128), W=[gcat], allow_slow_non_contiguous=True)
                with ExitStack() as ph2:
                    load_w(ph2, wout, [wout[:, c, :] for c in range(8)], [w_out[l, c * 128:(c + 1) * 128, :] for c in range(8)], D)
                    sy.barrier()
                oc = [sb(ph, "oc%d" % i, [128, 8, 512], F32) for i in range(2)]
                xg = [sb(ph, "xg2_%d" % i, [128, 4, D], F32) for i in range(2)]
                sq = sb(ph, "sq", [128, 8, 512], BF16)
                rs = [sb(ph, "rs%d" % i, [128, 512], F32) for i in range(3)]
                catT = [sb(ph, "catT%d" % i, [128, 8, 512], BF16) for i in range(2)]
                for (r0, ntl, gi) in groups:
                    N = ntl * 128
                    pb = gi % 2
                    OC, X, CT = oc[pb], xg[pb], catT[pb]
                    sy.dma("sp", OC[:, :, 0:N], OCAT[:, r0:r0 + N].rearrange("(c p) n -> p c n", p=128), W=[OC])
                    sy.dma("sp", X[:, 0:ntl, :], xsrc[r0:r0 + N, :].rearrange("(t p) d -> p t d", p=128), W=[X])
                    sy.do("act", lambda e: e.activation(out=sq[:, :, 0:N], in_=OC[:, :, 0:N], func=AF.Square), R=[OC], W=[sq])
                    for mi, (a0, a1, Wd) in enumerate(((0, 2, 256), (2, 6, 512), (6, 8, 256))):
                        bk = fb[mi]
                        for c in range(a0, a1):
                            mm(bk, bk[:, 0:N], onesb, onesb[:, :], sq, sq[:, c, 0:N], c == a0, c == a1 - 1)
                        R_ = rs[mi]
                        sy.do("dve", lambda e, R_=R_, bk=bk, Wd=Wd: e.tensor_scalar(out=R_[:, 0:N], in0=bk[:, 0:N], scalar1=1.0 / Wd, scalar2=EPS,
                                                                                   op0=ALU.mult, op1=ALU.add), R=[bk], W=[R_])
                        sy.do("act", lambda e, R_=R_: e.activation(out=R_[:, 0:N], in_=R_[:, 0:N], func=AF.Sqrt), R=[R_], W=[R_])
                        sy.do("dve", lambda e, R_=R_: e.reciprocal(out=R_[:, 0:N], in_=R_[:, 0:N]), R=[R_], W=[R_])
                        for c in range(a0, a1):
                            sy.do("dve", lambda e, c=c, R_=R_: e.scalar_tensor_tensor(out=CT[:, c, 0:N], in0=OC[:, c, 0:N], scalar=gcat[:, c:c + 1], in1=R_[:, 0:N],
                                                                                                  op0=ALU.mult, op1=ALU.mult), R=[OC, gcat, R_], W=[CT])
                    for t in range(ntl):
                        for hf in range(2):
                            bk = fb[3 + (2 * t + hf) % 4]
                            for c in range(8):
                                mm(bk, bk[:, :], CT, CT[:, c, t * 128:(t + 1) * 128], wout, wout[:, c, hf * 512:(hf + 1) * 512], c == 0, c == 7)
                            sy.do("dve", lambda e, t=t, hf=hf, bk=bk: e.tensor_tensor(out=X[:, t, hf * 512:(hf + 1) * 512], in0=X[:, t, hf * 512:(hf + 1) * 512], in1=bk[:, :], op=ALU.add),
                                  R=[bk], W=[X])
                    sy.dma("sp", XR[r0:r0 + N, :].rearrange("(t p) d -> p t d", p=128), X[:, 0:ntl, :], R=[X])
                sy.barrier()
                ckpt(8)

            with ExitStack() as ph:
                wup = sb(ph, "wup", [128, 8, DFF], BF16)
                wdn = sb(ph, "wdn", [128, 32, D], BF16)
                gffnb = sb(ph, "gffnb", [128, D], F32)
                sy.dma("sp", gffnb[:], g_ffn[l].partition_broadcast(128), W=[gffnb])
                with ExitStack() as ph2:
                    load_w(ph2, wup, [wup[:, c, hf * 2048:(hf + 1) * 2048] for c in range(8) for hf in range(2)],
                           [w_up[l, c * 128:(c + 1) * 128, hf * 2048:(hf + 1) * 2048] for c in range(8) for hf in range(2)], 2048)
                    load_w(ph2, wdn, [wdn[:, 2 * j:2 * j + 2, :] for j in range(16)],
                           [w_down[l, j * 256:(j + 1) * 256, :].rearrange("(f p) d -> p f d", p=128) for j in range(16)], 2048)
                    sy.barrier()
                xg = [sb(ph, "xg3_%d" % i, [128, 2, D], F32) for i in range(2)]
                scr = sb(ph, "scr3", [128, D], F32)
                ssx = sb(ph, "ssx3", [128, 1], F32)
                hb = [sb(ph, "hb3_%d" % i, [128, D], BF16) for i in range(2)]
                hT = [sb(ph, "hT3_0", [128, 8, 256], BF16)] * 2
                rb = [sb(ph, "rb%d" % i, [128, 256], F32) for i in range(2)]
                uT = [sb(ph, "uT0", [128, 32, 256], BF16)] * 2
                yb = [sb(ph, "yb0", [128, 2, D], F32)] * 2 if last else None
                for gi in range(NT // 256):
                    r0 = gi * 256
                    pb = gi % 2
                    X, HT, UT = xg[pb], hT[pb], uT[pb]
                    sy.dma("sp", X[:], XR[r0:r0 + 256, :].rearrange("(t p) d -> p t d", p=128), W=[X])
                    for t in range(2):
                        H = hb[t % 2]
                        rstd_rows(X, X[:, t, :], D, scr, ssx)
                        sy.do("dve", lambda e, t=t, H=H: e.scalar_tensor_tensor(out=H[:], in0=X[:, t, :], scalar=ssx[:, 0:1], in1=gffnb[:],
                                                                               op0=ALU.mult, op1=ALU.mult), R=[X, ssx, gffnb], W=[H])
                        for c in range(8):
                            sy.do("pe", lambda e, c=c, H=H: e.transpose(out=tb[:, c * 128:(c + 1) * 128], in_=H[:, c * 128:(c + 1) * 128], identity=identb[:]),
                                  R=[H, identb], W=[tb])
                        sy.do("act", lambda e, t=t: e.activation(out=HT[:, :, t * 128:(t + 1) * 128], in_=tb[:, :].rearrange("p (c n) -> p c n", c=8), func=AF.Copy),
                              R=[tb], W=[HT])
                    for fc in range(32):
                        bk = fb[fc % 3]
                        for c in range(8):
                            mm(bk, bk[:, 0:256], wup, wup[:, c, fc * 128:(fc + 1) * 128], HT, HT[:, c, :], c == 0, c == 7)
                        R_ = rb[fc % 2]
                        sy.do("act", lambda e, bk=bk, R_=R_: e.activation(out=R_[:], in_=bk[:, 0:256], func=AF.Relu), R=[bk], W=[R_])
                        sy.do("pool", lambda e, fc=fc, R_=R_: e.tensor_tensor(out=UT[:, fc, :], in0=R_[:], in1=R_[:], op=ALU.mult), R=[R_], W=[UT])
                    for t in range(2):
                        for hf in range(2):
                            bk = fb[3 + (2 * t + hf) % 4]
                            for fc in range(32):
                                mm(bk, bk[:, :], UT, UT[:, fc, t * 128:(t + 1) * 128], wdn, wdn[:, fc, hf * 512:(hf + 1) * 512], fc == 0, fc == 31)
                            sy.do("dve", lambda e, t=t, hf=hf, bk=bk: e.tensor_tensor(out=X[:, t, hf * 512:(hf + 1) * 512], in0=X[:, t, hf * 512:(hf + 1) * 512], in1=bk[:, :], op=ALU.add),
                                  R=[bk], W=[X])
                    if not last:
                        sy.dma("sp", XR[r0:r0 + 256, :].rearrange("(t p) d -> p t d", p=128), X[:], R=[X])
                    else:
                        Y = yb[pb]
                        for t in range(2):
                            rstd_rows(X, X[:, t, :], D, scr, ssx)
                            sy.do("dve", lambda e, t=t: e.scalar_tensor_tensor(out=Y[:, t, :], in0=X[:, t, :], scalar=ssx[:, 0:1], in1=gfinb[:],
                                                                              op0=ALU.mult, op1=ALU.mult), R=[X, ssx, gfinb], W=[Y])
                        sy.dma("sp", y[r0:r0 + 256, :].rearrange("(t p) d -> p t d", p=128), Y[:], R=[Y])
                sy.barrier()
                ckpt(9)
        print("instructions emitted:", sy.ninstr, {k: v for k, v in sy.cnt.items() if not k.startswith("d")})
    return nc


def _consts(S, PAST):
    NT = S + NSQ * TS
    j = np.arange(128)[:, None]
    t = np.arange(512)[None, :]
    tri = np.zeros((128, 4, 128), np.float32)
    jj, kk = np.arange(128)[:, None], np.arange(128)[None, :]
    tri[:, 0, :] = -1.0 * (jj >= kk)
    tri[:, 1, :] = -1.0 * (jj < kk)
    tri[:, 2, :] = -1.0
    tri[:, 3, :] = np.eye(128, dtype=np.float32)[::-1]
    msb = np.stack([(j + 128 * r < t) for r in range(4)], 1).astype(np.float32)
    mml = np.stack([((128 * r + j) // 64 <= t // 64) for r in range(4)], 1).astype(np.float32)
    val = []
    for i in range(8):
        d = 8 + t // 64 - (2 * i + j // 64)
        val.append((d >= 0) & (d <= 8))
    val = np.stack(val, 1).astype(np.float32)
    pos = np.concatenate([np.arange(S), np.tile(PAST + np.arange(TS), NSQ)]).astype(np.float32)
    inv = (10000.0 ** (-np.arange(32, dtype=np.float32) / np.float32(32))).astype(np.float32)
    ang = (pos[:, None] * inv[None, :]).astype(np.float32)
    cos, sin = np.cos(ang).astype(np.float32), np.sin(ang).astype(np.float32)
    ropeT = np.concatenate([cos, sin], 1).astype(np.float32)
    cosF = np.concatenate([cos, cos], 1).T
    sinF = np.concatenate([-sin, sin], 1).T
    ropeF = np.ascontiguousarray(np.stack([cosF, sinF], 1)).astype(np.float32)
    return dict(c_ident=np.eye(128, dtype=np.float32), c_tri=tri, c_msb=np.ascontiguousarray(msb),
                c_mml=np.ascontiguousarray(mml), c_val=np.ascontiguousarray(val), c_ropeT=ropeT, c_ropeF=ropeF)


_CACHE = {}


def kernel(x_prompt, x_sample, cache_a_k, cache_a_v, cache_mla_ckv, cache_mla_krope, cache_sb_k, cache_sb_v,
           g_mix, w_in, g_cq, g_ckv, w_uq, w_ukv, a_rel_bias, g_out_a, g_out_mla, g_out_sb, w_out,
           g_ffn, w_up, w_down, g_final):
    f = lambda a: np.ascontiguousarray(np.asarray(a, dtype=np.float32))
    x_prompt, x_sample = f(x_prompt), f(x_sample)
    B, S, _ = x_prompt.shape
    L = w_in.shape[0]
    PAST = cache_mla_ckv.shape[2]
    ncore = B
    assert x_sample.shape[0] == NSQ * ncore and x_sample.shape[1] == TS
    key = (S, PAST, L)
    if key not in _CACHE:
        _CACHE[key] = build(S, PAST, L)
    nc = _CACHE[key]
    cst = _consts(S, PAST)
    shared = dict(g_mix=f(g_mix), w_in=f(w_in), g_cq=f(g_cq), g_ckv=f(g_ckv),
                  w_uq=f(w_uq).reshape(L, 256, 768), w_ukv=f(w_ukv).reshape(L, 256, 1024),
                  a_rel=f(a_rel_bias).reshape(L, 768),
                  g_cat=np.ascontiguousarray(np.concatenate([f(g_out_a), f(g_out_mla), f(g_out_sb)], axis=1)),
                  w_out=f(w_out), g_ffn=f(g_ffn), w_up=f(w_up), w_down=f(w_down), g_fin=f(g_final), **cst)
    cak, cav = f(cache_a_k), f(cache_a_v)
    cckv, ckr, csk, csv = f(cache_mla_ckv), f(cache_mla_krope), f(cache_sb_k), f(cache_sb_v)
    in_maps = []
    for c in range(ncore):
        sl = slice(NSQ * c, NSQ * c + NSQ)
        m = dict(shared)
        m["x"] = np.ascontiguousarray(np.concatenate([x_prompt[c], x_sample[sl].reshape(NSQ * TS, D)], axis=0))
        m["cak"] = np.ascontiguousarray(cak[:, sl].reshape(L, NSQ, 512, 256))
        m["cav"] = np.ascontiguousarray(cav[:, sl].reshape(L, NSQ, 512, 256))
        m["cckv"] = np.ascontiguousarray(cckv[:, sl])
        m["ckr"] = np.ascontiguousarray(ckr[:, sl])
        m["csk"] = np.ascontiguousarray(csk[:, sl].reshape(L, NSQ, PAST, 256))
        m["csv"] = np.ascontiguousarray(csv[:, sl].reshape(L, NSQ, PAST, 256))
        in_maps.append(m)
    res = run_bass_kernel_spmd(nc, in_maps, core_ids=list(range(ncore)))
    R = res.results
    cat = lambda k, ax: np.concatenate([np.asarray(r[k]) for r in R], axis=ax)
    yall = np.stack([np.asarray(r["y"]) for r in R], 0)
    y_prompt = np.ascontiguousarray(yall[:, :S])
    y_sample = np.ascontiguousarray(yall[:, S:].reshape(ncore * NSQ, TS, D))
    st1 = lambda k: np.stack([np.asarray(r[k]) for r in R], 1)
    p_a_k = st1("pak").reshape(L, B, 512, 4, 64)
    p_a_v = st1("pav").reshape(L, B, 512, 4, 64)
    p_ckv = st1("pckv")
    p_krope = st1("pkr")
    p_sb_k = st1("psk").reshape(L, B, S, 4, 64)
    p_sb_v = st1("psv").reshape(L, B, S, 4, 64)
    s_a_k = cat("sak", 1).reshape(L, ncore * NSQ, 512, 4, 64)
    s_a_v = cat("sav", 1).reshape(L, ncore * NSQ, 512, 4, 64)
    s_ckv = cat("sckv", 1)
    s_krope = cat("skr", 1)
    s_sb_k = cat("ssk", 1).reshape(L, ncore * NSQ, TS, 4, 64)
    s_sb_v = cat("ssv", 1).reshape(L, ncore * NSQ, TS, 4, 64)
    outs = (y_prompt, y_sample, p_a_k, p_a_v, p_ckv, p_krope, p_sb_k, p_sb_v,
            s_a_k, s_a_v, s_ckv, s_krope, s_sb_k, s_sb_v)
    return tuple(np.ascontiguousarray(o, dtype=np.float32) for o in outs)
```

```python
import os
import numpy as np
DBG = int(os.environ.get('MK_DBG', '0'))
STOPN = int(os.environ.get('MK_STOPN', '-1'))
MAXOUT = int(os.environ.get('MK_MAXOUT', '4'))


class _Stop(Exception):
    pass
from contextlib import ExitStack
import concourse.bass as bass
import concourse.mybir as mybir
from concourse.bass_utils import run_bass_kernel_spmd

F32 = mybir.dt.float32
BF16 = mybir.dt.bfloat16
AF = mybir.ActivationFunctionType
ALU = mybir.AluOpType
AX = mybir.AxisListType

D = 1024
DFF = 4096
NSQ = 4
TS = 64
EPS = 1e-6
INC = 2112


class Buf:
    __slots__ = ("t", "w", "r", "dk", "x")

    def __init__(self, t, x=False):
        self.t = t
        self.w = None
        self.r = {}
        self.dk = None
        self.x = x

    def __getitem__(self, k):
        return self.t[k]


class Sy:
    def __init__(self, nc, st, ndma=int(os.environ.get('MK_NDMA', '72'))):
        self.nc = nc
        self.eng = {"pe": nc.tensor, "act": nc.scalar, "dve": nc.vector, "pool": nc.gpsimd, "sp": nc.sync}
        self.sem, self.cnt = {}, {}
        self.seen = {e: {} for e in self.eng}
        for e in self.eng:
            self.sem[e] = st.enter_context(nc.semaphore("s_" + e))
            self.cnt[e] = 0
        self.dpool = []
        for i in range(ndma):
            k = "d%d" % i
            self.sem[k] = st.enter_context(nc.semaphore(k))
            self.cnt[k] = 0
            self.dpool.append(k)
        self.dnext = 0
        self.ninstr = 0
        self.fifo = []

    def _wait(self, e, waits):
        for k, v in waits:
            if k == "pe" and e == "pe":
                continue
            if self.seen[e].get(k, 0) >= v:
                continue
            self.eng[e].wait_ge(self.sem[k], v)
            self.seen[e][k] = v

    def _deps(self, R, W):
        waits = []
        for b in R:
            if b.w:
                waits.append(b.w)
            if b.x:
                waits.extend(b.r.items())
        for b in W:
            if b.w:
                waits.append(b.w)
            waits.extend(b.r.items())
        return waits

    def do(self, e, fn, R=(), W=()):
        if self.ninstr == STOPN:
            self.barrier()
            raise _Stop()
        self._wait(e, self._deps(R, W))
        ins = fn(self.eng[e])
        self.cnt[e] += 1
        ins.then_inc(self.sem[e], 1)
        v = self.cnt[e]
        for b in R:
            if b.x:
                b.w = (e, v)
                b.r = {}
            else:
                b.r[e] = v
        for b in W:
            b.w = (e, v)
            b.r = {}
        self.ninstr += 1
        return (e, v)

    def dma(self, q, out, in_, R=(), W=(), **kw):
        if self.ninstr == STOPN:
            self.barrier()
            raise _Stop()
        if len(self.fifo) >= MAXOUT:
            self._wait(q, [self.fifo.pop(0)])
        self._wait(q, self._deps(R, W))
        b0 = (list(W) + list(R))[0]
        if b0.dk is None:
            b0.dk = self.dpool[self.dnext % len(self.dpool)]
            self.dnext += 1
        k = b0.dk
        ins = self.eng[q].dma_start(out=out, in_=in_, **kw)
        self.cnt[k] += 16
        ins.then_inc(self.sem[k], 16)
        v = self.cnt[k]
        for b in R:
            b.r[k] = v
        for b in W:
            b.w = (k, v)
            b.r = {}
        self.ninstr += 1
        self.fifo.append((k, v))
        return (k, v)

    def barrier(self):
        allw = [(k, v) for k, v in self.cnt.items() if v > 0]
        for e in self.eng:
            self._wait(e, allw)
        self.dnext = 0


def build(S, PAST, L, stop=None):
    NT = S + NSQ * TS
    NG = S // 512
    NKB = S // 128
    NPB = PAST // 128
    nc = bass.Bass("TRN2", target_bir_lowering=False)

    def din(name, shape, dt=F32):
        return nc.dram_tensor(name, list(shape), dt, kind="ExternalInput").ap()

    def dout(name, shape):
        return nc.dram_tensor(name, list(shape), F32, kind="ExternalOutput").ap()

    def dscr(name, shape, dt=BF16):
        return nc.dram_tensor(name, list(shape), dt).ap()

    xin = din("x", [NT, D])
    cak = din("cak", [L, NSQ, 512, 256]); cav = din("cav", [L, NSQ, 512, 256])
    cckv = din("cckv", [L, NSQ, PAST, 256]); ckr = din("ckr", [L, NSQ, PAST, 64])
    csk = din("csk", [L, NSQ, PAST, 256]); csv = din("csv", [L, NSQ, PAST, 256])
    g_mix = din("g_mix", [L, D]); w_in = din("w_in", [L, D, INC])
    g_cq = din("g_cq", [L, 256]); g_ckv = din("g_ckv", [L, 256])
    w_uq = din("w_uq", [L, 256, 768]); w_ukv = din("w_ukv", [L, 256, 1024])
    a_rel = din("a_rel", [L, 768]); g_cat = din("g_cat", [L, D])
    w_out = din("w_out", [L, D, D]); g_ffn = din("g_ffn", [L, D])
    w_up = din("w_up", [L, D, DFF]); w_down = din("w_down", [L, DFF, D]); g_fin = din("g_fin", [D])
    c_ident = din("c_ident", [128, 128]); c_tri = din("c_tri", [128, 4, 128])
    c_msb = din("c_msb", [128, 4, 512]); c_mml = din("c_mml", [128, 4, 512]); c_val = din("c_val", [128, 8, 512])
    c_ropeT = din("c_ropeT", [NT, 64]); c_ropeF = din("c_ropeF", [64, 2, NT])

    y = dout("y", [NT, D])
    pak = dout("pak", [L, 512, 256]); pav = dout("pav", [L, 512, 256])
    pckv = dout("pckv", [L, S, 256]); pkr = dout("pkr", [L, S, 64])
    psk = dout("psk", [L, S, 256]); psv = dout("psv", [L, S, 256])
    sak = dout("sak", [L, NSQ, 512, 256]); sav = dout("sav", [L, NSQ, 512, 256])
    sckv = dout("sckv", [L, NSQ, TS, 256]); skr = dout("skr", [L, NSQ, TS, 64])
    ssk = dout("ssk", [L, NSQ, TS, 256]); ssv = dout("ssv", [L, NSQ, TS, 256])

    XR = dscr("XR", [NT, D], F32)
    QA = dscr("QA", [256, NT]); KA = dscr("KA", [256, NT]); QC = dscr("QC", [256, NT]); KC = dscr("KC", [256, NT])
    VA = dscr("VA", [NT, 256]); VC = dscr("VC", [NT, 256])
    QN = dscr("QN", [512, NT]); QP = dscr("QP", [256, NT]); KN = dscr("KN", [512, NT]); KR = dscr("KR", [64, NT])
    VM = dscr("VM", [NT, 512])
    KAs = dscr("KAs", [NSQ, 256, 512]); VAs = dscr("VAs", [NSQ, 512, 256])
    KNs = dscr("KNs", [NSQ, 512, PAST]); KRs = dscr("KRs", [NSQ, 64, PAST]); VMs = dscr("VMs", [NSQ, PAST, 512])
    KCs = dscr("KCs", [NSQ, 256, PAST]); VCs = dscr("VCs", [NSQ, PAST, 256])
    OCAT = dscr("OCAT", [D, NT], F32)
    EXT = dscr("EXT", [4, 1536], F32)

    with ExitStack() as st:
        st.push(lambda et, ev, tb_: et is _Stop)
        sy = Sy(nc, st)

        uniq = [0]

        def sb(stk, name, shape, dt):
            uniq[0] += 1
            return Buf(stk.enter_context(nc.sbuf_tensor("%s_%d" % (name, uniq[0]), list(shape), dt)))

        fb = [Buf(st.enter_context(nc.psum_tensor("pf%d" % i, [128, 512], F32)), x=True) for i in range(7)]
        tb = Buf(st.enter_context(nc.psum_tensor("ptb", [128, 1024], BF16)), x=True)

        identb = sb(st, "identb", [128, 128], BF16)
        trib = sb(st, "trib", [128, 4, 128], BF16)
        onesb = sb(st, "onesb", [128, 128], BF16)
        ones1f = sb(st, "ones1f", [1, 128], F32)
        gfinb = sb(st, "gfinb", [128, D], F32)
        EBs = sb(st, "EBs", [128, 4, 5, 64], BF16)
        with ExitStack() as ph:
            t1 = sb(ph, "c_t1", [128, 128], F32)
            t2 = sb(ph, "c_t2", [128, 4, 128], F32)
            sy.dma("sp", t1[:], c_ident, W=[t1])
            sy.dma("sp", t2[:], c_tri, W=[t2])
            sy.dma("sp", gfinb[:], g_fin.partition_broadcast(128), W=[gfinb])
            sy.do("dve", lambda e: e.tensor_copy(out=identb[:], in_=t1[:]), R=[t1], W=[identb])
            sy.do("dve", lambda e: e.tensor_copy(out=trib[:], in_=t2[:]), R=[t2], W=[trib])
            sy.do("dve", lambda e: e.memset(onesb[:], 1.0), W=[onesb])
            sy.do("dve", lambda e: e.memset(ones1f[:], 1.0), W=[ones1f])
            sy.barrier()

        ddb = Buf(None)
        def mm(out_b, out_ap, lhsT_b, lhsT, rhs_b, rhs, start, stop):
            return sy.do("pe", lambda e: e.matmul(out_ap, lhsT=lhsT, rhs=rhs, start=start, stop=stop),
                         R=[lhsT_b, rhs_b], W=[out_b])

        cast_rr = [0]

        def load_w(ph, dst_b, views, srcs, width):
            stg = [sb(ph, "wst%d_%d" % (sy.ninstr, i), [128, width], F32) for i in range(2)]
            for i, (v, s_) in enumerate(zip(views, srcs)):
                sg = stg[i % 2]
                sy.dma("sp", sg[:], s_, W=[sg])
                eng = ("pool", "dve")[cast_rr[0] % 2]
                cast_rr[0] += 1
                sy.do(eng, lambda e, v=v, sg=sg: e.tensor_copy(out=v, in_=sg[:]), R=[sg], W=[dst_b])

        def rstd_rows(src_b, src_ap, width, scr_b, ss_b, sq_eng="act"):
            if sq_eng == "act":
                sy.do("act", lambda e: e.activation(out=scr_b[:, 0:width], in_=src_ap, func=AF.Square), R=[src_b], W=[scr_b])
            else:
                sy.do("dve", lambda e: e.tensor_tensor(out=scr_b[:, 0:width], in0=src_ap, in1=src_ap, op=ALU.mult), R=[src_b], W=[scr_b])
            sy.do("dve", lambda e: e.reduce_sum(out=ss_b[:, 0:1], in_=scr_b[:, 0:width], axis=AX.X), R=[scr_b], W=[ss_b])
            sy.do("dve", lambda e: e.tensor_scalar(out=ss_b[:, 0:1], in0=ss_b[:, 0:1], scalar1=1.0 / width, scalar2=EPS,
                                                   op0=ALU.mult, op1=ALU.add), R=[ss_b], W=[ss_b])
            if not (DBG & 4096):
                sy.do("act", lambda e: e.activation(out=ss_b[:, 0:1], in_=ss_b[:, 0:1], func=AF.Sqrt), R=[ss_b], W=[ss_b])
            sy.do("dve", lambda e: e.reciprocal(out=ss_b[:, 0:1], in_=ss_b[:, 0:1]), R=[ss_b], W=[ss_b])

        def ckpt(k):
            if stop == k:
                sy.barrier()
                raise _Stop()

        for l in range(L):
            xsrc = xin if l == 0 else XR
            last = (l == L - 1)
            ckpt(0)

            with ExitStack() as ph:
                win = sb(ph, "win", [128, 8, INC], BF16)
                wuq = sb(ph, "wuq", [128, 2, 768], BF16)
                wuqs = sb(ph, "wuqs", [128, 2, 256], BF16)
                wkn = sb(ph, "wkn", [128, 2, 512], BF16)
                wv = sb(ph, "wv", [128, 2, 512], BF16)
                gmixb = sb(ph, "gmixb", [128, D], F32)
                gcqb = sb(ph, "gcqb", [128, 256], F32)
                gckvb = sb(ph, "gckvb", [128, 256], F32)
                sy.dma("sp", gmixb[:], g_mix[l].partition_broadcast(128), W=[gmixb])
                sy.dma("sp", gcqb[:], g_cq[l].partition_broadcast(128), W=[gcqb])
                sy.dma("sp", gckvb[:], g_ckv[l].partition_broadcast(128), W=[gckvb])
                with ExitStack() as ph2:
                    load_w(ph2, win, [win[:, c, :] for c in range(8)], [w_in[l, c * 128:(c + 1) * 128, :] for c in range(8)], INC)
                    wq32 = sb(ph2, "wq32", [128, 2, 768], F32)
                    wkv32 = sb(ph2, "wkv32", [128, 2, 1024], F32)
                    sy.dma("sp", wq32[:], w_uq[l].rearrange("(k p) e -> p k e", p=128), W=[wq32])
                    sy.dma("sp", wkv32[:], w_ukv[l].rearrange("(k p) e -> p k e", p=128), W=[wkv32])
                    sy.do("dve", lambda e: e.tensor_copy(out=wuq[:], in_=wq32[:]), R=[wq32], W=[wuq])
                    for k in range(2):
                        q4 = wq32[:, k, :].rearrange("p (h e) -> p h e", h=4)
                        d4 = wuqs[:, k, :].rearrange("p (h e) -> p h e", h=4)
                        sy.do("dve", lambda e, q4=q4, d4=d4: e.tensor_copy(out=d4[:, :, 0:32], in_=q4[:, :, 160:192]), R=[wq32], W=[wuqs])
                        sy.do("dve", lambda e, q4=q4, d4=d4: e.tensor_copy(out=d4[:, :, 32:64], in_=q4[:, :, 128:160]), R=[wq32], W=[wuqs])
                        k4 = wkv32[:, k, :].rearrange("p (h e) -> p h e", h=4)
                        sy.do("dve", lambda e, k4=k4, k=k: e.tensor_copy(out=wkn[:, k, :].rearrange("p (h e) -> p h e", h=4), in_=k4[:, :, 0:128]), R=[wkv32], W=[wkn])
                        sy.do("dve", lambda e, k4=k4, k=k: e.tensor_copy(out=wv[:, k, :].rearrange("p (h e) -> p h e", h=4), in_=k4[:, :, 128:256]), R=[wkv32], W=[wv])
                    sy.barrier()

                ckpt(1)
                xt = [sb(ph, "xt%d" % i, [128, D], F32) for i in range(3)]
                rt = [sb(ph, "rt%d" % i, [128, 4, 64], F32) for i in range(2)]
                rf = [sb(ph, "rf0", [64, 2, 512], F32)] * 2
                scr = sb(ph, "scr", [128, D], F32)
                ssx = sb(ph, "ssx", [128, 1], F32)
                ss2 = sb(ph, "ss2", [128, 1], F32)
                ss3 = sb(ph, "ss3", [128, 1], F32)
                hb = [sb(ph, "hb%d" % i, [128, D], BF16) for i in range(2)]
                hT = [sb(ph, "hT0", [128, 8, 512], BF16)] * 2
                fo = [sb(ph, "fo0", [128, 8, 512], BF16)] * 2
                va_b = [sb(ph, "va_b%d" % i, [128, 4, 256], BF16) for i in range(2)]
                vaf = sb(ph, "vaf", [128, 4, 256], F32)
                kaf = sb(ph, "kaf", [128, 4, 256], F32)
                cqn_b = sb(ph, "cqn_b", [128, 256], BF16)
                ckvn_b = sb(ph, "ckvn_b", [128, 256], BF16)
                cqT = [sb(ph, "cqT%d" % i, [128, 2, 512], BF16) for i in range(2)]
                ckvT = [sb(ph, "ckvT%d" % i, [128, 2, 512], BF16) for i in range(2)]
                ckv_f = [sb(ph, "ckv_f%d" % i, [128, 4, 256], F32) for i in range(2)]
                krr = sb(ph, "krr", [128, 64], F32)
                krt = [sb(ph, "krt%d" % i, [128, 32], F32) for i in range(4)]
                kr_f = [sb(ph, "kr_f%d" % i, [128, 4, 64], F32) for i in range(2)]
                kr_b = sb(ph, "kr_b", [128, 64], BF16)
                krT = [sb(ph, "krT%d" % i, [64, 512], BF16) for i in range(2)]
                kcvc = [sb(ph, "kcvc%d" % i, [128, 4, 512], F32) for i in range(2)]
                vc_b = [sb(ph, "vc_b%d" % i, [128, 4, 256], BF16) for i in range(2)]
                qn_b = [sb(ph, "qn_b0", [128, 4, 512], BF16)] * 2
                qp_b = [sb(ph, "qp_b0", [64, 4, 512], BF16)] * 2
                kn_b = [sb(ph, "kn_b0", [128, 4, 512], BF16)] * 2
                vm_b = [sb(ph, "vm_b0", [128, 4, 512], BF16)] * 2
                qt1 = sb(ph, "qt1", [64, 512], F32)
                qt2 = sb(ph, "qt2", [64, 512], F32)
                c32 = [sb(ph, "c32_%d" % i, [128, 4, 256], F32) for i in range(2)]
                cb16 = [sb(ph, "cb16_%d" % i, [128, 4, 256], BF16) for i in range(2)]
                ckT = [sb(ph, "ckT%d" % i, [128, 2, 512], BF16) for i in range(2)]
                kr32 = sb(ph, "kr32", [128, 4, 64], F32)
                krb16 = sb(ph, "krb16", [128, 4, 64], BF16)
                krTc = [sb(ph, "krTc%d" % i, [64, 512], BF16) for i in range(2)]
                print('P1 sbuf remaining', nc.sbuf_bytes_remaining, 'vaf', vaf.t, 'kaf', kaf.t, 'krTc', krTc[1].t)
                ev_rr = [0]
                xti = [0]

                def evac(dst_b, dst_ap, src_b, src_ap):
                    eng = ("act", "dve")[ev_rr[0] % 2]
                    ev_rr[0] += 1
                    if eng == "act":
                        sy.do("act", lambda e: e.activation(out=dst_ap, in_=src_ap, func=AF.Copy), R=[src_b], W=[dst_b])
                    else:
                        sy.do("dve", lambda e: e.tensor_copy(out=dst_ap, in_=src_ap), R=[src_b], W=[dst_b])

                def kv_up(cT, N, ntl, knb, vmb):
                    for h in range(4):
                        bk = fb[4 + h % 2]
                        for k in range(2):
                            mm(bk, bk[:, 0:N], wkn, wkn[:, k, h * 128:(h + 1) * 128], cT, cT[:, k, 0:N], k == 0, k == 1)
                        evac(knb, knb[:, h, 0:N], bk, bk[:, 0:N])
                    for t in range(ntl):
                        bk = fb[4 + t % 2]
                        for k in range(2):
                            mm(bk, bk[:, 0:512], cT, cT[:, k, t * 128:(t + 1) * 128], wv, wv[:, k, :], k == 0, k == 1)
                        evac(vmb, vmb[:, t, :], bk, bk[:, 0:512])

                ci = 0
                for b in range(NSQ):
                    for (src, isk) in ((cak, True), (cav, False)):
                        cb = c32[ci % 2]; bb = cb16[ci % 2]; tt = ckT[ci % 2]; ci += 1
                        sy.dma("sp", cb[:], src[l, b].rearrange("(t p) c -> p t c", p=128), W=[cb])
                        sy.do("pool", lambda e, cb=cb, bb=bb: e.tensor_copy(out=bb[:], in_=cb[:]), R=[cb], W=[bb])
                        if isk:
                            for m in range(2):
                                for t in range(4):
                                    sy.do("pe", lambda e, m=m, t=t, bb=bb: e.transpose(out=tb[:, t * 128:(t + 1) * 128], in_=bb[:, t, m * 128:(m + 1) * 128], identity=identb[:]),
                                          R=[bb, identb], W=[tb])
                                evac(tt, tt[:, m, :], tb, tb[:, 0:512])
                            sy.dma("sp", KAs[b].rearrange("(m p) n -> p m n", p=128), tt[:], R=[tt])
                        else:
                            sy.dma("sp", VAs[b].rearrange("(t p) c -> p t c", p=128), bb[:], R=[bb])
                    for blk in range(PAST // 512):
                        r0 = blk * 512
                        cb = c32[ci % 2]; bb = cb16[ci % 2]; tt = ckT[ci % 2]; pb = ci % 2; ci += 1
                        sy.dma("sp", cb[:], cckv[l, b, r0:r0 + 512, :].rearrange("(t p) c -> p t c", p=128), W=[cb])
                        sy.do("pool", lambda e, cb=cb, bb=bb: e.tensor_copy(out=bb[:], in_=cb[:]), R=[cb], W=[bb])
                        for m in range(2):
                            for t in range(4):
                                sy.do("pe", lambda e, m=m, t=t, bb=bb: e.transpose(out=tb[:, t * 128:(t + 1) * 128], in_=bb[:, t, m * 128:(m + 1) * 128], identity=identb[:]),
                                      R=[bb, identb], W=[tb])
                            evac(tt, tt[:, m, :], tb, tb[:, 0:512])
                        kv_up(tt, 512, 4, kn_b[pb], vm_b[pb])
                        sy.dma("sp", KNs[b][:, r0:r0 + 512].rearrange("(h p) n -> p h n", p=128), kn_b[pb][:], R=[kn_b[pb]])
                        sy.dma("sp", VMs[b][r0:r0 + 512, :].rearrange("(t p) c -> p t c", p=128), vm_b[pb][:], R=[vm_b[pb]])
                        for (src, isk) in ((csk, True), (csv, False)):
                            cb = c32[ci % 2]; bb = cb16[ci % 2]; tt = ckT[ci % 2]; ci += 1
                            sy.dma("sp", cb[:], src[l, b, r0:r0 + 512, :].rearrange("(t p) c -> p t c", p=128), W=[cb])
                            sy.do("pool", lambda e, cb=cb, bb=bb: e.tensor_copy(out=bb[:], in_=cb[:]), R=[cb], W=[bb])
                            if isk:
                                for m in range(2):
                                    for t in range(4):
                                        sy.do("pe", lambda e, m=m, t=t, bb=bb: e.transpose(out=tb[:, t * 128:(t + 1) * 128], in_=bb[:, t, m * 128:(m + 1) * 128], identity=identb[:]),
                                              R=[bb, identb], W=[tb])
                                    evac(tt, tt[:, m, :], tb, tb[:, 0:512])
                                sy.dma("sp", KCs[b][:, r0:r0 + 512].rearrange("(m p) n -> p m n", p=128), tt[:], R=[tt])
                            else:
                                sy.dma("sp", VCs[b][r0:r0 + 512, :].rearrange("(t p) c -> p t c", p=128), bb[:], R=[bb])
                        kt = krTc[blk % 2]
                        sy.dma("sp", kr32[:], ckr[l, b, r0:r0 + 512, :].rearrange("(t p) c -> p t c", p=128), W=[kr32])
                        sy.do("pool", lambda e: e.tensor_copy(out=krb16[:], in_=kr32[:]), R=[kr32], W=[krb16])
                        for t in range(4):
                            sy.do("pe", lambda e, t=t: e.transpose(out=tb[0:64, t * 128:(t + 1) * 128], in_=krb16[:, t, :], identity=identb[:]),
                                  R=[krb16, identb], W=[tb])
                        evac(kt, kt[:, :], tb, tb[0:64, 0:512])
                        sy.dma("sp", KRs[b][:, r0:r0 + 512], kt[:], R=[kt])
                    sy.dma("sp", sak[l, b, 0:448, :], cak[l, b, 64:512, :], W=[ddb])
                    sy.dma("sp", sav[l, b, 0:448, :], cav[l, b, 64:512, :], W=[ddb])

                print('ninstr at P1 start', sy.ninstr)
                ckpt(2)
                groups = [(g * 512, 4, g) for g in range(NG)] + [(S, 2, NG)]
                for (r0, ntl, gi) in groups:
                    N = ntl * 128
                    pb = gi % 2
                    is_s = (gi == NG)
                    need_a = is_s or (gi == NG - 1)
                    RT, RF, HT = rt[pb], rf[pb], hT[pb]
                    sy.dma("sp", RT[:, 0:ntl, :], c_ropeT[r0:r0 + N, :].rearrange("(t p) d -> p t d", p=128), W=[RT])
                    sy.dma("sp", RF[:, :, 0:N], c_ropeF[:, :, r0:r0 + N], W=[RF])
                    for t in range(ntl):
                        H = hb[t % 2]
                        X = xt[xti[0] % 3]; xti[0] += 1
                        sy.dma("sp", X[:], xsrc[r0 + t * 128:r0 + (t + 1) * 128, :], W=[X])
                        rstd_rows(X, X[:], D, scr, ssx)
                        sy.do("dve", lambda e, X=X, H=H: e.scalar_tensor_tensor(out=H[:], in0=X[:], scalar=ssx[:, 0:1], in1=gmixb[:],
                                                                               op0=ALU.mult, op1=ALU.mult), R=[X, ssx, gmixb], W=[H])
                        for c in range(8):
                            sy.do("pe", lambda e, c=c, H=H: e.transpose(out=tb[:, c * 128:(c + 1) * 128], in_=H[:, c * 128:(c + 1) * 128], identity=identb[:]),
                                  R=[H, identb], W=[tb])
                        evac(HT, HT[:, :, t * 128:(t + 1) * 128], tb, tb[:, :].rearrange("p (c n) -> p c n", c=8))
                        tm = [(fb[0], 512, 1024), (fb[1], 1024, 1344), (fb[2], 1600, 2112)]
                        if need_a and not (DBG & 1):
                            tm.append((fb[3], 256, 512))
                        for (bk, c0, c1) in tm:
                            for c in range(8):
                                mm(bk, bk[:, 0:c1 - c0], HT, HT[:, c, t * 128:(t + 1) * 128], win, win[:, c, c0:c1], c == 0, c == 7)
                        sy.do("act", lambda e, t=t: e.activation(out=va_b[pb][:, t, :], in_=fb[0][:, 0:256], func=AF.Copy), R=[fb[0]], W=[va_b[pb]])
                        if need_a and not (DBG & 2):
                            if DBG & 16:
                                sy.do("dve", lambda e, t=t: e.tensor_copy(out=vaf[:, t, :], in_=gcqb[:]), R=[gcqb], W=[vaf])
                            elif DBG & 8:
                                sy.do("dve", lambda e, t=t: e.tensor_copy(out=scr[:, 0:256], in_=fb[0][:, 0:256]), R=[fb[0]], W=[scr])
                            else:
                                sy.do("dve", lambda e, t=t: e.tensor_copy(out=vaf[:, t, :], in_=fb[0][:, 0:256]), R=[fb[0]], W=[vaf])
                        if need_a and not (DBG & 4):
                            sy.do("act", lambda e, t=t: e.activation(out=kaf[:, t, :], in_=fb[3][:, 0:256], func=AF.Copy), R=[fb[3]], W=[kaf])
                        if not (DBG & 1024):
                            rstd_rows(fb[0], fb[0][:, 256:512], 256, scr, ss2)
                            sy.do("dve", lambda e: e.scalar_tensor_tensor(out=cqn_b[:], in0=fb[0][:, 256:512], scalar=ss2[:, 0:1], in1=gcqb[:],
                                                                          op0=ALU.mult, op1=ALU.mult), R=[fb[0], ss2, gcqb], W=[cqn_b])
                            rstd_rows(fb[1], fb[1][:, 0:256], 256, scr, ss3)
                            sy.do("dve", lambda e, t=t: e.scalar_tensor_tensor(out=ckv_f[pb][:, t, :], in0=fb[1][:, 0:256], scalar=ss3[:, 0:1], in1=gckvb[:],
                                                                               op0=ALU.mult, op1=ALU.mult), R=[fb[1], ss3, gckvb], W=[ckv_f[pb]])
                            sy.do("pool", lambda e, t=t: e.tensor_copy(out=ckvn_b[:], in_=ckv_f[pb][:, t, :]), R=[ckv_f[pb]], W=[ckvn_b])
                            for k in range(2):
                                sy.do("pe", lambda e, k=k: e.transpose(out=tb[:, k * 128:(k + 1) * 128], in_=cqn_b[:, k * 128:(k + 1) * 128], identity=identb[:]),
                                      R=[cqn_b, identb], W=[tb])
                            evac(cqT[pb], cqT[pb][:, :, t * 128:(t + 1) * 128], tb, tb[:, 0:256].rearrange("p (c n) -> p c n", c=2))
                            for k in range(2):
                                sy.do("pe", lambda e, k=k: e.transpose(out=tb[:, k * 128:(k + 1) * 128], in_=ckvn_b[:, k * 128:(k + 1) * 128], identity=identb[:]),
                                      R=[ckvn_b, identb], W=[tb])
                            evac(ckvT[pb], ckvT[pb][:, :, t * 128:(t + 1) * 128], tb, tb[:, 0:256].rearrange("p (c n) -> p c n", c=2))
                        if not (DBG & 64):
                            sy.do("act", lambda e: e.activation(out=krr[:], in_=fb[1][:, 256:320], func=AF.Copy), R=[fb[1]], W=[krr])
                            cs, sn = RT[:, t, 0:32], RT[:, t, 32:64]
                            x1, x2 = krr[:, 0:32], krr[:, 32:64]
                            sy.do("pool", lambda e, cs=cs, x1=x1: e.tensor_tensor(out=krt[0][:], in0=x1, in1=cs, op=ALU.mult), R=[krr, RT], W=[krt[0]])
                            sy.do("pool", lambda e, sn=sn, x2=x2: e.tensor_tensor(out=krt[1][:], in0=x2, in1=sn, op=ALU.mult), R=[krr, RT], W=[krt[1]])
                            sy.do("pool", lambda e, sn=sn, x1=x1: e.tensor_tensor(out=krt[2][:], in0=x1, in1=sn, op=ALU.mult), R=[krr, RT], W=[krt[2]])
                            sy.do("pool", lambda e, cs=cs, x2=x2: e.tensor_tensor(out=krt[3][:], in0=x2, in1=cs, op=ALU.mult), R=[krr, RT], W=[krt[3]])
                            sy.do("pool", lambda e, t=t: e.tensor_tensor(out=kr_f[pb][:, t, 0:32], in0=krt[0][:], in1=krt[1][:], op=ALU.subtract), R=[krt[0], krt[1]], W=[kr_f[pb]])
                            sy.do("pool", lambda e, t=t: e.tensor_tensor(out=kr_f[pb][:, t, 32:64], in0=krt[2][:], in1=krt[3][:], op=ALU.add), R=[krt[2], krt[3]], W=[kr_f[pb]])
                            sy.do("pool", lambda e, t=t: e.tensor_copy(out=kr_b[:], in_=kr_f[pb][:, t, :]), R=[kr_f[pb]], W=[kr_b])
                            sy.do("pe", lambda e: e.transpose(out=tb[0:64, 0:128], in_=kr_b[:], identity=identb[:]), R=[kr_b, identb], W=[tb])
                            evac(krT[pb], krT[pb][:, t * 128:(t + 1) * 128], tb, tb[0:64, 0:128])
                        sy.do("act", lambda e, t=t: e.activation(out=kcvc[pb][:, t, :], in_=fb[2][:, 0:512], func=AF.Copy), R=[fb[2]], W=[kcvc[pb]])
                        sy.do("pool", lambda e, t=t: e.tensor_copy(out=vc_b[pb][:, t, :], in_=kcvc[pb][:, t, 256:512]), R=[kcvc[pb]], W=[vc_b[pb]])
                    if gi == 0: print('ninstr after tiles g0', sy.ninstr)
                    if gi == 0: ckpt(20)
                    if is_s: ckpt(25)
                    for mi, c0 in enumerate([0, 128, 256, 384, 1344, 1472, 1600, 1728]):
                        bk = fb[4 + mi % 2]
                        for c in range(8):
                            mm(bk, bk[:, 0:N], win, win[:, c, c0:c0 + 128], HT, HT[:, c, 0:N], c == 0, c == 7)
                        evac(fo[pb], fo[pb][:, mi, 0:N], bk, bk[:, 0:N])
                    if gi == 0: ckpt(21)
                    if not (DBG & 128):
                        CQ = cqT[pb]
                        for h in range(4):
                            bk = fb[4 + h % 2]
                            for k in range(2):
                                mm(bk, bk[:, 0:N], wuq, wuq[:, k, h * 192:h * 192 + 128], CQ, CQ[:, k, 0:N], k == 0, k == 1)
                            evac(qn_b[pb], qn_b[pb][:, h, 0:N], bk, bk[:, 0:N])
                            for k in range(2):
                                mm(fb[6], fb[6][0:64, 0:N], wuq, wuq[:, k, h * 192 + 128:h * 192 + 192], CQ, CQ[:, k, 0:N], k == 0, k == 1)
                            for k in range(2):
                                mm(fb[3], fb[3][0:64, 0:N], wuqs, wuqs[:, k, h * 64:(h + 1) * 64], CQ, CQ[:, k, 0:N], k == 0, k == 1)
                            sy.do("dve", lambda e: e.tensor_tensor(out=qt1[:, 0:N], in0=fb[6][0:64, 0:N], in1=RF[:, 0, 0:N], op=ALU.mult), R=[fb[6], RF], W=[qt1])
                            sy.do("dve", lambda e: e.tensor_tensor(out=qt2[:, 0:N], in0=fb[3][0:64, 0:N], in1=RF[:, 1, 0:N], op=ALU.mult), R=[fb[3], RF], W=[qt2])
                            sy.do("pool", lambda e, h=h: e.tensor_tensor(out=qp_b[pb][:, h, 0:N], in0=qt1[:, 0:N], in1=qt2[:, 0:N], op=ALU.add), R=[qt1, qt2], W=[qp_b[pb]])
                    if gi == 0: ckpt(22)
                    if not (DBG & 256):
                        kv_up(ckvT[pb], N, ntl, kn_b[pb], vm_b[pb])
                    if gi == 0: ckpt(23)
                    cs_ = slice(r0, r0 + N)
                    for j, dst in enumerate((QA, KA, QC, KC)):
                        sy.dma("sp", dst[:, cs_].rearrange("(m p) n -> p m n", p=128), fo[pb][:, 2 * j:2 * j + 2, 0:N], R=[fo[pb]])
                    sy.dma("sp", QN[:, cs_].rearrange("(h p) n -> p h n", p=128), qn_b[pb][:, :, 0:N], R=[qn_b[pb]])
                    sy.dma("sp", QP[:, cs_].rearrange("(h p) n -> p h n", p=64), qp_b[pb][:, :, 0:N], R=[qp_b[pb]])
                    sy.dma("sp", KN[:, cs_].rearrange("(h p) n -> p h n", p=128), kn_b[pb][:, :, 0:N], R=[kn_b[pb]])
                    sy.dma("sp", KR[:, cs_], krT[pb][:, 0:N], R=[krT[pb]])
                    sy.dma("sp", VM[cs_, :].rearrange("(t p) c -> p t c", p=128), vm_b[pb][:, 0:ntl, :], R=[vm_b[pb]])
                    sy.dma("sp", VA[cs_, :].rearrange("(t p) c -> p t c", p=128), va_b[pb][:, 0:ntl, :], R=[va_b[pb]])
                    sy.dma("sp", VC[cs_, :].rearrange("(t p) c -> p t c", p=128), vc_b[pb][:, 0:ntl, :], R=[vc_b[pb]])
                    if gi == 0: print('ninstr before outs g0', sy.ninstr)
                    if gi == 0: ckpt(24)
                    if not is_s:
                        sy.dma("sp", pckv[l, cs_, :].rearrange("(t p) c -> p t c", p=128), ckv_f[pb][:, 0:ntl, :], R=[ckv_f[pb]])
                        sy.dma("sp", pkr[l, cs_, :].rearrange("(t p) c -> p t c", p=128), kr_f[pb][:, 0:ntl, :], R=[kr_f[pb]])
                        sy.dma("sp", psk[l, cs_, :].rearrange("(t p) c -> p t c", p=128), kcvc[pb][:, 0:ntl, 0:256], R=[kcvc[pb]])
                        sy.dma("sp", psv[l, cs_, :].rearrange("(t p) c -> p t c", p=128), kcvc[pb][:, 0:ntl, 256:512], R=[kcvc[pb]])
                        if gi == 0: ckpt(26)
                        if need_a: ckpt(29)
                        if need_a:
                            sy.dma("sp", pak[l].rearrange("(t p) c -> p t c", p=128), kaf[:, 0:ntl, :], R=[kaf])
                            sy.dma("sp", pav[l].rearrange("(t p) c -> p t c", p=128), vaf[:, 0:ntl, :], R=[vaf])
                        if gi == NG - 1: ckpt(27)
                    else:
                        for b in range(NSQ):
                            t_, p0 = b // 2, (b % 2) * 64
                            sy.dma("sp", sckv[l, b], ckv_f[pb][p0:p0 + 64, t_, :], R=[ckv_f[pb]])
                            sy.dma("sp", skr[l, b], kr_f[pb][p0:p0 + 64, t_, :], R=[kr_f[pb]])
                            sy.dma("sp", ssk[l, b], kcvc[pb][p0:p0 + 64, t_, 0:256], R=[kcvc[pb]])
                            sy.dma("sp", ssv[l, b], kcvc[pb][p0:p0 + 64, t_, 256:512], R=[kcvc[pb]])
                            sy.dma("sp", sak[l, b, 448:512, :], kaf[p0:p0 + 64, t_, :], R=[kaf])
                            sy.dma("sp", sav[l, b, 448:512, :], vaf[p0:p0 + 64, t_, :], R=[vaf])
                    if gi == 1: ckpt(28)
                    if DBG & 32: sy.barrier()
                sy.barrier()
                ckpt(3)

            def soft_res(ph, tag):
                return ([sb(ph, "P%s%d" % (tag, i), [128, 512], BF16) for i in range(4)], sb(ph, "rD" + tag, [128, 512], F32))

            def sb_res(ph, tag):
                return ([sb(ph, "E%s%d" % (tag, i), [128, 512], F32) for i in range(3)],
                        [sb(ph, "L%s%d" % (tag, i), [128, 512], BF16) for i in range(4)],
                        [sb(ph, "X%s%d" % (tag, i), [128, 512], F32) for i in range(2)],
                        [sb(ph, "A%s%d" % (tag, i), [128, 512], BF16) for i in range(4)],
                        [sb(ph, "cr%s%d" % (tag, i), [1, 512], F32) for i in range(2)])

            def attn_soft(res, jobs, scale):
                Sb = [fb[0], fb[1], fb[6]]
                Ob = [fb[2], fb[3]]
                Db = [fb[4], fb[5]]
                Pt, rD = res
                NP_ = len(Pt)
                items = []
                for ji, jb in enumerate(jobs):
                    nb = len(jb["blocks"])
                    for bi, blk in enumerate(jb["blocks"]):
                        items.append((ji, jb, bi, blk, bi == 0, bi == nb - 1))

                def stage_a(t):
                    ji, jb, bi, blk, first, lastb = items[t]
                    N, m = jb["N"], blk["m"]
                    bk = Sb[t % 3]
                    n = len(blk["mms"])
                    for i, (lb, lap, rb, rap) in enumerate(blk["mms"]):
                        mm(bk, bk[0:m, 0:N], lb, lap, rb, rap, i == 0, i == n - 1)

                def stage_b(t):
                    ji, jb, bi, blk, first, lastb = items[t]
                    N, m = jb["N"], blk["m"]
                    bk, P = Sb[t % 3], Pt[t % NP_]
                    sy.do("act", lambda e: e.activation(out=P[0:m, 0:N], in_=bk[0:m, 0:N], func=AF.Exp, scale=scale), R=[bk], W=[P])
                    if blk["mulb"] is not None:
                        sy.do("dve", lambda e: e.tensor_tensor(out=P[0:m, 0:N], in0=P[0:m, 0:N], in1=blk["mulap"], op=ALU.mult),
                              R=[blk["mulb"]], W=[P])

                def stage_f(t):
                    ji, jb, bi, blk, first, lastb = items[t]
                    N, m, ed = jb["N"], blk["m"], jb["e"]
                    P = Pt[t % NP_]
                    O, Dn = Ob[ji % 2], Db[ji % 2]
                    mm(O, O[0:ed, 0:N], blk["vb"], blk["vap"], P, P[0:m, 0:N], first, lastb)
                    mm(Dn, Dn[0:ed, 0:N], onesb, onesb[0:m, 0:ed], P, P[0:m, 0:N], first, lastb)
                    if lastb:
                        sy.do("dve", lambda e: e.reciprocal(out=rD[0:ed, 0:N], in_=Dn[0:ed, 0:N]), R=[Dn], W=[rD])
                        sy.do("dve", lambda e: e.tensor_tensor(out=jb["dstap"], in0=O[0:ed, 0:N], in1=rD[0:ed, 0:N], op=ALU.mult),
                              R=[O, rD], W=[jb["dstb"]])
                        if jb.get("after"):
                            jb["after"]()

                T = len(items)

                def issue_a(t):
                    if items[t][2] == 0 and items[t][1].get("before"):
                        items[t][1]["before"]()
                    stage_a(t)

                if T:
                    issue_a(0)
                for t in range(T + 2):
                    if t + 1 < T:
                        issue_a(t + 1)
                    if t < T:
                        stage_b(t)
                    if 2 <= t:
                        stage_f(t - 2)

            def attn_sb(res, jobs, scale):
                Zb = [fb[0], fb[1], fb[6]]
                Bb = [fb[2], fb[3]]
                Ob = [fb[4], fb[5]]
                Eb, Lb, Xb, Ab, crow = res
                NA_ = len(Ab)
                items = []
                for ji, jb in enumerate(jobs):
                    nb = len(jb["blocks"])
                    for bi, blk in enumerate(jb["blocks"]):
                        items.append((ji, jb, bi, blk, bi == 0, bi == nb - 1))

                def stage_a(t):
                    ji, jb, bi, blk, first, lastb = items[t]
                    N, m = jb["N"], blk["m"]
                    bk = Zb[t % 3]
                    lb, lap, rb, rap = blk["mm"]
                    mm(bk, bk[0:m, 0:N], lb, lap, rb, rap, True, True)

                def stage_b(t):
                    ji, jb, bi, blk, first, lastb = items[t]
                    N, m = jb["N"], blk["m"]
                    bk, E, Lt = Zb[t % 3], Eb[t % 3], Lb[t % 4]
                    sy.do("act", lambda e: e.activation(out=E[0:m, 0:N], in_=bk[0:m, 0:N], func=AF.Exp, scale=scale), R=[bk], W=[E])
                    if blk["mulb"] is not None:
                        sy.do("pool", lambda e: e.tensor_tensor(out=E[0:m, 0:N], in0=E[0:m, 0:N], in1=blk["mulap"], op=ALU.mult),
                              R=[blk["mulb"]], W=[E])
                    sy.do("act", lambda e: e.activation(out=Lt[0:m, 0:N], in_=E[0:m, 0:N], func=AF.Ln, bias=1.0), R=[E], W=[Lt])

                def stage_c(t):
                    ji, jb, bi, blk, first, lastb = items[t]
                    N, m = jb["N"], blk["m"]
                    B, Lt = Bb[ji % 2], Lb[t % 4]
                    mm(B, B[:, 0:N], trib, trib[0:m, 0, :], Lt, Lt[0:m, 0:N], first, True)

                def stage_c2(t):
                    ji, jb, bi, blk, first, lastb = items[t]
                    N, m = jb["N"], blk["m"]
                    B, Lt = Bb[ji % 2], Lb[t % 4]
                    if not lastb:
                        mm(B, B[:, 0:N], trib, trib[0:m, 1, :], Lt, Lt[0:m, 0:N], False, True)

                def stage_d(t):
                    ji, jb, bi, blk, first, lastb = items[t]
                    N, m = jb["N"], blk["m"]
                    B, X, E, A = Bb[ji % 2], Xb[t % 2], Eb[t % 3], Ab[t % NA_]
                    sy.do("act", lambda e: e.activation(out=X[0:m, 0:N], in_=B[0:m, 0:N], func=AF.Exp), R=[B], W=[X])
                    sy.do("dve", lambda e: e.tensor_tensor(out=A[0:m, 0:N], in0=E[0:m, 0:N], in1=X[0:m, 0:N], op=ALU.mult), R=[E, X], W=[A])

                def stage_f(t):
                    ji, jb, bi, blk, first, lastb = items[t]
                    N, m, ed = jb["N"], blk["m"], jb["e"]
                    A, O = Ab[t % NA_], Ob[ji % 2]
                    mm(O, O[0:ed, 0:N], blk["vb"], blk["vap"], A, A[0:m, 0:N], first, lastb)
                    if lastb:
                        sy.do("dve", lambda e: e.tensor_copy(out=jb["dstap"], in_=O[0:ed, 0:N]), R=[O], W=[jb["dstb"]])
                        if jb.get("after"):
                            jb["after"]()

                T = len(items)

                def issue_a(t):
                    if items[t][2] == 0 and items[t][1].get("before"):
                        items[t][1]["before"]()
                    stage_a(t)

                if T:
                    issue_a(0)
                for t in range(T + 3):
                    if t + 1 < T:
                        issue_a(t + 1)
                    if t < T:
                        stage_b(t)
                    if 2 <= t <= T + 1:
                        stage_c2(t - 2)
                    if 1 <= t <= T:
                        stage_c(t - 1)
                        stage_d(t - 1)
                    if 3 <= t:
                        stage_f(t - 3)

            with ExitStack() as ph:
                EB = sb(ph, "EB", [128, 4, 8, 512], BF16)
                with ExitStack() as ph2:
                    rel = sb(ph2, "rel", [1, 768], F32)
                    ext = sb(ph2, "ext", [1, 4, 1536], F32)
                    valb = sb(ph2, "valb", [128, 8, 512], F32)
                    ebr = [sb(ph2, "ebr%d" % i, [128, 512], F32) for i in range(2)]
                    ebrb = [sb(ph2, "ebrb%d" % i, [128, 512], BF16) for i in range(2)]
                    sy.dma("sp", rel[:], a_rel[l:l + 1, :], W=[rel])
                    sy.dma("sp", valb[:], c_val, W=[valb])
                    for h in range(4):
                        sy.do("dve", lambda e, h=h: e.tensor_copy(out=ext[0:1, h, 0:448], in_=rel[0:1, h * 192:h * 192 + 1].to_broadcast([1, 448])), R=[rel], W=[ext])
                        sy.do("dve", lambda e, h=h: e.tensor_copy(out=ext[0:1, h, 448:640], in_=rel[0:1, h * 192:(h + 1) * 192]), R=[rel], W=[ext])
                        sy.do("dve", lambda e, h=h: e.tensor_copy(out=ext[0:1, h, 640:1536], in_=rel[0:1, h * 192 + 191:h * 192 + 192].to_broadcast([1, 896])), R=[rel], W=[ext])
                    sy.do("act", lambda e: e.activation(out=ext[:], in_=ext[:], func=AF.Exp), R=[ext], W=[ext])
                    sy.dma("sp", EXT.rearrange("(o h) n -> o h n", o=1), ext[:], R=[ext])
                    sy.barrier()
                    k = 0
                    for h in range(4):
                        for i in range(8):
                            eb = ebr[k % 2]; ebb = ebrb[k % 2]; bk = fb[k % 2]; k += 1
                            src = bass.AP(EXT.tensor, h * 1536 + 896 - 128 * i, [[1, 128], [1, 512]])
                            sy.dma("sp", eb[:], src, W=[eb])
                            sy.do("pool", lambda e, eb=eb, ebb=ebb: e.tensor_copy(out=ebb[:], in_=eb[:]), R=[eb], W=[ebb])
                            mm(bk, bk[:, :], trib, trib[:, 3, :], ebb, ebb[:], True, True)
                            sy.do("dve", lambda e, h=h, i=i, bk=bk: e.tensor_tensor(out=EB[:, h, i, :], in0=bk[:, :], in1=valb[:, i, :], op=ALU.mult),
                                  R=[bk, valb], W=[EB])
                    sy.do("dve", lambda e: e.tensor_copy(out=EBs[:], in_=EB[:, :, 0:5, 0:64]), R=[EB], W=[EBs])
                    sy.barrier()
                kaT = sb(ph, "kaT", [128, 2, S], BF16)
                vaS = sb(ph, "vaS", [128, NKB, 256], BF16)
                sy.dma("sp", kaT[:], KA[:, 0:S].rearrange("(m p) n -> p m n", p=128), W=[kaT])
                sy.dma("sp", vaS[:], VA[0:S, :].rearrange("(t p) c -> p t c", p=128), W=[vaS])
                qa = [sb(ph, "qa%d" % i, [128, 2, 512], BF16) for i in range(2)]
                ost = [sb(ph, "osta%d" % i, [64, 4, 512], F32) for i in range(2)]
                jobs = []
                for g in range(NG):
                    pb = g % 2
                    for h in range(4):
                        hp, bs = h // 2, (h % 2) * 64
                        blocks = []
                        for i in range(8):
                            kb = 4 * g - 4 + i
                            if kb < 0:
                                continue
                            blocks.append(dict(mms=[(kaT, kaT[bs:bs + 64, hp, kb * 128:(kb + 1) * 128], qa[pb], qa[pb][bs:bs + 64, hp, :])],
                                               m=128, vb=vaS, vap=vaS[:, kb, h * 64:(h + 1) * 64], mulb=EB, mulap=EB[:, h, i, :]))
                        jb = dict(N=512, e=64, blocks=blocks, dstb=ost[pb], dstap=ost[pb][:, h, :])
                        if h == 0:
                            jb["before"] = (lambda g=g, pb=pb: sy.dma("sp", qa[pb][:], QA[:, g * 512:(g + 1) * 512].rearrange("(m p) n -> p m n", p=128), W=[qa[pb]]))
                        if h == 3:
                            jb["after"] = (lambda g=g, pb=pb: sy.dma("sp", OCAT[0:256, g * 512:(g + 1) * 512].rearrange("(h p) n -> p h n", p=64), ost[pb][:], R=[ost[pb]]))
                        jobs.append(jb)
                attn_soft(soft_res(ph, "a"), jobs, 0.125)
                sy.barrier()
                ckpt(4)

            with ExitStack() as ph:
                knT = sb(ph, "knT", [128, 4, S], BF16)
                krS = sb(ph, "krS", [64, S], BF16)
                vmS = sb(ph, "vmS", [128, NKB, 512], BF16)
                mml = sb(ph, "mml", [128, 4, 512], BF16)
                with ExitStack() as ph2:
                    m32 = sb(ph2, "m32", [128, 4, 512], F32)
                    sy.dma("sp", m32[:], c_mml, W=[m32])
                    sy.do("dve", lambda e: e.tensor_copy(out=mml[:], in_=m32[:]), R=[m32], W=[mml])
                    sy.barrier()
                for h in range(4):
                    sy.dma("sp", knT[:, h, :], KN[h * 128:(h + 1) * 128, 0:S], W=[knT])
                sy.dma("sp", krS[:], KR[:, 0:S], W=[krS])
                for q4 in range(0, NKB, 8):
                    sy.dma("sp", vmS[:, q4:q4 + 8, :], VM[q4 * 128:(q4 + 8) * 128, :].rearrange("(t p) c -> p t c", p=128), W=[vmS])
                qn = [sb(ph, "qn%d" % i, [128, 4, 512], BF16) for i in range(2)]
                qp = [sb(ph, "qp%d" % i, [64, 4, 512], BF16) for i in range(2)]
                ost = [sb(ph, "ostm0", [128, 4, 512], F32)] * 2
                jobs = []
                for g in range(NG):
                    pb = g % 2
                    for h in range(4):
                        blocks = []
                        for kb in range(4 * g + 4):
                            r = kb - 4 * g
                            blocks.append(dict(mms=[(knT, knT[:, h, kb * 128:(kb + 1) * 128], qn[pb], qn[pb][:, h, :]),
                                                    (krS, krS[:, kb * 128:(kb + 1) * 128], qp[pb], qp[pb][:, h, :])],
                                               m=128, vb=vmS, vap=vmS[:, kb, h * 128:(h + 1) * 128],
                                               mulb=(mml if r >= 0 else None), mulap=(mml[:, r, :] if r >= 0 else None)))
                        jb = dict(N=512, e=128, blocks=blocks, dstb=ost[pb], dstap=ost[pb][:, h, :])
                        if h == 0:
                            def bf(g=g, pb=pb):
                                sy.dma("sp", qn[pb][:], QN[:, g * 512:(g + 1) * 512].rearrange("(h p) n -> p h n", p=128), W=[qn[pb]])
                                sy.dma("sp", qp[pb][:], QP[:, g * 512:(g + 1) * 512].rearrange("(h p) n -> p h n", p=64), W=[qp[pb]])
                            jb["before"] = bf
                        if h == 3:
                            jb["after"] = (lambda g=g, pb=pb: sy.dma("sp", OCAT[256:768, g * 512:(g + 1) * 512].rearrange("(h p) n -> p h n", p=128), ost[pb][:], R=[ost[pb]]))
                        jobs.append(jb)
                attn_soft(soft_res(ph, "m"), jobs, 192.0 ** -0.5)
                sy.barrier()
                ckpt(5)

            with ExitStack() as ph:
                kcT = sb(ph, "kcT", [128, 2, S], BF16)
                vcS = sb(ph, "vcS", [128, NKB, 256], BF16)
                msb = sb(ph, "msb", [128, 4, 512], F32)
                sy.dma("sp", msb[:], c_msb, W=[msb])
                sy.dma("sp", kcT[:], KC[:, 0:S].rearrange("(m p) n -> p m n", p=128), W=[kcT])
                sy.dma("sp", vcS[:], VC[0:S, :].rearrange("(t p) c -> p t c", p=128), W=[vcS])
                qc = [sb(ph, "qc%d" % i, [128, 2, 512], BF16) for i in range(2)]
                ost = [sb(ph, "osts%d" % i, [64, 4, 512], F32) for i in range(2)]
                jobs = []
                for g in range(NG):
                    pb = g % 2
                    for h in range(4):
                        hp, bs = h // 2, (h % 2) * 64
                        blocks = []
                        for kb in range(4 * g + 3, -1, -1):
                            r = kb - 4 * g
                            blocks.append(dict(mm=(kcT, kcT[bs:bs + 64, hp, kb * 128:(kb + 1) * 128], qc[pb], qc[pb][bs:bs + 64, hp, :]),
                                               m=128, vb=vcS, vap=vcS[:, kb, h * 64:(h + 1) * 64],
                                               mulb=(msb if r >= 0 else None), mulap=(msb[:, r, :] if r >= 0 else None)))
                        jb = dict(N=512, e=64, blocks=blocks, dstb=ost[pb], dstap=ost[pb][:, h, :])
                        if h == 0:
                            jb["before"] = (lambda g=g, pb=pb: sy.dma("sp", qc[pb][:], QC[:, g * 512:(g + 1) * 512].rearrange("(m p) n -> p m n", p=128), W=[qc[pb]]))
                        if h == 3:
                            jb["after"] = (lambda g=g, pb=pb: sy.dma("sp", OCAT[768:1024, g * 512:(g + 1) * 512].rearrange("(h p) n -> p h n", p=64), ost[pb][:], R=[ost[pb]]))
                        jobs.append(jb)
                attn_sb(sb_res(ph, "s"), jobs, 0.125)
                sy.barrier()
                ckpt(6)

            with ExitStack() as ph:
                NQ = NSQ * TS
                KL = PAST + TS
                qaS = sb(ph, "qaS", [128, 2, NQ], BF16); qcS = sb(ph, "qcS", [128, 2, NQ], BF16)
                qnS = sb(ph, "qnS", [128, 4, NQ], BF16); qpS = sb(ph, "qpS", [64, 4, NQ], BF16)
                sy.dma("sp", qaS[:], QA[:, S:NT].rearrange("(m p) n -> p m n", p=128), W=[qaS])
                sy.dma("sp", qcS[:], QC[:, S:NT].rearrange("(m p) n -> p m n", p=128), W=[qcS])
                sy.dma("sp", qnS[:], QN[:, S:NT].rearrange("(h p) n -> p h n", p=128), W=[qnS])
                sy.dma("sp", qpS[:], QP[:, S:NT].rearrange("(h p) n -> p h n", p=64), W=[qpS])
                msb = sb(ph, "msbS", [64, 64], F32)
                sy.dma("sp", msb[:], c_msb[0:64, 0, 0:64], W=[msb])
                osA = sb(ph, "osA", [64, 4, NQ], F32); osM = sb(ph, "osM", [128, 4, NQ], F32); osS = sb(ph, "osS", [64, 4, NQ], F32)
                kaS = [sb(ph, "kaS%d" % i, [128, 2, 576], BF16) for i in range(2)]
                vaS = [sb(ph, "vaSs%d" % i, [128, 5, 256], BF16) for i in range(2)]
                knS = [sb(ph, "knS%d" % i, [128, 4, KL], BF16) for i in range(2)]
                krS = [sb(ph, "krSs%d" % i, [64, KL], BF16) for i in range(2)]
                vmS = [sb(ph, "vmSs%d" % i, [128, NPB + 1, 512], BF16) for i in range(2)]
                kcS = [sb(ph, "kcS%d" % i, [128, 2, KL], BF16) for i in range(2)]
                vcS = [sb(ph, "vcSs%d" % i, [128, NPB + 1, 256], BF16) for i in range(2)]
                ja, jm, js = [], [], []
                for b in range(NSQ):
                    pb = b % 2
                    c0 = S + b * TS
                    qs = slice(b * TS, (b + 1) * TS)

                    def ld(b=b, pb=pb, c0=c0):
                        sy.dma("sp", kaS[pb][:, :, 0:512], KAs[b].rearrange("(m p) n -> p m n", p=128), W=[kaS[pb]])
                        sy.dma("sp", kaS[pb][:, :, 512:576], KA[:, c0:c0 + TS].rearrange("(m p) n -> p m n", p=128), W=[kaS[pb]])
                        sy.dma("sp", vaS[pb][:, 0:4, :], VAs[b].rearrange("(t p) c -> p t c", p=128), W=[vaS[pb]])
                        sy.dma("sp", vaS[pb][0:64, 4, :], VA[c0:c0 + TS, :], W=[vaS[pb]])
                        sy.dma("sp", knS[pb][:, :, 0:PAST], KNs[b].rearrange("(h p) n -> p h n", p=128), W=[knS[pb]])
                        sy.dma("sp", knS[pb][:, :, PAST:KL], KN[:, c0:c0 + TS].rearrange("(h p) n -> p h n", p=128), W=[knS[pb]])
                        sy.dma("sp", krS[pb][:, 0:PAST], KRs[b], W=[krS[pb]])
                        sy.dma("sp", krS[pb][:, PAST:KL], KR[:, c0:c0 + TS], W=[krS[pb]])
                        sy.dma("sp", vmS[pb][:, 0:NPB, :], VMs[b].rearrange("(t p) c -> p t c", p=128), W=[vmS[pb]])
                        sy.dma("sp", vmS[pb][0:64, NPB, :], VM[c0:c0 + TS, :], W=[vmS[pb]])
                        sy.dma("sp", kcS[pb][:, :, 0:PAST], KCs[b].rearrange("(m p) n -> p m n", p=128), W=[kcS[pb]])
                        sy.dma("sp", kcS[pb][:, :, PAST:KL], KC[:, c0:c0 + TS].rearrange("(m p) n -> p m n", p=128), W=[kcS[pb]])
                        sy.dma("sp", vcS[pb][:, 0:NPB, :], VCs[b].rearrange("(t p) c -> p t c", p=128), W=[vcS[pb]])
                        sy.dma("sp", vcS[pb][0:64, NPB, :], VC[c0:c0 + TS, :], W=[vcS[pb]])
                    for h in range(4):
                        hp, bs = h // 2, (h % 2) * 64
                        blocks = []
                        for i in range(5):
                            m = 128 if i < 4 else 64
                            blocks.append(dict(mms=[(kaS[pb], kaS[pb][bs:bs + 64, hp, i * 128:i * 128 + m], qaS, qaS[bs:bs + 64, hp, qs])],
                                               m=m, vb=vaS[pb], vap=vaS[pb][0:m, i, h * 64:(h + 1) * 64], mulb=EBs, mulap=EBs[0:m, h, i, :]))
                        jb = dict(N=TS, e=64, blocks=blocks, dstb=osA, dstap=osA[:, h, qs])
                        if h == 0:
                            jb["before"] = ld
                        ja.append(jb)
                        blocks = []
                        for kb in range(NPB + 1):
                            m = 128 if kb < NPB else 64
                            blocks.append(dict(mms=[(knS[pb], knS[pb][:, h, kb * 128:kb * 128 + m], qnS, qnS[:, h, qs]),
                                                    (krS[pb], krS[pb][:, kb * 128:kb * 128 + m], qpS, qpS[:, h, qs])],
                                               m=m, vb=vmS[pb], vap=vmS[pb][0:m, kb, h * 128:(h + 1) * 128], mulb=None, mulap=None))
                        jm.append(dict(N=TS, e=128, blocks=blocks, dstb=osM, dstap=osM[:, h, qs]))
                        blocks = []
                        for kb in range(NPB, -1, -1):
                            m = 128 if kb < NPB else 64
                            blocks.append(dict(mm=(kcS[pb], kcS[pb][bs:bs + 64, hp, kb * 128:kb * 128 + m], qcS, qcS[bs:bs + 64, hp, qs]),
                                               m=m, vb=vcS[pb], vap=vcS[pb][0:m, kb, h * 64:(h + 1) * 64],
                                               mulb=(msb if kb == NPB else None), mulap=(msb[:, :] if kb == NPB else None)))
                        js.append(dict(N=TS, e=64, blocks=blocks, dstb=osS, dstap=osS[:, h, qs]))
                rsoft, rsb = soft_res(ph, "x"), sb_res(ph, "x")
                for b in range(NSQ):
                    attn_soft(rsoft, ja[4 * b:4 * b + 4], 0.125)
                    attn_soft(rsoft, jm[4 * b:4 * b + 4], 192.0 ** -0.5)
                    attn_sb(rsb, js[4 * b:4 * b + 4], 0.125)
                sy.dma("sp", OCAT[0:256, S:NT].rearrange("(h p) n -> p h n", p=64), osA[:], R=[osA])
                sy.dma("sp", OCAT[256:768, S:NT].rearrange("(h p) n -> p h n", p=128), osM[:], R=[osM])
                sy.dma("sp", OCAT[768:1024, S:NT].rearrange("(h p) n -> p h n", p=64), osS[:], R=[osS])
                sy.barrier()
                ckpt(7)

            with ExitStack() as ph:
                wout = sb(ph, "wout", [128, 8, D], BF16)
                gcat = sb(ph, "gcat", [128, 8], F32)
                sy.dma("sp", gcat[:], g_cat[l].rearrange("(c p) -> p c", p=128), W=[gcat], allow_slow_non_contiguous=True)
                with ExitStack() as ph2:
                    load_w(ph2, wout, [wout[:, c, :] for c in range(8)], [w_out[l, c * 128:(c + 1) * 128, :] for c in range(8)], D)
                    sy.barrier()
                oc = [sb(ph, "oc%d" % i, [128, 8, 512], F32) for i in range(2)]
                xg = [sb(ph, "xg2_%d" % i, [128, 4, D], F32) for i in range(2)]
                sq = sb(ph, "sq", [128, 8, 512], BF16)
                rs = [sb(ph, "rs%d" % i, [128, 512], F32) for i in range(3)]
                catT = [sb(ph, "catT%d" % i, [128, 8, 512], BF16) for i in range(2)]
                for (r0, ntl, gi) in groups:
                    N = ntl * 128
                    pb = gi % 2
                    OC, X, CT = oc[pb], xg[pb], catT[pb]
                    sy.dma("sp", OC[:, :, 0:N], OCAT[:, r0:r0 + N].rearrange("(c p) n -> p c n", p=128), W=[OC])
                    sy.dma("sp", X[:, 0:ntl, :], xsrc[r0:r0 + N, :].rearrange("(t p) d -> p t d", p=128), W=[X])
                    sy.do("act", lambda e: e.activation(out=sq[:, :, 0:N], in_=OC[:, :, 0:N], func=AF.Square), R=[OC], W=[sq])
                    for mi, (a0, a1, Wd) in enumerate(((0, 2, 256), (2, 6, 512), (6, 8, 256))):
                        bk = fb[mi]
                        for c in range(a0, a1):
                            mm(bk, bk[:, 0:N], onesb, onesb[:, :], sq, sq[:, c, 0:N], c == a0, c == a1 - 1)
                        R_ = rs[mi]
                        sy.do("dve", lambda e, R_=R_, bk=bk, Wd=Wd: e.tensor_scalar(out=R_[:, 0:N], in0=bk[:, 0:N], scalar1=1.0 / Wd, scalar2=EPS,
                                                                                   op0=ALU.mult, op1=ALU.add), R=[bk], W=[R_])
                        sy.do("act", lambda e, R_=R_: e.activation(out=R_[:, 0:N], in_=R_[:, 0:N], func=AF.Sqrt), R=[R_], W=[R_])
                        sy.do("dve", lambda e, R_=R_: e.reciprocal(out=R_[:, 0:N], in_=R_[:, 0:N]), R=[R_], W=[R_])
                        for c in range(a0, a1):
                            sy.do("dve", lambda e, c=c, R_=R_: e.scalar_tensor_tensor(out=CT[:, c, 0:N], in0=OC[:, c, 0:N], scalar=gcat[:, c:c + 1], in1=R_[:, 0:N],
                                                                                                  op0=ALU.mult, op1=ALU.mult), R=[OC, gcat, R_], W=[CT])
                    for t in range(ntl):
                        for hf in range(2):
                            bk = fb[3 + (2 * t + hf) % 4]
                            for c in range(8):
                                mm(bk, bk[:, :], CT, CT[:, c, t * 128:(t + 1) * 128], wout, wout[:, c, hf * 512:(hf + 1) * 512], c == 0, c == 7)
                            sy.do("dve", lambda e, t=t, hf=hf, bk=bk: e.tensor_tensor(out=X[:, t, hf * 512:(hf + 1) * 512], in0=X[:, t, hf * 512:(hf + 1) * 512], in1=bk[:, :], op=ALU.add),
                                  R=[bk], W=[X])
                    sy.dma("sp", XR[r0:r0 + N, :].rearrange("(t p) d -> p t d", p=128), X[:, 0:ntl, :], R=[X])
                sy.barrier()
                ckpt(8)

            with ExitStack() as ph:
                wup = sb(ph, "wup", [128, 8, DFF], BF16)
                wdn = sb(ph, "wdn", [128, 32, D], BF16)
                gffnb = sb(ph, "gffnb", [128, D], F32)
                sy.dma("sp", gffnb[:], g_ffn[l].partition_broadcast(128), W=[gffnb])
                with ExitStack() as ph2:
                    load_w(ph2, wup, [wup[:, c, hf * 2048:(hf + 1) * 2048] for c in range(8) for hf in range(2)],
                           [w_up[l, c * 128:(c + 1) * 128, hf * 2048:(hf + 1) * 2048] for c in range(8) for hf in range(2)], 2048)
                    load_w(ph2, wdn, [wdn[:, 2 * j:2 * j + 2, :] for j in range(16)],
                           [w_down[l, j * 256:(j + 1) * 256, :].rearrange("(f p) d -> p f d", p=128) for j in range(16)], 2048)
                    sy.barrier()
                xg = [sb(ph, "xg3_%d" % i, [128, 2, D], F32) for i in range(2)]
                scr = sb(ph, "scr3", [128, D], F32)
                ssx = sb(ph, "ssx3", [128, 1], F32)
                hb = [sb(ph, "hb3_%d" % i, [128, D], BF16) for i in range(2)]
                hT = [sb(ph, "hT3_0", [128, 8, 256], BF16)] * 2
                rb = [sb(ph, "rb%d" % i, [128, 256], F32) for i in range(2)]
                uT = [sb(ph, "uT0", [128, 32, 256], BF16)] * 2
                yb = [sb(ph, "yb0", [128, 2, D], F32)] * 2 if last else None
                for gi in range(NT // 256):
                    r0 = gi * 256
                    pb = gi % 2
                    X, HT, UT = xg[pb], hT[pb], uT[pb]
                    sy.dma("sp", X[:], XR[r0:r0 + 256, :].rearrange("(t p) d -> p t d", p=128), W=[X])
                    for t in range(2):
                        H = hb[t % 2]
                        rstd_rows(X, X[:, t, :], D, scr, ssx)
                        sy.do("dve", lambda e, t=t, H=H: e.scalar_tensor_tensor(out=H[:], in0=X[:, t, :], scalar=ssx[:, 0:1], in1=gffnb[:],
                                                                               op0=ALU.mult, op1=ALU.mult), R=[X, ssx, gffnb], W=[H])
                        for c in range(8):
                            sy.do("pe", lambda e, c=c, H=H: e.transpose(out=tb[:, c * 128:(c + 1) * 128], in_=H[:, c * 128:(c + 1) * 128], identity=identb[:]),
                                  R=[H, identb], W=[tb])
                        sy.do("act", lambda e, t=t: e.activation(out=HT[:, :, t * 128:(t + 1) * 128], in_=tb[:, :].rearrange("p (c n) -> p c n", c=8), func=AF.Copy),
                              R=[tb], W=[HT])
                    for fc in range(32):
                        bk = fb[fc % 3]
                        for c in range(8):
                            mm(bk, bk[:, 0:256], wup, wup[:, c, fc * 128:(fc + 1) * 128], HT, HT[:, c, :], c == 0, c == 7)
                        R_ = rb[fc % 2]
                        sy.do("act", lambda e, bk=bk, R_=R_: e.activation(out=R_[:], in_=bk[:, 0:256], func=AF.Relu), R=[bk], W=[R_])
                        sy.do("pool", lambda e, fc=fc, R_=R_: e.tensor_tensor(out=UT[:, fc, :], in0=R_[:], in1=R_[:], op=ALU.mult), R=[R_], W=[UT])
                    for t in range(2):
                        for hf in range(2):
                            bk = fb[3 + (2 * t + hf) % 4]
                            for fc in range(32):
                                mm(bk, bk[:, :], UT, UT[:, fc, t * 128:(t + 1) * 128], wdn, wdn[:, fc, hf * 512:(hf + 1) * 512], fc == 0, fc == 31)
                            sy.do("dve", lambda e, t=t, hf=hf, bk=bk: e.tensor_tensor(out=X[:, t, hf * 512:(hf + 1) * 512], in0=X[:, t, hf * 512:(hf + 1) * 512], in1=bk[:, :], op=ALU.add),
                                  R=[bk], W=[X])
                    if not last:
                        sy.dma("sp", XR[r0:r0 + 256, :].rearrange("(t p) d -> p t d", p=128), X[:], R=[X])
                    else:
                        Y = yb[pb]
                        for t in range(2):
                            rstd_rows(X, X[:, t, :], D, scr, ssx)
                            sy.do("dve", lambda e, t=t: e.scalar_tensor_tensor(out=Y[:, t, :], in0=X[:, t, :], scalar=ssx[:, 0:1], in1=gfinb[:],
                                                                              op0=ALU.mult, op1=ALU.mult), R=[X, ssx, gfinb], W=[Y])
                        sy.dma("sp", y[r0:r0 + 256, :].rearrange("(t p) d -> p t d", p=128), Y[:], R=[Y])
                sy.barrier()
                ckpt(9)
        print("instructions emitted:", sy.ninstr, {k: v for k, v in sy.cnt.items() if not k.startswith("d")})
    return nc


def _consts(S, PAST):
    NT = S + NSQ * TS
    j = np.arange(128)[:, None]
    t = np.arange(512)[None, :]
    tri = np.zeros((128, 4, 128), np.float32)
    jj, kk = np.arange(128)[:, None], np.arange(128)[None, :]
    tri[:, 0, :] = -1.0 * (jj >= kk)
    tri[:, 1, :] = -1.0 * (jj < kk)
    tri[:, 2, :] = -1.0
    tri[:, 3, :] = np.eye(128, dtype=np.float32)[::-1]
    msb = np.stack([(j + 128 * r < t) for r in range(4)], 1).astype(np.float32)
    mml = np.stack([((128 * r + j) // 64 <= t // 64) for r in range(4)], 1).astype(np.float32)
    val = []
    for i in range(8):
        d = 8 + t // 64 - (2 * i + j // 64)
        val.append((d >= 0) & (d <= 8))
    val = np.stack(val, 1).astype(np.float32)
    pos = np.concatenate([np.arange(S), np.tile(PAST + np.arange(TS), NSQ)]).astype(np.float32)
    inv = (10000.0 ** (-np.arange(32, dtype=np.float32) / np.float32(32))).astype(np.float32)
    ang = (pos[:, None] * inv[None, :]).astype(np.float32)
    cos, sin = np.cos(ang).astype(np.float32), np.sin(ang).astype(np.float32)
    ropeT = np.concatenate([cos, sin], 1).astype(np.float32)
    cosF = np.concatenate([cos, cos], 1).T
    sinF = np.concatenate([-sin, sin], 1).T
    ropeF = np.ascontiguousarray(np.stack([cosF, sinF], 1)).astype(np.float32)
    return dict(c_ident=np.eye(128, dtype=np.float32), c_tri=tri, c_msb=np.ascontiguousarray(msb),
                c_mml=np.ascontiguousarray(mml), c_val=np.ascontiguousarray(val), c_ropeT=ropeT, c_ropeF=ropeF)


_CACHE = {}


def kernel(x_prompt, x_sample, cache_a_k, cache_a_v, cache_mla_ckv, cache_mla_krope, cache_sb_k, cache_sb_v,
           g_mix, w_in, g_cq, g_ckv, w_uq, w_ukv, a_rel_bias, g_out_a, g_out_mla, g_out_sb, w_out,
           g_ffn, w_up, w_down, g_final):
    f = lambda a: np.ascontiguousarray(np.asarray(a, dtype=np.float32))
    x_prompt, x_sample = f(x_prompt), f(x_sample)
    B, S, _ = x_prompt.shape
    L = w_in.shape[0]
    PAST = cache_mla_ckv.shape[2]
    ncore = B
    assert x_sample.shape[0] == NSQ * ncore and x_sample.shape[1] == TS
    key = (S, PAST, L)
    if key not in _CACHE:
        _CACHE[key] = build(S, PAST, L)
    nc = _CACHE[key]
    cst = _consts(S, PAST)
    shared = dict(g_mix=f(g_mix), w_in=f(w_in), g_cq=f(g_cq), g_ckv=f(g_ckv),
                  w_uq=f(w_uq).reshape(L, 256, 768), w_ukv=f(w_ukv).reshape(L, 256, 1024),
                  a_rel=f(a_rel_bias).reshape(L, 768),
                  g_cat=np.ascontiguousarray(np.concatenate([f(g_out_a), f(g_out_mla), f(g_out_sb)], axis=1)),
                  w_out=f(w_out), g_ffn=f(g_ffn), w_up=f(w_up), w_down=f(w_down), g_fin=f(g_final), **cst)
    cak, cav = f(cache_a_k), f(cache_a_v)
    cckv, ckr, csk, csv = f(cache_mla_ckv), f(cache_mla_krope), f(cache_sb_k), f(cache_sb_v)
    in_maps = []
    for c in range(ncore):
        sl = slice(NSQ * c, NSQ * c + NSQ)
        m = dict(shared)
        m["x"] = np.ascontiguousarray(np.concatenate([x_prompt[c], x_sample[sl].reshape(NSQ * TS, D)], axis=0))
        m["cak"] = np.ascontiguousarray(cak[:, sl].reshape(L, NSQ, 512, 256))
        m["cav"] = np.ascontiguousarray(cav[:, sl].reshape(L, NSQ, 512, 256))
        m["cckv"] = np.ascontiguousarray(cckv[:, sl])
        m["ckr"] = np.ascontiguousarray(ckr[:, sl])
        m["csk"] = np.ascontiguousarray(csk[:, sl].reshape(L, NSQ, PAST, 256))
        m["csv"] = np.ascontiguousarray(csv[:, sl].reshape(L, NSQ, PAST, 256))
        in_maps.append(m)
    res = run_bass_kernel_spmd(nc, in_maps, core_ids=list(range(ncore)))
    R = res.results
    cat = lambda k, ax: np.concatenate([np.asarray(r[k]) for r in R], axis=ax)
    yall = np.stack([np.asarray(r["y"]) for r in R], 0)
    y_prompt = np.ascontiguousarray(yall[:, :S])
    y_sample = np.ascontiguousarray(yall[:, S:].reshape(ncore * NSQ, TS, D))
    st1 = lambda k: np.stack([np.asarray(r[k]) for r in R], 1)
    p_a_k = st1("pak").reshape(L, B, 512, 4, 64)
    p_a_v = st1("pav").reshape(L, B, 512, 4, 64)
    p_ckv = st1("pckv")
    p_krope = st1("pkr")
    p_sb_k = st1("psk").reshape(L, B, S, 4, 64)
    p_sb_v = st1("psv").reshape(L, B, S, 4, 64)
    s_a_k = cat("sak", 1).reshape(L, ncore * NSQ, 512, 4, 64)
    s_a_v = cat("sav", 1).reshape(L, ncore * NSQ, 512, 4, 64)
    s_ckv = cat("sckv", 1)
    s_krope = cat("skr", 1)
    s_sb_k = cat("ssk", 1).reshape(L, ncore * NSQ, TS, 4, 64)
    s_sb_v = cat("ssv", 1).reshape(L, ncore * NSQ, TS, 4, 64)
    outs = (y_prompt, y_sample, p_a_k, p_a_v, p_ckv, p_krope, p_sb_k, p_sb_v,
            s_a_k, s_a_v, s_ckv, s_krope, s_sb_k, s_sb_v)
    return tuple(np.ascontiguousarray(o, dtype=np.float32) for o in outs)
```

```python
import os
import numpy as np
DBG = int(os.environ.get('MK_DBG', '0'))
STOPN = int(os.environ.get('MK_STOPN', '-1'))
MAXOUT = int(os.environ.get('MK_MAXOUT', '4'))


class _Stop(Exception):
    pass
from contextlib import ExitStack
import concourse.bass as bass
import concourse.mybir as mybir
from concourse.bass_utils import run_bass_kernel_spmd

F32 = mybir.dt.float32
BF16 = mybir.dt.bfloat16
AF = mybir.ActivationFunctionType
ALU = mybir.AluOpType
AX = mybir.AxisListType

D = 1024
DFF = 4096
NSQ = 4
TS = 64
EPS = 1e-6
INC = 2112


class Buf:
    __slots__ = ("t", "w", "r", "dk", "x")

    def __init__(self, t, x=False):
        self.t = t
        self.w = None
        self.r = {}
        self.dk = None
        self.x = x

    def __getitem__(self, k):
        return self.t[k]


class Sy:
    def __init__(self, nc, st, ndma=int(os.environ.get('MK_NDMA', '72'))):
        self.nc = nc
        self.eng = {"pe": nc.tensor, "act": nc.scalar, "dve": nc.vector, "pool": nc.gpsimd, "sp": nc.sync}
        self.sem, self.cnt = {}, {}
        self.seen = {e: {} for e in self.eng}
        for e in self.eng:
            self.sem[e] = st.enter_context(nc.semaphore("s_" + e))
            self.cnt[e] = 0
        self.dpool = []
        for i in range(ndma):
            k = "d%d" % i
            self.sem[k] = st.enter_context(nc.semaphore(k))
            self.cnt[k] = 0
            self.dpool.append(k)
        self.dnext = 0
        self.ninstr = 0
        self.fifo = []

    def _wait(self, e, waits):
        for k, v in waits:
            if k == "pe" and e == "pe":
                continue
            if self.seen[e].get(k, 0) >= v:
                continue
            self.eng[e].wait_ge(self.sem[k], v)
            self.seen[e][k] = v

    def _deps(self, R, W):
        waits = []
        for b in R:
            if b.w:
                waits.append(b.w)
            if b.x:
                waits.extend(b.r.items())
        for b in W:
            if b.w:
                waits.append(b.w)
            waits.extend(b.r.items())
        return waits

    def do(self, e, fn, R=(), W=()):
        if self.ninstr == STOPN:
            self.barrier()
            raise _Stop()
        self._wait(e, self._deps(R, W))
        ins = fn(self.eng[e])
        self.cnt[e] += 1
        ins.then_inc(self.sem[e], 1)
        v = self.cnt[e]
        for b in R:
            if b.x:
                b.w = (e, v)
                b.r = {}
            else:
                b.r[e] = v
        for b in W:
            b.w = (e, v)
            b.r = {}
        self.ninstr += 1
        return (e, v)

    def dma(self, q, out, in_, R=(), W=(), **kw):
        if self.ninstr == STOPN:
            self.barrier()
            raise _Stop()
        if len(self.fifo) >= MAXOUT:
            self._wait(q, [self.fifo.pop(0)])
        self._wait(q, self._deps(R, W))
        b0 = (list(W) + list(R))[0]
        if b0.dk is None:
            b0.dk = self.dpool[self.dnext % len(self.dpool)]
            self.dnext += 1
        k = b0.dk
        ins = self.eng[q].dma_start(out=out, in_=in_, **kw)
        self.cnt[k] += 16
        ins.then_inc(self.sem[k], 16)
        v = self.cnt[k]
        for b in R:
            b.r[k] = v
        for b in W:
            b.w = (k, v)
            b.r = {}
        self.ninstr += 1
        self.fifo.append((k, v))
        return (k, v)

    def barrier(self):
        allw = [(k, v) for k, v in self.cnt.items() if v > 0]
        for e in self.eng:
            self._wait(e, allw)
        self.dnext = 0


def build(S, PAST, L, stop=None):
    NT = S + NSQ * TS
    NG = S // 512
    NKB = S // 128
    NPB = PAST // 128
    nc = bass.Bass("TRN2", target_bir_lowering=False)

    def din(name, shape, dt=F32):
        return nc.dram_tensor(name, list(shape), dt, kind="ExternalInput").ap()

    def dout(name, shape):
        return nc.dram_tensor(name, list(shape), F32, kind="ExternalOutput").ap()

    def dscr(name, shape, dt=BF16):
        return nc.dram_tensor(name, list(shape), dt).ap()

    xin = din("x", [NT, D])
    cak = din("cak", [L, NSQ, 512, 256]); cav = din("cav", [L, NSQ, 512, 256])
    cckv = din("cckv", [L, NSQ, PAST, 256]); ckr = din("ckr", [L, NSQ, PAST, 64])
    csk = din("csk", [L, NSQ, PAST, 256]); csv = din("csv", [L, NSQ, PAST, 256])
    g_mix = din("g_mix", [L, D]); w_in = din("w_in", [L, D, INC])
    g_cq = din("g_cq", [L, 256]); g_ckv = din("g_ckv", [L, 256])
    w_uq = din("w_uq", [L, 256, 768]); w_ukv = din("w_ukv", [L, 256, 1024])
    a_rel = din("a_rel", [L, 768]); g_cat = din("g_cat", [L, D])
    w_out = din("w_out", [L, D, D]); g_ffn = din("g_ffn", [L, D])
    w_up = din("w_up", [L, D, DFF]); w_down = din("w_down", [L, DFF, D]); g_fin = din("g_fin", [D])
    c_ident = din("c_ident", [128, 128]); c_tri = din("c_tri", [128, 4, 128])
    c_msb = din("c_msb", [128, 4, 512]); c_mml = din("c_mml", [128, 4, 512]); c_val = din("c_val", [128, 8, 512])
    c_ropeT = din("c_ropeT", [NT, 64]); c_ropeF = din("c_ropeF", [64, 2, NT])

    y = dout("y", [NT, D])
    pak = dout("pak", [L, 512, 256]); pav = dout("pav", [L, 512, 256])
    pckv = dout("pckv", [L, S, 256]); pkr = dout("pkr", [L, S, 64])
    psk = dout("psk", [L, S, 256]); psv = dout("psv", [L, S, 256])
    sak = dout("sak", [L, NSQ, 512, 256]); sav = dout("sav", [L, NSQ, 512, 256])
    sckv = dout("sckv", [L, NSQ, TS, 256]); skr = dout("skr", [L, NSQ, TS, 64])
    ssk = dout("ssk", [L, NSQ, TS, 256]); ssv = dout("ssv", [L, NSQ, TS, 256])

    XR = dscr("XR", [NT, D], F32)
    QA = dscr("QA", [256, NT]); KA = dscr("KA", [256, NT]); QC = dscr("QC", [256, NT]); KC = dscr("KC", [256, NT])
    VA = dscr("VA", [NT, 256]); VC = dscr("VC", [NT, 256])
    QN = dscr("QN", [512, NT]); QP = dscr("QP", [256, NT]); KN = dscr("KN", [512, NT]); KR = dscr("KR", [64, NT])
    VM = dscr("VM", [NT, 512])
    KAs = dscr("KAs", [NSQ, 256, 512]); VAs = dscr("VAs", [NSQ, 512, 256])
    KNs = dscr("KNs", [NSQ, 512, PAST]); KRs = dscr("KRs", [NSQ, 64, PAST]); VMs = dscr("VMs", [NSQ, PAST, 512])
    KCs = dscr("KCs", [NSQ, 256, PAST]); VCs = dscr("VCs", [NSQ, PAST, 256])
    OCAT = dscr("OCAT", [D, NT], F32)
    EXT = dscr("EXT", [4, 1536], F32)

    with ExitStack() as st:
        st.push(lambda et, ev, tb_: et is _Stop)
        sy = Sy(nc, st)

        uniq = [0]

        def sb(stk, name, shape, dt):
            uniq[0] += 1
            return Buf(stk.enter_context(nc.sbuf_tensor("%s_%d" % (name, uniq[0]), list(shape), dt)))

        fb = [Buf(st.enter_context(nc.psum_tensor("pf%d" % i, [128, 512], F32)), x=True) for i in range(7)]
        tb = Buf(st.enter_context(nc.psum_tensor("ptb", [128, 1024], BF16)), x=True)

        identb = sb(st, "identb", [128, 128], BF16)
        trib = sb(st, "trib", [128, 4, 128], BF16)
        onesb = sb(st, "onesb", [128, 128], BF16)
        ones1f = sb(st, "ones1f", [1, 128], F32)
        gfinb = sb(st, "gfinb", [128, D], F32)
        EBs = sb(st, "EBs", [128, 4, 5, 64], BF16)
        with ExitStack() as ph:
            t1 = sb(ph, "c_t1", [128, 128], F32)
            t2 = sb(ph, "c_t2", [128, 4, 128], F32)
            sy.dma("sp", t1[:], c_ident, W=[t1])
            sy.dma("sp", t2[:], c_tri, W=[t2])
            sy.dma("sp", gfinb[:], g_fin.partition_broadcast(128), W=[gfinb])
            sy.do("dve", lambda e: e.tensor_copy(out=identb[:], in_=t1[:]), R=[t1], W=[identb])
            sy.do("dve", lambda e: e.tensor_copy(out=trib[:], in_=t2[:]), R=[t2], W=[trib])
            sy.do("dve", lambda e: e.memset(onesb[:], 1.0), W=[onesb])
            sy.do("dve", lambda e: e.memset(ones1f[:], 1.0), W=[ones1f])
            sy.barrier()

        ddb = Buf(None)
        def mm(out_b, out_ap, lhsT_b, lhsT, rhs_b, rhs, start, stop, sgc=False):
            return sy.do("pe", lambda e: e.matmul(out_ap, lhsT=lhsT, rhs=rhs, start=start, stop=stop, skip_group_check=sgc),
                         R=[lhsT_b, rhs_b], W=[out_b])

        cast_rr = [0]

        def load_w(ph, dst_b, views, srcs, width):
            stg = [sb(ph, "wst%d_%d" % (sy.ninstr, i), [128, width], F32) for i in range(2)]
            for i, (v, s_) in enumerate(zip(views, srcs)):
                sg = stg[i % 2]
                sy.dma("sp", sg[:], s_, W=[sg])
                eng = ("pool", "dve")[cast_rr[0] % 2]
                cast_rr[0] += 1
                sy.do(eng, lambda e, v=v, sg=sg: e.tensor_copy(out=v, in_=sg[:]), R=[sg], W=[dst_b])

        def rstd_rows(src_b, src_ap, width, scr_b, ss_b, sq_eng="act"):
            if sq_eng == "act":
                sy.do("act", lambda e: e.activation(out=scr_b[:, 0:width], in_=src_ap, func=AF.Square), R=[src_b], W=[scr_b])
            else:
                sy.do("dve", lambda e: e.tensor_tensor(out=scr_b[:, 0:width], in0=src_ap, in1=src_ap, op=ALU.mult), R=[src_b], W=[scr_b])
            sy.do("dve", lambda e: e.reduce_sum(out=ss_b[:, 0:1], in_=scr_b[:, 0:width], axis=AX.X), R=[scr_b], W=[ss_b])
            sy.do("dve", lambda e: e.tensor_scalar(out=ss_b[:, 0:1], in0=ss_b[:, 0:1], scalar1=1.0 / width, scalar2=EPS,
                                                   op0=ALU.mult, op1=ALU.add), R=[ss_b], W=[ss_b])
            if not (DBG & 4096):
                sy.do("act", lambda e: e.activation(out=ss_b[:, 0:1], in_=ss_b[:, 0:1], func=AF.Sqrt), R=[ss_b], W=[ss_b])
            sy.do("dve", lambda e: e.reciprocal(out=ss_b[:, 0:1], in_=ss_b[:, 0:1]), R=[ss_b], W=[ss_b])

        def ckpt(k):
            if stop == k:
                sy.barrier()
                raise _Stop()

        for l in range(L):
            xsrc = xin if l == 0 else XR
            last = (l == L - 1)
            ckpt(0)

            with ExitStack() as ph:
                win = sb(ph, "win", [128, 8, INC], BF16)
                wuq = sb(ph, "wuq", [128, 2, 768], BF16)
                wuqs = sb(ph, "wuqs", [128, 2, 256], BF16)
                wkn = sb(ph, "wkn", [128, 2, 512], BF16)
                wv = sb(ph, "wv", [128, 2, 512], BF16)
                gmixb = sb(ph, "gmixb", [128, D], F32)
                gcqb = sb(ph, "gcqb", [128, 256], F32)
                gckvb = sb(ph, "gckvb", [128, 256], F32)
                sy.dma("sp", gmixb[:], g_mix[l].partition_broadcast(128), W=[gmixb])
                sy.dma("sp", gcqb[:], g_cq[l].partition_broadcast(128), W=[gcqb])
                sy.dma("sp", gckvb[:], g_ckv[l].partition_broadcast(128), W=[gckvb])
                with ExitStack() as ph2:
                    load_w(ph2, win, [win[:, c, :] for c in range(8)], [w_in[l, c * 128:(c + 1) * 128, :] for c in range(8)], INC)
                    wq32 = sb(ph2, "wq32", [128, 2, 768], F32)
                    wkv32 = sb(ph2, "wkv32", [128, 2, 1024], F32)
                    sy.dma("sp", wq32[:], w_uq[l].rearrange("(k p) e -> p k e", p=128), W=[wq32])
                    sy.dma("sp", wkv32[:], w_ukv[l].rearrange("(k p) e -> p k e", p=128), W=[wkv32])
                    sy.do("dve", lambda e: e.tensor_copy(out=wuq[:], in_=wq32[:]), R=[wq32], W=[wuq])
                    for k in range(2):
                        q4 = wq32[:, k, :].rearrange("p (h e) -> p h e", h=4)
                        d4 = wuqs[:, k, :].rearrange("p (h e) -> p h e", h=4)
                        sy.do("dve", lambda e, q4=q4, d4=d4: e.tensor_copy(out=d4[:, :, 0:32], in_=q4[:, :, 160:192]), R=[wq32], W=[wuqs])
                        sy.do("dve", lambda e, q4=q4, d4=d4: e.tensor_copy(out=d4[:, :, 32:64], in_=q4[:, :, 128:160]), R=[wq32], W=[wuqs])
                        k4 = wkv32[:, k, :].rearrange("p (h e) -> p h e", h=4)
                        sy.do("dve", lambda e, k4=k4, k=k: e.tensor_copy(out=wkn[:, k, :].rearrange("p (h e) -> p h e", h=4), in_=k4[:, :, 0:128]), R=[wkv32], W=[wkn])
                        sy.do("dve", lambda e, k4=k4, k=k: e.tensor_copy(out=wv[:, k, :].rearrange("p (h e) -> p h e", h=4), in_=k4[:, :, 128:256]), R=[wkv32], W=[wv])
                    sy.barrier()

                ckpt(1)
                xt = [sb(ph, "xt%d" % i, [128, D], F32) for i in range(3)]
                rt = [sb(ph, "rt%d" % i, [128, 4, 64], F32) for i in range(2)]
                rf = [sb(ph, "rf0", [64, 2, 512], F32)] * 2
                scr = sb(ph, "scr", [128, D], F32)
                ssx = sb(ph, "ssx", [128, 1], F32)
                ss2 = sb(ph, "ss2", [128, 1], F32)
                ss3 = sb(ph, "ss3", [128, 1], F32)
                hb = [sb(ph, "hb%d" % i, [128, D], BF16) for i in range(2)]
                hT = [sb(ph, "hT0", [128, 8, 512], BF16)] * 2
                fo = [sb(ph, "fo0", [128, 8, 512], BF16)] * 2
                va_b = [sb(ph, "va_b%d" % i, [128, 4, 256], BF16) for i in range(2)]
                vaf = sb(ph, "vaf", [128, 4, 256], F32)
                kaf = sb(ph, "kaf", [128, 4, 256], F32)
                cqn_b = sb(ph, "cqn_b", [128, 256], BF16)
                ckvn_b = sb(ph, "ckvn_b", [128, 256], BF16)
                cqT = [sb(ph, "cqT%d" % i, [128, 2, 512], BF16) for i in range(2)]
                ckvT = [sb(ph, "ckvT%d" % i, [128, 2, 512], BF16) for i in range(2)]
                ckv_f = [sb(ph, "ckv_f%d" % i, [128, 4, 256], F32) for i in range(2)]
                krr = sb(ph, "krr", [128, 64], F32)
                krt = [sb(ph, "krt%d" % i, [128, 32], F32) for i in range(4)]
                kr_f = [sb(ph, "kr_f%d" % i, [128, 4, 64], F32) for i in range(2)]
                kr_b = sb(ph, "kr_b", [128, 64], BF16)
                krT = [sb(ph, "krT%d" % i, [64, 512], BF16) for i in range(2)]
                kcvc = [sb(ph, "kcvc%d" % i, [128, 4, 512], F32) for i in range(2)]
                vc_b = [sb(ph, "vc_b%d" % i, [128, 4, 256], BF16) for i in range(2)]
                qn_b = [sb(ph, "qn_b0", [128, 4, 512], BF16)] * 2
                qp_b = [sb(ph, "qp_b0", [64, 4, 512], BF16)] * 2
                kn_b = [sb(ph, "kn_b0", [128, 4, 512], BF16)] * 2
                vm_b = [sb(ph, "vm_b0", [128, 4, 512], BF16)] * 2
                qt1 = sb(ph, "qt1", [64, 512], F32)
                qt2 = sb(ph, "qt2", [64, 512], F32)
                c32 = [sb(ph, "c32_%d" % i, [128, 4, 256], F32) for i in range(2)]
                cb16 = [sb(ph, "cb16_%d" % i, [128, 4, 256], BF16) for i in range(2)]
                ckT = [sb(ph, "ckT%d" % i, [128, 2, 512], BF16) for i in range(2)]
                kr32 = sb(ph, "kr32", [128, 4, 64], F32)
                krb16 = sb(ph, "krb16", [128, 4, 64], BF16)
                krTc = [sb(ph, "krTc%d" % i, [64, 512], BF16) for i in range(2)]
                print('P1 sbuf remaining', nc.sbuf_bytes_remaining, 'vaf', vaf.t, 'kaf', kaf.t, 'krTc', krTc[1].t)
                ev_rr = [0]
                xti = [0]

                def evac(dst_b, dst_ap, src_b, src_ap):
                    eng = ("act", "dve")[ev_rr[0] % 2]
                    ev_rr[0] += 1
                    if eng == "act":
                        sy.do("act", lambda e: e.activation(out=dst_ap, in_=src_ap, func=AF.Copy), R=[src_b], W=[dst_b])
                    else:
                        sy.do("dve", lambda e: e.tensor_copy(out=dst_ap, in_=src_ap), R=[src_b], W=[dst_b])

                def kv_up(cT, N, ntl, knb, vmb):
                    for h in range(4):
                        bk = fb[4 + h % 2]
                        for k in range(2):
                            mm(bk, bk[:, 0:N], wkn, wkn[:, k, h * 128:(h + 1) * 128], cT, cT[:, k, 0:N], k == 0, k == 1)
                        evac(knb, knb[:, h, 0:N], bk, bk[:, 0:N])
                    for t in range(ntl):
                        bk = fb[4 + t % 2]
                        for k in range(2):
                            mm(bk, bk[:, 0:512], cT, cT[:, k, t * 128:(t + 1) * 128], wv, wv[:, k, :], k == 0, k == 1)
                        evac(vmb, vmb[:, t, :], bk, bk[:, 0:512])

                ci = 0
                for b in range(NSQ):
                    for (src, isk) in ((cak, True), (cav, False)):
                        cb = c32[ci % 2]; bb = cb16[ci % 2]; tt = ckT[ci % 2]; ci += 1
                        sy.dma("sp", cb[:], src[l, b].rearrange("(t p) c -> p t c", p=128), W=[cb])
                        sy.do("pool", lambda e, cb=cb, bb=bb: e.tensor_copy(out=bb[:], in_=cb[:]), R=[cb], W=[bb])
                        if isk:
                            for m in range(2):
                                for t in range(4):
                                    sy.do("pe", lambda e, m=m, t=t, bb=bb: e.transpose(out=tb[:, t * 128:(t + 1) * 128], in_=bb[:, t, m * 128:(m + 1) * 128], identity=identb[:]),
                                          R=[bb, identb], W=[tb])
                                evac(tt, tt[:, m, :], tb, tb[:, 0:512])
                            sy.dma("sp", KAs[b].rearrange("(m p) n -> p m n", p=128), tt[:], R=[tt])
                        else:
                            sy.dma("sp", VAs[b].rearrange("(t p) c -> p t c", p=128), bb[:], R=[bb])
                    for blk in range(PAST // 512):
                        r0 = blk * 512
                        cb = c32[ci % 2]; bb = cb16[ci % 2]; tt = ckT[ci % 2]; pb = ci % 2; ci += 1
                        sy.dma("sp", cb[:], cckv[l, b, r0:r0 + 512, :].rearrange("(t p) c -> p t c", p=128), W=[cb])
                        sy.do("pool", lambda e, cb=cb, bb=bb: e.tensor_copy(out=bb[:], in_=cb[:]), R=[cb], W=[bb])
                        for m in range(2):
                            for t in range(4):
                                sy.do("pe", lambda e, m=m, t=t, bb=bb: e.transpose(out=tb[:, t * 128:(t + 1) * 128], in_=bb[:, t, m * 128:(m + 1) * 128], identity=identb[:]),
                                      R=[bb, identb], W=[tb])
                            evac(tt, tt[:, m, :], tb, tb[:, 0:512])
                        kv_up(tt, 512, 4, kn_b[pb], vm_b[pb])
                        sy.dma("sp", KNs[b][:, r0:r0 + 512].rearrange("(h p) n -> p h n", p=128), kn_b[pb][:], R=[kn_b[pb]])
                        sy.dma("sp", VMs[b][r0:r0 + 512, :].rearrange("(t p) c -> p t c", p=128), vm_b[pb][:], R=[vm_b[pb]])
                        for (src, isk) in ((csk, True), (csv, False)):
                            cb = c32[ci % 2]; bb = cb16[ci % 2]; tt = ckT[ci % 2]; ci += 1
                            sy.dma("sp", cb[:], src[l, b, r0:r0 + 512, :].rearrange("(t p) c -> p t c", p=128), W=[cb])
                            sy.do("pool", lambda e, cb=cb, bb=bb: e.tensor_copy(out=bb[:], in_=cb[:]), R=[cb], W=[bb])
                            if isk:
                                for m in range(2):
                                    for t in range(4):
                                        sy.do("pe", lambda e, m=m, t=t, bb=bb: e.transpose(out=tb[:, t * 128:(t + 1) * 128], in_=bb[:, t, m * 128:(m + 1) * 128], identity=identb[:]),
                                              R=[bb, identb], W=[tb])
                                    evac(tt, tt[:, m, :], tb, tb[:, 0:512])
                                sy.dma("sp", KCs[b][:, r0:r0 + 512].rearrange("(m p) n -> p m n", p=128), tt[:], R=[tt])
                            else:
                                sy.dma("sp", VCs[b][r0:r0 + 512, :].rearrange("(t p) c -> p t c", p=128), bb[:], R=[bb])
                        kt = krTc[blk % 2]
                        sy.dma("sp", kr32[:], ckr[l, b, r0:r0 + 512, :].rearrange("(t p) c -> p t c", p=128), W=[kr32])
                        sy.do("pool", lambda e: e.tensor_copy(out=krb16[:], in_=kr32[:]), R=[kr32], W=[krb16])
                        for t in range(4):
                            sy.do("pe", lambda e, t=t: e.transpose(out=tb[0:64, t * 128:(t + 1) * 128], in_=krb16[:, t, :], identity=identb[:]),
                                  R=[krb16, identb], W=[tb])
                        evac(kt, kt[:, :], tb, tb[0:64, 0:512])
                        sy.dma("sp", KRs[b][:, r0:r0 + 512], kt[:], R=[kt])
                    sy.dma("sp", sak[l, b, 0:448, :], cak[l, b, 64:512, :], W=[ddb])
                    sy.dma("sp", sav[l, b, 0:448, :], cav[l, b, 64:512, :], W=[ddb])

                print('ninstr at P1 start', sy.ninstr)
                ckpt(2)
                groups = [(g * 512, 4, g) for g in range(NG)] + [(S, 2, NG)]
                xpre = {}

                def xload(r0_, t_):
                    X = xt[xti[0] % 3]; xti[0] += 1
                    sy.dma("sp", X[:], xsrc[r0_ + t_ * 128:r0_ + (t_ + 1) * 128, :], W=[X])
                    return X

                for gidx, (r0, ntl, gi) in enumerate(groups):
                    N = ntl * 128
                    pb = gi % 2
                    is_s = (gi == NG)
                    need_a = is_s or (gi == NG - 1)
                    RT, RF, HT = rt[pb], rf[pb], hT[pb]
                    sy.dma("sp", RT[:, 0:ntl, :], c_ropeT[r0:r0 + N, :].rearrange("(t p) d -> p t d", p=128), W=[RT])
                    sy.dma("sp", RF[:, :, 0:N], c_ropeF[:, :, r0:r0 + N], W=[RF])
                    for t in range(ntl):
                        H = hb[t % 2]
                        X = xpre.pop((gi, t), None)
                        if X is None:
                            X = xload(r0, t)
                        rstd_rows(X, X[:], D, scr, ssx)
                        sy.do("dve", lambda e, X=X, H=H: e.scalar_tensor_tensor(out=H[:], in0=X[:], scalar=ssx[:, 0:1], in1=gmixb[:],
                                                                               op0=ALU.mult, op1=ALU.mult), R=[X, ssx, gmixb], W=[H])
                        for c in range(8):
                            sy.do("pe", lambda e, c=c, H=H: e.transpose(out=tb[:, c * 128:(c + 1) * 128], in_=H[:, c * 128:(c + 1) * 128], identity=identb[:]),
                                  R=[H, identb], W=[tb])
                        evac(HT, HT[:, :, t * 128:(t + 1) * 128], tb, tb[:, :].rearrange("p (c n) -> p c n", c=8))
                        tm = [(fb[0], 512, 1024), (fb[1], 1024, 1344), (fb[2], 1600, 2112)]
                        if need_a and not (DBG & 1):
                            tm.append((fb[3], 256, 512))
                        for (bk, c0, c1) in tm:
                            for c in range(8):
                                mm(bk, bk[:, 0:c1 - c0], HT, HT[:, c, t * 128:(t + 1) * 128], win, win[:, c, c0:c1], c == 0, c == 7)
                        sy.do("act", lambda e, t=t: e.activation(out=va_b[pb][:, t, :], in_=fb[0][:, 0:256], func=AF.Copy), R=[fb[0]], W=[va_b[pb]])
                        if need_a and not (DBG & 2):
                            if DBG & 16:
                                sy.do("dve", lambda e, t=t: e.tensor_copy(out=vaf[:, t, :], in_=gcqb[:]), R=[gcqb], W=[vaf])
                            elif DBG & 8:
                                sy.do("dve", lambda e, t=t: e.tensor_copy(out=scr[:, 0:256], in_=fb[0][:, 0:256]), R=[fb[0]], W=[scr])
                            else:
                                sy.do("dve", lambda e, t=t: e.tensor_copy(out=vaf[:, t, :], in_=fb[0][:, 0:256]), R=[fb[0]], W=[vaf])
                        if need_a and not (DBG & 4):
                            sy.do("act", lambda e, t=t: e.activation(out=kaf[:, t, :], in_=fb[3][:, 0:256], func=AF.Copy), R=[fb[3]], W=[kaf])
                        if not (DBG & 1024):
                            rstd_rows(fb[0], fb[0][:, 256:512], 256, scr, ss2)
                            sy.do("dve", lambda e: e.scalar_tensor_tensor(out=cqn_b[:], in0=fb[0][:, 256:512], scalar=ss2[:, 0:1], in1=gcqb[:],
                                                                          op0=ALU.mult, op1=ALU.mult), R=[fb[0], ss2, gcqb], W=[cqn_b])
                            rstd_rows(fb[1], fb[1][:, 0:256], 256, scr, ss3)
                            sy.do("dve", lambda e, t=t: e.scalar_tensor_tensor(out=ckv_f[pb][:, t, :], in0=fb[1][:, 0:256], scalar=ss3[:, 0:1], in1=gckvb[:],
                                                                               op0=ALU.mult, op1=ALU.mult), R=[fb[1], ss3, gckvb], W=[ckv_f[pb]])
                            sy.do("pool", lambda e, t=t: e.tensor_copy(out=ckvn_b[:], in_=ckv_f[pb][:, t, :]), R=[ckv_f[pb]], W=[ckvn_b])
                            for k in range(2):
                                sy.do("pe", lambda e, k=k: e.transpose(out=tb[:, k * 128:(k + 1) * 128], in_=cqn_b[:, k * 128:(k + 1) * 128], identity=identb[:]),
                                      R=[cqn_b, identb], W=[tb])
                            evac(cqT[pb], cqT[pb][:, :, t * 128:(t + 1) * 128], tb, tb[:, 0:256].rearrange("p (c n) -> p c n", c=2))
                            for k in range(2):
                                sy.do("pe", lambda e, k=k: e.transpose(out=tb[:, k * 128:(k + 1) * 128], in_=ckvn_b[:, k * 128:(k + 1) * 128], identity=identb[:]),
                                      R=[ckvn_b, identb], W=[tb])
                            evac(ckvT[pb], ckvT[pb][:, :, t * 128:(t + 1) * 128], tb, tb[:, 0:256].rearrange("p (c n) -> p c n", c=2))
                        if not (DBG & 64):
                            sy.do("act", lambda e: e.activation(out=krr[:], in_=fb[1][:, 256:320], func=AF.Copy), R=[fb[1]], W=[krr])
                            cs, sn = RT[:, t, 0:32], RT[:, t, 32:64]
                            x1, x2 = krr[:, 0:32], krr[:, 32:64]
                            sy.do("pool", lambda e, cs=cs, x1=x1: e.tensor_tensor(out=krt[0][:], in0=x1, in1=cs, op=ALU.mult), R=[krr, RT], W=[krt[0]])
                            sy.do("pool", lambda e, sn=sn, x2=x2: e.tensor_tensor(out=krt[1][:], in0=x2, in1=sn, op=ALU.mult), R=[krr, RT], W=[krt[1]])
                            sy.do("pool", lambda e, sn=sn, x1=x1: e.tensor_tensor(out=krt[2][:], in0=x1, in1=sn, op=ALU.mult), R=[krr, RT], W=[krt[2]])
                            sy.do("pool", lambda e, cs=cs, x2=x2: e.tensor_tensor(out=krt[3][:], in0=x2, in1=cs, op=ALU.mult), R=[krr, RT], W=[krt[3]])
                            sy.do("pool", lambda e, t=t: e.tensor_tensor(out=kr_f[pb][:, t, 0:32], in0=krt[0][:], in1=krt[1][:], op=ALU.subtract), R=[krt[0], krt[1]], W=[kr_f[pb]])
                            sy.do("pool", lambda e, t=t: e.tensor_tensor(out=kr_f[pb][:, t, 32:64], in0=krt[2][:], in1=krt[3][:], op=ALU.add), R=[krt[2], krt[3]], W=[kr_f[pb]])
                            sy.do("pool", lambda e, t=t: e.tensor_copy(out=kr_b[:], in_=kr_f[pb][:, t, :]), R=[kr_f[pb]], W=[kr_b])
                            sy.do("pe", lambda e: e.transpose(out=tb[0:64, 0:128], in_=kr_b[:], identity=identb[:]), R=[kr_b, identb], W=[tb])
                            evac(krT[pb], krT[pb][:, t * 128:(t + 1) * 128], tb, tb[0:64, 0:128])
                        sy.do("act", lambda e, t=t: e.activation(out=kcvc[pb][:, t, :], in_=fb[2][:, 0:512], func=AF.Copy), R=[fb[2]], W=[kcvc[pb]])
                        sy.do("pool", lambda e, t=t: e.tensor_copy(out=vc_b[pb][:, t, :], in_=kcvc[pb][:, t, 256:512]), R=[kcvc[pb]], W=[vc_b[pb]])
                    if gi == 0: print('ninstr after tiles g0', sy.ninstr)
                    if gi == 0: ckpt(20)
                    if is_s: ckpt(25)
                    for mi, c0 in enumerate([0, 128, 256, 384, 1344, 1472, 1600, 1728]):
                        bk = fb[4 + mi % 2]
                        for c in range(8):
                            mm(bk, bk[:, 0:N], win, win[:, c, c0:c0 + 128], HT, HT[:, c, 0:N], c == 0, c == 7)
                        evac(fo[pb], fo[pb][:, mi, 0:N], bk, bk[:, 0:N])
                    if gi == 0: ckpt(21)
                    if not (DBG & 128):
                        CQ = cqT[pb]
                        for h in range(4):
                            bk = fb[4 + h % 2]
                            for k in range(2):
                                mm(bk, bk[:, 0:N], wuq, wuq[:, k, h * 192:h * 192 + 128], CQ, CQ[:, k, 0:N], k == 0, k == 1)
                            evac(qn_b[pb], qn_b[pb][:, h, 0:N], bk, bk[:, 0:N])
                            for k in range(2):
                                mm(fb[6], fb[6][0:64, 0:N], wuq, wuq[:, k, h * 192 + 128:h * 192 + 192], CQ, CQ[:, k, 0:N], k == 0, k == 1)
                            for k in range(2):
                                mm(fb[3], fb[3][0:64, 0:N], wuqs, wuqs[:, k, h * 64:(h + 1) * 64], CQ, CQ[:, k, 0:N], k == 0, k == 1)
                            sy.do("dve", lambda e: e.tensor_tensor(out=qt1[:, 0:N], in0=fb[6][0:64, 0:N], in1=RF[:, 0, 0:N], op=ALU.mult), R=[fb[6], RF], W=[qt1])
                            sy.do("dve", lambda e: e.tensor_tensor(out=qt2[:, 0:N], in0=fb[3][0:64, 0:N], in1=RF[:, 1, 0:N], op=ALU.mult), R=[fb[3], RF], W=[qt2])
                            sy.do("pool", lambda e, h=h: e.tensor_tensor(out=qp_b[pb][:, h, 0:N], in0=qt1[:, 0:N], in1=qt2[:, 0:N], op=ALU.add), R=[qt1, qt2], W=[qp_b[pb]])
                    if gi == 0: ckpt(22)
                    if not (DBG & 256):
                        kv_up(ckvT[pb], N, ntl, kn_b[pb], vm_b[pb])
                    if gi == 0: ckpt(23)
                    if gidx + 1 < len(groups):
                        nr0, nntl, ngi = groups[gidx + 1]
                        for t_ in range(2):
                            xpre[(ngi, t_)] = xload(nr0, t_)
                    cs_ = slice(r0, r0 + N)
                    for j, dst in enumerate((QA, KA, QC, KC)):
                        sy.dma("sp", dst[:, cs_].rearrange("(m p) n -> p m n", p=128), fo[pb][:, 2 * j:2 * j + 2, 0:N], R=[fo[pb]])
                    sy.dma("sp", QN[:, cs_].rearrange("(h p) n -> p h n", p=128), qn_b[pb][:, :, 0:N], R=[qn_b[pb]])
                    sy.dma("sp", QP[:, cs_].rearrange("(h p) n -> p h n", p=64), qp_b[pb][:, :, 0:N], R=[qp_b[pb]])
                    sy.dma("sp", KN[:, cs_].rearrange("(h p) n -> p h n", p=128), kn_b[pb][:, :, 0:N], R=[kn_b[pb]])
                    sy.dma("sp", KR[:, cs_], krT[pb][:, 0:N], R=[krT[pb]])
                    sy.dma("sp", VM[cs_, :].rearrange("(t p) c -> p t c", p=128), vm_b[pb][:, 0:ntl, :], R=[vm_b[pb]])
                    sy.dma("sp", VA[cs_, :].rearrange("(t p) c -> p t c", p=128), va_b[pb][:, 0:ntl, :], R=[va_b[pb]])
                    sy.dma("sp", VC[cs_, :].rearrange("(t p) c -> p t c", p=128), vc_b[pb][:, 0:ntl, :], R=[vc_b[pb]])
                    if gi == 0: print('ninstr before outs g0', sy.ninstr)
                    if gi == 0: ckpt(24)
                    if not is_s:
                        sy.dma("sp", pckv[l, cs_, :].rearrange("(t p) c -> p t c", p=128), ckv_f[pb][:, 0:ntl, :], R=[ckv_f[pb]])
                        sy.dma("sp", pkr[l, cs_, :].rearrange("(t p) c -> p t c", p=128), kr_f[pb][:, 0:ntl, :], R=[kr_f[pb]])
                        sy.dma("sp", psk[l, cs_, :].rearrange("(t p) c -> p t c", p=128), kcvc[pb][:, 0:ntl, 0:256], R=[kcvc[pb]])
                        sy.dma("sp", psv[l, cs_, :].rearrange("(t p) c -> p t c", p=128), kcvc[pb][:, 0:ntl, 256:512], R=[kcvc[pb]])
                        if gi == 0: ckpt(26)
                        if need_a: ckpt(29)
                        if need_a:
                            sy.dma("sp", pak[l].rearrange("(t p) c -> p t c", p=128), kaf[:, 0:ntl, :], R=[kaf])
                            sy.dma("sp", pav[l].rearrange("(t p) c -> p t c", p=128), vaf[:, 0:ntl, :], R=[vaf])
                        if gi == NG - 1: ckpt(27)
                    else:
                        for b in range(NSQ):
                            t_, p0 = b // 2, (b % 2) * 64
                            sy.dma("sp", sckv[l, b], ckv_f[pb][p0:p0 + 64, t_, :], R=[ckv_f[pb]])
                            sy.dma("sp", skr[l, b], kr_f[pb][p0:p0 + 64, t_, :], R=[kr_f[pb]])
                            sy.dma("sp", ssk[l, b], kcvc[pb][p0:p0 + 64, t_, 0:256], R=[kcvc[pb]])
                            sy.dma("sp", ssv[l, b], kcvc[pb][p0:p0 + 64, t_, 256:512], R=[kcvc[pb]])
                            sy.dma("sp", sak[l, b, 448:512, :], kaf[p0:p0 + 64, t_, :], R=[kaf])
                            sy.dma("sp", sav[l, b, 448:512, :], vaf[p0:p0 + 64, t_, :], R=[vaf])
                    if gi == 1: ckpt(28)
                    if DBG & 32: sy.barrier()
                sy.barrier()
                ckpt(3)

            def soft_res(ph, tag):
                return ([sb(ph, "P%s%d" % (tag, i), [128, 512], BF16) for i in range(4)], sb(ph, "rD" + tag, [128, 512], F32))

            def sb_res(ph, tag):
                return ([sb(ph, "E%s%d" % (tag, i), [128, 512], F32) for i in range(3)],
                        [sb(ph, "L%s%d" % (tag, i), [128, 512], BF16) for i in range(4)],
                        [sb(ph, "X%s%d" % (tag, i), [128, 512], F32) for i in range(2)],
                        [sb(ph, "A%s%d" % (tag, i), [128, 512], BF16) for i in range(4)],
                        [sb(ph, "cr%s%d" % (tag, i), [1, 512], F32) for i in range(2)])

            def attn_soft(res, jobs, scale):
                Sb = [fb[0], fb[1], fb[6]]
                Ob = [fb[2], fb[3]]
                Db = [fb[4], fb[5]]
                Pt, rD = res
                NP_ = len(Pt)
                items = []
                for ji, jb in enumerate(jobs):
                    nb = len(jb["blocks"])
                    for bi, blk in enumerate(jb["blocks"]):
                        items.append((ji, jb, bi, blk, bi == 0, bi == nb - 1))

                def stage_a(t):
                    ji, jb, bi, blk, first, lastb = items[t]
                    N, m = jb["N"], blk["m"]
                    bk = Sb[t % 3]
                    n = len(blk["mms"])
                    for i, (lb, lap, rb, rap) in enumerate(blk["mms"]):
                        mm(bk, bk[0:m, 0:N], lb, lap, rb, rap, i == 0, i == n - 1)

                def stage_b(t):
                    ji, jb, bi, blk, first, lastb = items[t]
                    N, m = jb["N"], blk["m"]
                    bk, P = Sb[t % 3], Pt[t % NP_]
                    sy.do("act", lambda e: e.activation(out=P[0:m, 0:N], in_=bk[0:m, 0:N], func=AF.Exp, scale=scale), R=[bk], W=[P])
                    if blk["mulb"] is not None:
                        sy.do("dve", lambda e: e.tensor_tensor(out=P[0:m, 0:N], in0=P[0:m, 0:N], in1=blk["mulap"], op=ALU.mult),
                              R=[blk["mulb"]], W=[P])

                def stage_f(t):
                    ji, jb, bi, blk, first, lastb = items[t]
                    N, m, ed = jb["N"], blk["m"], jb["e"]
                    P = Pt[t % NP_]
                    O, Dn = Ob[ji % 2], Db[ji % 2]
                    mm(O, O[0:ed, 0:N], blk["vb"], blk["vap"], P, P[0:m, 0:N], first, lastb)
                    mm(Dn, Dn[0:ed, 0:N], onesb, onesb[0:m, 0:ed], P, P[0:m, 0:N], first, lastb)
                    if lastb:
                        sy.do("dve", lambda e: e.reciprocal(out=rD[0:ed, 0:N], in_=Dn[0:ed, 0:N]), R=[Dn], W=[rD])
                        sy.do("dve", lambda e: e.tensor_tensor(out=jb["dstap"], in0=O[0:ed, 0:N], in1=rD[0:ed, 0:N], op=ALU.mult),
                              R=[O, rD], W=[jb["dstb"]])
                        if jb.get("after"):
                            jb["after"]()

                T = len(items)

                def issue_a(t):
                    if items[t][2] == 0 and items[t][1].get("before"):
                        items[t][1]["before"]()
                    stage_a(t)

                if T:
                    issue_a(0)
                for t in range(T + 2):
                    if t + 1 < T:
                        issue_a(t + 1)
                    if t < T:
                        stage_b(t)
                    if 2 <= t:
                        stage_f(t - 2)

            def attn_sb(res, jobs, scale):
                Zb = [fb[0], fb[1], fb[6]]
                Bb = [fb[2], fb[3]]
                Ob = [fb[4], fb[5]]
                Eb, Lb, Xb, Ab, crow = res
                NA_ = len(Ab)
                items = []
                for ji, jb in enumerate(jobs):
                    nb = len(jb["blocks"])
                    for bi, blk in enumerate(jb["blocks"]):
                        items.append((ji, jb, bi, blk, bi == 0, bi == nb - 1))

                def stage_a(t):
                    ji, jb, bi, blk, first, lastb = items[t]
                    N, m = jb["N"], blk["m"]
                    bk = Zb[t % 3]
                    lb, lap, rb, rap = blk["mm"]
                    mm(bk, bk[0:m, 0:N], lb, lap, rb, rap, True, True)

                def stage_b(t):
                    ji, jb, bi, blk, first, lastb = items[t]
                    N, m = jb["N"], blk["m"]
                    bk, E, Lt = Zb[t % 3], Eb[t % 3], Lb[t % 4]
                    sy.do("act", lambda e: e.activation(out=E[0:m, 0:N], in_=bk[0:m, 0:N], func=AF.Exp, scale=scale), R=[bk], W=[E])
                    if blk["mulb"] is not None:
                        sy.do("pool", lambda e: e.tensor_tensor(out=E[0:m, 0:N], in0=E[0:m, 0:N], in1=blk["mulap"], op=ALU.mult),
                              R=[blk["mulb"]], W=[E])
                    sy.do("act", lambda e: e.activation(out=Lt[0:m, 0:N], in_=E[0:m, 0:N], func=AF.Ln, bias=1.0), R=[E], W=[Lt])

                def stage_c(t):
                    ji, jb, bi, blk, first, lastb = items[t]
                    N, m = jb["N"], blk["m"]
                    B, Lt = Bb[ji % 2], Lb[t % 4]
                    mm(B, B[:, 0:N], trib, trib[0:m, 0, :], Lt, Lt[0:m, 0:N], first, True, sgc=True)

                def stage_c2(t):
                    ji, jb, bi, blk, first, lastb = items[t]
                    N, m = jb["N"], blk["m"]
                    B, Lt = Bb[ji % 2], Lb[t % 4]
                    if not lastb:
                        mm(B, B[:, 0:N], trib, trib[0:m, 1, :], Lt, Lt[0:m, 0:N], False, True, sgc=True)

                def stage_d(t):
                    ji, jb, bi, blk, first, lastb = items[t]
                    N, m = jb["N"], blk["m"]
                    B, X, E, A = Bb[ji % 2], Xb[t % 2], Eb[t % 3], Ab[t % NA_]
                    sy.do("act", lambda e: e.activation(out=X[0:m, 0:N], in_=B[0:m, 0:N], func=AF.Exp), R=[B], W=[X])
                    sy.do("dve", lambda e: e.tensor_tensor(out=A[0:m, 0:N], in0=E[0:m, 0:N], in1=X[0:m, 0:N], op=ALU.mult), R=[E, X], W=[A])

                def stage_f(t):
                    ji, jb, bi, blk, first, lastb = items[t]
                    N, m, ed = jb["N"], blk["m"], jb["e"]
                    A, O = Ab[t % NA_], Ob[ji % 2]
                    mm(O, O[0:ed, 0:N], blk["vb"], blk["vap"], A, A[0:m, 0:N], first, lastb)
                    if lastb:
                        sy.do("dve", lambda e: e.tensor_copy(out=jb["dstap"], in_=O[0:ed, 0:N]), R=[O], W=[jb["dstb"]])
                        if jb.get("after"):
                            jb["after"]()

                T = len(items)

                def issue_a(t):
                    if items[t][2] == 0 and items[t][1].get("before"):
                        items[t][1]["before"]()
                    stage_a(t)

                if T:
                    issue_a(0)
                for t in range(T + 3):
                    if t + 1 < T:
                        issue_a(t + 1)
                    if t < T:
                        stage_b(t)
                    if 2 <= t <= T + 1:
                        stage_c2(t - 2)
                    if 1 <= t <= T:
                        stage_c(t - 1)
                        stage_d(t - 1)
                    if 3 <= t:
                        stage_f(t - 3)

            with ExitStack() as ph:
                EB = sb(ph, "EB", [128, 4, 8, 512], BF16)
                with ExitStack() as ph2:
                    rel = sb(ph2, "rel", [1, 768], F32)
                    ext = sb(ph2, "ext", [1, 4, 1536], F32)
                    valb = sb(ph2, "valb", [128, 8, 512], F32)
                    ebr = [sb(ph2, "ebr%d" % i, [128, 512], F32) for i in range(2)]
                    ebrb = [sb(ph2, "ebrb%d" % i, [128, 512], BF16) for i in range(2)]
                    sy.dma("sp", rel[:], a_rel[l:l + 1, :], W=[rel])
                    sy.dma("sp", valb[:], c_val, W=[valb])
                    for h in range(4):
                        sy.do("dve", lambda e, h=h: e.tensor_copy(out=ext[0:1, h, 0:448], in_=rel[0:1, h * 192:h * 192 + 1].to_broadcast([1, 448])), R=[rel], W=[ext])
                        sy.do("dve", lambda e, h=h: e.tensor_copy(out=ext[0:1, h, 448:640], in_=rel[0:1, h * 192:(h + 1) * 192]), R=[rel], W=[ext])
                        sy.do("dve", lambda e, h=h: e.tensor_copy(out=ext[0:1, h, 640:1536], in_=rel[0:1, h * 192 + 191:h * 192 + 192].to_broadcast([1, 896])), R=[rel], W=[ext])
                    sy.do("act", lambda e: e.activation(out=ext[:], in_=ext[:], func=AF.Exp), R=[ext], W=[ext])
                    sy.dma("sp", EXT.rearrange("(o h) n -> o h n", o=1), ext[:], R=[ext])
                    sy.barrier()
                    k = 0
                    for h in range(4):
                        for i in range(8):
                            eb = ebr[k % 2]; ebb = ebrb[k % 2]; bk = fb[k % 2]; k += 1
                            src = bass.AP(EXT.tensor, h * 1536 + 896 - 128 * i, [[1, 128], [1, 512]])
                            sy.dma("sp", eb[:], src, W=[eb])
                            sy.do("pool", lambda e, eb=eb, ebb=ebb: e.tensor_copy(out=ebb[:], in_=eb[:]), R=[eb], W=[ebb])
                            mm(bk, bk[:, :], trib, trib[:, 3, :], ebb, ebb[:], True, True)
                            sy.do("dve", lambda e, h=h, i=i, bk=bk: e.tensor_tensor(out=EB[:, h, i, :], in0=bk[:, :], in1=valb[:, i, :], op=ALU.mult),
                                  R=[bk, valb], W=[EB])
                    sy.do("dve", lambda e: e.tensor_copy(out=EBs[:], in_=EB[:, :, 0:5, 0:64]), R=[EB], W=[EBs])
                    sy.barrier()
                kaT = sb(ph, "kaT", [128, 2, S], BF16)
                vaS = sb(ph, "vaS", [128, NKB, 256], BF16)
                sy.dma("sp", kaT[:], KA[:, 0:S].rearrange("(m p) n -> p m n", p=128), W=[kaT])
                sy.dma("sp", vaS[:], VA[0:S, :].rearrange("(t p) c -> p t c", p=128), W=[vaS])
                qa = [sb(ph, "qa%d" % i, [128, 2, 512], BF16) for i in range(2)]
                ost = [sb(ph, "osta%d" % i, [64, 4, 512], F32) for i in range(2)]
                jobs = []
                for g in range(NG):
                    pb = g % 2
                    for h in range(4):
                        hp, bs = h // 2, (h % 2) * 64
                        blocks = []
                        for i in range(8):
                            kb = 4 * g - 4 + i
                            if kb < 0:
                                continue
                            blocks.append(dict(mms=[(kaT, kaT[bs:bs + 64, hp, kb * 128:(kb + 1) * 128], qa[pb], qa[pb][bs:bs + 64, hp, :])],
                                               m=128, vb=vaS, vap=vaS[:, kb, h * 64:(h + 1) * 64], mulb=EB, mulap=EB[:, h, i, :]))
                        jb = dict(N=512, e=64, blocks=blocks, dstb=ost[pb], dstap=ost[pb][:, h, :])
                        if h == 0:
                            jb["before"] = (lambda g=g, pb=pb: sy.dma("sp", qa[pb][:], QA[:, g * 512:(g + 1) * 512].rearrange("(m p) n -> p m n", p=128), W=[qa[pb]]))
                        if h == 3:
                            jb["after"] = (lambda g=g, pb=pb: sy.dma("sp", OCAT[0:256, g * 512:(g + 1) * 512].rearrange("(h p) n -> p h n", p=64), ost[pb][:], R=[ost[pb]]))
                        jobs.append(jb)
                attn_soft(soft_res(ph, "a"), jobs, 0.125)
                sy.barrier()
                ckpt(4)

            with ExitStack() as ph:
                knT = sb(ph, "knT", [128, 4, S], BF16)
                krS = sb(ph, "krS", [64, S], BF16)
                vmS = sb(ph, "vmS", [128, NKB, 512], BF16)
                mml = sb(ph, "mml", [128, 4, 512], BF16)
                with ExitStack() as ph2:
                    m32 = sb(ph2, "m32", [128, 4, 512], F32)
                    sy.dma("sp", m32[:], c_mml, W=[m32])
                    sy.do("dve", lambda e: e.tensor_copy(out=mml[:], in_=m32[:]), R=[m32], W=[mml])
                    sy.barrier()
                for h in range(4):
                    sy.dma("sp", knT[:, h, :], KN[h * 128:(h + 1) * 128, 0:S], W=[knT])
                sy.dma("sp", krS[:], KR[:, 0:S], W=[krS])
                for q4 in range(0, NKB, 8):
                    sy.dma("sp", vmS[:, q4:q4 + 8, :], VM[q4 * 128:(q4 + 8) * 128, :].rearrange("(t p) c -> p t c", p=128), W=[vmS])
                qn = [sb(ph, "qn%d" % i, [128, 4, 512], BF16) for i in range(2)]
                qp = [sb(ph, "qp%d" % i, [64, 4, 512], BF16) for i in range(2)]
                ost = [sb(ph, "ostm0", [128, 4, 512], F32)] * 2
                jobs = []
                for g in range(NG):
                    pb = g % 2
                    for h in range(4):
                        blocks = []
                        for kb in range(4 * g + 4):
                            r = kb - 4 * g
                            blocks.append(dict(mms=[(knT, knT[:, h, kb * 128:(kb + 1) * 128], qn[pb], qn[pb][:, h, :]),
                                                    (krS, krS[:, kb * 128:(kb + 1) * 128], qp[pb], qp[pb][:, h, :])],
                                               m=128, vb=vmS, vap=vmS[:, kb, h * 128:(h + 1) * 128],
                                               mulb=(mml if r >= 0 else None), mulap=(mml[:, r, :] if r >= 0 else None)))
                        jb = dict(N=512, e=128, blocks=blocks, dstb=ost[pb], dstap=ost[pb][:, h, :])
                        if h == 0:
                            def bf(g=g, pb=pb):
                                sy.dma("sp", qn[pb][:], QN[:, g * 512:(g + 1) * 512].rearrange("(h p) n -> p h n", p=128), W=[qn[pb]])
                                sy.dma("sp", qp[pb][:], QP[:, g * 512:(g + 1) * 512].rearrange("(h p) n -> p h n", p=64), W=[qp[pb]])
                            jb["before"] = bf
                        if h == 3:
                            jb["after"] = (lambda g=g, pb=pb: sy.dma("sp", OCAT[256:768, g * 512:(g + 1) * 512].rearrange("(h p) n -> p h n", p=128), ost[pb][:], R=[ost[pb]]))
                        jobs.append(jb)
                attn_soft(soft_res(ph, "m"), jobs, 192.0 ** -0.5)
                sy.barrier()
                ckpt(5)

            with ExitStack() as ph:
                kcT = sb(ph, "kcT", [128, 2, S], BF16)
                vcS = sb(ph, "vcS", [128, NKB, 256], BF16)
                msb = sb(ph, "msb", [128, 4, 512], F32)
                sy.dma("sp", msb[:], c_msb, W=[msb])
                sy.dma("sp", kcT[:], KC[:, 0:S].rearrange("(m p) n -> p m n", p=128), W=[kcT])
                sy.dma("sp", vcS[:], VC[0:S, :].rearrange("(t p) c -> p t c", p=128), W=[vcS])
                qc = [sb(ph, "qc%d" % i, [128, 2, 512], BF16) for i in range(2)]
                ost = [sb(ph, "osts%d" % i, [64, 4, 512], F32) for i in range(2)]
                jobs = []
                for g in range(NG):
                    pb = g % 2
                    for h in range(4):
                        hp, bs = h // 2, (h % 2) * 64
                        blocks = []
                        for kb in range(4 * g + 3, -1, -1):
                            r = kb - 4 * g
                            blocks.append(dict(mm=(kcT, kcT[bs:bs + 64, hp, kb * 128:(kb + 1) * 128], qc[pb], qc[pb][bs:bs + 64, hp, :]),
                                               m=128, vb=vcS, vap=vcS[:, kb, h * 64:(h + 1) * 64],
                                               mulb=(msb if r >= 0 else None), mulap=(msb[:, r, :] if r >= 0 else None)))
                        jb = dict(N=512, e=64, blocks=blocks, dstb=ost[pb], dstap=ost[pb][:, h, :])
                        if h == 0:
                            jb["before"] = (lambda g=g, pb=pb: sy.dma("sp", qc[pb][:], QC[:, g * 512:(g + 1) * 512].rearrange("(m p) n -> p m n", p=128), W=[qc[pb]]))
                        if h == 3:
                            jb["after"] = (lambda g=g, pb=pb: sy.dma("sp", OCAT[768:1024, g * 512:(g + 1) * 512].rearrange("(h p) n -> p h n", p=64), ost[pb][:], R=[ost[pb]]))
                        jobs.append(jb)
                attn_sb(sb_res(ph, "s"), jobs, 0.125)
                sy.barrier()
                ckpt(6)

            with ExitStack() as ph:
                NQ = NSQ * TS
                KL = PAST + TS
                qaS = sb(ph, "qaS", [128, 2, NQ], BF16); qcS = sb(ph, "qcS", [128, 2, NQ], BF16)
                qnS = sb(ph, "qnS", [128, 4, NQ], BF16); qpS = sb(ph, "qpS", [64, 4, NQ], BF16)
                sy.dma("sp", qaS[:], QA[:, S:NT].rearrange("(m p) n -> p m n", p=128), W=[qaS])
                sy.dma("sp", qcS[:], QC[:, S:NT].rearrange("(m p) n -> p m n", p=128), W=[qcS])
                sy.dma("sp", qnS[:], QN[:, S:NT].rearrange("(h p) n -> p h n", p=128), W=[qnS])
                sy.dma("sp", qpS[:], QP[:, S:NT].rearrange("(h p) n -> p h n", p=64), W=[qpS])
                msb = sb(ph, "msbS", [64, 64], F32)
                sy.dma("sp", msb[:], c_msb[0:64, 0, 0:64], W=[msb])
                osA = sb(ph, "osA", [64, 4, NQ], F32); osM = sb(ph, "osM", [128, 4, NQ], F32); osS = sb(ph, "osS", [64, 4, NQ], F32)
                kaS = [sb(ph, "kaS%d" % i, [128, 2, 576], BF16) for i in range(2)]
                vaS = [sb(ph, "vaSs%d" % i, [128, 5, 256], BF16) for i in range(2)]
                knS = [sb(ph, "knS%d" % i, [128, 4, KL], BF16) for i in range(2)]
                krS = [sb(ph, "krSs%d" % i, [64, KL], BF16) for i in range(2)]
                vmS = [sb(ph, "vmSs%d" % i, [128, NPB + 1, 512], BF16) for i in range(2)]
                kcS = [sb(ph, "kcS%d" % i, [128, 2, KL], BF16) for i in range(2)]
                vcS = [sb(ph, "vcSs%d" % i, [128, NPB + 1, 256], BF16) for i in range(2)]
                ja, jm, js = [], [], []
                for b in range(NSQ):
                    pb = b % 2
                    c0 = S + b * TS
                    qs = slice(b * TS, (b + 1) * TS)

                    def ld(b=b, pb=pb, c0=c0):
                        sy.dma("sp", kaS[pb][:, :, 0:512], KAs[b].rearrange("(m p) n -> p m n", p=128), W=[kaS[pb]])
                        sy.dma("sp", kaS[pb][:, :, 512:576], KA[:, c0:c0 + TS].rearrange("(m p) n -> p m n", p=128), W=[kaS[pb]])
                        sy.dma("sp", vaS[pb][:, 0:4, :], VAs[b].rearrange("(t p) c -> p t c", p=128), W=[vaS[pb]])
                        sy.dma("sp", vaS[pb][0:64, 4, :], VA[c0:c0 + TS, :], W=[vaS[pb]])
                        sy.dma("sp", knS[pb][:, :, 0:PAST], KNs[b].rearrange("(h p) n -> p h n", p=128), W=[knS[pb]])
                        sy.dma("sp", knS[pb][:, :, PAST:KL], KN[:, c0:c0 + TS].rearrange("(h p) n -> p h n", p=128), W=[knS[pb]])
                        sy.dma("sp", krS[pb][:, 0:PAST], KRs[b], W=[krS[pb]])
                        sy.dma("sp", krS[pb][:, PAST:KL], KR[:, c0:c0 + TS], W=[krS[pb]])
                        sy.dma("sp", vmS[pb][:, 0:NPB, :], VMs[b].rearrange("(t p) c -> p t c", p=128), W=[vmS[pb]])
                        sy.dma("sp", vmS[pb][0:64, NPB, :], VM[c0:c0 + TS, :], W=[vmS[pb]])
                        sy.dma("sp", kcS[pb][:, :, 0:PAST], KCs[b].rearrange("(m p) n -> p m n", p=128), W=[kcS[pb]])
                        sy.dma("sp", kcS[pb][:, :, PAST:KL], KC[:, c0:c0 + TS].rearrange("(m p) n -> p m n", p=128), W=[kcS[pb]])
                        sy.dma("sp", vcS[pb][:, 0:NPB, :], VCs[b].rearrange("(t p) c -> p t c", p=128), W=[vcS[pb]])
                        sy.dma("sp", vcS[pb][0:64, NPB, :], VC[c0:c0 + TS, :], W=[vcS[pb]])
                    for h in range(4):
                        hp, bs = h // 2, (h % 2) * 64
                        blocks = []
                        for i in range(5):
                            m = 128 if i < 4 else 64
                            blocks.append(dict(mms=[(kaS[pb], kaS[pb][bs:bs + 64, hp, i * 128:i * 128 + m], qaS, qaS[bs:bs + 64, hp, qs])],
                                               m=m, vb=vaS[pb], vap=vaS[pb][0:m, i, h * 64:(h + 1) * 64], mulb=EBs, mulap=EBs[0:m, h, i, :]))
                        jb = dict(N=TS, e=64, blocks=blocks, dstb=osA, dstap=osA[:, h, qs])
                        if h == 0:
                            jb["before"] = ld
                        ja.append(jb)
                        blocks = []
                        for kb in range(NPB + 1):
                            m = 128 if kb < NPB else 64
                            blocks.append(dict(mms=[(knS[pb], knS[pb][:, h, kb * 128:kb * 128 + m], qnS, qnS[:, h, qs]),
                                                    (krS[pb], krS[pb][:, kb * 128:kb * 128 + m], qpS, qpS[:, h, qs])],
                                               m=m, vb=vmS[pb], vap=vmS[pb][0:m, kb, h * 128:(h + 1) * 128], mulb=None, mulap=None))
                        jm.append(dict(N=TS, e=128, blocks=blocks, dstb=osM, dstap=osM[:, h, qs]))
                        blocks = []
                        for kb in range(NPB, -1, -1):
                            m = 128 if kb < NPB else 64
                            blocks.append(dict(mm=(kcS[pb], kcS[pb][bs:bs + 64, hp, kb * 128:kb * 128 + m], qcS, qcS[bs:bs + 64, hp, qs]),
                                               m=m, vb=vcS[pb], vap=vcS[pb][0:m, kb, h * 64:(h + 1) * 64],
                                               mulb=(msb if kb == NPB else None), mulap=(msb[:, :] if kb == NPB else None)))
                        js.append(dict(N=TS, e=64, blocks=blocks, dstb=osS, dstap=osS[:, h, qs]))
                rsoft, rsb = soft_res(ph, "x"), sb_res(ph, "x")
                for b in range(NSQ):
                    attn_soft(rsoft, ja[4 * b:4 * b + 4], 0.125)
                    attn_soft(rsoft, jm[4 * b:4 * b + 4], 192.0 ** -0.5)
                    attn_sb(rsb, js[4 * b:4 * b + 4], 0.125)
                sy.dma("sp", OCAT[0:256, S:NT].rearrange("(h p) n -> p h n", p=64), osA[:], R=[osA])
                sy.dma("sp", OCAT[256:768, S:NT].rearrange("(h p) n -> p h n", p=128), osM[:], R=[osM])
                sy.dma("sp", OCAT[768:1024, S:NT].rearrange("(h p) n -> p h n", p=64), osS[:], R=[osS])
                sy.barrier()
                ckpt(7)

            with ExitStack() as ph:
                wout = sb(ph, "wout", [128, 8, D], BF16)
                gcat = sb(ph, "gcat", [128, 8], F32)
                sy.dma("sp", gcat[:], g_cat[l].rearrange("(c p) -> p c", p=128), W=[gcat], allow_slow_non_contiguous=True)
                with ExitStack() as ph2:
                    load_w(ph2, wout, [wout[:, c, :] for c in range(8)], [w_out[l, c * 128:(c + 1) * 128, :] for c in range(8)], D)
                    sy.barrier()
                oc = [sb(ph, "oc%d" % i, [128, 8, 512], F32) for i in range(2)]
                xg = [sb(ph, "xg2_%d" % i, [128, 4, D], F32) for i in range(2)]
                sq = sb(ph, "sq", [128, 8, 512], BF16)
                rs = [sb(ph, "rs%d" % i, [128, 512], F32) for i in range(3)]
                catT = [sb(ph, "catT%d" % i, [128, 8, 512], BF16) for i in range(2)]
                def ld2(r0, ntl, gi):
                    N = ntl * 128
                    OC, X = oc[gi % 2], xg[gi % 2]
                    sy.dma("sp", OC[:, :, 0:N], OCAT[:, r0:r0 + N].rearrange("(c p) n -> p c n", p=128), W=[OC])
                    sy.dma("sp", X[:, 0:ntl, :], xsrc[r0:r0 + N, :].rearrange("(t p) d -> p t d", p=128), W=[X])

                ld2(*groups[0])
                for gidx, (r0, ntl, gi) in enumerate(groups):
                    N = ntl * 128
                    pb = gi % 2
                    OC, X, CT = oc[pb], xg[pb], catT[pb]
                    if gidx + 1 < len(groups):
                        ld2(*groups[gidx + 1])
                    sy.do("act", lambda e: e.activation(out=sq[:, :, 0:N], in_=OC[:, :, 0:N], func=AF.Square), R=[OC], W=[sq])
                    for mi, (a0, a1, Wd) in enumerate(((0, 2, 256), (2, 6, 512), (6, 8, 256))):
                        bk = fb[mi]
                        for c in range(a0, a1):
                            mm(bk, bk[:, 0:N], onesb, onesb[:, :], sq, sq[:, c, 0:N], c == a0, c == a1 - 1)
                        R_ = rs[mi]
                        sy.do("dve", lambda e, R_=R_, bk=bk, Wd=Wd: e.tensor_scalar(out=R_[:, 0:N], in0=bk[:, 0:N], scalar1=1.0 / Wd, scalar2=EPS,
                                                                                   op0=ALU.mult, op1=ALU.add), R=[bk], W=[R_])
                        sy.do("act", lambda e, R_=R_: e.activation(out=R_[:, 0:N], in_=R_[:, 0:N], func=AF.Sqrt), R=[R_], W=[R_])
                        sy.do("dve", lambda e, R_=R_: e.reciprocal(out=R_[:, 0:N], in_=R_[:, 0:N]), R=[R_], W=[R_])
                        for c in range(a0, a1):
                            sy.do("dve", lambda e, c=c, R_=R_: e.scalar_tensor_tensor(out=CT[:, c, 0:N], in0=OC[:, c, 0:N], scalar=gcat[:, c:c + 1], in1=R_[:, 0:N],
                                                                                                  op0=ALU.mult, op1=ALU.mult), R=[OC, gcat, R_], W=[CT])
                    for t in range(ntl):
                        for hf in range(2):
                            bk = fb[3 + (2 * t + hf) % 4]
                            for c in range(8):
                                mm(bk, bk[:, :], CT, CT[:, c, t * 128:(t + 1) * 128], wout, wout[:, c, hf * 512:(hf + 1) * 512], c == 0, c == 7)
                            sy.do("dve", lambda e, t=t, hf=hf, bk=bk: e.tensor_tensor(out=X[:, t, hf * 512:(hf + 1) * 512], in0=X[:, t, hf * 512:(hf + 1) * 512], in1=bk[:, :], op=ALU.add),
                                  R=[bk], W=[X])
                    sy.dma("sp", XR[r0:r0 + N, :].rearrange("(t p) d -> p t d", p=128), X[:, 0:ntl, :], R=[X])
                sy.barrier()
                ckpt(8)

            with ExitStack() as ph:
                wup = sb(ph, "wup", [128, 8, DFF], BF16)
                wdn = sb(ph, "wdn", [128, 32, D], BF16)
                gffnb = sb(ph, "gffnb", [128, D], F32)
                sy.dma("sp", gffnb[:], g_ffn[l].partition_broadcast(128), W=[gffnb])
                with ExitStack() as ph2:
                    load_w(ph2, wup, [wup[:, c, hf * 2048:(hf + 1) * 2048] for c in range(8) for hf in range(2)],
                           [w_up[l, c * 128:(c + 1) * 128, hf * 2048:(hf + 1) * 2048] for c in range(8) for hf in range(2)], 2048)
                    load_w(ph2, wdn, [wdn[:, 2 * j:2 * j + 2, :] for j in range(16)],
                           [w_down[l, j * 256:(j + 1) * 256, :].rearrange("(f p) d -> p f d", p=128) for j in range(16)], 2048)
                    sy.barrier()
                xg = [sb(ph, "xg3_%d" % i, [128, 2, D], F32) for i in range(2)]
                scr = sb(ph, "scr3", [128, D], F32)
                ssx = sb(ph, "ssx3", [128, 1], F32)
                hb = [sb(ph, "hb3_%d" % i, [128, D], BF16) for i in range(2)]
                hT = [sb(ph, "hT3_0", [128, 8, 256], BF16)] * 2
                rb = [sb(ph, "rb%d" % i, [128, 256], F32) for i in range(2)]
                uT = [sb(ph, "uT0", [128, 32, 256], BF16)] * 2
                yb = [sb(ph, "yb0", [128, 2, D], F32)] * 2 if last else None
                def ld3(gi):
                    X = xg[gi % 2]
                    sy.dma("sp", X[:], XR[gi * 256:gi * 256 + 256, :].rearrange("(t p) d -> p t d", p=128), W=[X])

                ld3(0)
                for gi in range(NT // 256):
                    r0 = gi * 256
                    pb = gi % 2
                    X, HT, UT = xg[pb], hT[pb], uT[pb]
                    if gi + 1 < NT // 256:
                        ld3(gi + 1)
                    for t in range(2):
                        H = hb[t % 2]
                        rstd_rows(X, X[:, t, :], D, scr, ssx)
                        sy.do("dve", lambda e, t=t, H=H: e.scalar_tensor_tensor(out=H[:], in0=X[:, t, :], scalar=ssx[:, 0:1], in1=gffnb[:],
                                                                               op0=ALU.mult, op1=ALU.mult), R=[X, ssx, gffnb], W=[H])
                        for c in range(8):
                            sy.do("pe", lambda e, c=c, H=H: e.transpose(out=tb[:, c * 128:(c + 1) * 128], in_=H[:, c * 128:(c + 1) * 128], identity=identb[:]),
                                  R=[H, identb], W=[tb])
                        sy.do("act", lambda e, t=t: e.activation(out=HT[:, :, t * 128:(t + 1) * 128], in_=tb[:, :].rearrange("p (c n) -> p c n", c=8), func=AF.Copy),
                              R=[tb], W=[HT])
                    for fc in range(32):
                        bk = fb[fc % 3]
                        for c in range(8):
                            mm(bk, bk[:, 0:256], wup, wup[:, c, fc * 128:(fc + 1) * 128], HT, HT[:, c, :], c == 0, c == 7)
                        R_ = rb[fc % 2]
                        sy.do("act", lambda e, bk=bk, R_=R_: e.activation(out=R_[:], in_=bk[:, 0:256], func=AF.Relu), R=[bk], W=[R_])
                        sy.do("pool", lambda e, fc=fc, R_=R_: e.tensor_tensor(out=UT[:, fc, :], in0=R_[:], in1=R_[:], op=ALU.mult), R=[R_], W=[UT])
                    for t in range(2):
                        for hf in range(2):
                            bk = fb[3 + (2 * t + hf) % 4]
                            for fc in range(32):
                                mm(bk, bk[:, :], UT, UT[:, fc, t * 128:(t + 1) * 128], wdn, wdn[:, fc, hf * 512:(hf + 1) * 512], fc == 0, fc == 31)
                            sy.do("dve", lambda e, t=t, hf=hf, bk=bk: e.tensor_tensor(out=X[:, t, hf * 512:(hf + 1) * 512], in0=X[:, t, hf * 512:(hf + 1) * 512], in1=bk[:, :], op=ALU.add),
                                  R=[bk], W=[X])
                    if not last:
                        sy.dma("sp", XR[r0:r0 + 256, :].rearrange("(t p) d -> p t d", p=128), X[:], R=[X])
                    else:
                        Y = yb[pb]
                        for t in range(2):
                            rstd_rows(X, X[:, t, :], D, scr, ssx)
                            sy.do("dve", lambda e, t=t: e.scalar_tensor_tensor(out=Y[:, t, :], in0=X[:, t, :], scalar=ssx[:, 0:1], in1=gfinb[:],
                                                                              op0=ALU.mult, op1=ALU.mult), R=[X, ssx, gfinb], W=[Y])
                        sy.dma("sp", y[r0:r0 + 256, :].rearrange("(t p) d -> p t d", p=128), Y[:], R=[Y])
                sy.barrier()
                ckpt(9)
        print("instructions emitted:", sy.ninstr, {k: v for k, v in sy.cnt.items() if not k.startswith("d")})
    return nc


def _consts(S, PAST):
    NT = S + NSQ * TS
    j = np.arange(128)[:, None]
    t = np.arange(512)[None, :]
    tri = np.zeros((128, 4, 128), np.float32)
    jj, kk = np.arange(128)[:, None], np.arange(128)[None, :]
    tri[:, 0, :] = -1.0 * (jj >= kk)
    tri[:, 1, :] = -1.0 * (jj < kk)
    tri[:, 2, :] = -1.0
    tri[:, 3, :] = np.eye(128, dtype=np.float32)[::-1]
    msb = np.stack([(j + 128 * r < t) for r in range(4)], 1).astype(np.float32)
    mml = np.stack([((128 * r + j) // 64 <= t // 64) for r in range(4)], 1).astype(np.float32)
    val = []
    for i in range(8):
        d = 8 + t // 64 - (2 * i + j // 64)
        val.append((d >= 0) & (d <= 8))
    val = np.stack(val, 1).astype(np.float32)
    pos = np.concatenate([np.arange(S), np.tile(PAST + np.arange(TS), NSQ)]).astype(np.float32)
    inv = (10000.0 ** (-np.arange(32, dtype=np.float32) / np.float32(32))).astype(np.float32)
    ang = (pos[:, None] * inv[None, :]).astype(np.float32)
    cos, sin = np.cos(ang).astype(np.float32), np.sin(ang).astype(np.float32)
    ropeT = np.concatenate([cos, sin], 1).astype(np.float32)
    cosF = np.concatenate([cos, cos], 1).T
    sinF = np.concatenate([-sin, sin], 1).T
    ropeF = np.ascontiguousarray(np.stack([cosF, sinF], 1)).astype(np.float32)
    return dict(c_ident=np.eye(128, dtype=np.float32), c_tri=tri, c_msb=np.ascontiguousarray(msb),
                c_mml=np.ascontiguousarray(mml), c_val=np.ascontiguousarray(val), c_ropeT=ropeT, c_ropeF=ropeF)


_CACHE = {}


def kernel(x_prompt, x_sample, cache_a_k, cache_a_v, cache_mla_ckv, cache_mla_krope, cache_sb_k, cache_sb_v,
           g_mix, w_in, g_cq, g_ckv, w_uq, w_ukv, a_rel_bias, g_out_a, g_out_mla, g_out_sb, w_out,
           g_ffn, w_up, w_down, g_final):
    f = lambda a: np.ascontiguousarray(np.asarray(a, dtype=np.float32))
    x_prompt, x_sample = f(x_prompt), f(x_sample)
    B, S, _ = x_prompt.shape
    L = w_in.shape[0]
    PAST = cache_mla_ckv.shape[2]
    ncore = B
    assert x_sample.shape[0] == NSQ * ncore and x_sample.shape[1] == TS
    key = (S, PAST, L)
    if key not in _CACHE:
        _CACHE[key] = build(S, PAST, L)
    nc = _CACHE[key]
    cst = _consts(S, PAST)
    shared = dict(g_mix=f(g_mix), w_in=f(w_in), g_cq=f(g_cq), g_ckv=f(g_ckv),
                  w_uq=f(w_uq).reshape(L, 256, 768), w_ukv=f(w_ukv).reshape(L, 256, 1024),
                  a_rel=f(a_rel_bias).reshape(L, 768),
                  g_cat=np.ascontiguousarray(np.concatenate([f(g_out_a), f(g_out_mla), f(g_out_sb)], axis=1)),
                  w_out=f(w_out), g_ffn=f(g_ffn), w_up=f(w_up), w_down=f(w_down), g_fin=f(g_final), **cst)
    cak, cav = f(cache_a_k), f(cache_a_v)
    cckv, ckr, csk, csv = f(cache_mla_ckv), f(cache_mla_krope), f(cache_sb_k), f(cache_sb_v)
    in_maps = []
    for c in range(ncore):
        sl = slice(NSQ * c, NSQ * c + NSQ)
        m = dict(shared)
        m["x"] = np.ascontiguousarray(np.concatenate([x_prompt[c], x_sample[sl].reshape(NSQ * TS, D)], axis=0))
        m["cak"] = np.ascontiguousarray(cak[:, sl].reshape(L, NSQ, 512, 256))
        m["cav"] = np.ascontiguousarray(cav[:, sl].reshape(L, NSQ, 512, 256))
        m["cckv"] = np.ascontiguousarray(cckv[:, sl])
        m["ckr"] = np.ascontiguousarray(ckr[:, sl])
        m["csk"] = np.ascontiguousarray(csk[:, sl].reshape(L, NSQ, PAST, 256))
        m["csv"] = np.ascontiguousarray(csv[:, sl].reshape(L, NSQ, PAST, 256))
        in_maps.append(m)
    res = run_bass_kernel_spmd(nc, in_maps, core_ids=list(range(ncore)))
    R = res.results
    cat = lambda k, ax: np.concatenate([np.asarray(r[k]) for r in R], axis=ax)
    yall = np.stack([np.asarray(r["y"]) for r in R], 0)
    y_prompt = np.ascontiguousarray(yall[:, :S])
    y_sample = np.ascontiguousarray(yall[:, S:].reshape(ncore * NSQ, TS, D))
    st1 = lambda k: np.stack([np.asarray(r[k]) for r in R], 1)
    p_a_k = st1("pak").reshape(L, B, 512, 4, 64)
    p_a_v = st1("pav").reshape(L, B, 512, 4, 64)
    p_ckv = st1("pckv")
    p_krope = st1("pkr")
    p_sb_k = st1("psk").reshape(L, B, S, 4, 64)
    p_sb_v = st1("psv").reshape(L, B, S, 4, 64)
    s_a_k = cat("sak", 1).reshape(L, ncore * NSQ, 512, 4, 64)
    s_a_v = cat("sav", 1).reshape(L, ncore * NSQ, 512, 4, 64)
    s_ckv = cat("sckv", 1)
    s_krope = cat("skr", 1)
    s_sb_k = cat("ssk", 1).reshape(L, ncore * NSQ, TS, 4, 64)
    s_sb_v = cat("ssv", 1).reshape(L, ncore * NSQ, TS, 4, 64)
    outs = (y_prompt, y_sample, p_a_k, p_a_v, p_ckv, p_krope, p_sb_k, p_sb_v,
            s_a_k, s_a_v, s_ckv, s_krope, s_sb_k, s_sb_v)
    return tuple(np.ascontiguousarray(o, dtype=np.float32) for o in outs)
```

```python
import os
import numpy as np
DBG = int(os.environ.get('MK_DBG', '0'))
STOPN = int(os.environ.get('MK_STOPN', '-1'))
MAXOUT = int(os.environ.get('MK_MAXOUT', '4'))


class _Stop(Exception):
    pass
from contextlib import ExitStack
import concourse.bass as bass
import concourse.mybir as mybir
from concourse.bass_utils import run_bass_kernel_spmd

F32 = mybir.dt.float32
BF16 = mybir.dt.bfloat16
AF = mybir.ActivationFunctionType
ALU = mybir.AluOpType
AX = mybir.AxisListType

D = 1024
DFF = 4096
NSQ = 4
TS = 64
EPS = 1e-6
INC = 2112


class Buf:
    __slots__ = ("t", "w", "r", "dk", "x")

    def __init__(self, t, x=False):
        self.t = t
        self.w = None
        self.r = {}
        self.dk = None
        self.x = x

    def __getitem__(self, k):
        return self.t[k]


class Sy:
    def __init__(self, nc, st, ndma=int(os.environ.get('MK_NDMA', '72'))):
        self.nc = nc
        self.eng = {"pe": nc.tensor, "act": nc.scalar, "dve": nc.vector, "pool": nc.gpsimd, "sp": nc.sync}
        self.sem, self.cnt = {}, {}
        self.seen = {e: {} for e in self.eng}
        for e in self.eng:
            self.sem[e] = st.enter_context(nc.semaphore("s_" + e))
            self.cnt[e] = 0
        self.dpool = []
        for i in range(ndma):
            k = "d%d" % i
            self.sem[k] = st.enter_context(nc.semaphore(k))
            self.cnt[k] = 0
            self.dpool.append(k)
        self.dnext = 0
        self.ninstr = 0
        self.fifo = []

    def _wait(self, e, waits):
        for k, v in waits:
            if k == "pe" and e == "pe":
                continue
            if self.seen[e].get(k, 0) >= v:
                continue
            self.eng[e].wait_ge(self.sem[k], v)
            self.seen[e][k] = v

    def _deps(self, R, W):
        waits = []
        for b in R:
            if b.w:
                waits.append(b.w)
            if b.x:
                waits.extend(b.r.items())
        for b in W:
            if b.w:
                waits.append(b.w)
            waits.extend(b.r.items())
        return waits

    def do(self, e, fn, R=(), W=()):
        if self.ninstr == STOPN:
            self.barrier()
            raise _Stop()
        self._wait(e, self._deps(R, W))
        ins = fn(self.eng[e])
        self.cnt[e] += 1
        ins.then_inc(self.sem[e], 1)
        v = self.cnt[e]
        for b in R:
            if b.x:
                b.w = (e, v)
                b.r = {}
            else:
                b.r[e] = v
        for b in W:
            b.w = (e, v)
            b.r = {}
        self.ninstr += 1
        return (e, v)

    def dma(self, q, out, in_, R=(), W=(), **kw):
        if self.ninstr == STOPN:
            self.barrier()
            raise _Stop()
        if len(self.fifo) >= MAXOUT:
            self._wait(q, [self.fifo.pop(0)])
        self._wait(q, self._deps(R, W))
        b0 = (list(W) + list(R))[0]
        if b0.dk is None:
            b0.dk = self.dpool[self.dnext % len(self.dpool)]
            self.dnext += 1
        k = b0.dk
        ins = self.eng[q].dma_start(out=out, in_=in_, **kw)
        self.cnt[k] += 16
        ins.then_inc(self.sem[k], 16)
        v = self.cnt[k]
        for b in R:
            b.r[k] = v
        for b in W:
            b.w = (k, v)
            b.r = {}
        self.ninstr += 1
        self.fifo.append((k, v))
        return (k, v)

    def barrier(self):
        allw = [(k, v) for k, v in self.cnt.items() if v > 0]
        for e in self.eng:
            self._wait(e, allw)
        self.dnext = 0


def build(S, PAST, L, stop=None):
    NT = S + NSQ * TS
    NG = S // 512
    NKB = S // 128
    NPB = PAST // 128
    nc = bass.Bass("TRN2", target_bir_lowering=False)

    def din(name, shape, dt=F32):
        return nc.dram_tensor(name, list(shape), dt, kind="ExternalInput").ap()

    def dout(name, shape):
        return nc.dram_tensor(name, list(shape), F32, kind="ExternalOutput").ap()

    def dscr(name, shape, dt=BF16):
        return nc.dram_tensor(name, list(shape), dt).ap()

    xin = din("x", [NT, D])
    cak = din("cak", [L, NSQ, 512, 256]); cav = din("cav", [L, NSQ, 512, 256])
    cckv = din("cckv", [L, NSQ, PAST, 256]); ckr = din("ckr", [L, NSQ, PAST, 64])
    csk = din("csk", [L, NSQ, PAST, 256]); csv = din("csv", [L, NSQ, PAST, 256])
    g_mix = din("g_mix", [L, D]); w_in = din("w_in", [L, D, INC])
    g_cq = din("g_cq", [L, 256]); g_ckv = din("g_ckv", [L, 256])
    w_uq = din("w_uq", [L, 256, 768]); w_ukv = din("w_ukv", [L, 256, 1024])
    a_rel = din("a_rel", [L, 768]); g_cat = din("g_cat", [L, D])
    w_out = din("w_out", [L, D, D]); g_ffn = din("g_ffn", [L, D])
    w_up = din("w_up", [L, D, DFF]); w_down = din("w_down", [L, DFF, D]); g_fin = din("g_fin", [D])
    c_ident = din("c_ident", [128, 128]); c_tri = din("c_tri", [128, 4, 128])
    c_msb = din("c_msb", [128, 4, 512]); c_mml = din("c_mml", [128, 4, 512]); c_val = din("c_val", [128, 8, 512])
    c_ropeT = din("c_ropeT", [NT, 64]); c_ropeF = din("c_ropeF", [64, 2, NT])

    y = dout("y", [NT, D])
    pak = dout("pak", [L, 512, 256]); pav = dout("pav", [L, 512, 256])
    pckv = dout("pckv", [L, S, 256]); pkr = dout("pkr", [L, S, 64])
    psk = dout("psk", [L, S, 256]); psv = dout("psv", [L, S, 256])
    sak = dout("sak", [L, NSQ, 512, 256]); sav = dout("sav", [L, NSQ, 512, 256])
    sckv = dout("sckv", [L, NSQ, TS, 256]); skr = dout("skr", [L, NSQ, TS, 64])
    ssk = dout("ssk", [L, NSQ, TS, 256]); ssv = dout("ssv", [L, NSQ, TS, 256])

    XR = dscr("XR", [NT, D], F32)
    QA = dscr("QA", [256, NT]); KA = dscr("KA", [256, NT]); QC = dscr("QC", [256, NT]); KC = dscr("KC", [256, NT])
    VA = dscr("VA", [NT, 256]); VC = dscr("VC", [NT, 256])
    QN = dscr("QN", [512, NT]); QP = dscr("QP", [256, NT]); KN = dscr("KN", [512, NT]); KR = dscr("KR", [64, NT])
    VM = dscr("VM", [NT, 512])
    KAs = dscr("KAs", [NSQ, 256, 512]); VAs = dscr("VAs", [NSQ, 512, 256])
    KNs = dscr("KNs", [NSQ, 512, PAST]); KRs = dscr("KRs", [NSQ, 64, PAST]); VMs = dscr("VMs", [NSQ, PAST, 512])
    KCs = dscr("KCs", [NSQ, 256, PAST]); VCs = dscr("VCs", [NSQ, PAST, 256])
    OCAT = dscr("OCAT", [D, NT], F32)
    EXT = dscr("EXT", [4, 1536], F32)

    with ExitStack() as st:
        st.push(lambda et, ev, tb_: et is _Stop)
        sy = Sy(nc, st)

        uniq = [0]

        def sb(stk, name, shape, dt):
            uniq[0] += 1
            return Buf(stk.enter_context(nc.sbuf_tensor("%s_%d" % (name, uniq[0]), list(shape), dt)))

        fb = [Buf(st.enter_context(nc.psum_tensor("pf%d" % i, [128, 512], F32)), x=True) for i in range(7)]
        tb = Buf(st.enter_context(nc.psum_tensor("ptb", [128, 1024], BF16)), x=True)

        identb = sb(st, "identb", [128, 128], BF16)
        trib = sb(st, "trib", [128, 4, 128], BF16)
        onesb = sb(st, "onesb", [128, 128], BF16)
        ones1f = sb(st, "ones1f", [1, 128], F32)
        gfinb = sb(st, "gfinb", [128, D], F32)
        EBs = sb(st, "EBs", [128, 4, 5, 64], BF16)
        with ExitStack() as ph:
            t1 = sb(ph, "c_t1", [128, 128], F32)
            t2 = sb(ph, "c_t2", [128, 4, 128], F32)
            sy.dma("sp", t1[:], c_ident, W=[t1])
            sy.dma("sp", t2[:], c_tri, W=[t2])
            sy.dma("sp", gfinb[:], g_fin.partition_broadcast(128), W=[gfinb])
            sy.do("dve", lambda e: e.tensor_copy(out=identb[:], in_=t1[:]), R=[t1], W=[identb])
            sy.do("dve", lambda e: e.tensor_copy(out=trib[:], in_=t2[:]), R=[t2], W=[trib])
            sy.do("dve", lambda e: e.memset(onesb[:], 1.0), W=[onesb])
            sy.do("dve", lambda e: e.memset(ones1f[:], 1.0), W=[ones1f])
            sy.barrier()

        ddb = Buf(None)
        def mm(out_b, out_ap, lhsT_b, lhsT, rhs_b, rhs, start, stop, sgc=False):
            return sy.do("pe", lambda e: e.matmul(out_ap, lhsT=lhsT, rhs=rhs, start=start, stop=stop, skip_group_check=sgc),
                         R=[lhsT_b, rhs_b], W=[out_b])

        cast_rr = [0]

        def load_w(ph, dst_b, views, srcs, width):
            stg = [sb(ph, "wst%d_%d" % (sy.ninstr, i), [128, width], F32) for i in range(2)]
            for i, (v, s_) in enumerate(zip(views, srcs)):
                sg = stg[i % 2]
                sy.dma("sp", sg[:], s_, W=[sg])
                eng = ("pool", "dve")[cast_rr[0] % 2]
                cast_rr[0] += 1
                sy.do(eng, lambda e, v=v, sg=sg: e.tensor_copy(out=v, in_=sg[:]), R=[sg], W=[dst_b])

        def rstd_rows(src_b, src_ap, width, scr_b, ss_b, sq_eng="act"):
            if sq_eng == "act":
                sy.do("act", lambda e: e.activation(out=scr_b[:, 0:width], in_=src_ap, func=AF.Square), R=[src_b], W=[scr_b])
            else:
                sy.do("dve", lambda e: e.tensor_tensor(out=scr_b[:, 0:width], in0=src_ap, in1=src_ap, op=ALU.mult), R=[src_b], W=[scr_b])
            sy.do("dve", lambda e: e.reduce_sum(out=ss_b[:, 0:1], in_=scr_b[:, 0:width], axis=AX.X), R=[scr_b], W=[ss_b])
            sy.do("dve", lambda e: e.tensor_scalar(out=ss_b[:, 0:1], in0=ss_b[:, 0:1], scalar1=1.0 / width, scalar2=EPS,
                                                   op0=ALU.mult, op1=ALU.add), R=[ss_b], W=[ss_b])
            if not (DBG & 4096):
                sy.do("act", lambda e: e.activation(out=ss_b[:, 0:1], in_=ss_b[:, 0:1], func=AF.Sqrt), R=[ss_b], W=[ss_b])
            sy.do("dve", lambda e: e.reciprocal(out=ss_b[:, 0:1], in_=ss_b[:, 0:1]), R=[ss_b], W=[ss_b])

        def ckpt(k):
            if stop == k:
                sy.barrier()
                raise _Stop()

        for l in range(L):
            xsrc = xin if l == 0 else XR
            last = (l == L - 1)
            ckpt(0)

            with ExitStack() as ph:
                win = sb(ph, "win", [128, 8, INC], BF16)
                wuq = sb(ph, "wuq", [128, 2, 768], BF16)
                wuqs = sb(ph, "wuqs", [128, 2, 256], BF16)
                wkn = sb(ph, "wkn", [128, 2, 512], BF16)
                wv = sb(ph, "wv", [128, 2, 512], BF16)
                gmixb = sb(ph, "gmixb", [128, D], F32)
                gcqb = sb(ph, "gcqb", [128, 256], F32)
                gckvb = sb(ph, "gckvb", [128, 256], F32)
                sy.dma("sp", gmixb[:], g_mix[l].partition_broadcast(128), W=[gmixb])
                sy.dma("sp", gcqb[:], g_cq[l].partition_broadcast(128), W=[gcqb])
                sy.dma("sp", gckvb[:], g_ckv[l].partition_broadcast(128), W=[gckvb])
                with ExitStack() as ph2:
                    load_w(ph2, win, [win[:, c, :] for c in range(8)], [w_in[l, c * 128:(c + 1) * 128, :] for c in range(8)], INC)
                    wq32 = sb(ph2, "wq32", [128, 2, 768], F32)
                    wkv32 = sb(ph2, "wkv32", [128, 2, 1024], F32)
                    sy.dma("sp", wq32[:], w_uq[l].rearrange("(k p) e -> p k e", p=128), W=[wq32])
                    sy.dma("sp", wkv32[:], w_ukv[l].rearrange("(k p) e -> p k e", p=128), W=[wkv32])
                    sy.do("dve", lambda e: e.tensor_copy(out=wuq[:], in_=wq32[:]), R=[wq32], W=[wuq])
                    for k in range(2):
                        q4 = wq32[:, k, :].rearrange("p (h e) -> p h e", h=4)
                        d4 = wuqs[:, k, :].rearrange("p (h e) -> p h e", h=4)
                        sy.do("dve", lambda e, q4=q4, d4=d4: e.tensor_copy(out=d4[:, :, 0:32], in_=q4[:, :, 160:192]), R=[wq32], W=[wuqs])
                        sy.do("dve", lambda e, q4=q4, d4=d4: e.tensor_copy(out=d4[:, :, 32:64], in_=q4[:, :, 128:160]), R=[wq32], W=[wuqs])
                        k4 = wkv32[:, k, :].rearrange("p (h e) -> p h e", h=4)
                        sy.do("dve", lambda e, k4=k4, k=k: e.tensor_copy(out=wkn[:, k, :].rearrange("p (h e) -> p h e", h=4), in_=k4[:, :, 0:128]), R=[wkv32], W=[wkn])
                        sy.do("dve", lambda e, k4=k4, k=k: e.tensor_copy(out=wv[:, k, :].rearrange("p (h e) -> p h e", h=4), in_=k4[:, :, 128:256]), R=[wkv32], W=[wv])
                    sy.barrier()

                ckpt(1)
                xt = [sb(ph, "xt%d" % i, [128, D], F32) for i in range(3)]
                rt = [sb(ph, "rt%d" % i, [128, 4, 64], F32) for i in range(2)]
                rf = [sb(ph, "rf0", [64, 2, 512], F32)] * 2
                scr = sb(ph, "scr", [128, D], F32)
                ssx = sb(ph, "ssx", [128, 1], F32)
                ss2 = sb(ph, "ss2", [128, 1], F32)
                ss3 = sb(ph, "ss3", [128, 1], F32)
                hb = [sb(ph, "hb%d" % i, [128, D], BF16) for i in range(2)]
                hT = [sb(ph, "hT0", [128, 8, 512], BF16)] * 2
                fo = [sb(ph, "fo0", [128, 8, 512], BF16)] * 2
                va_b = [sb(ph, "va_b%d" % i, [128, 4, 256], BF16) for i in range(2)]
                vaf = sb(ph, "vaf", [128, 4, 256], F32)
                kaf = sb(ph, "kaf", [128, 4, 256], F32)
                cqn_b = sb(ph, "cqn_b", [128, 256], BF16)
                ckvn_b = sb(ph, "ckvn_b", [128, 256], BF16)
                cqT = [sb(ph, "cqT%d" % i, [128, 2, 512], BF16) for i in range(2)]
                ckvT = [sb(ph, "ckvT%d" % i, [128, 2, 512], BF16) for i in range(2)]
                ckv_f = [sb(ph, "ckv_f%d" % i, [128, 4, 256], F32) for i in range(2)]
                krr = sb(ph, "krr", [128, 64], F32)
                krt = [sb(ph, "krt%d" % i, [128, 32], F32) for i in range(4)]
                kr_f = [sb(ph, "kr_f%d" % i, [128, 4, 64], F32) for i in range(2)]
                kr_b = sb(ph, "kr_b", [128, 64], BF16)
                krT = [sb(ph, "krT%d" % i, [64, 512], BF16) for i in range(2)]
                kcvc = [sb(ph, "kcvc%d" % i, [128, 4, 512], F32) for i in range(2)]
                vc_b = [sb(ph, "vc_b%d" % i, [128, 4, 256], BF16) for i in range(2)]
                qn_b = [sb(ph, "qn_b0", [128, 4, 512], BF16)] * 2
                qp_b = [sb(ph, "qp_b0", [64, 4, 512], BF16)] * 2
                kn_b = [sb(ph, "kn_b0", [128, 4, 512], BF16)] * 2
                vm_b = [sb(ph, "vm_b0", [128, 4, 512], BF16)] * 2
                qt1 = sb(ph, "qt1", [64, 512], F32)
                qt2 = sb(ph, "qt2", [64, 512], F32)
                c32 = [sb(ph, "c32_%d" % i, [128, 4, 256], F32) for i in range(2)]
                cb16 = [sb(ph, "cb16_%d" % i, [128, 4, 256], BF16) for i in range(2)]
                ckT = [sb(ph, "ckT%d" % i, [128, 2, 512], BF16) for i in range(2)]
                kr32 = sb(ph, "kr32", [128, 4, 64], F32)
                krb16 = sb(ph, "krb16", [128, 4, 64], BF16)
                krTc = [sb(ph, "krTc%d" % i, [64, 512], BF16) for i in range(2)]
                print('P1 sbuf remaining', nc.sbuf_bytes_remaining, 'vaf', vaf.t, 'kaf', kaf.t, 'krTc', krTc[1].t)
                ev_rr = [0]
                xti = [0]

                def evac(dst_b, dst_ap, src_b, src_ap):
                    eng = ("act", "dve")[ev_rr[0] % 2]
                    ev_rr[0] += 1
                    if eng == "act":
                        sy.do("act", lambda e: e.activation(out=dst_ap, in_=src_ap, func=AF.Copy), R=[src_b], W=[dst_b])
                    else:
                        sy.do("dve", lambda e: e.tensor_copy(out=dst_ap, in_=src_ap), R=[src_b], W=[dst_b])

                def kv_up(cT, N, ntl, knb, vmb):
                    for h in range(4):
                        bk = fb[4 + h % 2]
                        for k in range(2):
                            mm(bk, bk[:, 0:N], wkn, wkn[:, k, h * 128:(h + 1) * 128], cT, cT[:, k, 0:N], k == 0, k == 1)
                        evac(knb, knb[:, h, 0:N], bk, bk[:, 0:N])
                    for t in range(ntl):
                        bk = fb[4 + t % 2]
                        for k in range(2):
                            mm(bk, bk[:, 0:512], cT, cT[:, k, t * 128:(t + 1) * 128], wv, wv[:, k, :], k == 0, k == 1)
                        evac(vmb, vmb[:, t, :], bk, bk[:, 0:512])

                ci = 0
                for b in range(NSQ):
                    for (src, isk) in ((cak, True), (cav, False)):
                        cb = c32[ci % 2]; bb = cb16[ci % 2]; tt = ckT[ci % 2]; ci += 1
                        sy.dma("sp", cb[:], src[l, b].rearrange("(t p) c -> p t c", p=128), W=[cb])
                        sy.do(("pool", "dve")[ci % 2], lambda e, cb=cb, bb=bb: e.tensor_copy(out=bb[:], in_=cb[:]), R=[cb], W=[bb])
                        if isk:
                            for m in range(2):
                                for t in range(4):
                                    sy.do("pe", lambda e, m=m, t=t, bb=bb: e.transpose(out=tb[:, t * 128:(t + 1) * 128], in_=bb[:, t, m * 128:(m + 1) * 128], identity=identb[:]),
                                          R=[bb, identb], W=[tb])
                                evac(tt, tt[:, m, :], tb, tb[:, 0:512])
                            sy.dma("sp", KAs[b].rearrange("(m p) n -> p m n", p=128), tt[:], R=[tt])
                        else:
                            sy.dma("sp", VAs[b].rearrange("(t p) c -> p t c", p=128), bb[:], R=[bb])
                    for blk in range(PAST // 512):
                        r0 = blk * 512
                        cb = c32[ci % 2]; bb = cb16[ci % 2]; tt = ckT[ci % 2]; pb = ci % 2; ci += 1
                        sy.dma("sp", cb[:], cckv[l, b, r0:r0 + 512, :].rearrange("(t p) c -> p t c", p=128), W=[cb])
                        sy.do(("pool", "dve")[ci % 2], lambda e, cb=cb, bb=bb: e.tensor_copy(out=bb[:], in_=cb[:]), R=[cb], W=[bb])
                        for m in range(2):
                            for t in range(4):
                                sy.do("pe", lambda e, m=m, t=t, bb=bb: e.transpose(out=tb[:, t * 128:(t + 1) * 128], in_=bb[:, t, m * 128:(m + 1) * 128], identity=identb[:]),
                                      R=[bb, identb], W=[tb])
                            evac(tt, tt[:, m, :], tb, tb[:, 0:512])
                        kv_up(tt, 512, 4, kn_b[pb], vm_b[pb])
                        sy.dma("sp", KNs[b][:, r0:r0 + 512].rearrange("(h p) n -> p h n", p=128), kn_b[pb][:], R=[kn_b[pb]])
                        sy.dma("sp", VMs[b][r0:r0 + 512, :].rearrange("(t p) c -> p t c", p=128), vm_b[pb][:], R=[vm_b[pb]])
                        for (src, isk) in ((csk, True), (csv, False)):
                            cb = c32[ci % 2]; bb = cb16[ci % 2]; tt = ckT[ci % 2]; ci += 1
                            sy.dma("sp", cb[:], src[l, b, r0:r0 + 512, :].rearrange("(t p) c -> p t c", p=128), W=[cb])
                            sy.do(("pool", "dve")[ci % 2], lambda e, cb=cb, bb=bb: e.tensor_copy(out=bb[:], in_=cb[:]), R=[cb], W=[bb])
                            if isk:
                                for m in range(2):
                                    for t in range(4):
                                        sy.do("pe", lambda e, m=m, t=t, bb=bb: e.transpose(out=tb[:, t * 128:(t + 1) * 128], in_=bb[:, t, m * 128:(m + 1) * 128], identity=identb[:]),
                                              R=[bb, identb], W=[tb])
                                    evac(tt, tt[:, m, :], tb, tb[:, 0:512])
                                sy.dma("sp", KCs[b][:, r0:r0 + 512].rearrange("(m p) n -> p m n", p=128), tt[:], R=[tt])
                            else:
                                sy.dma("sp", VCs[b][r0:r0 + 512, :].rearrange("(t p) c -> p t c", p=128), bb[:], R=[bb])
                        kt = krTc[blk % 2]
                        sy.dma("sp", kr32[:], ckr[l, b, r0:r0 + 512, :].rearrange("(t p) c -> p t c", p=128), W=[kr32])
                        sy.do("pool", lambda e: e.tensor_copy(out=krb16[:], in_=kr32[:]), R=[kr32], W=[krb16])
                        for t in range(4):
                            sy.do("pe", lambda e, t=t: e.transpose(out=tb[0:64, t * 128:(t + 1) * 128], in_=krb16[:, t, :], identity=identb[:]),
                                  R=[krb16, identb], W=[tb])
                        evac(kt, kt[:, :], tb, tb[0:64, 0:512])
                        sy.dma("sp", KRs[b][:, r0:r0 + 512], kt[:], R=[kt])
                    sy.dma("sp", sak[l, b, 0:448, :], cak[l, b, 64:512, :], W=[ddb])
                    sy.dma("sp", sav[l, b, 0:448, :], cav[l, b, 64:512, :], W=[ddb])

                print('ninstr at P1 start', sy.ninstr)
                ckpt(2)
                groups = [(g * 512, 4, g) for g in range(NG)] + [(S, 2, NG)]
                xpre = {}

                def xload(r0_, t_):
                    X = xt[xti[0] % 3]; xti[0] += 1
                    sy.dma("sp", X[:], xsrc[r0_ + t_ * 128:r0_ + (t_ + 1) * 128, :], W=[X])
                    return X

                for gidx, (r0, ntl, gi) in enumerate(groups):
                    N = ntl * 128
                    pb = gi % 2
                    is_s = (gi == NG)
                    need_a = is_s or (gi == NG - 1)
                    RT, RF, HT = rt[pb], rf[pb], hT[pb]
                    sy.dma("sp", RT[:, 0:ntl, :], c_ropeT[r0:r0 + N, :].rearrange("(t p) d -> p t d", p=128), W=[RT])
                    sy.dma("sp", RF[:, :, 0:N], c_ropeF[:, :, r0:r0 + N], W=[RF])
                    for t in range(ntl):
                        H = hb[t % 2]
                        X = xpre.pop((gi, t), None)
                        if X is None:
                            X = xload(r0, t)
                        rstd_rows(X, X[:], D, scr, ssx)
                        sy.do("dve", lambda e, X=X, H=H: e.scalar_tensor_tensor(out=H[:], in0=X[:], scalar=ssx[:, 0:1], in1=gmixb[:],
                                                                               op0=ALU.mult, op1=ALU.mult), R=[X, ssx, gmixb], W=[H])
                        for c in range(8):
                            sy.do("pe", lambda e, c=c, H=H: e.transpose(out=tb[:, c * 128:(c + 1) * 128], in_=H[:, c * 128:(c + 1) * 128], identity=identb[:]),
                                  R=[H, identb], W=[tb])
                        evac(HT, HT[:, :, t * 128:(t + 1) * 128], tb, tb[:, :].rearrange("p (c n) -> p c n", c=8))
                        tm = [(fb[0], 512, 1024), (fb[1], 1024, 1344), (fb[2], 1600, 2112)]
                        if need_a and not (DBG & 1):
                            tm.append((fb[3], 256, 512))
                        for (bk, c0, c1) in tm:
                            for c in range(8):
                                mm(bk, bk[:, 0:c1 - c0], HT, HT[:, c, t * 128:(t + 1) * 128], win, win[:, c, c0:c1], c == 0, c == 7)
                        sy.do("act", lambda e, t=t: e.activation(out=va_b[pb][:, t, :], in_=fb[0][:, 0:256], func=AF.Copy), R=[fb[0]], W=[va_b[pb]])
                        if need_a and not (DBG & 2):
                            if DBG & 16:
                                sy.do("dve", lambda e, t=t: e.tensor_copy(out=vaf[:, t, :], in_=gcqb[:]), R=[gcqb], W=[vaf])
                            elif DBG & 8:
                                sy.do("dve", lambda e, t=t: e.tensor_copy(out=scr[:, 0:256], in_=fb[0][:, 0:256]), R=[fb[0]], W=[scr])
                            else:
                                sy.do("dve", lambda e, t=t: e.tensor_copy(out=vaf[:, t, :], in_=fb[0][:, 0:256]), R=[fb[0]], W=[vaf])
                        if need_a and not (DBG & 4):
                            sy.do("act", lambda e, t=t: e.activation(out=kaf[:, t, :], in_=fb[3][:, 0:256], func=AF.Copy), R=[fb[3]], W=[kaf])
                        if not (DBG & 1024):
                            rstd_rows(fb[0], fb[0][:, 256:512], 256, scr, ss2)
                            sy.do("dve", lambda e: e.scalar_tensor_tensor(out=cqn_b[:], in0=fb[0][:, 256:512], scalar=ss2[:, 0:1], in1=gcqb[:],
                                                                          op0=ALU.mult, op1=ALU.mult), R=[fb[0], ss2, gcqb], W=[cqn_b])
                            rstd_rows(fb[1], fb[1][:, 0:256], 256, scr, ss3)
                            sy.do("dve", lambda e, t=t: e.scalar_tensor_tensor(out=ckv_f[pb][:, t, :], in0=fb[1][:, 0:256], scalar=ss3[:, 0:1], in1=gckvb[:],
                                                                               op0=ALU.mult, op1=ALU.mult), R=[fb[1], ss3, gckvb], W=[ckv_f[pb]])
                            sy.do("pool", lambda e, t=t: e.tensor_copy(out=ckvn_b[:], in_=ckv_f[pb][:, t, :]), R=[ckv_f[pb]], W=[ckvn_b])
                            for k in range(2):
                                sy.do("pe", lambda e, k=k: e.transpose(out=tb[:, k * 128:(k + 1) * 128], in_=cqn_b[:, k * 128:(k + 1) * 128], identity=identb[:]),
                                      R=[cqn_b, identb], W=[tb])
                            evac(cqT[pb], cqT[pb][:, :, t * 128:(t + 1) * 128], tb, tb[:, 0:256].rearrange("p (c n) -> p c n", c=2))
                            for k in range(2):
                                sy.do("pe", lambda e, k=k: e.transpose(out=tb[:, k * 128:(k + 1) * 128], in_=ckvn_b[:, k * 128:(k + 1) * 128], identity=identb[:]),
                                      R=[ckvn_b, identb], W=[tb])
                            evac(ckvT[pb], ckvT[pb][:, :, t * 128:(t + 1) * 128], tb, tb[:, 0:256].rearrange("p (c n) -> p c n", c=2))
                        if not (DBG & 64):
                            sy.do("act", lambda e: e.activation(out=krr[:], in_=fb[1][:, 256:320], func=AF.Copy), R=[fb[1]], W=[krr])
                            cs, sn = RT[:, t, 0:32], RT[:, t, 32:64]
                            x1, x2 = krr[:, 0:32], krr[:, 32:64]
                            sy.do("pool", lambda e, cs=cs, x1=x1: e.tensor_tensor(out=krt[0][:], in0=x1, in1=cs, op=ALU.mult), R=[krr, RT], W=[krt[0]])
                            sy.do("pool", lambda e, sn=sn, x2=x2: e.tensor_tensor(out=krt[1][:], in0=x2, in1=sn, op=ALU.mult), R=[krr, RT], W=[krt[1]])
                            sy.do("pool", lambda e, sn=sn, x1=x1: e.tensor_tensor(out=krt[2][:], in0=x1, in1=sn, op=ALU.mult), R=[krr, RT], W=[krt[2]])
                            sy.do("pool", lambda e, cs=cs, x2=x2: e.tensor_tensor(out=krt[3][:], in0=x2, in1=cs, op=ALU.mult), R=[krr, RT], W=[krt[3]])
                            sy.do("pool", lambda e, t=t: e.tensor_tensor(out=kr_f[pb][:, t, 0:32], in0=krt[0][:], in1=krt[1][:], op=ALU.subtract), R=[krt[0], krt[1]], W=[kr_f[pb]])
                            sy.do("pool", lambda e, t=t: e.tensor_tensor(out=kr_f[pb][:, t, 32:64], in0=krt[2][:], in1=krt[3][:], op=ALU.add), R=[krt[2], krt[3]], W=[kr_f[pb]])
                            sy.do("pool", lambda e, t=t: e.tensor_copy(out=kr_b[:], in_=kr_f[pb][:, t, :]), R=[kr_f[pb]], W=[kr_b])
                            sy.do("pe", lambda e: e.transpose(out=tb[0:64, 0:128], in_=kr_b[:], identity=identb[:]), R=[kr_b, identb], W=[tb])
                            evac(krT[pb], krT[pb][:, t * 128:(t + 1) * 128], tb, tb[0:64, 0:128])
                        sy.do("act", lambda e, t=t: e.activation(out=kcvc[pb][:, t, :], in_=fb[2][:, 0:512], func=AF.Copy), R=[fb[2]], W=[kcvc[pb]])
                        sy.do("pool", lambda e, t=t: e.tensor_copy(out=vc_b[pb][:, t, :], in_=kcvc[pb][:, t, 256:512]), R=[kcvc[pb]], W=[vc_b[pb]])
                    if gi == 0: print('ninstr after tiles g0', sy.ninstr)
                    if gi == 0: ckpt(20)
                    if is_s: ckpt(25)
                    for mi, c0 in enumerate([0, 128, 256, 384, 1344, 1472, 1600, 1728]):
                        bk = fb[4 + mi % 2]
                        for c in range(8):
                            mm(bk, bk[:, 0:N], win, win[:, c, c0:c0 + 128], HT, HT[:, c, 0:N], c == 0, c == 7)
                        evac(fo[pb], fo[pb][:, mi, 0:N], bk, bk[:, 0:N])
                    if gi == 0: ckpt(21)
                    if not (DBG & 128):
                        CQ = cqT[pb]
                        for h in range(4):
                            bk = fb[4 + h % 2]
                            for k in range(2):
                                mm(bk, bk[:, 0:N], wuq, wuq[:, k, h * 192:h * 192 + 128], CQ, CQ[:, k, 0:N], k == 0, k == 1)
                            evac(qn_b[pb], qn_b[pb][:, h, 0:N], bk, bk[:, 0:N])
                            for k in range(2):
                                mm(fb[6], fb[6][0:64, 0:N], wuq, wuq[:, k, h * 192 + 128:h * 192 + 192], CQ, CQ[:, k, 0:N], k == 0, k == 1)
                            for k in range(2):
                                mm(fb[3], fb[3][0:64, 0:N], wuqs, wuqs[:, k, h * 64:(h + 1) * 64], CQ, CQ[:, k, 0:N], k == 0, k == 1)
                            sy.do("dve", lambda e: e.tensor_tensor(out=qt1[:, 0:N], in0=fb[6][0:64, 0:N], in1=RF[:, 0, 0:N], op=ALU.mult), R=[fb[6], RF], W=[qt1])
                            sy.do("dve", lambda e: e.tensor_tensor(out=qt2[:, 0:N], in0=fb[3][0:64, 0:N], in1=RF[:, 1, 0:N], op=ALU.mult), R=[fb[3], RF], W=[qt2])
                            sy.do("pool", lambda e, h=h: e.tensor_tensor(out=qp_b[pb][:, h, 0:N], in0=qt1[:, 0:N], in1=qt2[:, 0:N], op=ALU.add), R=[qt1, qt2], W=[qp_b[pb]])
                    if gi == 0: ckpt(22)
                    if not (DBG & 256):
                        kv_up(ckvT[pb], N, ntl, kn_b[pb], vm_b[pb])
                    if gi == 0: ckpt(23)
                    if gidx + 1 < len(groups):
                        nr0, nntl, ngi = groups[gidx + 1]
                        for t_ in range(2):
                            xpre[(ngi, t_)] = xload(nr0, t_)
                    cs_ = slice(r0, r0 + N)
                    for j, dst in enumerate((QA, KA, QC, KC)):
                        sy.dma("sp", dst[:, cs_].rearrange("(m p) n -> p m n", p=128), fo[pb][:, 2 * j:2 * j + 2, 0:N], R=[fo[pb]])
                    sy.dma("sp", QN[:, cs_].rearrange("(h p) n -> p h n", p=128), qn_b[pb][:, :, 0:N], R=[qn_b[pb]])
                    sy.dma("sp", QP[:, cs_].rearrange("(h p) n -> p h n", p=64), qp_b[pb][:, :, 0:N], R=[qp_b[pb]])
                    sy.dma("sp", KN[:, cs_].rearrange("(h p) n -> p h n", p=128), kn_b[pb][:, :, 0:N], R=[kn_b[pb]])
                    sy.dma("sp", KR[:, cs_], krT[pb][:, 0:N], R=[krT[pb]])
                    sy.dma("sp", VM[cs_, :].rearrange("(t p) c -> p t c", p=128), vm_b[pb][:, 0:ntl, :], R=[vm_b[pb]])
                    sy.dma("sp", VA[cs_, :].rearrange("(t p) c -> p t c", p=128), va_b[pb][:, 0:ntl, :], R=[va_b[pb]])
                    sy.dma("sp", VC[cs_, :].rearrange("(t p) c -> p t c", p=128), vc_b[pb][:, 0:ntl, :], R=[vc_b[pb]])
                    if gi == 0: print('ninstr before outs g0', sy.ninstr)
                    if gi == 0: ckpt(24)
                    if not is_s:
                        sy.dma("sp", pckv[l, cs_, :].rearrange("(t p) c -> p t c", p=128), ckv_f[pb][:, 0:ntl, :], R=[ckv_f[pb]])
                        sy.dma("sp", pkr[l, cs_, :].rearrange("(t p) c -> p t c", p=128), kr_f[pb][:, 0:ntl, :], R=[kr_f[pb]])
                        sy.dma("sp", psk[l, cs_, :].rearrange("(t p) c -> p t c", p=128), kcvc[pb][:, 0:ntl, 0:256], R=[kcvc[pb]])
                        sy.dma("sp", psv[l, cs_, :].rearrange("(t p) c -> p t c", p=128), kcvc[pb][:, 0:ntl, 256:512], R=[kcvc[pb]])
                        if gi == 0: ckpt(26)
                        if need_a: ckpt(29)
                        if need_a:
                            sy.dma("sp", pak[l].rearrange("(t p) c -> p t c", p=128), kaf[:, 0:ntl, :], R=[kaf])
                            sy.dma("sp", pav[l].rearrange("(t p) c -> p t c", p=128), vaf[:, 0:ntl, :], R=[vaf])
                        if gi == NG - 1: ckpt(27)
                    else:
                        for b in range(NSQ):
                            t_, p0 = b // 2, (b % 2) * 64
                            sy.dma("sp", sckv[l, b], ckv_f[pb][p0:p0 + 64, t_, :], R=[ckv_f[pb]])
                            sy.dma("sp", skr[l, b], kr_f[pb][p0:p0 + 64, t_, :], R=[kr_f[pb]])
                            sy.dma("sp", ssk[l, b], kcvc[pb][p0:p0 + 64, t_, 0:256], R=[kcvc[pb]])
                            sy.dma("sp", ssv[l, b], kcvc[pb][p0:p0 + 64, t_, 256:512], R=[kcvc[pb]])
                            sy.dma("sp", sak[l, b, 448:512, :], kaf[p0:p0 + 64, t_, :], R=[kaf])
                            sy.dma("sp", sav[l, b, 448:512, :], vaf[p0:p0 + 64, t_, :], R=[vaf])
                    if gi == 1: ckpt(28)
                    if DBG & 32: sy.barrier()
                sy.barrier()
                ckpt(3)

            def soft_res(ph, tag):
                return ([sb(ph, "P%s%d" % (tag, i), [128, 512], BF16) for i in range(4)], sb(ph, "rD" + tag, [128, 512], F32))

            def sb_res(ph, tag):
                return ([sb(ph, "E%s%d" % (tag, i), [128, 512], F32) for i in range(3)],
                        [sb(ph, "L%s%d" % (tag, i), [128, 512], BF16) for i in range(4)],
                        [sb(ph, "X%s%d" % (tag, i), [128, 512], F32) for i in range(2)],
                        [sb(ph, "A%s%d" % (tag, i), [128, 512], BF16) for i in range(4)],
                        [sb(ph, "cr%s%d" % (tag, i), [1, 512], F32) for i in range(2)])

            def attn_soft(res, jobs, scale):
                Sb = [fb[0], fb[1], fb[6]]
                Ob = [fb[2], fb[3]]
                Db = [fb[4], fb[5]]
                Pt, rD = res
                NP_ = len(Pt)
                items = []
                for ji, jb in enumerate(jobs):
                    nb = len(jb["blocks"])
                    for bi, blk in enumerate(jb["blocks"]):
                        items.append((ji, jb, bi, blk, bi == 0, bi == nb - 1))

                def stage_a(t):
                    ji, jb, bi, blk, first, lastb = items[t]
                    N, m = jb["N"], blk["m"]
                    bk = Sb[t % 3]
                    n = len(blk["mms"])
                    for i, (lb, lap, rb, rap) in enumerate(blk["mms"]):
                        mm(bk, bk[0:m, 0:N], lb, lap, rb, rap, i == 0, i == n - 1)

                def stage_b(t):
                    ji, jb, bi, blk, first, lastb = items[t]
                    N, m = jb["N"], blk["m"]
                    bk, P = Sb[t % 3], Pt[t % NP_]
                    sy.do("act", lambda e: e.activation(out=P[0:m, 0:N], in_=bk[0:m, 0:N], func=AF.Exp, scale=scale), R=[bk], W=[P])
                    if blk["mulb"] is not None:
                        sy.do("dve", lambda e: e.tensor_tensor(out=P[0:m, 0:N], in0=P[0:m, 0:N], in1=blk["mulap"], op=ALU.mult),
                              R=[blk["mulb"]], W=[P])

                def stage_f(t):
                    ji, jb, bi, blk, first, lastb = items[t]
                    N, m, ed = jb["N"], blk["m"], jb["e"]
                    P = Pt[t % NP_]
                    O, Dn = Ob[ji % 2], Db[ji % 2]
                    mm(O, O[0:ed, 0:N], blk["vb"], blk["vap"], P, P[0:m, 0:N], first, lastb)
                    mm(Dn, Dn[0:ed, 0:N], onesb, onesb[0:m, 0:ed], P, P[0:m, 0:N], first, lastb)
                    if lastb:
                        sy.do("dve", lambda e: e.reciprocal(out=rD[0:ed, 0:N], in_=Dn[0:ed, 0:N]), R=[Dn], W=[rD])
                        sy.do("dve", lambda e: e.tensor_tensor(out=jb["dstap"], in0=O[0:ed, 0:N], in1=rD[0:ed, 0:N], op=ALU.mult),
                              R=[O, rD], W=[jb["dstb"]])
                        if jb.get("after"):
                            jb["after"]()

                T = len(items)

                def issue_a(t):
                    if items[t][2] == 0 and items[t][1].get("before"):
                        items[t][1]["before"]()
                    stage_a(t)

                if T:
                    issue_a(0)
                for t in range(T + 2):
                    if t + 1 < T:
                        issue_a(t + 1)
                    if t < T:
                        stage_b(t)
                    if 2 <= t:
                        stage_f(t - 2)

            def attn_sb(res, jobs, scale):
                Zb = [fb[0], fb[1], fb[6]]
                Bb = [fb[2], fb[3]]
                Ob = [fb[4], fb[5]]
                Eb, Lb, Xb, Ab, crow = res
                NA_ = len(Ab)
                items = []
                for ji, jb in enumerate(jobs):
                    nb = len(jb["blocks"])
                    for bi, blk in enumerate(jb["blocks"]):
                        items.append((ji, jb, bi, blk, bi == 0, bi == nb - 1))

                def stage_a(t):
                    ji, jb, bi, blk, first, lastb = items[t]
                    N, m = jb["N"], blk["m"]
                    bk = Zb[t % 3]
                    lb, lap, rb, rap = blk["mm"]
                    mm(bk, bk[0:m, 0:N], lb, lap, rb, rap, True, True)

                def stage_b(t):
                    ji, jb, bi, blk, first, lastb = items[t]
                    N, m = jb["N"], blk["m"]
                    bk, E, Lt = Zb[t % 3], Eb[t % 3], Lb[t % 4]
                    sy.do("act", lambda e: e.activation(out=E[0:m, 0:N], in_=bk[0:m, 0:N], func=AF.Exp, scale=scale), R=[bk], W=[E])
                    if blk["mulb"] is not None:
                        sy.do("pool", lambda e: e.tensor_tensor(out=E[0:m, 0:N], in0=E[0:m, 0:N], in1=blk["mulap"], op=ALU.mult),
                              R=[blk["mulb"]], W=[E])
                    sy.do("act", lambda e: e.activation(out=Lt[0:m, 0:N], in_=E[0:m, 0:N], func=AF.Ln, bias=1.0), R=[E], W=[Lt])

                def stage_c(t):
                    ji, jb, bi, blk, first, lastb = items[t]
                    N, m = jb["N"], blk["m"]
                    B, Lt = Bb[ji % 2], Lb[t % 4]
                    mm(B, B[:, 0:N], trib, trib[0:m, 0, :], Lt, Lt[0:m, 0:N], first, True, sgc=True)

                def stage_c2(t):
                    ji, jb, bi, blk, first, lastb = items[t]
                    N, m = jb["N"], blk["m"]
                    B, Lt = Bb[ji % 2], Lb[t % 4]
                    if not lastb:
                        mm(B, B[:, 0:N], trib, trib[0:m, 1, :], Lt, Lt[0:m, 0:N], False, True, sgc=True)

                def stage_d(t):
                    ji, jb, bi, blk, first, lastb = items[t]
                    N, m = jb["N"], blk["m"]
                    B, X, E, A = Bb[ji % 2], Xb[t % 2], Eb[t % 3], Ab[t % NA_]
                    sy.do("act", lambda e: e.activation(out=X[0:m, 0:N], in_=B[0:m, 0:N], func=AF.Exp), R=[B], W=[X])
                    sy.do("dve", lambda e: e.tensor_tensor(out=A[0:m, 0:N], in0=E[0:m, 0:N], in1=X[0:m, 0:N], op=ALU.mult), R=[E, X], W=[A])

                def stage_f(t):
                    ji, jb, bi, blk, first, lastb = items[t]
                    N, m, ed = jb["N"], blk["m"], jb["e"]
                    A, O = Ab[t % NA_], Ob[ji % 2]
                    mm(O, O[0:ed, 0:N], blk["vb"], blk["vap"], A, A[0:m, 0:N], first, lastb)
                    if lastb:
                        sy.do("dve", lambda e: e.tensor_copy(out=jb["dstap"], in_=O[0:ed, 0:N]), R=[O], W=[jb["dstb"]])
                        if jb.get("after"):
                            jb["after"]()

                T = len(items)

                def issue_a(t):
                    if items[t][2] == 0 and items[t][1].get("before"):
                        items[t][1]["before"]()
                    stage_a(t)

                if T:
                    issue_a(0)
                for t in range(T + 3):
                    if t + 1 < T:
                        issue_a(t + 1)
                    if t < T:
                        stage_b(t)
                    if 2 <= t <= T + 1:
                        stage_c2(t - 2)
                    if 1 <= t <= T:
                        stage_c(t - 1)
                        stage_d(t - 1)
                    if 3 <= t:
                        stage_f(t - 3)

            with ExitStack() as ph:
                EB = sb(ph, "EB", [128, 4, 8, 512], BF16)
                with ExitStack() as ph2:
                    rel = sb(ph2, "rel", [1, 768], F32)
                    ext = sb(ph2, "ext", [1, 4, 1536], F32)
                    valb = sb(ph2, "valb", [128, 8, 512], F32)
                    ebr = [sb(ph2, "ebr%d" % i, [128, 512], F32) for i in range(2)]
                    ebrb = [sb(ph2, "ebrb%d" % i, [128, 512], BF16) for i in range(2)]
                    sy.dma("sp", rel[:], a_rel[l:l + 1, :], W=[rel])
                    sy.dma("sp", valb[:], c_val, W=[valb])
                    for h in range(4):
                        sy.do("dve", lambda e, h=h: e.tensor_copy(out=ext[0:1, h, 0:448], in_=rel[0:1, h * 192:h * 192 + 1].to_broadcast([1, 448])), R=[rel], W=[ext])
                        sy.do("dve", lambda e, h=h: e.tensor_copy(out=ext[0:1, h, 448:640], in_=rel[0:1, h * 192:(h + 1) * 192]), R=[rel], W=[ext])
                        sy.do("dve", lambda e, h=h: e.tensor_copy(out=ext[0:1, h, 640:1536], in_=rel[0:1, h * 192 + 191:h * 192 + 192].to_broadcast([1, 896])), R=[rel], W=[ext])
                    sy.do("act", lambda e: e.activation(out=ext[:], in_=ext[:], func=AF.Exp), R=[ext], W=[ext])
                    sy.dma("sp", EXT.rearrange("(o h) n -> o h n", o=1), ext[:], R=[ext])
                    sy.barrier()
                    k = 0
                    for h in range(4):
                        for i in range(8):
                            eb = ebr[k % 2]; ebb = ebrb[k % 2]; bk = fb[k % 2]; k += 1
                            src = bass.AP(EXT.tensor, h * 1536 + 896 - 128 * i, [[1, 128], [1, 512]])
                            sy.dma("sp", eb[:], src, W=[eb])
                            sy.do("pool", lambda e, eb=eb, ebb=ebb: e.tensor_copy(out=ebb[:], in_=eb[:]), R=[eb], W=[ebb])
                            mm(bk, bk[:, :], trib, trib[:, 3, :], ebb, ebb[:], True, True)
                            sy.do("dve", lambda e, h=h, i=i, bk=bk: e.tensor_tensor(out=EB[:, h, i, :], in0=bk[:, :], in1=valb[:, i, :], op=ALU.mult),
                                  R=[bk, valb], W=[EB])
                    sy.do("dve", lambda e: e.tensor_copy(out=EBs[:], in_=EB[:, :, 0:5, 0:64]), R=[EB], W=[EBs])
                    sy.barrier()
                kaT = sb(ph, "kaT", [128, 2, S], BF16)
                vaS = sb(ph, "vaS", [128, NKB, 256], BF16)
                sy.dma("sp", kaT[:], KA[:, 0:S].rearrange("(m p) n -> p m n", p=128), W=[kaT])
                sy.dma("sp", vaS[:], VA[0:S, :].rearrange("(t p) c -> p t c", p=128), W=[vaS])
                qa = [sb(ph, "qa%d" % i, [128, 2, 512], BF16) for i in range(2)]
                ost = [sb(ph, "osta%d" % i, [64, 4, 512], F32) for i in range(2)]
                jobs = []
                for g in range(NG):
                    pb = g % 2
                    for h in range(4):
                        hp, bs = h // 2, (h % 2) * 64
                        blocks = []
                        for i in range(8):
                            kb = 4 * g - 4 + i
                            if kb < 0:
                                continue
                            blocks.append(dict(mms=[(kaT, kaT[bs:bs + 64, hp, kb * 128:(kb + 1) * 128], qa[pb], qa[pb][bs:bs + 64, hp, :])],
                                               m=128, vb=vaS, vap=vaS[:, kb, h * 64:(h + 1) * 64], mulb=EB, mulap=EB[:, h, i, :]))
                        jb = dict(N=512, e=64, blocks=blocks, dstb=ost[pb], dstap=ost[pb][:, h, :])
                        if h == 0:
                            jb["before"] = (lambda g=g, pb=pb: sy.dma("sp", qa[pb][:], QA[:, g * 512:(g + 1) * 512].rearrange("(m p) n -> p m n", p=128), W=[qa[pb]]))
                        if h == 3:
                            jb["after"] = (lambda g=g, pb=pb: sy.dma("sp", OCAT[0:256, g * 512:(g + 1) * 512].rearrange("(h p) n -> p h n", p=64), ost[pb][:], R=[ost[pb]]))
                        jobs.append(jb)
                attn_soft(soft_res(ph, "a"), jobs, 0.125)
                sy.barrier()
                ckpt(4)

            with ExitStack() as ph:
                knT = sb(ph, "knT", [128, 4, S], BF16)
                krS = sb(ph, "krS", [64, S], BF16)
                vmS = sb(ph, "vmS", [128, NKB, 512], BF16)
                mml = sb(ph, "mml", [128, 4, 512], BF16)
                with ExitStack() as ph2:
                    m32 = sb(ph2, "m32", [128, 4, 512], F32)
                    sy.dma("sp", m32[:], c_mml, W=[m32])
                    sy.do("dve", lambda e: e.tensor_copy(out=mml[:], in_=m32[:]), R=[m32], W=[mml])
                    sy.barrier()
                for h in range(4):
                    sy.dma("sp", knT[:, h, :], KN[h * 128:(h + 1) * 128, 0:S], W=[knT])
                sy.dma("sp", krS[:], KR[:, 0:S], W=[krS])
                for q4 in range(0, NKB, 8):
                    sy.dma("sp", vmS[:, q4:q4 + 8, :], VM[q4 * 128:(q4 + 8) * 128, :].rearrange("(t p) c -> p t c", p=128), W=[vmS])
                qn = [sb(ph, "qn%d" % i, [128, 4, 512], BF16) for i in range(2)]
                qp = [sb(ph, "qp%d" % i, [64, 4, 512], BF16) for i in range(2)]
                ost = [sb(ph, "ostm0", [128, 4, 512], F32)] * 2
                jobs = []
                for g in range(NG):
                    pb = g % 2
                    for h in range(4):
                        blocks = []
                        for kb in range(4 * g + 4):
                            r = kb - 4 * g
                            blocks.append(dict(mms=[(knT, knT[:, h, kb * 128:(kb + 1) * 128], qn[pb], qn[pb][:, h, :]),
                                                    (krS, krS[:, kb * 128:(kb + 1) * 128], qp[pb], qp[pb][:, h, :])],
                                               m=128, vb=vmS, vap=vmS[:, kb, h * 128:(h + 1) * 128],
                                               mulb=(mml if r >= 0 else None), mulap=(mml[:, r, :] if r >= 0 else None)))
                        jb = dict(N=512, e=128, blocks=blocks, dstb=ost[pb], dstap=ost[pb][:, h, :])
                        if h == 0:
                            def bf(g=g, pb=pb):
                                sy.dma("sp", qn[pb][:], QN[:, g * 512:(g + 1) * 512].rearrange("(h p) n -> p h n", p=128), W=[qn[pb]])
                                sy.dma("sp", qp[pb][:], QP[:, g * 512:(g + 1) * 512].rearrange("(h p) n -> p h n", p=64), W=[qp[pb]])
                            jb["before"] = bf
                        if h == 3:
                            jb["after"] = (lambda g=g, pb=pb: sy.dma("sp", OCAT[256:768, g * 512:(g + 1) * 512].rearrange("(h p) n -> p h n", p=128), ost[pb][:], R=[ost[pb]]))
                        jobs.append(jb)
                attn_soft(soft_res(ph, "m"), jobs, 192.0 ** -0.5)
                sy.barrier()
                ckpt(5)

            with ExitStack() as ph:
                kcT = sb(ph, "kcT", [128, 2, S], BF16)
                vcS = sb(ph, "vcS", [128, NKB, 256], BF16)
                msb = sb(ph, "msb", [128, 4, 512], F32)
                sy.dma("sp", msb[:], c_msb, W=[msb])
                sy.dma("sp", kcT[:], KC[:, 0:S].rearrange("(m p) n -> p m n", p=128), W=[kcT])
                sy.dma("sp", vcS[:], VC[0:S, :].rearrange("(t p) c -> p t c", p=128), W=[vcS])
                qc = [sb(ph, "qc%d" % i, [128, 2, 512], BF16) for i in range(2)]
                ost = [sb(ph, "osts%d" % i, [64, 4, 512], F32) for i in range(2)]
                jobs = []
                for g in range(NG):
                    pb = g % 2
                    for h in range(4):
                        hp, bs = h // 2, (h % 2) * 64
                        blocks = []
                        for kb in range(4 * g + 3, -1, -1):
                            r = kb - 4 * g
                            blocks.append(dict(mm=(kcT, kcT[bs:bs + 64, hp, kb * 128:(kb + 1) * 128], qc[pb], qc[pb][bs:bs + 64, hp, :]),
                                               m=128, vb=vcS, vap=vcS[:, kb, h * 64:(h + 1) * 64],
                                               mulb=(msb if r >= 0 else None), mulap=(msb[:, r, :] if r >= 0 else None)))
                        jb = dict(N=512, e=64, blocks=blocks, dstb=ost[pb], dstap=ost[pb][:, h, :])
                        if h == 0:
                            jb["before"] = (lambda g=g, pb=pb: sy.dma("sp", qc[pb][:], QC[:, g * 512:(g + 1) * 512].rearrange("(m p) n -> p m n", p=128), W=[qc[pb]]))
                        if h == 3:
                            jb["after"] = (lambda g=g, pb=pb: sy.dma("sp", OCAT[768:1024, g * 512:(g + 1) * 512].rearrange("(h p) n -> p h n", p=64), ost[pb][:], R=[ost[pb]]))
                        jobs.append(jb)
                attn_sb(sb_res(ph, "s"), jobs, 0.125)
                sy.barrier()
                ckpt(6)

            with ExitStack() as ph:
                NQ = NSQ * TS
                KL = PAST + TS
                qaS = sb(ph, "qaS", [128, 2, NQ], BF16); qcS = sb(ph, "qcS", [128, 2, NQ], BF16)
                qnS = sb(ph, "qnS", [128, 4, NQ], BF16); qpS = sb(ph, "qpS", [64, 4, NQ], BF16)
                sy.dma("sp", qaS[:], QA[:, S:NT].rearrange("(m p) n -> p m n", p=128), W=[qaS])
                sy.dma("sp", qcS[:], QC[:, S:NT].rearrange("(m p) n -> p m n", p=128), W=[qcS])
                sy.dma("sp", qnS[:], QN[:, S:NT].rearrange("(h p) n -> p h n", p=128), W=[qnS])
                sy.dma("sp", qpS[:], QP[:, S:NT].rearrange("(h p) n -> p h n", p=64), W=[qpS])
                msb = sb(ph, "msbS", [64, 64], F32)
                sy.dma("sp", msb[:], c_msb[0:64, 0, 0:64], W=[msb])
                osA = sb(ph, "osA", [64, 4, NQ], F32); osM = sb(ph, "osM", [128, 4, NQ], F32); osS = sb(ph, "osS", [64, 4, NQ], F32)
                kaS = [sb(ph, "kaS%d" % i, [128, 2, 576], BF16) for i in range(2)]
                vaS = [sb(ph, "vaSs%d" % i, [128, 5, 256], BF16) for i in range(2)]
                knS = [sb(ph, "knS%d" % i, [128, 4, KL], BF16) for i in range(2)]
                krS = [sb(ph, "krSs%d" % i, [64, KL], BF16) for i in range(2)]
                vmS = [sb(ph, "vmSs%d" % i, [128, NPB + 1, 512], BF16) for i in range(2)]
                kcS = [sb(ph, "kcS%d" % i, [128, 2, KL], BF16) for i in range(2)]
                vcS = [sb(ph, "vcSs%d" % i, [128, NPB + 1, 256], BF16) for i in range(2)]
                ja, jm, js = [], [], []
                for b in range(NSQ):
                    pb = b % 2
                    c0 = S + b * TS
                    qs = slice(b * TS, (b + 1) * TS)

                    def ld(b=b, pb=pb, c0=c0):
                        sy.dma("sp", kaS[pb][:, :, 0:512], KAs[b].rearrange("(m p) n -> p m n", p=128), W=[kaS[pb]])
                        sy.dma("sp", kaS[pb][:, :, 512:576], KA[:, c0:c0 + TS].rearrange("(m p) n -> p m n", p=128), W=[kaS[pb]])
                        sy.dma("sp", vaS[pb][:, 0:4, :], VAs[b].rearrange("(t p) c -> p t c", p=128), W=[vaS[pb]])
                        sy.dma("sp", vaS[pb][0:64, 4, :], VA[c0:c0 + TS, :], W=[vaS[pb]])
                        sy.dma("sp", knS[pb][:, :, 0:PAST], KNs[b].rearrange("(h p) n -> p h n", p=128), W=[knS[pb]])
                        sy.dma("sp", knS[pb][:, :, PAST:KL], KN[:, c0:c0 + TS].rearrange("(h p) n -> p h n", p=128), W=[knS[pb]])
                        sy.dma("sp", krS[pb][:, 0:PAST], KRs[b], W=[krS[pb]])
                        sy.dma("sp", krS[pb][:, PAST:KL], KR[:, c0:c0 + TS], W=[krS[pb]])
                        sy.dma("sp", vmS[pb][:, 0:NPB, :], VMs[b].rearrange("(t p) c -> p t c", p=128), W=[vmS[pb]])
                        sy.dma("sp", vmS[pb][0:64, NPB, :], VM[c0:c0 + TS, :], W=[vmS[pb]])
                        sy.dma("sp", kcS[pb][:, :, 0:PAST], KCs[b].rearrange("(m p) n -> p m n", p=128), W=[kcS[pb]])
                        sy.dma("sp", kcS[pb][:, :, PAST:KL], KC[:, c0:c0 + TS].rearrange("(m p) n -> p m n", p=128), W=[kcS[pb]])
                        sy.dma("sp", vcS[pb][:, 0:NPB, :], VCs[b].rearrange("(t p) c -> p t c", p=128), W=[vcS[pb]])
                        sy.dma("sp", vcS[pb][0:64, NPB, :], VC[c0:c0 + TS, :], W=[vcS[pb]])
                    for h in range(4):
                        hp, bs = h // 2, (h % 2) * 64
                        blocks = []
                        for i in range(5):
                            m = 128 if i < 4 else 64
                            blocks.append(dict(mms=[(kaS[pb], kaS[pb][bs:bs + 64, hp, i * 128:i * 128 + m], qaS, qaS[bs:bs + 64, hp, qs])],
                                               m=m, vb=vaS[pb], vap=vaS[pb][0:m, i, h * 64:(h + 1) * 64], mulb=EBs, mulap=EBs[0:m, h, i, :]))
                        jb = dict(N=TS, e=64, blocks=blocks, dstb=osA, dstap=osA[:, h, qs])
                        if h == 0:
                            jb["before"] = ld
                        ja.append(jb)
                        blocks = []
                        for kb in range(NPB + 1):
                            m = 128 if kb < NPB else 64
                            blocks.append(dict(mms=[(knS[pb], knS[pb][:, h, kb * 128:kb * 128 + m], qnS, qnS[:, h, qs]),
                                                    (krS[pb], krS[pb][:, kb * 128:kb * 128 + m], qpS, qpS[:, h, qs])],
                                               m=m, vb=vmS[pb], vap=vmS[pb][0:m, kb, h * 128:(h + 1) * 128], mulb=None, mulap=None))
                        jm.append(dict(N=TS, e=128, blocks=blocks, dstb=osM, dstap=osM[:, h, qs]))
                        blocks = []
                        for kb in range(NPB, -1, -1):
                            m = 128 if kb < NPB else 64
                            blocks.append(dict(mm=(kcS[pb], kcS[pb][bs:bs + 64, hp, kb * 128:kb * 128 + m], qcS, qcS[bs:bs + 64, hp, qs]),
                                               m=m, vb=vcS[pb], vap=vcS[pb][0:m, kb, h * 64:(h + 1) * 64],
                                               mulb=(msb if kb == NPB else None), mulap=(msb[:, :] if kb == NPB else None)))
                        js.append(dict(N=TS, e=64, blocks=blocks, dstb=osS, dstap=osS[:, h, qs]))
                rsoft, rsb = soft_res(ph, "x"), sb_res(ph, "x")
                for b in range(NSQ):
                    attn_soft(rsoft, ja[4 * b:4 * b + 4], 0.125)
                    attn_soft(rsoft, jm[4 * b:4 * b + 4], 192.0 ** -0.5)
                    attn_sb(rsb, js[4 * b:4 * b + 4], 0.125)
                sy.dma("sp", OCAT[0:256, S:NT].rearrange("(h p) n -> p h n", p=64), osA[:], R=[osA])
                sy.dma("sp", OCAT[256:768, S:NT].rearrange("(h p) n -> p h n", p=128), osM[:], R=[osM])
                sy.dma("sp", OCAT[768:1024, S:NT].rearrange("(h p) n -> p h n", p=64), osS[:], R=[osS])
                sy.barrier()
                ckpt(7)

            with ExitStack() as ph:
                wout = sb(ph, "wout", [128, 8, D], BF16)
                gcat = sb(ph, "gcat", [128, 8], F32)
                sy.dma("sp", gcat[:], g_cat[l].rearrange("(c p) -> p c", p=128), W=[gcat], allow_slow_non_contiguous=True)
                with ExitStack() as ph2:
                    load_w(ph2, wout, [wout[:, c, :] for c in range(8)], [w_out[l, c * 128:(c + 1) * 128, :] for c in range(8)], D)
                    sy.barrier()
                oc = [sb(ph, "oc%d" % i, [128, 8, 512], F32) for i in range(2)]
                xg = [sb(ph, "xg2_%d" % i, [128, 4, D], F32) for i in range(2)]
                sq = sb(ph, "sq", [128, 8, 512], BF16)
                rs = [sb(ph, "rs%d" % i, [128, 512], F32) for i in range(3)]
                catT = [sb(ph, "catT%d" % i, [128, 8, 512], BF16) for i in range(2)]
                def ld2(r0, ntl, gi):
                    N = ntl * 128
                    OC, X = oc[gi % 2], xg[gi % 2]
                    sy.dma("sp", OC[:, :, 0:N], OCAT[:, r0:r0 + N].rearrange("(c p) n -> p c n", p=128), W=[OC])
                    sy.dma("sp", X[:, 0:ntl, :], xsrc[r0:r0 + N, :].rearrange("(t p) d -> p t d", p=128), W=[X])

                ld2(*groups[0])
                for gidx, (r0, ntl, gi) in enumerate(groups):
                    N = ntl * 128
                    pb = gi % 2
                    OC, X, CT = oc[pb], xg[pb], catT[pb]
                    if gidx + 1 < len(groups):
                        ld2(*groups[gidx + 1])
                    sy.do("act", lambda e: e.activation(out=sq[:, :, 0:N], in_=OC[:, :, 0:N], func=AF.Square), R=[OC], W=[sq])
                    for mi, (a0, a1, Wd) in enumerate(((0, 2, 256), (2, 6, 512), (6, 8, 256))):
                        bk = fb[mi]
                        for c in range(a0, a1):
                            mm(bk, bk[:, 0:N], onesb, onesb[:, :], sq, sq[:, c, 0:N], c == a0, c == a1 - 1)
                        R_ = rs[mi]
                        sy.do("dve", lambda e, R_=R_, bk=bk, Wd=Wd: e.tensor_scalar(out=R_[:, 0:N], in0=bk[:, 0:N], scalar1=1.0 / Wd, scalar2=EPS,
                                                                                   op0=ALU.mult, op1=ALU.add), R=[bk], W=[R_])
                        sy.do("act", lambda e, R_=R_: e.activation(out=R_[:, 0:N], in_=R_[:, 0:N], func=AF.Sqrt), R=[R_], W=[R_])
                        sy.do("dve", lambda e, R_=R_: e.reciprocal(out=R_[:, 0:N], in_=R_[:, 0:N]), R=[R_], W=[R_])
                        for c in range(a0, a1):
                            sy.do("dve", lambda e, c=c, R_=R_: e.scalar_tensor_tensor(out=CT[:, c, 0:N], in0=OC[:, c, 0:N], scalar=gcat[:, c:c + 1], in1=R_[:, 0:N],
                                                                                                  op0=ALU.mult, op1=ALU.mult), R=[OC, gcat, R_], W=[CT])
                    for t in range(ntl):
                        for hf in range(2):
                            bk = fb[3 + (2 * t + hf) % 4]
                            for c in range(8):
                                mm(bk, bk[:, :], CT, CT[:, c, t * 128:(t + 1) * 128], wout, wout[:, c, hf * 512:(hf + 1) * 512], c == 0, c == 7)
                            sy.do("dve", lambda e, t=t, hf=hf, bk=bk: e.tensor_tensor(out=X[:, t, hf * 512:(hf + 1) * 512], in0=X[:, t, hf * 512:(hf + 1) * 512], in1=bk[:, :], op=ALU.add),
                                  R=[bk], W=[X])
                    sy.dma("sp", XR[r0:r0 + N, :].rearrange("(t p) d -> p t d", p=128), X[:, 0:ntl, :], R=[X])
                sy.barrier()
                ckpt(8)

            with ExitStack() as ph:
                wup = sb(ph, "wup", [128, 8, DFF], BF16)
                wdn = sb(ph, "wdn", [128, 32, D], BF16)
                gffnb = sb(ph, "gffnb", [128, D], F32)
                sy.dma("sp", gffnb[:], g_ffn[l].partition_broadcast(128), W=[gffnb])
                with ExitStack() as ph2:
                    load_w(ph2, wup, [wup[:, c, hf * 2048:(hf + 1) * 2048] for c in range(8) for hf in range(2)],
                           [w_up[l, c * 128:(c + 1) * 128, hf * 2048:(hf + 1) * 2048] for c in range(8) for hf in range(2)], 2048)
                    load_w(ph2, wdn, [wdn[:, 2 * j:2 * j + 2, :] for j in range(16)],
                           [w_down[l, j * 256:(j + 1) * 256, :].rearrange("(f p) d -> p f d", p=128) for j in range(16)], 2048)
                    sy.barrier()
                xg = [sb(ph, "xg3_%d" % i, [128, 2, D], F32) for i in range(2)]
                scr = sb(ph, "scr3", [128, D], F32)
                ssx = sb(ph, "ssx3", [128, 1], F32)
                hb = [sb(ph, "hb3_%d" % i, [128, D], BF16) for i in range(2)]
                hT = [sb(ph, "hT3_0", [128, 8, 256], BF16)] * 2
                rb = [sb(ph, "rb%d" % i, [128, 256], F32) for i in range(2)]
                uT = [sb(ph, "uT0", [128, 32, 256], BF16)] * 2
                yb = [sb(ph, "yb0", [128, 2, D], F32)] * 2 if last else None
                def ld3(gi):
                    X = xg[gi % 2]
                    sy.dma("sp", X[:], XR[gi * 256:gi * 256 + 256, :].rearrange("(t p) d -> p t d", p=128), W=[X])

                def normA(gi):
                    X = xg[gi % 2]
                    for t in range(2):
                        H = hb[t % 2]
                        rstd_rows(X, X[:, t, :], D, scr, ssx)
                        sy.do("dve", lambda e, t=t, H=H, X=X: e.scalar_tensor_tensor(out=H[:], in0=X[:, t, :], scalar=ssx[:, 0:1], in1=gffnb[:],
                                                                                    op0=ALU.mult, op1=ALU.mult), R=[X, ssx, gffnb], W=[H])

                def normB(gi):
                    HT = hT[gi % 2]
                    for t in range(2):
                        H = hb[t % 2]
                        for c in range(8):
                            sy.do("pe", lambda e, c=c, H=H: e.transpose(out=tb[:, c * 128:(c + 1) * 128], in_=H[:, c * 128:(c + 1) * 128], identity=identb[:]),
                                  R=[H, identb], W=[tb])
                        sy.do("act", lambda e, t=t, HT=HT: e.activation(out=HT[:, :, t * 128:(t + 1) * 128], in_=tb[:, :].rearrange("p (c n) -> p c n", c=8), func=AF.Copy),
                              R=[tb], W=[HT])

                NGF = NT // 256
                ld3(0)
                normA(0)
                normB(0)
                for gi in range(NGF):
                    r0 = gi * 256
                    pb = gi % 2
                    X, HT, UT = xg[pb], hT[pb], uT[pb]
                    if gi + 1 < NGF:
                        ld3(gi + 1)
                    for fc in range(32):
                        bk = fb[fc % 3]
                        for c in range(8):
                            mm(bk, bk[:, 0:256], wup, wup[:, c, fc * 128:(fc + 1) * 128], HT, HT[:, c, :], c == 0, c == 7)
                        R_ = rb[fc % 2]
                        sy.do("act", lambda e, bk=bk, R_=R_: e.activation(out=R_[:], in_=bk[:, 0:256], func=AF.Relu), R=[bk], W=[R_])
                        sy.do("pool", lambda e, fc=fc, R_=R_, UT=UT: e.tensor_tensor(out=UT[:, fc, :], in0=R_[:], in1=R_[:], op=ALU.mult), R=[R_], W=[UT])
                    if gi + 1 < NGF:
                        normA(gi + 1)
                    for t in range(2):
                        for hf in range(2):
                            bk = fb[3 + (2 * t + hf) % 4]
                            for fc in range(32):
                                mm(bk, bk[:, :], UT, UT[:, fc, t * 128:(t + 1) * 128], wdn, wdn[:, fc, hf * 512:(hf + 1) * 512], fc == 0, fc == 31)
                            sy.do("dve", lambda e, t=t, hf=hf, bk=bk, X=X: e.tensor_tensor(out=X[:, t, hf * 512:(hf + 1) * 512], in0=X[:, t, hf * 512:(hf + 1) * 512], in1=bk[:, :], op=ALU.add),
                                  R=[bk], W=[X])
                    if not last:
                        sy.dma("sp", XR[r0:r0 + 256, :].rearrange("(t p) d -> p t d", p=128), X[:], R=[X])
                    else:
                        Y = yb[pb]
                        for t in range(2):
                            rstd_rows(X, X[:, t, :], D, scr, ssx)
                            sy.do("dve", lambda e, t=t, X=X, Y=Y: e.scalar_tensor_tensor(out=Y[:, t, :], in0=X[:, t, :], scalar=ssx[:, 0:1], in1=gfinb[:],
                                                                                        op0=ALU.mult, op1=ALU.mult), R=[X, ssx, gfinb], W=[Y])
                        sy.dma("sp", y[r0:r0 + 256, :].rearrange("(t p) d -> p t d", p=128), Y[:], R=[Y])
                    if gi + 1 < NGF:
                        normB(gi + 1)
                sy.barrier()
                ckpt(9)
        print("instructions emitted:", sy.ninstr, {k: v for k, v in sy.cnt.items() if not k.startswith("d")})
    return nc


def _consts(S, PAST):
    NT = S + NSQ * TS
    j = np.arange(128)[:, None]
    t = np.arange(512)[None, :]
    tri = np.zeros((128, 4, 128), np.float32)
    jj, kk = np.arange(128)[:, None], np.arange(128)[None, :]
    tri[:, 0, :] = -1.0 * (jj >= kk)
    tri[:, 1, :] = -1.0 * (jj < kk)
    tri[:, 2, :] = -1.0
    tri[:, 3, :] = np.eye(128, dtype=np.float32)[::-1]
    msb = np.stack([(j + 128 * r < t) for r in range(4)], 1).astype(np.float32)
    mml = np.stack([((128 * r + j) // 64 <= t // 64) for r in range(4)], 1).astype(np.float32)
    val = []
    for i in range(8):
        d = 8 + t // 64 - (2 * i + j // 64)
        val.append((d >= 0) & (d <= 8))
    val = np.stack(val, 1).astype(np.float32)
    pos = np.concatenate([np.arange(S), np.tile(PAST + np.arange(TS), NSQ)]).astype(np.float32)
    inv = (10000.0 ** (-np.arange(32, dtype=np.float32) / np.float32(32))).astype(np.float32)
    ang = (pos[:, None] * inv[None, :]).astype(np.float32)
    cos, sin = np.cos(ang).astype(np.float32), np.sin(ang).astype(np.float32)
    ropeT = np.concatenate([cos, sin], 1).astype(np.float32)
    cosF = np.concatenate([cos, cos], 1).T
    sinF = np.concatenate([-sin, sin], 1).T
    ropeF = np.ascontiguousarray(np.stack([cosF, sinF], 1)).astype(np.float32)
    return dict(c_ident=np.eye(128, dtype=np.float32), c_tri=tri, c_msb=np.ascontiguousarray(msb),
                c_mml=np.ascontiguousarray(mml), c_val=np.ascontiguousarray(val), c_ropeT=ropeT, c_ropeF=ropeF)


_CACHE = {}


def kernel(x_prompt, x_sample, cache_a_k, cache_a_v, cache_mla_ckv, cache_mla_krope, cache_sb_k, cache_sb_v,
           g_mix, w_in, g_cq, g_ckv, w_uq, w_ukv, a_rel_bias, g_out_a, g_out_mla, g_out_sb, w_out,
           g_ffn, w_up, w_down, g_final):
    f = lambda a: np.ascontiguousarray(np.asarray(a, dtype=np.float32))
    x_prompt, x_sample = f(x_prompt), f(x_sample)
    B, S, _ = x_prompt.shape
    L = w_in.shape[0]
    PAST = cache_mla_ckv.shape[2]
    ncore = B
    assert x_sample.shape[0] == NSQ * ncore and x_sample.shape[1] == TS
    key = (S, PAST, L)
    if key not in _CACHE:
        _CACHE[key] = build(S, PAST, L)
    nc = _CACHE[key]
    cst = _consts(S, PAST)
    shared = dict(g_mix=f(g_mix), w_in=f(w_in), g_cq=f(g_cq), g_ckv=f(g_ckv),
                  w_uq=f(w_uq).reshape(L, 256, 768), w_ukv=f(w_ukv).reshape(L, 256, 1024),
                  a_rel=f(a_rel_bias).reshape(L, 768),
                  g_cat=np.ascontiguousarray(np.concatenate([f(g_out_a), f(g_out_mla), f(g_out_sb)], axis=1)),
                  w_out=f(w_out), g_ffn=f(g_ffn), w_up=f(w_up), w_down=f(w_down), g_fin=f(g_final), **cst)
    cak, cav = f(cache_a_k), f(cache_a_v)
    cckv, ckr, csk, csv = f(cache_mla_ckv), f(cache_mla_krope), f(cache_sb_k), f(cache_sb_v)
    in_maps = []
    for c in range(ncore):
        sl = slice(NSQ * c, NSQ * c + NSQ)
        m = dict(shared)
        m["x"] = np.ascontiguousarray(np.concatenate([x_prompt[c], x_sample[sl].reshape(NSQ * TS, D)], axis=0))
        m["cak"] = np.ascontiguousarray(cak[:, sl].reshape(L, NSQ, 512, 256))
        m["cav"] = np.ascontiguousarray(cav[:, sl].reshape(L, NSQ, 512, 256))
        m["cckv"] = np.ascontiguousarray(cckv[:, sl])
        m["ckr"] = np.ascontiguousarray(ckr[:, sl])
        m["csk"] = np.ascontiguousarray(csk[:, sl].reshape(L, NSQ, PAST, 256))
        m["csv"] = np.ascontiguousarray(csv[:, sl].reshape(L, NSQ, PAST, 256))
        in_maps.append(m)
    res = run_bass_kernel_spmd(nc, in_maps, core_ids=list(range(ncore)))
    R = res.results
    cat = lambda k, ax: np.concatenate([np.asarray(r[k]) for r in R], axis=ax)
    yall = np.stack([np.asarray(r["y"]) for r in R], 0)
    y_prompt = np.ascontiguousarray(yall[:, :S])
    y_sample = np.ascontiguousarray(yall[:, S:].reshape(ncore * NSQ, TS, D))
    st1 = lambda k: np.stack([np.asarray(r[k]) for r in R], 1)
    p_a_k = st1("pak").reshape(L, B, 512, 4, 64)
    p_a_v = st1("pav").reshape(L, B, 512, 4, 64)
    p_ckv = st1("pckv")
    p_krope = st1("pkr")
    p_sb_k = st1("psk").reshape(L, B, S, 4, 64)
    p_sb_v = st1("psv").reshape(L, B, S, 4, 64)
    s_a_k = cat("sak", 1).reshape(L, ncore * NSQ, 512, 4, 64)
    s_a_v = cat("sav", 1).reshape(L, ncore * NSQ, 512, 4, 64)
    s_ckv = cat("sckv", 1)
    s_krope = cat("skr", 1)
    s_sb_k = cat("ssk", 1).reshape(L, ncore * NSQ, TS, 4, 64)
    s_sb_v = cat("ssv", 1).reshape(L, ncore * NSQ, TS, 4, 64)
    outs = (y_prompt, y_sample, p_a_k, p_a_v, p_ckv, p_krope, p_sb_k, p_sb_v,
            s_a_k, s_a_v, s_ckv, s_krope, s_sb_k, s_sb_v)
    return tuple(np.ascontiguousarray(o, dtype=np.float32) for o in outs)
```

```python
import os
import numpy as np
DBG = int(os.environ.get('MK_DBG', '0'))
STOPN = int(os.environ.get('MK_STOPN', '-1'))
MAXOUT = int(os.environ.get('MK_MAXOUT', '4'))


class _Stop(Exception):
    pass
from contextlib import ExitStack
import concourse.bass as bass
import concourse.mybir as mybir
from concourse.bass_utils import run_bass_kernel_spmd

F32 = mybir.dt.float32
BF16 = mybir.dt.bfloat16
AF = mybir.ActivationFunctionType
ALU = mybir.AluOpType
AX = mybir.AxisListType

D = 1024
DFF = 4096
NSQ = 4
TS = 64
EPS = 1e-6
INC = 2112


class Buf:
    __slots__ = ("t", "w", "r", "dk", "x")

    def __init__(self, t, x=False):
        self.t = t
        self.w = None
        self.r = {}
        self.dk = None
        self.x = x

    def __getitem__(self, k):
        return self.t[k]


class Sy:
    def __init__(self, nc, st, ndma=int(os.environ.get('MK_NDMA', '72'))):
        self.nc = nc
        self.eng = {"pe": nc.tensor, "act": nc.scalar, "dve": nc.vector, "pool": nc.gpsimd, "sp": nc.sync}
        self.sem, self.cnt = {}, {}
        self.seen = {e: {} for e in self.eng}
        for e in self.eng:
            self.sem[e] = st.enter_context(nc.semaphore("s_" + e))
            self.cnt[e] = 0
        self.dpool = []
        for i in range(ndma):
            k = "d%d" % i
            self.sem[k] = st.enter_context(nc.semaphore(k))
            self.cnt[k] = 0
            self.dpool.append(k)
        self.dnext = 0
        self.ninstr = 0
        self.fifo = []

    def _wait(self, e, waits):
        for k, v in waits:
            if k == "pe" and e == "pe":
                continue
            if self.seen[e].get(k, 0) >= v:
                continue
            self.eng[e].wait_ge(self.sem[k], v)
            self.seen[e][k] = v

    def _deps(self, R, W):
        waits = []
        for b in R:
            if b.w:
                waits.append(b.w)
            if b.x:
                waits.extend(b.r.items())
        for b in W:
            if b.w:
                waits.append(b.w)
            waits.extend(b.r.items())
        return waits

    def do(self, e, fn, R=(), W=()):
        if self.ninstr == STOPN:
            self.barrier()
            raise _Stop()
        self._wait(e, self._deps(R, W))
        ins = fn(self.eng[e])
        self.cnt[e] += 1
        ins.then_inc(self.sem[e], 1)
        v = self.cnt[e]
        for b in R:
            if b.x:
                b.w = (e, v)
                b.r = {}
            else:
                b.r[e] = v
        for b in W:
            b.w = (e, v)
            b.r = {}
        self.ninstr += 1
        return (e, v)

    def dma(self, q, out, in_, R=(), W=(), **kw):
        if self.ninstr == STOPN:
            self.barrier()
            raise _Stop()
        if len(self.fifo) >= MAXOUT:
            self._wait(q, [self.fifo.pop(0)])
        self._wait(q, self._deps(R, W))
        b0 = (list(W) + list(R))[0]
        if b0.dk is None:
            b0.dk = self.dpool[self.dnext % len(self.dpool)]
            self.dnext += 1
        k = b0.dk
        ins = self.eng[q].dma_start(out=out, in_=in_, **kw)
        self.cnt[k] += 16
        ins.then_inc(self.sem[k], 16)
        v = self.cnt[k]
        for b in R:
            b.r[k] = v
        for b in W:
            b.w = (k, v)
            b.r = {}
        self.ninstr += 1
        self.fifo.append((k, v))
        return (k, v)

    def barrier(self):
        allw = [(k, v) for k, v in self.cnt.items() if v > 0]
        for e in self.eng:
            self._wait(e, allw)
        self.dnext = 0


def build(S, PAST, L, stop=None):
    NT = S + NSQ * TS
    NG = S // 512
    NKB = S // 128
    NPB = PAST // 128
    nc = bass.Bass("TRN2", target_bir_lowering=False)

    def din(name, shape, dt=F32):
        return nc.dram_tensor(name, list(shape), dt, kind="ExternalInput").ap()

    def dout(name, shape):
        return nc.dram_tensor(name, list(shape), F32, kind="ExternalOutput").ap()

    def dscr(name, shape, dt=BF16):
        return nc.dram_tensor(name, list(shape), dt).ap()

    xin = din("x", [NT, D])
    cak = din("cak", [L, NSQ, 512, 256]); cav = din("cav", [L, NSQ, 512, 256])
    cckv = din("cckv", [L, NSQ, PAST, 256]); ckr = din("ckr", [L, NSQ, PAST, 64])
    csk = din("csk", [L, NSQ, PAST, 256]); csv = din("csv", [L, NSQ, PAST, 256])
    g_mix = din("g_mix", [L, D]); w_in = din("w_in", [L, D, INC])
    g_cq = din("g_cq", [L, 256]); g_ckv = din("g_ckv", [L, 256])
    w_uq = din("w_uq", [L, 256, 768]); w_ukv = din("w_ukv", [L, 256, 1024])
    a_rel = din("a_rel", [L, 768]); g_cat = din("g_cat", [L, D])
    w_out = din("w_out", [L, D, D]); g_ffn = din("g_ffn", [L, D])
    w_up = din("w_up", [L, D, DFF]); w_down = din("w_down", [L, DFF, D]); g_fin = din("g_fin", [D])
    c_ident = din("c_ident", [128, 128]); c_tri = din("c_tri", [128, 4, 128])
    c_msb = din("c_msb", [128, 4, 512]); c_mml = din("c_mml", [128, 4, 512]); c_val = din("c_val", [128, 8, 512])
    c_ropeT = din("c_ropeT", [NT, 64]); c_ropeF = din("c_ropeF", [64, 2, NT])

    y = dout("y", [NT, D])
    pak = dout("pak", [L, 512, 256]); pav = dout("pav", [L, 512, 256])
    pckv = dout("pckv", [L, S, 256]); pkr = dout("pkr", [L, S, 64])
    psk = dout("psk", [L, S, 256]); psv = dout("psv", [L, S, 256])
    sak = dout("sak", [L, NSQ, 512, 256]); sav = dout("sav", [L, NSQ, 512, 256])
    sckv = dout("sckv", [L, NSQ, TS, 256]); skr = dout("skr", [L, NSQ, TS, 64])
    ssk = dout("ssk", [L, NSQ, TS, 256]); ssv = dout("ssv", [L, NSQ, TS, 256])

    XR = dscr("XR", [NT, D], F32)
    QA = dscr("QA", [256, NT]); KA = dscr("KA", [256, NT]); QC = dscr("QC", [256, NT]); KC = dscr("KC", [256, NT])
    VA = dscr("VA", [NT, 256]); VC = dscr("VC", [NT, 256])
    QN = dscr("QN", [512, NT]); QP = dscr("QP", [256, NT]); KN = dscr("KN", [512, NT]); KR = dscr("KR", [64, NT])
    VM = dscr("VM", [NT, 512])
    KAs = dscr("KAs", [NSQ, 256, 512]); VAs = dscr("VAs", [NSQ, 512, 256])
    KNs = dscr("KNs", [NSQ, 512, PAST]); KRs = dscr("KRs", [NSQ, 64, PAST]); VMs = dscr("VMs", [NSQ, PAST, 512])
    KCs = dscr("KCs", [NSQ, 256, PAST]); VCs = dscr("VCs", [NSQ, PAST, 256])
    OCAT = dscr("OCAT", [D, NT], F32)
    EXT = dscr("EXT", [4, 1536], F32)

    with ExitStack() as st:
        st.push(lambda et, ev, tb_: et is _Stop)
        sy = Sy(nc, st)

        uniq = [0]

        def sb(stk, name, shape, dt):
            uniq[0] += 1
            return Buf(stk.enter_context(nc.sbuf_tensor("%s_%d" % (name, uniq[0]), list(shape), dt)))

        fb = [Buf(st.enter_context(nc.psum_tensor("pf%d" % i, [128, 512], F32)), x=True) for i in range(7)]
        tb = Buf(st.enter_context(nc.psum_tensor("ptb", [128, 1024], BF16)), x=True)

        identb = sb(st, "identb", [128, 128], BF16)
        trib = sb(st, "trib", [128, 4, 128], BF16)
        onesb = sb(st, "onesb", [128, 128], BF16)
        ones1f = sb(st, "ones1f", [1, 128], F32)
        gfinb = sb(st, "gfinb", [128, D], F32)
        EBs = sb(st, "EBs", [128, 4, 5, 64], BF16)
        with ExitStack() as ph:
            t1 = sb(ph, "c_t1", [128, 128], F32)
            t2 = sb(ph, "c_t2", [128, 4, 128], F32)
            sy.dma("sp", t1[:], c_ident, W=[t1])
            sy.dma("sp", t2[:], c_tri, W=[t2])
            sy.dma("sp", gfinb[:], g_fin.partition_broadcast(128), W=[gfinb])
            sy.do("dve", lambda e: e.tensor_copy(out=identb[:], in_=t1[:]), R=[t1], W=[identb])
            sy.do("dve", lambda e: e.tensor_copy(out=trib[:], in_=t2[:]), R=[t2], W=[trib])
            sy.do("dve", lambda e: e.memset(onesb[:], 1.0), W=[onesb])
            sy.do("dve", lambda e: e.memset(ones1f[:], 1.0), W=[ones1f])
            sy.barrier()

        ddb = Buf(None)
        def mm(out_b, out_ap, lhsT_b, lhsT, rhs_b, rhs, start, stop, sgc=False):
            return sy.do("pe", lambda e: e.matmul(out_ap, lhsT=lhsT, rhs=rhs, start=start, stop=stop, skip_group_check=sgc),
                         R=[lhsT_b, rhs_b], W=[out_b])

        cast_rr = [0]

        def load_w(ph, dst_b, views, srcs, width):
            stg = [sb(ph, "wst%d_%d" % (sy.ninstr, i), [128, width], F32) for i in range(2)]
            for i, (v, s_) in enumerate(zip(views, srcs)):
                sg = stg[i % 2]
                sy.dma("sp", sg[:], s_, W=[sg])
                eng = ("pool", "dve")[cast_rr[0] % 2]
                cast_rr[0] += 1
                sy.do(eng, lambda e, v=v, sg=sg: e.tensor_copy(out=v, in_=sg[:]), R=[sg], W=[dst_b])

        def rstd_rows(src_b, src_ap, width, scr_b, ss_b, sq_eng="act"):
            if sq_eng == "act":
                sy.do("act", lambda e: e.activation(out=scr_b[:, 0:width], in_=src_ap, func=AF.Square), R=[src_b], W=[scr_b])
            else:
                sy.do("dve", lambda e: e.tensor_tensor(out=scr_b[:, 0:width], in0=src_ap, in1=src_ap, op=ALU.mult), R=[src_b], W=[scr_b])
            sy.do("dve", lambda e: e.reduce_sum(out=ss_b[:, 0:1], in_=scr_b[:, 0:width], axis=AX.X), R=[scr_b], W=[ss_b])
            sy.do("dve", lambda e: e.tensor_scalar(out=ss_b[:, 0:1], in0=ss_b[:, 0:1], scalar1=1.0 / width, scalar2=EPS,
                                                   op0=ALU.mult, op1=ALU.add), R=[ss_b], W=[ss_b])
            if not (DBG & 4096):
                sy.do("act", lambda e: e.activation(out=ss_b[:, 0:1], in_=ss_b[:, 0:1], func=AF.Sqrt), R=[ss_b], W=[ss_b])
            sy.do("dve", lambda e: e.reciprocal(out=ss_b[:, 0:1], in_=ss_b[:, 0:1]), R=[ss_b], W=[ss_b])

        def ckpt(k):
            if stop == k:
                sy.barrier()
                raise _Stop()

        for l in range(L):
            xsrc = xin if l == 0 else XR
            last = (l == L - 1)
            ckpt(0)

            with ExitStack() as ph:
                win = sb(ph, "win", [128, 8, INC], BF16)
                wuq = sb(ph, "wuq", [128, 2, 768], BF16)
                wuqs = sb(ph, "wuqs", [128, 2, 256], BF16)
                wkn = sb(ph, "wkn", [128, 2, 512], BF16)
                wv = sb(ph, "wv", [128, 2, 512], BF16)
                gmixb = sb(ph, "gmixb", [128, D], F32)
                gcqb = sb(ph, "gcqb", [128, 256], F32)
                gckvb = sb(ph, "gckvb", [128, 256], F32)
                sy.dma("sp", gmixb[:], g_mix[l].partition_broadcast(128), W=[gmixb])
                sy.dma("sp", gcqb[:], g_cq[l].partition_broadcast(128), W=[gcqb])
                sy.dma("sp", gckvb[:], g_ckv[l].partition_broadcast(128), W=[gckvb])
                with ExitStack() as ph2:
                    load_w(ph2, win, [win[:, c, :] for c in range(8)], [w_in[l, c * 128:(c + 1) * 128, :] for c in range(8)], INC)
                    wq32 = sb(ph2, "wq32", [128, 2, 768], F32)
                    wkv32 = sb(ph2, "wkv32", [128, 2, 1024], F32)
                    sy.dma("sp", wq32[:], w_uq[l].rearrange("(k p) e -> p k e", p=128), W=[wq32])
                    sy.dma("sp", wkv32[:], w_ukv[l].rearrange("(k p) e -> p k e", p=128), W=[wkv32])
                    sy.do("dve", lambda e: e.tensor_copy(out=wuq[:], in_=wq32[:]), R=[wq32], W=[wuq])
                    for k in range(2):
                        q4 = wq32[:, k, :].rearrange("p (h e) -> p h e", h=4)
                        d4 = wuqs[:, k, :].rearrange("p (h e) -> p h e", h=4)
                        sy.do("dve", lambda e, q4=q4, d4=d4: e.tensor_copy(out=d4[:, :, 0:32], in_=q4[:, :, 160:192]), R=[wq32], W=[wuqs])
                        sy.do("dve", lambda e, q4=q4, d4=d4: e.tensor_copy(out=d4[:, :, 32:64], in_=q4[:, :, 128:160]), R=[wq32], W=[wuqs])
                        k4 = wkv32[:, k, :].rearrange("p (h e) -> p h e", h=4)
                        sy.do("dve", lambda e, k4=k4, k=k: e.tensor_copy(out=wkn[:, k, :].rearrange("p (h e) -> p h e", h=4), in_=k4[:, :, 0:128]), R=[wkv32], W=[wkn])
                        sy.do("dve", lambda e, k4=k4, k=k: e.tensor_copy(out=wv[:, k, :].rearrange("p (h e) -> p h e", h=4), in_=k4[:, :, 128:256]), R=[wkv32], W=[wv])
                    sy.barrier()

                ckpt(1)
                xt = [sb(ph, "xt%d" % i, [128, D], F32) for i in range(3)]
                rt = [sb(ph, "rt%d" % i, [128, 4, 64], F32) for i in range(2)]
                rf = [sb(ph, "rf0", [64, 2, 512], F32)] * 2
                scr = sb(ph, "scr", [128, D], F32)
                ssx = sb(ph, "ssx", [128, 1], F32)
                ss2 = sb(ph, "ss2", [128, 1], F32)
                ss3 = sb(ph, "ss3", [128, 1], F32)
                hb = [sb(ph, "hb%d" % i, [128, D], BF16) for i in range(2)]
                hT = [sb(ph, "hT0", [128, 8, 512], BF16)] * 2
                fo = [sb(ph, "fo0", [128, 8, 512], BF16)] * 2
                va_b = [sb(ph, "va_b%d" % i, [128, 4, 256], BF16) for i in range(2)]
                vaf = sb(ph, "vaf", [128, 4, 256], F32)
                kaf = sb(ph, "kaf", [128, 4, 256], F32)
                cqn_b = sb(ph, "cqn_b", [128, 256], BF16)
                ckvn_b = sb(ph, "ckvn_b", [128, 256], BF16)
                cqT = [sb(ph, "cqT%d" % i, [128, 2, 512], BF16) for i in range(2)]
                ckvT = [sb(ph, "ckvT%d" % i, [128, 2, 512], BF16) for i in range(2)]
                ckv_f = [sb(ph, "ckv_f%d" % i, [128, 4, 256], F32) for i in range(2)]
                krr = sb(ph, "krr", [128, 64], F32)
                krt = [sb(ph, "krt%d" % i, [128, 32], F32) for i in range(4)]
                kr_f = [sb(ph, "kr_f%d" % i, [128, 4, 64], F32) for i in range(2)]
                kr_b = sb(ph, "kr_b", [128, 64], BF16)
                krT = [sb(ph, "krT%d" % i, [64, 512], BF16) for i in range(2)]
                kcvc = [sb(ph, "kcvc%d" % i, [128, 4, 512], F32) for i in range(2)]
                vc_b = [sb(ph, "vc_b%d" % i, [128, 4, 256], BF16) for i in range(2)]
                qn_b = [sb(ph, "qn_b0", [128, 4, 512], BF16)] * 2
                qp_b = [sb(ph, "qp_b0", [64, 4, 512], BF16)] * 2
                kn_b = [sb(ph, "kn_b0", [128, 4, 512], BF16)] * 2
                vm_b = [sb(ph, "vm_b0", [128, 4, 512], BF16)] * 2
                qt1 = sb(ph, "qt1", [64, 512], F32)
                qt2 = sb(ph, "qt2", [64, 512], F32)
                c32 = [sb(ph, "c32_%d" % i, [128, 4, 256], F32) for i in range(2)]
                cb16 = [sb(ph, "cb16_%d" % i, [128, 4, 256], BF16) for i in range(2)]
                ckT = [sb(ph, "ckT%d" % i, [128, 2, 512], BF16) for i in range(2)]
                kr32 = sb(ph, "kr32", [128, 4, 64], F32)
                krb16 = sb(ph, "krb16", [128, 4, 64], BF16)
                krTc = [sb(ph, "krTc%d" % i, [64, 512], BF16) for i in range(2)]
                print('P1 sbuf remaining', nc.sbuf_bytes_remaining, 'vaf', vaf.t, 'kaf', kaf.t, 'krTc', krTc[1].t)
                ev_rr = [0]
                xti = [0]

                def evac(dst_b, dst_ap, src_b, src_ap):
                    eng = ("act", "dve")[ev_rr[0] % 2]
                    ev_rr[0] += 1
                    if eng == "act":
                        sy.do("act", lambda e: e.activation(out=dst_ap, in_=src_ap, func=AF.Copy), R=[src_b], W=[dst_b])
                    else:
                        sy.do("dve", lambda e: e.tensor_copy(out=dst_ap, in_=src_ap), R=[src_b], W=[dst_b])

                def kv_up(cT, N, ntl, knb, vmb):
                    for h in range(4):
                        bk = fb[4 + h % 2]
                        for k in range(2):
                            mm(bk, bk[:, 0:N], wkn, wkn[:, k, h * 128:(h + 1) * 128], cT, cT[:, k, 0:N], k == 0, k == 1)
                        evac(knb, knb[:, h, 0:N], bk, bk[:, 0:N])
                    for t in range(ntl):
                        bk = fb[4 + t % 2]
                        for k in range(2):
                            mm(bk, bk[:, 0:512], cT, cT[:, k, t * 128:(t + 1) * 128], wv, wv[:, k, :], k == 0, k == 1)
                        evac(vmb, vmb[:, t, :], bk, bk[:, 0:512])

                ci = 0
                for b in range(NSQ):
                    for (src, isk) in ((cak, True), (cav, False)):
                        cb = c32[ci % 2]; bb = cb16[ci % 2]; tt = ckT[ci % 2]; ci += 1
                        sy.dma("sp", cb[:], src[l, b].rearrange("(t p) c -> p t c", p=128), W=[cb])
                        sy.do(("pool", "dve")[ci % 2], lambda e, cb=cb, bb=bb: e.tensor_copy(out=bb[:], in_=cb[:]), R=[cb], W=[bb])
                        if isk:
                            for m in range(2):
                                for t in range(4):
                                    sy.do("pe", lambda e, m=m, t=t, bb=bb: e.transpose(out=tb[:, t * 128:(t + 1) * 128], in_=bb[:, t, m * 128:(m + 1) * 128], identity=identb[:]),
                                          R=[bb, identb], W=[tb])
                                evac(tt, tt[:, m, :], tb, tb[:, 0:512])
                            sy.dma("sp", KAs[b].rearrange("(m p) n -> p m n", p=128), tt[:], R=[tt])
                        else:
                            sy.dma("sp", VAs[b].rearrange("(t p) c -> p t c", p=128), bb[:], R=[bb])
                    for blk in range(PAST // 512):
                        r0 = blk * 512
                        cb = c32[ci % 2]; bb = cb16[ci % 2]; tt = ckT[ci % 2]; pb = ci % 2; ci += 1
                        sy.dma("sp", cb[:], cckv[l, b, r0:r0 + 512, :].rearrange("(t p) c -> p t c", p=128), W=[cb])
                        sy.do(("pool", "dve")[ci % 2], lambda e, cb=cb, bb=bb: e.tensor_copy(out=bb[:], in_=cb[:]), R=[cb], W=[bb])
                        for m in range(2):
                            for t in range(4):
                                sy.do("pe", lambda e, m=m, t=t, bb=bb: e.transpose(out=tb[:, t * 128:(t + 1) * 128], in_=bb[:, t, m * 128:(m + 1) * 128], identity=identb[:]),
                                      R=[bb, identb], W=[tb])
                            evac(tt, tt[:, m, :], tb, tb[:, 0:512])
                        kv_up(tt, 512, 4, kn_b[pb], vm_b[pb])
                        sy.dma("sp", KNs[b][:, r0:r0 + 512].rearrange("(h p) n -> p h n", p=128), kn_b[pb][:], R=[kn_b[pb]])
                        sy.dma("sp", VMs[b][r0:r0 + 512, :].rearrange("(t p) c -> p t c", p=128), vm_b[pb][:], R=[vm_b[pb]])
                        for (src, isk) in ((csk, True), (csv, False)):
                            cb = c32[ci % 2]; bb = cb16[ci % 2]; tt = ckT[ci % 2]; ci += 1
                            sy.dma("sp", cb[:], src[l, b, r0:r0 + 512, :].rearrange("(t p) c -> p t c", p=128), W=[cb])
                            sy.do(("pool", "dve")[ci % 2], lambda e, cb=cb, bb=bb: e.tensor_copy(out=bb[:], in_=cb[:]), R=[cb], W=[bb])
                            if isk:
                                for m in range(2):
                                    for t in range(4):
                                        sy.do("pe", lambda e, m=m, t=t, bb=bb: e.transpose(out=tb[:, t * 128:(t + 1) * 128], in_=bb[:, t, m * 128:(m + 1) * 128], identity=identb[:]),
                                              R=[bb, identb], W=[tb])
                                    evac(tt, tt[:, m, :], tb, tb[:, 0:512])
                                sy.dma("sp", KCs[b][:, r0:r0 + 512].rearrange("(m p) n -> p m n", p=128), tt[:], R=[tt])
                            else:
                                sy.dma("sp", VCs[b][r0:r0 + 512, :].rearrange("(t p) c -> p t c", p=128), bb[:], R=[bb])
                        kt = krTc[blk % 2]
                        sy.dma("sp", kr32[:], ckr[l, b, r0:r0 + 512, :].rearrange("(t p) c -> p t c", p=128), W=[kr32])
                        sy.do("pool", lambda e: e.tensor_copy(out=krb16[:], in_=kr32[:]), R=[kr32], W=[krb16])
                        for t in range(4):
                            sy.do("pe", lambda e, t=t: e.transpose(out=tb[0:64, t * 128:(t + 1) * 128], in_=krb16[:, t, :], identity=identb[:]),
                                  R=[krb16, identb], W=[tb])
                        evac(kt, kt[:, :], tb, tb[0:64, 0:512])
                        sy.dma("sp", KRs[b][:, r0:r0 + 512], kt[:], R=[kt])
                    sy.dma("sp", sak[l, b, 0:448, :], cak[l, b, 64:512, :], W=[ddb])
                    sy.dma("sp", sav[l, b, 0:448, :], cav[l, b, 64:512, :], W=[ddb])

                print('ninstr at P1 start', sy.ninstr)
                ckpt(2)
                groups = [(g * 512, 4, g) for g in range(NG)] + [(S, 2, NG)]
                xpre = {}

                def xload(r0_, t_):
                    X = xt[xti[0] % 3]; xti[0] += 1
                    sy.dma("sp", X[:], xsrc[r0_ + t_ * 128:r0_ + (t_ + 1) * 128, :], W=[X])
                    return X

                for gidx, (r0, ntl, gi) in enumerate(groups):
                    N = ntl * 128
                    pb = gi % 2
                    is_s = (gi == NG)
                    need_a = is_s or (gi == NG - 1)
                    RT, RF, HT = rt[pb], rf[pb], hT[pb]
                    sy.dma("sp", RT[:, 0:ntl, :], c_ropeT[r0:r0 + N, :].rearrange("(t p) d -> p t d", p=128), W=[RT])
                    sy.dma("sp", RF[:, :, 0:N], c_ropeF[:, :, r0:r0 + N], W=[RF])
                    def pre_norm(t, r0=r0, gi=gi):
                        H = hb[t % 2]
                        X = xpre.pop((gi, t), None)
                        if X is None:
                            X = xload(r0, t)
                        rstd_rows(X, X[:], D, scr, ssx)
                        sy.do("dve", lambda e, X=X, H=H: e.scalar_tensor_tensor(out=H[:], in0=X[:], scalar=ssx[:, 0:1], in1=gmixb[:],
                                                                               op0=ALU.mult, op1=ALU.mult), R=[X, ssx, gmixb], W=[H])
                        return H

                    Hs = {0: pre_norm(0)}
                    for t in range(ntl):
                        H = Hs.pop(t)
                        for c in range(8):
                            sy.do("pe", lambda e, c=c, H=H: e.transpose(out=tb[:, c * 128:(c + 1) * 128], in_=H[:, c * 128:(c + 1) * 128], identity=identb[:]),
                                  R=[H, identb], W=[tb])
                        evac(HT, HT[:, :, t * 128:(t + 1) * 128], tb, tb[:, :].rearrange("p (c n) -> p c n", c=8))
                        tm = [(fb[0], 512, 1024), (fb[1], 1024, 1344), (fb[2], 1600, 2112)]
                        if need_a and not (DBG & 1):
                            tm.append((fb[3], 256, 512))
                        for (bk, c0, c1) in tm:
                            for c in range(8):
                                mm(bk, bk[:, 0:c1 - c0], HT, HT[:, c, t * 128:(t + 1) * 128], win, win[:, c, c0:c1], c == 0, c == 7)
                        if t + 1 < ntl:
                            Hs[t + 1] = pre_norm(t + 1)
                        sy.do("act", lambda e, t=t: e.activation(out=va_b[pb][:, t, :], in_=fb[0][:, 0:256], func=AF.Copy), R=[fb[0]], W=[va_b[pb]])
                        if need_a and not (DBG & 2):
                            if DBG & 16:
                                sy.do("dve", lambda e, t=t: e.tensor_copy(out=vaf[:, t, :], in_=gcqb[:]), R=[gcqb], W=[vaf])
                            elif DBG & 8:
                                sy.do("dve", lambda e, t=t: e.tensor_copy(out=scr[:, 0:256], in_=fb[0][:, 0:256]), R=[fb[0]], W=[scr])
                            else:
                                sy.do("dve", lambda e, t=t: e.tensor_copy(out=vaf[:, t, :], in_=fb[0][:, 0:256]), R=[fb[0]], W=[vaf])
                        if need_a and not (DBG & 4):
                            sy.do("act", lambda e, t=t: e.activation(out=kaf[:, t, :], in_=fb[3][:, 0:256], func=AF.Copy), R=[fb[3]], W=[kaf])
                        if not (DBG & 1024):
                            rstd_rows(fb[0], fb[0][:, 256:512], 256, scr, ss2)
                            sy.do("dve", lambda e: e.scalar_tensor_tensor(out=cqn_b[:], in0=fb[0][:, 256:512], scalar=ss2[:, 0:1], in1=gcqb[:],
                                                                          op0=ALU.mult, op1=ALU.mult), R=[fb[0], ss2, gcqb], W=[cqn_b])
                            rstd_rows(fb[1], fb[1][:, 0:256], 256, scr, ss3)
                            sy.do("dve", lambda e, t=t: e.scalar_tensor_tensor(out=ckv_f[pb][:, t, :], in0=fb[1][:, 0:256], scalar=ss3[:, 0:1], in1=gckvb[:],
                                                                               op0=ALU.mult, op1=ALU.mult), R=[fb[1], ss3, gckvb], W=[ckv_f[pb]])
                            sy.do("pool", lambda e, t=t: e.tensor_copy(out=ckvn_b[:], in_=ckv_f[pb][:, t, :]), R=[ckv_f[pb]], W=[ckvn_b])
                            for k in range(2):
                                sy.do("pe", lambda e, k=k: e.transpose(out=tb[:, k * 128:(k + 1) * 128], in_=cqn_b[:, k * 128:(k + 1) * 128], identity=identb[:]),
                                      R=[cqn_b, identb], W=[tb])
                            evac(cqT[pb], cqT[pb][:, :, t * 128:(t + 1) * 128], tb, tb[:, 0:256].rearrange("p (c n) -> p c n", c=2))
                            for k in range(2):
                                sy.do("pe", lambda e, k=k: e.transpose(out=tb[:, k * 128:(k + 1) * 128], in_=ckvn_b[:, k * 128:(k + 1) * 128], identity=identb[:]),
                                      R=[ckvn_b, identb], W=[tb])
                            evac(ckvT[pb], ckvT[pb][:, :, t * 128:(t + 1) * 128], tb, tb[:, 0:256].rearrange("p (c n) -> p c n", c=2))
                        if not (DBG & 64):
                            sy.do("act", lambda e: e.activation(out=krr[:], in_=fb[1][:, 256:320], func=AF.Copy), R=[fb[1]], W=[krr])
                            cs, sn = RT[:, t, 0:32], RT[:, t, 32:64]
                            x1, x2 = krr[:, 0:32], krr[:, 32:64]
                            sy.do("pool", lambda e, cs=cs, x1=x1: e.tensor_tensor(out=krt[0][:], in0=x1, in1=cs, op=ALU.mult), R=[krr, RT], W=[krt[0]])
                            sy.do("pool", lambda e, sn=sn, x2=x2: e.tensor_tensor(out=krt[1][:], in0=x2, in1=sn, op=ALU.mult), R=[krr, RT], W=[krt[1]])
                            sy.do("pool", lambda e, sn=sn, x1=x1: e.tensor_tensor(out=krt[2][:], in0=x1, in1=sn, op=ALU.mult), R=[krr, RT], W=[krt[2]])
                            sy.do("pool", lambda e, cs=cs, x2=x2: e.tensor_tensor(out=krt[3][:], in0=x2, in1=cs, op=ALU.mult), R=[krr, RT], W=[krt[3]])
                            sy.do("pool", lambda e, t=t: e.tensor_tensor(out=kr_f[pb][:, t, 0:32], in0=krt[0][:], in1=krt[1][:], op=ALU.subtract), R=[krt[0], krt[1]], W=[kr_f[pb]])
                            sy.do("pool", lambda e, t=t: e.tensor_tensor(out=kr_f[pb][:, t, 32:64], in0=krt[2][:], in1=krt[3][:], op=ALU.add), R=[krt[2], krt[3]], W=[kr_f[pb]])
                            sy.do("pool", lambda e, t=t: e.tensor_copy(out=kr_b[:], in_=kr_f[pb][:, t, :]), R=[kr_f[pb]], W=[kr_b])
                            sy.do("pe", lambda e: e.transpose(out=tb[0:64, 0:128], in_=kr_b[:], identity=identb[:]), R=[kr_b, identb], W=[tb])
                            evac(krT[pb], krT[pb][:, t * 128:(t + 1) * 128], tb, tb[0:64, 0:128])
                        sy.do("act", lambda e, t=t: e.activation(out=kcvc[pb][:, t, :], in_=fb[2][:, 0:512], func=AF.Copy), R=[fb[2]], W=[kcvc[pb]])
                        sy.do("pool", lambda e, t=t: e.tensor_copy(out=vc_b[pb][:, t, :], in_=kcvc[pb][:, t, 256:512]), R=[kcvc[pb]], W=[vc_b[pb]])
                    if gi == 0: print('ninstr after tiles g0', sy.ninstr)
                    if gi == 0: ckpt(20)
                    if is_s: ckpt(25)
                    for mi, c0 in enumerate([0, 128, 256, 384, 1344, 1472, 1600, 1728]):
                        bk = fb[4 + mi % 2]
                        for c in range(8):
                            mm(bk, bk[:, 0:N], win, win[:, c, c0:c0 + 128], HT, HT[:, c, 0:N], c == 0, c == 7)
                        evac(fo[pb], fo[pb][:, mi, 0:N], bk, bk[:, 0:N])
                    if gi == 0: ckpt(21)
                    if not (DBG & 128):
                        CQ = cqT[pb]
                        for h in range(4):
                            bk = fb[4 + h % 2]
                            for k in range(2):
                                mm(bk, bk[:, 0:N], wuq, wuq[:, k, h * 192:h * 192 + 128], CQ, CQ[:, k, 0:N], k == 0, k == 1)
                            evac(qn_b[pb], qn_b[pb][:, h, 0:N], bk, bk[:, 0:N])
                            for k in range(2):
                                mm(fb[6], fb[6][0:64, 0:N], wuq, wuq[:, k, h * 192 + 128:h * 192 + 192], CQ, CQ[:, k, 0:N], k == 0, k == 1)
                            for k in range(2):
                                mm(fb[3], fb[3][0:64, 0:N], wuqs, wuqs[:, k, h * 64:(h + 1) * 64], CQ, CQ[:, k, 0:N], k == 0, k == 1)
                            sy.do("dve", lambda e: e.tensor_tensor(out=qt1[:, 0:N], in0=fb[6][0:64, 0:N], in1=RF[:, 0, 0:N], op=ALU.mult), R=[fb[6], RF], W=[qt1])
                            sy.do("dve", lambda e: e.tensor_tensor(out=qt2[:, 0:N], in0=fb[3][0:64, 0:N], in1=RF[:, 1, 0:N], op=ALU.mult), R=[fb[3], RF], W=[qt2])
                            sy.do("pool", lambda e, h=h: e.tensor_tensor(out=qp_b[pb][:, h, 0:N], in0=qt1[:, 0:N], in1=qt2[:, 0:N], op=ALU.add), R=[qt1, qt2], W=[qp_b[pb]])
                    if gi == 0: ckpt(22)
                    if not (DBG & 256):
                        kv_up(ckvT[pb], N, ntl, kn_b[pb], vm_b[pb])
                    if gi == 0: ckpt(23)
                    if gidx + 1 < len(groups):
                        nr0, nntl, ngi = groups[gidx + 1]
                        for t_ in range(2):
                            xpre[(ngi, t_)] = xload(nr0, t_)
                    cs_ = slice(r0, r0 + N)
                    for j, dst in enumerate((QA, KA, QC, KC)):
                        sy.dma("sp", dst[:, cs_].rearrange("(m p) n -> p m n", p=128), fo[pb][:, 2 * j:2 * j + 2, 0:N], R=[fo[pb]])
                    sy.dma("sp", QN[:, cs_].rearrange("(h p) n -> p h n", p=128), qn_b[pb][:, :, 0:N], R=[qn_b[pb]])
                    sy.dma("sp", QP[:, cs_].rearrange("(h p) n -> p h n", p=64), qp_b[pb][:, :, 0:N], R=[qp_b[pb]])
                    sy.dma("sp", KN[:, cs_].rearrange("(h p) n -> p h n", p=128), kn_b[pb][:, :, 0:N], R=[kn_b[pb]])
                    sy.dma("sp", KR[:, cs_], krT[pb][:, 0:N], R=[krT[pb]])
                    sy.dma("sp", VM[cs_, :].rearrange("(t p) c -> p t c", p=128), vm_b[pb][:, 0:ntl, :], R=[vm_b[pb]])
                    sy.dma("sp", VA[cs_, :].rearrange("(t p) c -> p t c", p=128), va_b[pb][:, 0:ntl, :], R=[va_b[pb]])
                    sy.dma("sp", VC[cs_, :].rearrange("(t p) c -> p t c", p=128), vc_b[pb][:, 0:ntl, :], R=[vc_b[pb]])
                    if gi == 0: print('ninstr before outs g0', sy.ninstr)
                    if gi == 0: ckpt(24)
                    if not is_s:
                        sy.dma("sp", pckv[l, cs_, :].rearrange("(t p) c -> p t c", p=128), ckv_f[pb][:, 0:ntl, :], R=[ckv_f[pb]])
                        sy.dma("sp", pkr[l, cs_, :].rearrange("(t p) c -> p t c", p=128), kr_f[pb][:, 0:ntl, :], R=[kr_f[pb]])
                        sy.dma("sp", psk[l, cs_, :].rearrange("(t p) c -> p t c", p=128), kcvc[pb][:, 0:ntl, 0:256], R=[kcvc[pb]])
                        sy.dma("sp", psv[l, cs_, :].rearrange("(t p) c -> p t c", p=128), kcvc[pb][:, 0:ntl, 256:512], R=[kcvc[pb]])
                        if gi == 0: ckpt(26)
                        if need_a: ckpt(29)
                        if need_a:
                            sy.dma("sp", pak[l].rearrange("(t p) c -> p t c", p=128), kaf[:, 0:ntl, :], R=[kaf])
                            sy.dma("sp", pav[l].rearrange("(t p) c -> p t c", p=128), vaf[:, 0:ntl, :], R=[vaf])
                        if gi == NG - 1: ckpt(27)
                    else:
                        for b in range(NSQ):
                            t_, p0 = b // 2, (b % 2) * 64
                            sy.dma("sp", sckv[l, b], ckv_f[pb][p0:p0 + 64, t_, :], R=[ckv_f[pb]])
                            sy.dma("sp", skr[l, b], kr_f[pb][p0:p0 + 64, t_, :], R=[kr_f[pb]])
                            sy.dma("sp", ssk[l, b], kcvc[pb][p0:p0 + 64, t_, 0:256], R=[kcvc[pb]])
                            sy.dma("sp", ssv[l, b], kcvc[pb][p0:p0 + 64, t_, 256:512], R=[kcvc[pb]])
                            sy.dma("sp", sak[l, b, 448:512, :], kaf[p0:p0 + 64, t_, :], R=[kaf])
                            sy.dma("sp", sav[l, b, 448:512, :], vaf[p0:p0 + 64, t_, :], R=[vaf])
                    if gi == 1: ckpt(28)
                    if DBG & 32: sy.barrier()
                sy.barrier()
                ckpt(3)

            def soft_res(ph, tag):
                return ([sb(ph, "P%s%d" % (tag, i), [128, 512], BF16) for i in range(4)], sb(ph, "rD" + tag, [128, 512], F32))

            def sb_res(ph, tag):
                return ([sb(ph, "E%s%d" % (tag, i), [128, 512], F32) for i in range(3)],
                        [sb(ph, "L%s%d" % (tag, i), [128, 512], BF16) for i in range(4)],
                        [sb(ph, "X%s%d" % (tag, i), [128, 512], F32) for i in range(2)],
                        [sb(ph, "A%s%d" % (tag, i), [128, 512], BF16) for i in range(4)],
                        [sb(ph, "cr%s%d" % (tag, i), [1, 512], F32) for i in range(2)])

            def attn_soft(res, jobs, scale):
                Sb = [fb[0], fb[1], fb[6]]
                Ob = [fb[2], fb[3]]
                Db = [fb[4], fb[5]]
                Pt, rD = res
                NP_ = len(Pt)
                items = []
                for ji, jb in enumerate(jobs):
                    nb = len(jb["blocks"])
                    for bi, blk in enumerate(jb["blocks"]):
                        items.append((ji, jb, bi, blk, bi == 0, bi == nb - 1))

                def stage_a(t):
                    ji, jb, bi, blk, first, lastb = items[t]
                    N, m = jb["N"], blk["m"]
                    bk = Sb[t % 3]
                    n = len(blk["mms"])
                    for i, (lb, lap, rb, rap) in enumerate(blk["mms"]):
                        mm(bk, bk[0:m, 0:N], lb, lap, rb, rap, i == 0, i == n - 1)

                def stage_b(t):
                    ji, jb, bi, blk, first, lastb = items[t]
                    N, m = jb["N"], blk["m"]
                    bk, P = Sb[t % 3], Pt[t % NP_]
                    sy.do("act", lambda e: e.activation(out=P[0:m, 0:N], in_=bk[0:m, 0:N], func=AF.Exp, scale=scale), R=[bk], W=[P])
                    if blk["mulb"] is not None:
                        sy.do("dve", lambda e: e.tensor_tensor(out=P[0:m, 0:N], in0=P[0:m, 0:N], in1=blk["mulap"], op=ALU.mult),
                              R=[blk["mulb"]], W=[P])

                def stage_f(t):
                    ji, jb, bi, blk, first, lastb = items[t]
                    N, m, ed = jb["N"], blk["m"], jb["e"]
                    P = Pt[t % NP_]
                    O, Dn = Ob[ji % 2], Db[ji % 2]
                    mm(O, O[0:ed, 0:N], blk["vb"], blk["vap"], P, P[0:m, 0:N], first, lastb)
                    mm(Dn, Dn[0:ed, 0:N], onesb, onesb[0:m, 0:ed], P, P[0:m, 0:N], first, lastb)
                    if lastb:
                        sy.do("dve", lambda e: e.reciprocal(out=rD[0:ed, 0:N], in_=Dn[0:ed, 0:N]), R=[Dn], W=[rD])
                        sy.do("dve", lambda e: e.tensor_tensor(out=jb["dstap"], in0=O[0:ed, 0:N], in1=rD[0:ed, 0:N], op=ALU.mult),
                              R=[O, rD], W=[jb["dstb"]])
                        if jb.get("after"):
                            jb["after"]()

                T = len(items)

                def issue_a(t):
                    if items[t][2] == 0 and items[t][1].get("before"):
                        items[t][1]["before"]()
                    stage_a(t)

                if T:
                    issue_a(0)
                for t in range(T + 2):
                    if t + 1 < T:
                        issue_a(t + 1)
                    if t < T:
                        stage_b(t)
                    if 2 <= t:
                        stage_f(t - 2)

            def attn_sb(res, jobs, scale):
                Zb = [fb[0], fb[1], fb[6]]
                Bb = [fb[2], fb[3]]
                Ob = [fb[4], fb[5]]
                Eb, Lb, Xb, Ab, crow = res
                NA_ = len(Ab)
                items = []
                for ji, jb in enumerate(jobs):
                    nb = len(jb["blocks"])
                    for bi, blk in enumerate(jb["blocks"]):
                        items.append((ji, jb, bi, blk, bi == 0, bi == nb - 1))

                def stage_a(t):
                    ji, jb, bi, blk, first, lastb = items[t]
                    N, m = jb["N"], blk["m"]
                    bk = Zb[t % 3]
                    lb, lap, rb, rap = blk["mm"]
                    mm(bk, bk[0:m, 0:N], lb, lap, rb, rap, True, True)

                def stage_b(t):
                    ji, jb, bi, blk, first, lastb = items[t]
                    N, m = jb["N"], blk["m"]
                    bk, E, Lt = Zb[t % 3], Eb[t % 3], Lb[t % 4]
                    sy.do("act", lambda e: e.activation(out=E[0:m, 0:N], in_=bk[0:m, 0:N], func=AF.Exp, scale=scale), R=[bk], W=[E])
                    if blk["mulb"] is not None:
                        sy.do("pool", lambda e: e.tensor_tensor(out=E[0:m, 0:N], in0=E[0:m, 0:N], in1=blk["mulap"], op=ALU.mult),
                              R=[blk["mulb"]], W=[E])
                    sy.do("act", lambda e: e.activation(out=Lt[0:m, 0:N], in_=E[0:m, 0:N], func=AF.Ln, bias=1.0), R=[E], W=[Lt])

                def stage_c(t):
                    ji, jb, bi, blk, first, lastb = items[t]
                    N, m = jb["N"], blk["m"]
                    B, Lt = Bb[ji % 2], Lb[t % 4]
                    mm(B, B[:, 0:N], trib, trib[0:m, 0, :], Lt, Lt[0:m, 0:N], first, True, sgc=True)

                def stage_c2(t):
                    ji, jb, bi, blk, first, lastb = items[t]
                    N, m = jb["N"], blk["m"]
                    B, Lt = Bb[ji % 2], Lb[t % 4]
                    if not lastb:
                        mm(B, B[:, 0:N], trib, trib[0:m, 1, :], Lt, Lt[0:m, 0:N], False, True, sgc=True)

                def stage_d(t):
                    ji, jb, bi, blk, first, lastb = items[t]
                    N, m = jb["N"], blk["m"]
                    B, X, E, A = Bb[ji % 2], Xb[t % 2], Eb[t % 3], Ab[t % NA_]
                    sy.do("act", lambda e: e.activation(out=X[0:m, 0:N], in_=B[0:m, 0:N], func=AF.Exp), R=[B], W=[X])
                    sy.do("dve", lambda e: e.tensor_tensor(out=A[0:m, 0:N], in0=E[0:m, 0:N], in1=X[0:m, 0:N], op=ALU.mult), R=[E, X], W=[A])

                def stage_f(t):
                    ji, jb, bi, blk, first, lastb = items[t]
                    N, m, ed = jb["N"], blk["m"], jb["e"]
                    A, O = Ab[t % NA_], Ob[ji % 2]
                    mm(O, O[0:ed, 0:N], blk["vb"], blk["vap"], A, A[0:m, 0:N], first, lastb)
                    if lastb:
                        sy.do("dve", lambda e: e.tensor_copy(out=jb["dstap"], in_=O[0:ed, 0:N]), R=[O], W=[jb["dstb"]])
                        if jb.get("after"):
                            jb["after"]()

                T = len(items)

                def issue_a(t):
                    if items[t][2] == 0 and items[t][1].get("before"):
                        items[t][1]["before"]()
                    stage_a(t)

                if T:
                    issue_a(0)
                for t in range(T + 3):
                    if t + 1 < T:
                        issue_a(t + 1)
                    if t < T:
                        stage_b(t)
                    if 2 <= t <= T + 1:
                        stage_c2(t - 2)
                    if 1 <= t <= T:
                        stage_c(t - 1)
                        stage_d(t - 1)
                    if 3 <= t:
                        stage_f(t - 3)

            with ExitStack() as ph:
                EB = sb(ph, "EB", [128, 4, 8, 512], BF16)
                with ExitStack() as ph2:
                    rel = sb(ph2, "rel", [1, 768], F32)
                    ext = sb(ph2, "ext", [1, 4, 1536], F32)
                    valb = sb(ph2, "valb", [128, 8, 512], F32)
                    ebr = [sb(ph2, "ebr%d" % i, [128, 512], F32) for i in range(2)]
                    ebrb = [sb(ph2, "ebrb%d" % i, [128, 512], BF16) for i in range(2)]
                    sy.dma("sp", rel[:], a_rel[l:l + 1, :], W=[rel])
                    sy.dma("sp", valb[:], c_val, W=[valb])
                    for h in range(4):
                        sy.do("dve", lambda e, h=h: e.tensor_copy(out=ext[0:1, h, 0:448], in_=rel[0:1, h * 192:h * 192 + 1].to_broadcast([1, 448])), R=[rel], W=[ext])
                        sy.do("dve", lambda e, h=h: e.tensor_copy(out=ext[0:1, h, 448:640], in_=rel[0:1, h * 192:(h + 1) * 192]), R=[rel], W=[ext])
                        sy.do("dve", lambda e, h=h: e.tensor_copy(out=ext[0:1, h, 640:1536], in_=rel[0:1, h * 192 + 191:h * 192 + 192].to_broadcast([1, 896])), R=[rel], W=[ext])
                    sy.do("act", lambda e: e.activation(out=ext[:], in_=ext[:], func=AF.Exp), R=[ext], W=[ext])
                    sy.dma("sp", EXT.rearrange("(o h) n -> o h n", o=1), ext[:], R=[ext])
                    sy.barrier()
                    k = 0
                    for h in range(4):
                        for i in range(8):
                            eb = ebr[k % 2]; ebb = ebrb[k % 2]; bk = fb[k % 2]; k += 1
                            src = bass.AP(EXT.tensor, h * 1536 + 896 - 128 * i, [[1, 128], [1, 512]])
                            sy.dma("sp", eb[:], src, W=[eb])
                            sy.do("pool", lambda e, eb=eb, ebb=ebb: e.tensor_copy(out=ebb[:], in_=eb[:]), R=[eb], W=[ebb])
                            mm(bk, bk[:, :], trib, trib[:, 3, :], ebb, ebb[:], True, True)
                            sy.do("dve", lambda e, h=h, i=i, bk=bk: e.tensor_tensor(out=EB[:, h, i, :], in0=bk[:, :], in1=valb[:, i, :], op=ALU.mult),
                                  R=[bk, valb], W=[EB])
                    sy.do("dve", lambda e: e.tensor_copy(out=EBs[:], in_=EB[:, :, 0:5, 0:64]), R=[EB], W=[EBs])
                    sy.barrier()
                kaT = sb(ph, "kaT", [128, 2, S], BF16)
                vaS = sb(ph, "vaS", [128, NKB, 256], BF16)
                sy.dma("sp", kaT[:], KA[:, 0:S].rearrange("(m p) n -> p m n", p=128), W=[kaT])
                sy.dma("sp", vaS[:], VA[0:S, :].rearrange("(t p) c -> p t c", p=128), W=[vaS])
                qa = [sb(ph, "qa%d" % i, [128, 2, 512], BF16) for i in range(2)]
                ost = [sb(ph, "osta%d" % i, [64, 4, 512], F32) for i in range(2)]
                jobs = []
                for g in range(NG):
                    pb = g % 2
                    for h in range(4):
                        hp, bs = h // 2, (h % 2) * 64
                        blocks = []
                        for i in range(8):
                            kb = 4 * g - 4 + i
                            if kb < 0:
                                continue
                            blocks.append(dict(mms=[(kaT, kaT[bs:bs + 64, hp, kb * 128:(kb + 1) * 128], qa[pb], qa[pb][bs:bs + 64, hp, :])],
                                               m=128, vb=vaS, vap=vaS[:, kb, h * 64:(h + 1) * 64], mulb=EB, mulap=EB[:, h, i, :]))
                        jb = dict(N=512, e=64, blocks=blocks, dstb=ost[pb], dstap=ost[pb][:, h, :])
                        if h == 0:
                            jb["before"] = (lambda g=g, pb=pb: sy.dma("sp", qa[pb][:], QA[:, g * 512:(g + 1) * 512].rearrange("(m p) n -> p m n", p=128), W=[qa[pb]]))
                        if h == 3:
                            jb["after"] = (lambda g=g, pb=pb: sy.dma("sp", OCAT[0:256, g * 512:(g + 1) * 512].rearrange("(h p) n -> p h n", p=64), ost[pb][:], R=[ost[pb]]))
                        jobs.append(jb)
                attn_soft(soft_res(ph, "a"), jobs, 0.125)
                sy.barrier()
                ckpt(4)

            with ExitStack() as ph:
                knT = sb(ph, "knT", [128, 4, S], BF16)
                krS = sb(ph, "krS", [64, S], BF16)
                vmS = sb(ph, "vmS", [128, NKB, 512], BF16)
                mml = sb(ph, "mml", [128, 4, 512], BF16)
                with ExitStack() as ph2:
                    m32 = sb(ph2, "m32", [128, 4, 512], F32)
                    sy.dma("sp", m32[:], c_mml, W=[m32])
                    sy.do("dve", lambda e: e.tensor_copy(out=mml[:], in_=m32[:]), R=[m32], W=[mml])
                    sy.barrier()
                for h in range(4):
                    sy.dma("sp", knT[:, h, :], KN[h * 128:(h + 1) * 128, 0:S], W=[knT])
                sy.dma("sp", krS[:], KR[:, 0:S], W=[krS])
                for q4 in range(0, NKB, 8):
                    sy.dma("sp", vmS[:, q4:q4 + 8, :], VM[q4 * 128:(q4 + 8) * 128, :].rearrange("(t p) c -> p t c", p=128), W=[vmS])
                qn = [sb(ph, "qn%d" % i, [128, 4, 512], BF16) for i in range(2)]
                qp = [sb(ph, "qp%d" % i, [64, 4, 512], BF16) for i in range(2)]
                ost = [sb(ph, "ostm0", [128, 4, 512], F32)] * 2
                jobs = []
                for g in range(NG):
                    pb = g % 2
                    for h in range(4):
                        blocks = []
                        for kb in range(4 * g + 4):
                            r = kb - 4 * g
                            blocks.append(dict(mms=[(knT, knT[:, h, kb * 128:(kb + 1) * 128], qn[pb], qn[pb][:, h, :]),
                                                    (krS, krS[:, kb * 128:(kb + 1) * 128], qp[pb], qp[pb][:, h, :])],
                                               m=128, vb=vmS, vap=vmS[:, kb, h * 128:(h + 1) * 128],
                                               mulb=(mml if r >= 0 else None), mulap=(mml[:, r, :] if r >= 0 else None)))
                        jb = dict(N=512, e=128, blocks=blocks, dstb=ost[pb], dstap=ost[pb][:, h, :])
                        if h == 0:
                            def bf(g=g, pb=pb):
                                sy.dma("sp", qn[pb][:], QN[:, g * 512:(g + 1) * 512].rearrange("(h p) n -> p h n", p=128), W=[qn[pb]])
                                sy.dma("sp", qp[pb][:], QP[:, g * 512:(g + 1) * 512].rearrange("(h p) n -> p h n", p=64), W=[qp[pb]])
                            jb["before"] = bf
                        if h == 3:
                            jb["after"] = (lambda g=g, pb=pb: sy.dma("sp", OCAT[256:768, g * 512:(g + 1) * 512].rearrange("(h p) n -> p h n", p=128), ost[pb][:], R=[ost[pb]]))
                        jobs.append(jb)
                attn_soft(soft_res(ph, "m"), jobs, 192.0 ** -0.5)
                sy.barrier()
                ckpt(5)

            with ExitStack() as ph:
                kcT = sb(ph, "kcT", [128, 2, S], BF16)
                vcS = sb(ph, "vcS", [128, NKB, 256], BF16)
                msb = sb(ph, "msb", [128, 4, 512], F32)
                sy.dma("sp", msb[:], c_msb, W=[msb])
                sy.dma("sp", kcT[:], KC[:, 0:S].rearrange("(m p) n -> p m n", p=128), W=[kcT])
                sy.dma("sp", vcS[:], VC[0:S, :].rearrange("(t p) c -> p t c", p=128), W=[vcS])
                qc = [sb(ph, "qc%d" % i, [128, 2, 512], BF16) for i in range(2)]
                ost = [sb(ph, "osts%d" % i, [64, 4, 512], F32) for i in range(2)]
                jobs = []
                for g in range(NG):
                    pb = g % 2
                    for h in range(4):
                        hp, bs = h // 2, (h % 2) * 64
                        blocks = []
                        for kb in range(4 * g + 3, -1, -1):
                            r = kb - 4 * g
                            blocks.append(dict(mm=(kcT, kcT[bs:bs + 64, hp, kb * 128:(kb + 1) * 128], qc[pb], qc[pb][bs:bs + 64, hp, :]),
                                               m=128, vb=vcS, vap=vcS[:, kb, h * 64:(h + 1) * 64],
                                               mulb=(msb if r >= 0 else None), mulap=(msb[:, r, :] if r >= 0 else None)))
                        jb = dict(N=512, e=64, blocks=blocks, dstb=ost[pb], dstap=ost[pb][:, h, :])
                        if h == 0:
                            jb["before"] = (lambda g=g, pb=pb: sy.dma("sp", qc[pb][:], QC[:, g * 512:(g + 1) * 512].rearrange("(m p) n -> p m n", p=128), W=[qc[pb]]))
                        if h == 3:
                            jb["after"] = (lambda g=g, pb=pb: sy.dma("sp", OCAT[768:1024, g * 512:(g + 1) * 512].rearrange("(h p) n -> p h n", p=64), ost[pb][:], R=[ost[pb]]))
                        jobs.append(jb)
                attn_sb(sb_res(ph, "s"), jobs, 0.125)
                sy.barrier()
                ckpt(6)

            with ExitStack() as ph:
                NQ = NSQ * TS
                KL = PAST + TS
                qaS = sb(ph, "qaS", [128, 2, NQ], BF16); qcS = sb(ph, "qcS", [128, 2, NQ], BF16)
                qnS = sb(ph, "qnS", [128, 4, NQ], BF16); qpS = sb(ph, "qpS", [64, 4, NQ], BF16)
                sy.dma("sp", qaS[:], QA[:, S:NT].rearrange("(m p) n -> p m n", p=128), W=[qaS])
                sy.dma("sp", qcS[:], QC[:, S:NT].rearrange("(m p) n -> p m n", p=128), W=[qcS])
                sy.dma("sp", qnS[:], QN[:, S:NT].rearrange("(h p) n -> p h n", p=128), W=[qnS])
                sy.dma("sp", qpS[:], QP[:, S:NT].rearrange("(h p) n -> p h n", p=64), W=[qpS])
                msb = sb(ph, "msbS", [64, 64], F32)
                sy.dma("sp", msb[:], c_msb[0:64, 0, 0:64], W=[msb])
                osA = sb(ph, "osA", [64, 4, NQ], F32); osM = sb(ph, "osM", [128, 4, NQ], F32); osS = sb(ph, "osS", [64, 4, NQ], F32)
                kaS = [sb(ph, "kaS%d" % i, [128, 2, 576], BF16) for i in range(2)]
                vaS = [sb(ph, "vaSs%d" % i, [128, 5, 256], BF16) for i in range(2)]
                knS = [sb(ph, "knS%d" % i, [128, 4, KL], BF16) for i in range(2)]
                krS = [sb(ph, "krSs%d" % i, [64, KL], BF16) for i in range(2)]
                vmS = [sb(ph, "vmSs%d" % i, [128, NPB + 1, 512], BF16) for i in range(2)]
                kcS = [sb(ph, "kcS%d" % i, [128, 2, KL], BF16) for i in range(2)]
                vcS = [sb(ph, "vcSs%d" % i, [128, NPB + 1, 256], BF16) for i in range(2)]
                ja, jm, js = [], [], []
                for b in range(NSQ):
                    pb = b % 2
                    c0 = S + b * TS
                    qs = slice(b * TS, (b + 1) * TS)

                    def ld(b=b, pb=pb, c0=c0):
                        sy.dma("sp", kaS[pb][:, :, 0:512], KAs[b].rearrange("(m p) n -> p m n", p=128), W=[kaS[pb]])
                        sy.dma("sp", kaS[pb][:, :, 512:576], KA[:, c0:c0 + TS].rearrange("(m p) n -> p m n", p=128), W=[kaS[pb]])
                        sy.dma("sp", vaS[pb][:, 0:4, :], VAs[b].rearrange("(t p) c -> p t c", p=128), W=[vaS[pb]])
                        sy.dma("sp", vaS[pb][0:64, 4, :], VA[c0:c0 + TS, :], W=[vaS[pb]])
                        sy.dma("sp", knS[pb][:, :, 0:PAST], KNs[b].rearrange("(h p) n -> p h n", p=128), W=[knS[pb]])
                        sy.dma("sp", knS[pb][:, :, PAST:KL], KN[:, c0:c0 + TS].rearrange("(h p) n -> p h n", p=128), W=[knS[pb]])
                        sy.dma("sp", krS[pb][:, 0:PAST], KRs[b], W=[krS[pb]])
                        sy.dma("sp", krS[pb][:, PAST:KL], KR[:, c0:c0 + TS], W=[krS[pb]])
                        sy.dma("sp", vmS[pb][:, 0:NPB, :], VMs[b].rearrange("(t p) c -> p t c", p=128), W=[vmS[pb]])
                        sy.dma("sp", vmS[pb][0:64, NPB, :], VM[c0:c0 + TS, :], W=[vmS[pb]])
                        sy.dma("sp", kcS[pb][:, :, 0:PAST], KCs[b].rearrange("(m p) n -> p m n", p=128), W=[kcS[pb]])
                        sy.dma("sp", kcS[pb][:, :, PAST:KL], KC[:, c0:c0 + TS].rearrange("(m p) n -> p m n", p=128), W=[kcS[pb]])
                        sy.dma("sp", vcS[pb][:, 0:NPB, :], VCs[b].rearrange("(t p) c -> p t c", p=128), W=[vcS[pb]])
                        sy.dma("sp", vcS[pb][0:64, NPB, :], VC[c0:c0 + TS, :], W=[vcS[pb]])
                    for h in range(4):
                        hp, bs = h // 2, (h % 2) * 64
                        blocks = []
                        for i in range(5):
                            m = 128 if i < 4 else 64
                            blocks.append(dict(mms=[(kaS[pb], kaS[pb][bs:bs + 64, hp, i * 128:i * 128 + m], qaS, qaS[bs:bs + 64, hp, qs])],
                                               m=m, vb=vaS[pb], vap=vaS[pb][0:m, i, h * 64:(h + 1) * 64], mulb=EBs, mulap=EBs[0:m, h, i, :]))
                        jb = dict(N=TS, e=64, blocks=blocks, dstb=osA, dstap=osA[:, h, qs])
                        if h == 0:
                            jb["before"] = ld
                        ja.append(jb)
                        blocks = []
                        for kb in range(NPB + 1):
                            m = 128 if kb < NPB else 64
                            blocks.append(dict(mms=[(knS[pb], knS[pb][:, h, kb * 128:kb * 128 + m], qnS, qnS[:, h, qs]),
                                                    (krS[pb], krS[pb][:, kb * 128:kb * 128 + m], qpS, qpS[:, h, qs])],
                                               m=m, vb=vmS[pb], vap=vmS[pb][0:m, kb, h * 128:(h + 1) * 128], mulb=None, mulap=None))
                        jm.append(dict(N=TS, e=128, blocks=blocks, dstb=osM, dstap=osM[:, h, qs]))
                        blocks = []
                        for kb in range(NPB, -1, -1):
                            m = 128 if kb < NPB else 64
                            blocks.append(dict(mm=(kcS[pb], kcS[pb][bs:bs + 64, hp, kb * 128:kb * 128 + m], qcS, qcS[bs:bs + 64, hp, qs]),
                                               m=m, vb=vcS[pb], vap=vcS[pb][0:m, kb, h * 64:(h + 1) * 64],
                                               mulb=(msb if kb == NPB else None), mulap=(msb[:, :] if kb == NPB else None)))
                        js.append(dict(N=TS, e=64, blocks=blocks, dstb=osS, dstap=osS[:, h, qs]))
                rsoft, rsb = soft_res(ph, "x"), sb_res(ph, "x")
                for b in range(NSQ):
                    attn_soft(rsoft, ja[4 * b:4 * b + 4], 0.125)
                    attn_soft(rsoft, jm[4 * b:4 * b + 4], 192.0 ** -0.5)
                    attn_sb(rsb, js[4 * b:4 * b + 4], 0.125)
                sy.dma("sp", OCAT[0:256, S:NT].rearrange("(h p) n -> p h n", p=64), osA[:], R=[osA])
                sy.dma("sp", OCAT[256:768, S:NT].rearrange("(h p) n -> p h n", p=128), osM[:], R=[osM])
                sy.dma("sp", OCAT[768:1024, S:NT].rearrange("(h p) n -> p h n", p=64), osS[:], R=[osS])
                sy.barrier()
                ckpt(7)

            with ExitStack() as ph:
                wout = sb(ph, "wout", [128, 8, D], BF16)
                gcat = sb(ph, "gcat", [128, 8], F32)
                sy.dma("sp", gcat[:], g_cat[l].rearrange("(c p) -> p c", p=128), W=[gcat], allow_slow_non_contiguous=True)
                with ExitStack() as ph2:
                    load_w(ph2, wout, [wout[:, c, :] for c in range(8)], [w_out[l, c * 128:(c + 1) * 128, :] for c in range(8)], D)
                    sy.barrier()
                oc = [sb(ph, "oc%d" % i, [128, 8, 512], F32) for i in range(2)]
                xg = [sb(ph, "xg2_%d" % i, [128, 4, D], F32) for i in range(2)]
                sq = sb(ph, "sq", [128, 8, 512], BF16)
                rs = [sb(ph, "rs%d" % i, [128, 512], F32) for i in range(3)]
                catT = [sb(ph, "catT%d" % i, [128, 8, 512], BF16) for i in range(2)]
                def ld2(r0, ntl, gi):
                    N = ntl * 128
                    OC, X = oc[gi % 2], xg[gi % 2]
                    sy.dma("sp", OC[:, :, 0:N], OCAT[:, r0:r0 + N].rearrange("(c p) n -> p c n", p=128), W=[OC])
                    sy.dma("sp", X[:, 0:ntl, :], xsrc[r0:r0 + N, :].rearrange("(t p) d -> p t d", p=128), W=[X])

                ld2(*groups[0])
                for gidx, (r0, ntl, gi) in enumerate(groups):
                    N = ntl * 128
                    pb = gi % 2
                    OC, X, CT = oc[pb], xg[pb], catT[pb]
                    if gidx + 1 < len(groups):
                        ld2(*groups[gidx + 1])
                    sy.do("act", lambda e: e.activation(out=sq[:, :, 0:N], in_=OC[:, :, 0:N], func=AF.Square), R=[OC], W=[sq])
                    for mi, (a0, a1, Wd) in enumerate(((0, 2, 256), (2, 6, 512), (6, 8, 256))):
                        bk = fb[mi]
                        for c in range(a0, a1):
                            mm(bk, bk[:, 0:N], onesb, onesb[:, :], sq, sq[:, c, 0:N], c == a0, c == a1 - 1)
                        R_ = rs[mi]
                        sy.do("dve", lambda e, R_=R_, bk=bk, Wd=Wd: e.tensor_scalar(out=R_[:, 0:N], in0=bk[:, 0:N], scalar1=1.0 / Wd, scalar2=EPS,
                                                                                   op0=ALU.mult, op1=ALU.add), R=[bk], W=[R_])
                        sy.do("act", lambda e, R_=R_: e.activation(out=R_[:, 0:N], in_=R_[:, 0:N], func=AF.Sqrt), R=[R_], W=[R_])
                        sy.do("dve", lambda e, R_=R_: e.reciprocal(out=R_[:, 0:N], in_=R_[:, 0:N]), R=[R_], W=[R_])
                        for c in range(a0, a1):
                            sy.do("dve", lambda e, c=c, R_=R_: e.scalar_tensor_tensor(out=CT[:, c, 0:N], in0=OC[:, c, 0:N], scalar=gcat[:, c:c + 1], in1=R_[:, 0:N],
                                                                                                  op0=ALU.mult, op1=ALU.mult), R=[OC, gcat, R_], W=[CT])
                    for t in range(ntl):
                        for hf in range(2):
                            bk = fb[3 + (2 * t + hf) % 4]
                            for c in range(8):
                                mm(bk, bk[:, :], CT, CT[:, c, t * 128:(t + 1) * 128], wout, wout[:, c, hf * 512:(hf + 1) * 512], c == 0, c == 7)
                            sy.do("dve", lambda e, t=t, hf=hf, bk=bk: e.tensor_tensor(out=X[:, t, hf * 512:(hf + 1) * 512], in0=X[:, t, hf * 512:(hf + 1) * 512], in1=bk[:, :], op=ALU.add),
                                  R=[bk], W=[X])
                    sy.dma("sp", XR[r0:r0 + N, :].rearrange("(t p) d -> p t d", p=128), X[:, 0:ntl, :], R=[X])
                sy.barrier()
                ckpt(8)

            with ExitStack() as ph:
                wup = sb(ph, "wup", [128, 8, DFF], BF16)
                wdn = sb(ph, "wdn", [128, 32, D], BF16)
                gffnb = sb(ph, "gffnb", [128, D], F32)
                sy.dma("sp", gffnb[:], g_ffn[l].partition_broadcast(128), W=[gffnb])
                with ExitStack() as ph2:
                    load_w(ph2, wup, [wup[:, c, hf * 2048:(hf + 1) * 2048] for c in range(8) for hf in range(2)],
                           [w_up[l, c * 128:(c + 1) * 128, hf * 2048:(hf + 1) * 2048] for c in range(8) for hf in range(2)], 2048)
                    load_w(ph2, wdn, [wdn[:, 2 * j:2 * j + 2, :] for j in range(16)],
                           [w_down[l, j * 256:(j + 1) * 256, :].rearrange("(f p) d -> p f d", p=128) for j in range(16)], 2048)
                    sy.barrier()
                xg = [sb(ph, "xg3_%d" % i, [128, 2, D], F32) for i in range(2)]
                scr = sb(ph, "scr3", [128, D], F32)
                ssx = sb(ph, "ssx3", [128, 1], F32)
                hb = [sb(ph, "hb3_%d" % i, [128, D], BF16) for i in range(2)]
                hT = [sb(ph, "hT3_0", [128, 8, 256], BF16)] * 2
                rb = [sb(ph, "rb%d" % i, [128, 256], F32) for i in range(2)]
                uT = [sb(ph, "uT0", [128, 32, 256], BF16)] * 2
                yb = [sb(ph, "yb0", [128, 2, D], F32)] * 2 if last else None
                def ld3(gi):
                    X = xg[gi % 2]
                    sy.dma("sp", X[:], XR[gi * 256:gi * 256 + 256, :].rearrange("(t p) d -> p t d", p=128), W=[X])

                def normA(gi):
                    X = xg[gi % 2]
                    for t in range(2):
                        H = hb[t % 2]
                        rstd_rows(X, X[:, t, :], D, scr, ssx)
                        sy.do("dve", lambda e, t=t, H=H, X=X: e.scalar_tensor_tensor(out=H[:], in0=X[:, t, :], scalar=ssx[:, 0:1], in1=gffnb[:],
                                                                                    op0=ALU.mult, op1=ALU.mult), R=[X, ssx, gffnb], W=[H])

                def normB(gi):
                    HT = hT[gi % 2]
                    for t in range(2):
                        H = hb[t % 2]
                        for c in range(8):
                            sy.do("pe", lambda e, c=c, H=H: e.transpose(out=tb[:, c * 128:(c + 1) * 128], in_=H[:, c * 128:(c + 1) * 128], identity=identb[:]),
                                  R=[H, identb], W=[tb])
                        sy.do("act", lambda e, t=t, HT=HT: e.activation(out=HT[:, :, t * 128:(t + 1) * 128], in_=tb[:, :].rearrange("p (c n) -> p c n", c=8), func=AF.Copy),
                              R=[tb], W=[HT])

                NGF = NT // 256
                ld3(0)
                normA(0)
                normB(0)
                for gi in range(NGF):
                    r0 = gi * 256
                    pb = gi % 2
                    X, HT, UT = xg[pb], hT[pb], uT[pb]
                    if gi + 1 < NGF:
                        ld3(gi + 1)
                    for fc in range(32):
                        bk = fb[fc % 3]
                        for c in range(8):
                            mm(bk, bk[:, 0:256], wup, wup[:, c, fc * 128:(fc + 1) * 128], HT, HT[:, c, :], c == 0, c == 7)
                        R_ = rb[fc % 2]
                        sy.do("act", lambda e, bk=bk, R_=R_: e.activation(out=R_[:], in_=bk[:, 0:256], func=AF.Relu), R=[bk], W=[R_])
                        sy.do("pool", lambda e, fc=fc, R_=R_, UT=UT: e.tensor_tensor(out=UT[:, fc, :], in0=R_[:], in1=R_[:], op=ALU.mult), R=[R_], W=[UT])
                    if gi + 1 < NGF:
                        normA(gi + 1)
                    for t in range(2):
                        for hf in range(2):
                            bk = fb[3 + (2 * t + hf) % 4]
                            for fc in range(32):
                                mm(bk, bk[:, :], UT, UT[:, fc, t * 128:(t + 1) * 128], wdn, wdn[:, fc, hf * 512:(hf + 1) * 512], fc == 0, fc == 31)
                            sy.do("dve", lambda e, t=t, hf=hf, bk=bk, X=X: e.tensor_tensor(out=X[:, t, hf * 512:(hf + 1) * 512], in0=X[:, t, hf * 512:(hf + 1) * 512], in1=bk[:, :], op=ALU.add),
                                  R=[bk], W=[X])
                    if not last:
                        sy.dma("sp", XR[r0:r0 + 256, :].rearrange("(t p) d -> p t d", p=128), X[:], R=[X])
                    else:
                        Y = yb[pb]
                        for t in range(2):
                            rstd_rows(X, X[:, t, :], D, scr, ssx)
                            sy.do("dve", lambda e, t=t, X=X, Y=Y: e.scalar_tensor_tensor(out=Y[:, t, :], in0=X[:, t, :], scalar=ssx[:, 0:1], in1=gfinb[:],
                                                                                        op0=ALU.mult, op1=ALU.mult), R=[X, ssx, gfinb], W=[Y])
                        sy.dma("sp", y[r0:r0 + 256, :].rearrange("(t p) d -> p t d", p=128), Y[:], R=[Y])
                    if gi + 1 < NGF:
                        normB(gi + 1)
                sy.barrier()
                ckpt(9)
        print("instructions emitted:", sy.ninstr, {k: v for k, v in sy.cnt.items() if not k.startswith("d")})
    return nc


def _consts(S, PAST):
    NT = S + NSQ * TS
    j = np.arange(128)[:, None]
    t = np.arange(512)[None, :]
    tri = np.zeros((128, 4, 128), np.float32)
    jj, kk = np.arange(128)[:, None], np.arange(128)[None, :]
    tri[:, 0, :] = -1.0 * (jj >= kk)
    tri[:, 1, :] = -1.0 * (jj < kk)
    tri[:, 2, :] = -1.0
    tri[:, 3, :] = np.eye(128, dtype=np.float32)[::-1]
    msb = np.stack([(j + 128 * r < t) for r in range(4)], 1).astype(np.float32)
    mml = np.stack([((128 * r + j) // 64 <= t // 64) for r in range(4)], 1).astype(np.float32)
    val = []
    for i in range(8):
        d = 8 + t // 64 - (2 * i + j // 64)
        val.append((d >= 0) & (d <= 8))
    val = np.stack(val, 1).astype(np.float32)
    pos = np.concatenate([np.arange(S), np.tile(PAST + np.arange(TS), NSQ)]).astype(np.float32)
    inv = (10000.0 ** (-np.arange(32, dtype=np.float32) / np.float32(32))).astype(np.float32)
    ang = (pos[:, None] * inv[None, :]).astype(np.float32)
    cos, sin = np.cos(ang).astype(np.float32), np.sin(ang).astype(np.float32)
    ropeT = np.concatenate([cos, sin], 1).astype(np.float32)
    cosF = np.concatenate([cos, cos], 1).T
    sinF = np.concatenate([-sin, sin], 1).T
    ropeF = np.ascontiguousarray(np.stack([cosF, sinF], 1)).astype(np.float32)
    return dict(c_ident=np.eye(128, dtype=np.float32), c_tri=tri, c_msb=np.ascontiguousarray(msb),
                c_mml=np.ascontiguousarray(mml), c_val=np.ascontiguousarray(val), c_ropeT=ropeT, c_ropeF=ropeF)


_CACHE = {}


def kernel(x_prompt, x_sample, cache_a_k, cache_a_v, cache_mla_ckv, cache_mla_krope, cache_sb_k, cache_sb_v,
           g_mix, w_in, g_cq, g_ckv, w_uq, w_ukv, a_rel_bias, g_out_a, g_out_mla, g_out_sb, w_out,
           g_ffn, w_up, w_down, g_final):
    f = lambda a: np.ascontiguousarray(np.asarray(a, dtype=np.float32))
    x_prompt, x_sample = f(x_prompt), f(x_sample)
    B, S, _ = x_prompt.shape
    L = w_in.shape[0]
    PAST = cache_mla_ckv.shape[2]
    ncore = B
    assert x_sample.shape[0] == NSQ * ncore and x_sample.shape[1] == TS
    key = (S, PAST, L)
    if key not in _CACHE:
        _CACHE[key] = build(S, PAST, L)
    nc = _CACHE[key]
    cst = _consts(S, PAST)
    shared = dict(g_mix=f(g_mix), w_in=f(w_in), g_cq=f(g_cq), g_ckv=f(g_ckv),
                  w_uq=f(w_uq).reshape(L, 256, 768), w_ukv=f(w_ukv).reshape(L, 256, 1024),
                  a_rel=f(a_rel_bias).reshape(L, 768),
                  g_cat=np.ascontiguousarray(np.concatenate([f(g_out_a), f(g_out_mla), f(g_out_sb)], axis=1)),
                  w_out=f(w_out), g_ffn=f(g_ffn), w_up=f(w_up), w_down=f(w_down), g_fin=f(g_final), **cst)
    cak, cav = f(cache_a_k), f(cache_a_v)
    cckv, ckr, csk, csv = f(cache_mla_ckv), f(cache_mla_krope), f(cache_sb_k), f(cache_sb_v)
    in_maps = []
    for c in range(ncore):
        sl = slice(NSQ * c, NSQ * c + NSQ)
        m = dict(shared)
        m["x"] = np.ascontiguousarray(np.concatenate([x_prompt[c], x_sample[sl].reshape(NSQ * TS, D)], axis=0))
        m["cak"] = np.ascontiguousarray(cak[:, sl].reshape(L, NSQ, 512, 256))
        m["cav"] = np.ascontiguousarray(cav[:, sl].reshape(L, NSQ, 512, 256))
        m["cckv"] = np.ascontiguousarray(cckv[:, sl])
        m["ckr"] = np.ascontiguousarray(ckr[:, sl])
        m["csk"] = np.ascontiguousarray(csk[:, sl].reshape(L, NSQ, PAST, 256))
        m["csv"] = np.ascontiguousarray(csv[:, sl].reshape(L, NSQ, PAST, 256))
        in_maps.append(m)
    res = run_bass_kernel_spmd(nc, in_maps, core_ids=list(range(ncore)))
    R = res.results
    cat = lambda k, ax: np.concatenate([np.asarray(r[k]) for r in R], axis=ax)
    yall = np.stack([np.asarray(r["y"]) for r in R], 0)
    y_prompt = np.ascontiguousarray(yall[:, :S])
    y_sample = np.ascontiguousarray(yall[:, S:].reshape(ncore * NSQ, TS, D))
    st1 = lambda k: np.stack([np.asarray(r[k]) for r in R], 1)
    p_a_k = st1("pak").reshape(L, B, 512, 4, 64)
    p_a_v = st1("pav").reshape(L, B, 512, 4, 64)
    p_ckv = st1("pckv")
    p_krope = st1("pkr")
    p_sb_k = st1("psk").reshape(L, B, S, 4, 64)
    p_sb_v = st1("psv").reshape(L, B, S, 4, 64)
    s_a_k = cat("sak", 1).reshape(L, ncore * NSQ, 512, 4, 64)
    s_a_v = cat("sav", 1).reshape(L, ncore * NSQ, 512, 4, 64)
    s_ckv = cat("sckv", 1)
    s_krope = cat("skr", 1)
    s_sb_k = cat("ssk", 1).reshape(L, ncore * NSQ, TS, 4, 64)
    s_sb_v = cat("ssv", 1).reshape(L, ncore * NSQ, TS, 4, 64)
    outs = (y_prompt, y_sample, p_a_k, p_a_v, p_ckv, p_krope, p_sb_k, p_sb_v,
            s_a_k, s_a_v, s_ckv, s_krope, s_sb_k, s_sb_v)
    return tuple(np.ascontiguousarray(o, dtype=np.float32) for o in outs)
```
